# Optimizing a Trainium2 kernel written in Bass

```python
import math
import jax
import jax.numpy as jnp
from jax import lax
import numpy as np

D_MODEL = 1024
BATCH = 4
SEQ = 4096
DEPTH = 4
DEC_BATCH = 16
DEC_SEQ = 64
PAST_LEN = 4096

CHUNK = 64
N_MIXERS = 3
N_HEADS = 16
N_KV_HEADS = 4
HEAD_DIM = 64
GQA_GROUP = N_HEADS // N_KV_HEADS
WINDOW = 128
WIN_CHUNKS = WINDOW // CHUNK
NUM_BUCKETS = 32
MAX_DISTANCE = 128
CONV_WIDTH = 31
MIX_BLOCK = 128
CMLP_GROUPS = 4
CMLP_DIM = 2 * D_MODEL
D_FF = 2816
FFN_CONV_WIDTH = 3
N_ATTN_LAYERS = (DEPTH + 2) // 3
N_CONV_LAYERS = (DEPTH + 1) // 3
N_CMLP_LAYERS = DEPTH // 3
RMS_EPS = 1e-6
LN_EPS = 1e-5

kernel_name = 'hybrid_chunk_causal_streaming_encoder_step'


def _rmsnorm(x, g):
    xf = x.astype(jnp.float32)
    y = xf * lax.rsqrt(jnp.mean(xf * xf, axis=-1, keepdims=True) + RMS_EPS)
    return (y * g.astype(jnp.float32)).astype(x.dtype)


def _layernorm(x, g, b):
    xf = x.astype(jnp.float32)
    mu = jnp.mean(xf, axis=-1, keepdims=True)
    var = jnp.mean(jnp.square(xf - mu), axis=-1, keepdims=True)
    y = (xf - mu) * lax.rsqrt(var + LN_EPS) * g.astype(jnp.float32) + b.astype(jnp.float32)
    return y.astype(x.dtype)


def _t5_bucket(rel):
    half = NUM_BUCKETS // 2
    max_exact = half // 2
    n = jnp.abs(rel)
    log_ratio = jnp.log(jnp.maximum(n, 1).astype(jnp.float32) / max_exact) / math.log(MAX_DISTANCE / max_exact)
    large = jnp.minimum(max_exact + (log_ratio * (half - max_exact)).astype(jnp.int32), half - 1)
    return jnp.where(rel > 0, half, 0) + jnp.where(n < max_exact, n, large)


def _rel_bias(table, q_pos, k_pos):
    b = table[_t5_bucket(k_pos[None, :] - q_pos[:, None])]
    b = jnp.transpose(b, (2, 0, 1)).astype(jnp.float32)
    return b.reshape(N_KV_HEADS, GQA_GROUP, q_pos.shape[0], k_pos.shape[0])


def _sink_attend(q, k, v, bias, valid, sinks):
    s = jnp.einsum('...qkgd,...jkd->...kgqj', q, k, preferred_element_type=jnp.float32)
    s = jnp.where(valid, s * (HEAD_DIM ** -0.5) + bias, -jnp.inf)
    sink = sinks.astype(jnp.float32).reshape(N_KV_HEADS, GQA_GROUP, 1, 1)
    m = jnp.maximum(jnp.max(s, axis=-1, keepdims=True), sink)
    p = jnp.exp(s - m)
    p = p / (jnp.sum(p, axis=-1, keepdims=True) + jnp.exp(sink - m))
    return jnp.einsum('...kgqj,...jkd->...qkgd', p.astype(v.dtype), v)


def _qkv(h, w_qkv, b_qkv):
    qkv = h @ w_qkv + b_qkv
    return jnp.split(qkv, [N_HEADS * HEAD_DIM, (N_HEADS + N_KV_HEADS) * HEAD_DIM], axis=-1)


def _attn_prompt(h, w_qkv, b_qkv, w_o, b_o, sinks, rel_table):
    B, S, _ = h.shape
    nc = S // CHUNK
    q, k, v = _qkv(h, w_qkv, b_qkv)
    q = q.reshape(B, nc, CHUNK, N_KV_HEADS, GQA_GROUP, HEAD_DIM)
    k = k.reshape(B, S, N_KV_HEADS, HEAD_DIM)
    v = v.reshape(B, S, N_KV_HEADS, HEAD_DIM)
    pad = ((0, 0), (WINDOW, 0), (0, 0), (0, 0))
    kc = jnp.pad(k, pad).reshape(B, nc + WIN_CHUNKS, CHUNK, N_KV_HEADS, HEAD_DIM)
    vc = jnp.pad(v, pad).reshape(B, nc + WIN_CHUNKS, CHUNK, N_KV_HEADS, HEAD_DIM)
    kb = jnp.concatenate([kc[:, i:i + nc] for i in range(WIN_CHUNKS + 1)], axis=2)
    vb = jnp.concatenate([vc[:, i:i + nc] for i in range(WIN_CHUNKS + 1)], axis=2)
    q_pos = jnp.arange(CHUNK, dtype=jnp.int32)
    k_pos = jnp.arange(WINDOW + CHUNK, dtype=jnp.int32) - WINDOW
    bias = _rel_bias(rel_table, q_pos, k_pos)
    valid = (jnp.arange(nc, dtype=jnp.int32)[:, None] * CHUNK + k_pos[None, :]) >= 0
    valid = valid[:, None, None, None, :]
    o = _sink_attend(q, kb, vb, bias, valid, sinks)
    y = o.reshape(B, S, N_HEADS * HEAD_DIM) @ w_o + b_o
    return y, k[:, -WINDOW:], v[:, -WINDOW:]


def _attn_sample(h, cache_k, cache_v, w_qkv, b_qkv, w_o, b_o, sinks, rel_table):
    B, T, _ = h.shape
    W = cache_k.shape[1]
    q, k, v = _qkv(h, w_qkv, b_qkv)
    q = q.reshape(B, T, N_KV_HEADS, GQA_GROUP, HEAD_DIM)
    k = k.reshape(B, T, N_KV_HEADS, HEAD_DIM)
    v = v.reshape(B, T, N_KV_HEADS, HEAD_DIM)
    kk = jnp.concatenate([cache_k, k], axis=1)
    vv = jnp.concatenate([cache_v, v], axis=1)
    q_pos = jnp.arange(T, dtype=jnp.int32)
    k_pos = jnp.arange(W + T, dtype=jnp.int32) - W
    bias = _rel_bias(rel_table, q_pos, k_pos)
    o = _sink_attend(q, kk, vv, bias, True, sinks)
    y = o.reshape(B, T, N_HEADS * HEAD_DIM) @ w_o + b_o
    return y, k, v


def _dwconv(xp, w, b):
    y = lax.conv_general_dilated(xp, w[:, None, :], (1,), 'VALID',
                                 dimension_numbers=('NWC', 'WIO', 'NWC'),
                                 feature_group_count=xp.shape[-1])
    return y + b


def _conv_module(h, state, w_pw1, b_pw1, w_dw, b_dw, ln_g, ln_b, w_pw2, b_pw2):
    a, g = jnp.split(h @ w_pw1 + b_pw1, 2, axis=-1)
    glu = a * jax.nn.sigmoid(g)
    xp = jnp.concatenate([state, glu], axis=1)
    c = jax.nn.silu(_layernorm(_dwconv(xp, w_dw, b_dw), ln_g, ln_b))
    return c @ w_pw2 + b_pw2, xp[:, -(CONV_WIDTH - 1):]


def _chunk_mlp(h, w_in, b_in, ln_g, ln_b, w_s, b_s, w_out, b_out):
    B, T, _ = h.shape
    L = min(T, MIX_BLOCK)
    n = T // L
    u, v = jnp.split(jax.nn.gelu(h @ w_in + b_in), 2, axis=-1)
    v = _layernorm(v, ln_g, ln_b)
    vb = v.reshape(B, n, L, CMLP_GROUPS, CMLP_DIM // CMLP_GROUPS)
    pos = jnp.arange(L)
    mask = (pos[None, :] // CHUNK) <= (pos[:, None] // CHUNK)
    ws = jnp.where(mask, w_s[:, :L, :L], 0.0)
    gate = jnp.einsum('gij,bnjgc->bnigc', ws, vb) + b_s[:, :L].T[None, None, :, :, None]
    y = u * gate.reshape(B, T, CMLP_DIM)
    return y @ w_out + b_out, v


def _conv_ffn(h, state, w_up, w_dw, b_dw, w_down):
    up = h @ w_up
    xp = jnp.concatenate([state, up], axis=1)
    g, u = jnp.split(_dwconv(xp, w_dw, b_dw), 2, axis=-1)
    return (jax.nn.gelu(g) * u) @ w_down, xp[:, -(FFN_CONV_WIDTH - 1):]


def _trunk(x, caches, p):
    prompt = caches is None
    B = x.shape[0]
    new_k, new_v, new_conv, new_cv, new_ffn = [], [], [], [], []
    for i in range(DEPTH):
        kind, j = i % N_MIXERS, i // N_MIXERS
        g = p['norm_gain'][i]
        h = _rmsnorm(x, g[0])
        if kind == 0:
            args = (p['attn_w_qkv'][j], p['attn_b_qkv'][j], p['attn_w_o'][j], p['attn_b_o'][j],
                    p['attn_sinks'][j], p['rel_bias_table'])
            if prompt:
                y, k_rows, v_rows = _attn_prompt(h, *args)
            else:
                y, k_rows, v_rows = _attn_sample(h, caches[0][j], caches[1][j], *args)
            new_k.append(k_rows)
            new_v.append(v_rows)
        elif kind == 1:
            st = jnp.zeros((B, CONV_WIDTH - 1, D_MODEL), x.dtype) if prompt else caches[2][j]
            y, s = _conv_module(h, st, p['conv_w_pw1'][j], p['conv_b_pw1'][j], p['conv_w_dw'][j],
                                p['conv_b_dw'][j], p['conv_ln_g'][j], p['conv_ln_b'][j],
                                p['conv_w_pw2'][j], p['conv_b_pw2'][j])
            new_conv.append(s)
        else:
            y, v_rows = _chunk_mlp(h, p['cmlp_w_in'][j], p['cmlp_b_in'][j], p['cmlp_ln_g'][j],
                                   p['cmlp_ln_b'][j], p['cmlp_w_s'][j], p['cmlp_b_s'][j],
                                   p['cmlp_w_out'][j], p['cmlp_b_out'][j])
            if not prompt:
                new_cv.append(v_rows)
        x = x + _rmsnorm(y, g[1])
        h = _rmsnorm(x, g[2])
        st = jnp.zeros((B, FFN_CONV_WIDTH - 1, 2 * D_FF), x.dtype) if prompt else caches[3][i]
        y, s = _conv_ffn(h, st, p['ffn_w_up'][i], p['ffn_w_dw'][i], p['ffn_b_dw'][i], p['ffn_w_down'][i])
        new_ffn.append(s)
        x = x + _rmsnorm(y, g[3])
    cv = jnp.stack(new_cv) if new_cv else None
    return x, jnp.stack(new_k), jnp.stack(new_v), jnp.stack(new_conv), cv, jnp.stack(new_ffn)


def setup_inputs(seed: int = 0) -> dict:
    key = jax.random.key(seed)
    ks = iter(jax.random.split(key, 40))

    def nrm(shape, scale):
        return jax.random.normal(next(ks), shape, jnp.float32) * scale

    win = min(WINDOW, PAST_LEN)
    qkv_dim = (N_HEADS + 2 * N_KV_HEADS) * HEAD_DIM
    na, nc, nm = N_ATTN_LAYERS, N_CONV_LAYERS, N_CMLP_LAYERS
    return {
        'x_prompt': nrm((BATCH, SEQ, D_MODEL), 1.0),
        'x_sample': nrm((DEC_BATCH, DEC_SEQ, D_MODEL), 1.0),
        'cache_attn_k': nrm((na, DEC_BATCH, win, N_KV_HEADS, HEAD_DIM), 1.0),
        'cache_attn_v': nrm((na, DEC_BATCH, win, N_KV_HEADS, HEAD_DIM), 1.0),
        'state_conv': nrm((nc, DEC_BATCH, CONV_WIDTH - 1, D_MODEL), 0.5),
        'state_ffn_conv': nrm((DEPTH, DEC_BATCH, FFN_CONV_WIDTH - 1, 2 * D_FF), 1.0),
        'rel_bias_table': nrm((NUM_BUCKETS, N_HEADS), 0.5),
        'norm_gain': 1.0 + nrm((DEPTH, 4, D_MODEL), 0.05),
        'attn_w_qkv': nrm((na, D_MODEL, qkv_dim), D_MODEL ** -0.5),
        'attn_b_qkv': nrm((na, qkv_dim), 0.02),
        'attn_w_o': nrm((na, N_HEADS * HEAD_DIM, D_MODEL), (N_HEADS * HEAD_DIM) ** -0.5),
        'attn_b_o': nrm((na, D_MODEL), 0.02),
        'attn_sinks': nrm((na, N_HEADS), 1.0),
        'conv_w_pw1': nrm((nc, D_MODEL, 2 * D_MODEL), D_MODEL ** -0.5),
        'conv_b_pw1': nrm((nc, 2 * D_MODEL), 0.02),
        'conv_w_dw': nrm((nc, CONV_WIDTH, D_MODEL), CONV_WIDTH ** -0.5),
        'conv_b_dw': nrm((nc, D_MODEL), 0.02),
        'conv_ln_g': 1.0 + nrm((nc, D_MODEL), 0.05),
        'conv_ln_b': nrm((nc, D_MODEL), 0.02),
        'conv_w_pw2': nrm((nc, D_MODEL, D_MODEL), D_MODEL ** -0.5),
        'conv_b_pw2': nrm((nc, D_MODEL), 0.02),
        'cmlp_w_in': nrm((nm, D_MODEL, 2 * CMLP_DIM), D_MODEL ** -0.5),
        'cmlp_b_in': nrm((nm, 2 * CMLP_DIM), 0.02),
        'cmlp_ln_g': 1.0 + nrm((nm, CMLP_DIM), 0.05),
        'cmlp_ln_b': nrm((nm, CMLP_DIM), 0.02),
        'cmlp_w_s': nrm((nm, CMLP_GROUPS, MIX_BLOCK, MIX_BLOCK), MIX_BLOCK ** -0.5),
        'cmlp_b_s': 1.0 + nrm((nm, CMLP_GROUPS, MIX_BLOCK), 0.1),
        'cmlp_w_out': nrm((nm, CMLP_DIM, D_MODEL), CMLP_DIM ** -0.5),
        'cmlp_b_out': nrm((nm, D_MODEL), 0.02),
        'ffn_w_up': nrm((DEPTH, D_MODEL, 2 * D_FF), D_MODEL ** -0.5),
        'ffn_w_dw': nrm((DEPTH, FFN_CONV_WIDTH, 2 * D_FF), FFN_CONV_WIDTH ** -0.5),
        'ffn_b_dw': nrm((DEPTH, 2 * D_FF), 0.02),
        'ffn_w_down': nrm((DEPTH, D_FF, D_MODEL), D_FF ** -0.5),
    }


def reference(x_prompt, x_sample, cache_attn_k, cache_attn_v, state_conv, state_ffn_conv,
              rel_bias_table, norm_gain, attn_w_qkv, attn_b_qkv, attn_w_o, attn_b_o, attn_sinks,
              conv_w_pw1, conv_b_pw1, conv_w_dw, conv_b_dw, conv_ln_g, conv_ln_b, conv_w_pw2,
              conv_b_pw2, cmlp_w_in, cmlp_b_in, cmlp_ln_g, cmlp_ln_b, cmlp_w_s, cmlp_b_s,
              cmlp_w_out, cmlp_b_out, ffn_w_up, ffn_w_dw, ffn_b_dw, ffn_w_down):
    p = {
        'rel_bias_table': rel_bias_table, 'norm_gain': norm_gain,
        'attn_w_qkv': attn_w_qkv, 'attn_b_qkv': attn_b_qkv, 'attn_w_o': attn_w_o,
        'attn_b_o': attn_b_o, 'attn_sinks': attn_sinks,
        'conv_w_pw1': conv_w_pw1, 'conv_b_pw1': conv_b_pw1, 'conv_w_dw': conv_w_dw,
        'conv_b_dw': conv_b_dw, 'conv_ln_g': conv_ln_g, 'conv_ln_b': conv_ln_b,
        'conv_w_pw2': conv_w_pw2, 'conv_b_pw2': conv_b_pw2,
        'cmlp_w_in': cmlp_w_in, 'cmlp_b_in': cmlp_b_in, 'cmlp_ln_g': cmlp_ln_g,
        'cmlp_ln_b': cmlp_ln_b, 'cmlp_w_s': cmlp_w_s, 'cmlp_b_s': cmlp_b_s,
        'cmlp_w_out': cmlp_w_out, 'cmlp_b_out': cmlp_b_out,
        'ffn_w_up': ffn_w_up, 'ffn_w_dw': ffn_w_dw, 'ffn_b_dw': ffn_b_dw, 'ffn_w_down': ffn_w_down,
    }
    y_prompt, p_attn_k, p_attn_v, p_conv, _, p_ffn_conv = _trunk(x_prompt, None, p)
    y_sample, s_attn_k, s_attn_v, s_conv, s_cmlp_v, s_ffn_conv = _trunk(
        x_sample, (cache_attn_k, cache_attn_v, state_conv, state_ffn_conv), p)
    return (y_prompt, y_sample, p_attn_k, p_attn_v, p_conv, p_ffn_conv,
            s_attn_k, s_attn_v, s_conv, s_cmlp_v, s_ffn_conv)
```

```python
import contextlib
import numpy as np
import concourse.bass as bass
import concourse.mybir as mybir
from concourse.bass_utils import run_bass_kernel_spmd

F32 = mybir.dt.float32
BF16 = mybir.dt.bfloat16
AF = mybir.ActivationFunctionType
ALU = mybir.AluOpType

ENGS = ("pe", "act", "dve", "pool", "sp")
NPR = 1152
WIN = 384
TP = 2304
NTOK = 2432
DFF = 2816
NFC = 44


class Buf:
    __slots__ = ("name", "w", "r")

    def __init__(self, name, fence=None):
        self.name = name
        self.w = None
        self.r = list(fence) if fence else []


class Op:
    __slots__ = ("eng", "fn", "deps", "dma", "has_dep", "tok", "id")


class Sched:
    def __init__(self, nc, n_dma_sems=24):
        self.nc = nc
        self.ops = []
        self.per_eng = {e: [] for e in ENGS}
        self.n_dma_sems = n_dma_sems

    def op(self, eng, fn, reads=(), writes=(), dma=False):
        o = Op()
        o.eng, o.fn, o.dma, o.has_dep, o.tok, o.id = eng, fn, dma, False, None, len(self.ops)
        deps = set()
        for b in reads:
            if b.w is not None:
                deps.add(b.w)
        for b in writes:
            if b.w is not None:
                deps.add(b.w)
            deps.update(b.r)
        if eng == "pe":
            deps = {dd for dd in deps if self.ops[dd].eng != "pe"}
        o.deps = deps
        for b in reads:
            b.r.append(o.id)
        for b in writes:
            b.w = o.id
            b.r = []
        self.ops.append(o)
        self.per_eng[eng].append(o)
        return o

    def emit(self, final_wait_ops=()):
        nc, ops = self.nc, self.ops
        for o in ops:
            for d in o.deps:
                ops[d].has_dep = True
        for o in final_wait_ops:
            o.has_dep = True
        with contextlib.ExitStack() as st:
            esem = {e: st.enter_context(nc.semaphore("s_" + e)) for e in ENGS}
            dsems = {e: [st.enter_context(nc.semaphore("d_%s_%d" % (e, i)))
                         for i in range(self.n_dma_sems)] for e in ("sp", "pool")}
            ecnt = {e: 0 for e in ENGS}
            dcnt = {e: [0] * self.n_dma_sems for e in dsems}
            drr = {e: 0 for e in dsems}
            for e in ENGS:
                for o in self.per_eng[e]:
                    if not o.has_dep:
                        continue
                    if o.dma:
                        j = drr[e]
                        drr[e] = (j + 1) % self.n_dma_sems
                        prev = dcnt[e][j]
                        dcnt[e][j] += 16
                        o.tok = (dsems[e][j], dcnt[e][j], prev)
                    else:
                        ecnt[e] += 1
                        o.tok = (esem[e], ecnt[e], None)
            block = st.enter_context(nc.Block())

            def run(e, eng):
                waited = {}
                for o in self.per_eng[e]:
                    need = {}
                    for d in o.deps:
                        sem, val, _ = ops[d].tok
                        k = id(sem)
                        if k not in need or need[k][1] < val:
                            need[k] = (sem, val)
                    if o.dma and o.tok is not None and o.tok[2]:
                        sem, _, prev = o.tok
                        k = id(sem)
                        if k not in need or need[k][1] < prev:
                            need[k] = (sem, prev)
                    for k, (sem, val) in need.items():
                        if waited.get(k, 0) >= val:
                            continue
                        eng.wait_ge(sem, val)
                        waited[k] = val
                    ins = o.fn(eng)
                    if o.tok is not None:
                        ins.then_inc(o.tok[0], 16 if o.dma else 1)
                if e == "sp":
                    for o in final_wait_ops:
                        sem, val, _ = o.tok
                        eng.wait_ge(sem, val)

            @block.tensor
            def _(eng):
                run("pe", eng)

            @block.scalar
            def _(eng):
                run("act", eng)

            @block.vector
            def _(eng):
                run("dve", eng)

            @block.gpsimd
            def _(eng):
                run("pool", eng)

            @block.sync
            def _(eng):
                run("sp", eng)


class Ring:
    def __init__(self, name, aps, bufs):
        self.aps = aps
        self.bufs = bufs
        self.i = 0

    def next(self):
        j = self.i
        self.i = (j + 1) % len(self.aps)
        return self.aps[j], self.bufs[j]


def q_lo(j):
    return j if j < 4 else 8 + (j - 4)


def q_up(j):
    return 4 + j if j < 4 else 12 + (j - 4)


def t5_bucket_np(rel):
    half, max_exact = 16, 8
    n = np.abs(rel)
    log_ratio = np.log(np.maximum(n, 1).astype(np.float32) / np.float32(max_exact)) / np.float32(np.log(128 / max_exact))
    large = np.minimum(max_exact + (log_ratio * np.float32(half - max_exact)).astype(np.int32), half - 1)
    return np.where(rel > 0, half, 0) + np.where(n < max_exact, n, large)


def onehot2_const():
    import ml_dtypes
    oh = onehot_const()
    return np.ascontiguousarray(np.concatenate([oh, oh], axis=0).astype(ml_dtypes.bfloat16))


def onehot_const():
    q = np.arange(64)[:, None]
    j = np.arange(192)[None, :]
    bk = t5_bucket_np((j - 128) - q)
    oh = np.zeros((32, 64, 192), np.float32)
    for b in range(32):
        oh[b] = (bk == b)
    return oh


class Prog:
    def __init__(self):
        self.nc = nc = bass.Bass("TRN2", target_bir_lowering=False)
        self.S = Sched(nc)
        self.st = contextlib.ExitStack()
        self.d = {}
        self.outs = []
        self.live = []
        self.phase = 0
        self.phase_bufs = []
        self.pending = []
        self.flushed_ops = []


    def din(self, name, shape):
        self.d[name] = self.nc.dram_tensor(name, list(shape), F32, kind="ExternalInput").ap()
        return self.d[name]

    def dout(self, name, shape):
        self.d[name] = self.nc.dram_tensor(name, list(shape), F32, kind="ExternalOutput").ap()
        return self.d[name]

    def mm(self, out, lhsT, rhs, start, stop, R, W):
        return self.S.op("pe", lambda e: e.matmul(out, lhsT=lhsT, rhs=rhs, start=start, stop=stop), R, W)

    def tr(self, out, in_, R, W):
        ident = self.IDENT[0:in_.shape[0], 0:in_.shape[0]]
        return self.S.op("pe", lambda e: e.transpose(out, in_, ident), R, W)

    def act(self, out, in_, func, R, W, bias=None, scale=1.0):
        if bias is None:
            return self.S.op("act", lambda e: e.activation(out=out, in_=in_, func=func, scale=scale), R, W)
        return self.S.op("act", lambda e: e.activation(out=out, in_=in_, func=func, bias=bias, scale=scale), R, W)

    def stt(self, out, in0, scalar, in1, op0, op1, R, W, eng="dve"):
        return self.S.op(eng, lambda e: e.scalar_tensor_tensor(out=out, in0=in0, scalar=scalar, in1=in1, op0=op0, op1=op1), R, W)

    def tt(self, out, in0, in1, op, R, W, eng="dve"):
        return self.S.op(eng, lambda e: e.tensor_tensor(out=out, in0=in0, in1=in1, op=op), R, W)

    def cp(self, out, in_, R, W, eng="dve"):
        return self.S.op(eng, lambda e: e.tensor_copy(out=out, in_=in_), R, W)

    def recip(self, out, in_, R, W):
        return self.S.op("dve", lambda e: e.reciprocal(out=out, in_=in_), R, W)

    def memset(self, ap, val, W, eng="dve"):
        return self.S.op(eng, lambda e: e.memset(ap, val), (), W)

    def dma(self, out, in_, R, W, q="sp"):
        return self.S.op(q, lambda e: e.dma_start(out=out, in_=in_), R, W, dma=True)

    def carve(self, nwords):
        a = self.aoff
        self.aoff += nwords
        assert self.aoff <= self.NA, ("arena overflow", self.aoff, self.NA)
        self.last_rng = (a, a + nwords)
        return self.ARENA[:, a:a + nwords]

    def carve_bf(self, nel):
        return self.carve((nel + 1) // 2).bitcast(BF16)[:, 0:nel]

    def mkbufs(self, names, rng=None):
        rng = rng or self.last_rng
        fence = []
        keep = []
        for (a, b, bf, ph) in self.live:
            if a < rng[1] and rng[0] < b:
                if bf.w is not None:
                    fence.append(bf.w)
                fence.extend(bf.r)
                if ph < self.phase and rng[0] <= a and b <= rng[1]:
                    continue
            keep.append((a, b, bf, ph))
        self.live = keep
        out = [Buf(n, fence) for n in names]
        for bf in out:
            self.live.append((rng[0], rng[1], bf, self.phase))
            self.phase_bufs.append(bf)
        return out

    def mkbuf(self, name, rng=None):
        return self.mkbufs([name], rng)[0]

    def ring(self, name, n, nwords, bf16=False):
        aps, bufs = [], []
        for i in range(n):
            ap = self.carve_bf(nwords) if bf16 else self.carve(nwords)
            aps.append(ap)
            bufs.append(self.mkbuf("%s%d" % (name, i)))
        return Ring(name, aps, bufs)

    def scratch_reset(self):
        self.aoff = self.scratch_base
        self.phase += 1
        self.phase_bufs = []

    def flush(self, w=None):
        a = len(self.S.ops)
        keep = []
        for (pw, fn) in self.pending:
            if w is None or pw == w:
                fn()
            else:
                keep.append((pw, fn))
        self.pending = keep
        self.flushed_ops += list(range(a, len(self.S.ops)))

    def patch_fences(self):
        if self.flushed_ops:
            for bf in self.phase_bufs:
                if bf.w is None:
                    bf.r.extend(self.flushed_ops)
        self.flushed_ops = []

    def windows(self, sg):
        w = [(i, i * WIN, WIN, False) for i in range(3)]
        if sg == 1:
            w.append((3, NPR, 128, True))
        return w

    def nsg(self, sg):
        return NPR + (128 if sg == 1 else 0)

    def build(self):
        nc, S, st = self.nc, self.S, self.st
        din, dout = self.din, self.dout
        xin = din("xin", (NTOK, 1024))
        ck = din("ck", (2, 2, 128, 256))
        cv = din("cv", (2, 2, 128, 256))
        stc = din("stc", (2, 30, 1024))
        stf = din("stf", (4, 2, 2, 5632))
        self.d["oh"] = nc.dram_tensor("oh", [64, 64, 192], BF16, kind="ExternalInput").ap()
        relt = din("rel_bias_table", (32, 16))
        ng = din("norm_gain", (4, 4, 1024))
        wqkv = din("attn_w_qkv", (2, 1024, 1536))
        bqkv = din("attn_b_qkv", (2, 1536))
        wo = din("attn_w_o", (2, 1024, 1024))
        bo = din("attn_b_o", (2, 1024))
        sinks = din("attn_sinks", (2, 16))
        cw1 = din("conv_w_pw1", (1, 1024, 2048))
        cb1 = din("conv_b_pw1", (1, 2048))
        cwd = din("conv_w_dw", (1, 31, 1024))
        cbd = din("conv_b_dw", (1, 1024))
        clg = din("conv_ln_g", (1, 1024))
        clb = din("conv_ln_b", (1, 1024))
        cw2 = din("conv_w_pw2", (1, 1024, 1024))
        cb2 = din("conv_b_pw2", (1, 1024))
        mwi = din("cmlp_w_in", (1, 1024, 4096))
        mbi = din("cmlp_b_in", (1, 4096))
        mlg = din("cmlp_ln_g", (1, 2048))
        mlb = din("cmlp_ln_b", (1, 2048))
        mws = din("cmlp_w_s", (1, 4, 128, 128))
        mbs = din("cmlp_b_s", (1, 4, 128))
        mwo = din("cmlp_w_out", (1, 2048, 1024))
        mbo = din("cmlp_b_out", (1, 1024))
        fwu = din("ffn_w_up", (4, 1024, 5632))
        fwd = din("ffn_w_dw", (4, 3, 5632))
        fbd = din("ffn_b_dw", (4, 5632))
        fwdn = din("ffn_w_down", (4, 2816, 1024))
        yout = dout("yout", (NTOK, 1024))
        pk = dout("pk", (2, 128, 256))
        pv = dout("pv", (2, 128, 256))
        sk = dout("sk", (2, 128, 256))
        sv = dout("sv", (2, 128, 256))
        pconv = dout("pconv", (30, 1024))
        sconv = dout("sconv", (2, 30, 1024))
        pffn = dout("pffn", (4, 2, 5632))
        sffn = dout("sffn", (4, 2, 2, 5632))
        scv = dout("scv", (128, 2048))

        self.NA = 207 * 256 - 64
        self.ARENA = st.enter_context(nc.sbuf_tensor("arena", [128, self.NA], F32))
        self.PSUM = st.enter_context(nc.psum_tensor("psum", [128, 8, 512], F32))
        self.aoff = 0
        self.PS = Ring("ps", [self.PSUM[:, i, :] for i in range(8)], [Buf("ps%d" % i) for i in range(8)])

        self.X = self.carve(8 * 1280).rearrange("p (k t) -> p k t", k=8)
        self.BX = self.mkbufs(["X0", "X1", "X2", "X3"])
        self.IDENT = self.carve(128)
        self.BC = self.mkbuf("consts")
        self.ONESM = self.carve_bf(128)
        self.ONES1 = self.carve_bf(128)
        self.ONESF = self.carve(128)
        self.EPS = self.carve(2)
        plist = []

        def chunks(ap1d):
            return ap1d.rearrange("(c p) -> c p", p=128)

        pcol = {}
        ncol = 0

        def addp(key, ap2d):
            nonlocal ncol
            pcol[key] = ncol
            plist.append((ncol, ap2d))
            ncol += ap2d.shape[0]

        addp("ng", chunks(ng.rearrange("a b c -> (a b c)")))
        self.bq_special = []
        for j in range(2):
            pcol[("bq", j)] = ncol
            bq16 = bqkv[j, 0:1024].rearrange("(h d) -> h d", d=64)
            self.bq_special.append((ncol, bq16))
            ncol += 8
            addp(("bk", j), chunks(bqkv[j, 1024:1280]))
            addp(("bo", j), chunks(bo[j]))
        addp("cb1", chunks(cb1[0]))
        addp("cwd", chunks(cwd[0].rearrange("k c -> (k c)")))
        addp("cbd", chunks(cbd[0]))
        addp("clg", chunks(clg[0]))
        addp("clb", chunks(clb[0]))
        addp("cb2", chunks(cb2[0]))
        addp("mbu", chunks(mbi[0, 0:2048]))
        addp("mbo", chunks(mbo[0]))
        for l in range(4):
            addp(("fwd", l), chunks(fwd[l].rearrange("k c -> (k c)")))
            addp(("fbd", l), chunks(fbd[l]))
            addp(("stf", l), chunks(stf[l].rearrange("s r c -> (s r c)")))
        addp("stc", chunks(stc.rearrange("s r c -> (s r c)")))
        self.pcol = pcol
        npad = ((ncol + 127) // 128) * 128
        self.PARAMS = self.carve(npad)
        self.BP = self.mkbuf("params")
        a0 = self.aoff
        self.ACARRY = [self.carve_bf(8 * 128).rearrange("p (k t) -> p k t", k=8) for _ in range(2)]
        self.CCARRY = self.carve_bf(8 * 30).rearrange("p (k t) -> p k t", k=8)
        self.FCARRY = [self.carve_bf(8 * 2).rearrange("p (k t) -> p k t", k=8) for _ in range(4)]
        self.BCARRY = self.mkbuf("carry", (a0, self.aoff))
        self.NSLOT = 5
        self.SLOTW = 1408
        self.RW = self.ring("rw", self.NSLOT, self.SLOTW)
        sq = self.carve_bf(8 * WIN).rearrange("p (k t) -> p k t", k=8)
        self.SQ = Ring("sq", [sq], [self.mkbuf("sq")])
        self.RS = self.ring("rs", 2, WIN)
        self.scratch_base = self.aoff

        S.op("pool", lambda e: e.memset(self.IDENT, 1.0), (), [self.BC])
        S.op("pool", lambda e: e.affine_select(out=self.IDENT, in_=self.IDENT, pattern=[[-1, 128]],
                                               compare_op=ALU.is_equal, fill=0.0, base=0, channel_multiplier=1),
             [self.BC], [self.BC])
        self.memset(self.ONESM, 1.0 / 1024.0, [self.BC])
        self.memset(self.ONES1, 1.0, [self.BC])
        self.memset(self.ONESF, 1.0, [self.BC])
        self.memset(self.EPS[:, 0:1], 1e-6, [self.BC])
        self.memset(self.EPS[:, 1:2], 1e-5, [self.BC])

        self.scratch_reset()
        stg = self.ring("pstg", 3, 128)
        for t0 in range(0, npad, 128):
            sap, sb = stg.next()
            self.memset(sap, 0.0, [sb])
            for (c0, ap2d) in plist:
                n = ap2d.shape[0]
                lo, hi = max(c0, t0), min(c0 + n, t0 + 128)
                if lo < hi:
                    self.dma(sap[lo - t0:hi - t0, :], ap2d[lo - c0:hi - c0, :], [], [sb])
            for (c0, bq16) in self.bq_special:
                if t0 <= c0 < t0 + 128:
                    assert c0 + 8 <= t0 + 128
                    r = c0 - t0
                    self.dma(sap[r:r + 4, 0:64], bq16[0:4, :], [], [sb])
                    self.dma(sap[r:r + 4, 64:128], bq16[4:8, :], [], [sb])
                    self.dma(sap[r + 4:r + 8, 0:64], bq16[8:12, :], [], [sb])
                    self.dma(sap[r + 4:r + 8, 64:128], bq16[12:16, :], [], [sb])
            ps, pb = self.PS.next()
            self.tr(ps[:, 0:128], sap, [sb, self.BC], [pb])
            self.act(self.PARAMS[:, t0:t0 + 128], ps[:, 0:128], AF.Identity, [pb], [self.BP])

        finals = []
        self.finals = finals
        for sg in range(2):
            self.load_x(sg)
            for l in range(4):
                kind, j = l % 3, l // 3
                if kind == 0:
                    self.attn(l, j, sg)
                elif kind == 1:
                    self.convmod(l, sg)
                else:
                    self.cmlp(l, sg)
                self.ffn(l, sg)
            self.store_y(sg)
        S.emit(final_wait_ops=finals)
        return nc

    def load_x(self, sg):
        self.flush()
        self.flushed_ops = []
        self.scratch_reset()
        xin = self.d["xin"]
        stg = self.ring("xstg", 3, 1024)
        n = self.nsg(sg)
        for t in range(n // 128):
            r0 = sg * NPR + t * 128 if t < 9 else TP
            w = t // 3
            sap, sb = stg.next()
            self.dma(sap, xin[r0:r0 + 128, :], [], [sb])
            for h in range(2):
                ps, pb = self.PS.next()
                for kk in range(4):
                    k = h * 4 + kk
                    self.tr(ps[:, kk * 128:(kk + 1) * 128], sap[:, k * 128:(k + 1) * 128], [sb, self.BC], [pb])
                self.act(self.X[:, h * 4:(h + 1) * 4, t * 128:(t + 1) * 128],
                         ps.rearrange("p (a b) -> p a b", a=4), AF.Identity, [pb], [self.BX[w]])

    def store_y(self, sg):
        self.flush()
        self.flushed_ops = []
        self.scratch_reset()
        yout = self.d["yout"]
        stg = self.ring("ystg", 3, 1024)
        n = self.nsg(sg)
        for t in range(n // 128):
            r0 = sg * NPR + t * 128 if t < 9 else TP
            w = t // 3
            sap, sb = stg.next()
            for h in range(2):
                ps, pb = self.PS.next()
                for kk in range(4):
                    k = h * 4 + kk
                    self.tr(ps[:, kk * 128:(kk + 1) * 128], self.X[:, k, t * 128:(t + 1) * 128], [self.BX[w], self.BC], [pb])
                self.act(sap[:, h * 512:(h + 1) * 512], ps, AF.Identity, [pb], [sb])
            self.finals.append(self.dma(yout[r0:r0 + 128, :], sap, [sb], []))

    def rstd_of(self, src3, n, R):
        sq, sqb = self.SQ.next()
        self.act(sq[:, :, 0:n], src3, AF.Square, R, [sqb])
        ps, pb = self.PS.next()
        for k in range(8):
            self.mm(ps[:, 0:n], self.ONESM, sq[:, k, 0:n], k == 0, k == 7, [sqb, self.BC], [pb])
        rs, rb = self.RS.next()
        self.act(rs[:, 0:n], ps[:, 0:n], AF.Sqrt, [pb, self.BC], [rb], bias=self.EPS[:, 0:1])
        self.recip(rs[:, 0:n], rs[:, 0:n], [rb], [rb])
        return rs, rb

    def norm_h(self, sg, gcol, out_fn, BH):
        for (w, t0, n, samp) in self.windows(sg):
            self.flush(w)
            rs, rb = self.rstd_of(self.X[:, :, t0:t0 + n], n, [self.BX[w]])
            for k in range(8):
                dst = out_fn(k, t0, n, samp)
                xin_, rin = self.X[:, k, t0:t0 + n], rs[:, 0:n]
                if len(dst.shape) == 3:
                    xin_ = xin_.rearrange("p (s q) -> p s q", s=2)
                    rin = rin.rearrange("p (s q) -> p s q", s=2)
                self.stt(dst, xin_, self.PARAMS[:, gcol + k:gcol + k + 1], rin, ALU.mult, ALU.mult,
                         [self.BX[w], rb, self.BP], [BH[w]])
        self.flush()
        self.patch_fences()

    def resid(self, w, t0, n, Y3, BY, gcol):
        rs, rb = self.rstd_of(Y3, n, [BY])
        for k in range(8):
            self.stt(Y3[:, k, :], Y3[:, k, :], self.PARAMS[:, gcol + k:gcol + k + 1], rs[:, 0:n], ALU.mult, ALU.mult,
                     [BY, rb, self.BP], [BY])
        self.tt(self.X[:, :, t0:t0 + n], self.X[:, :, t0:t0 + n], Y3, ALU.add, [BY, self.BX[w]], [self.BX[w]])

    def wslot(self, kc):
        sl, sb = self.RW.next()
        v = sl.bitcast(BF16)[:, 0:kc * 128].rearrange("p (k c) -> p k c", k=kc)
        return v, sb

    def load_w(self, dst, src, sb):
        self.dma(dst, src, [], [sb], q="pool")

    def out_linear(self, sg, KC, rhs_fn, wload, bcol, gcol, RB, Y, BY):
        Ys = Y if isinstance(Y, list) else [(Y, BY)]
        for wi_, (w, t0, n, samp) in enumerate(self.windows(sg)):
            Yw, BYw = Ys[wi_ % len(Ys)]
            for oc in range(8):
                wv, wb = self.wslot(KC)
                wload(wv, oc, wb)
                ps, pb = self.PS.next()
                for k in range(KC):
                    self.mm(ps[:, 0:n], wv[:, k, :], rhs_fn(k, t0, n), k == 0, k == KC - 1, [wb, RB[w]], [pb])
                self.act(Yw[:, oc, 0:n], ps[:, 0:n], AF.Identity, [pb, self.BP], [BYw],
                         bias=self.PARAMS[:, bcol + oc:bcol + oc + 1])
            self.resid(w, t0, n, Yw[:, :, 0:n], BYw, gcol)

    def ffn(self, l, sg):
        self.flush()
        self.flushed_ops = []
        self.scratch_reset()
        d = self.d
        HW = 2 + NPR + 132
        HT = self.carve_bf(8 * HW).rearrange("p (k t) -> p k t", k=8)
        BH = self.mkbufs(["fh0", "fh1", "fh2", "fh3", "fhc"])
        BHC = BH[4]
        ACTH = self.carve_bf(11 * 1280).rearrange("p (k t) -> p k t", k=11)
        BA = self.mkbufs(["fa0", "fa1", "fa2", "fa3"])
        Y = self.carve(8 * 1280).rearrange("p (k t) -> p k t", k=8)
        BY = self.mkbufs(["fy0", "fy1", "fy2", "fy3"])
        TG = self.ring("tg", 3, WIN)
        TU = self.ring("tu", 3, WIN)
        UPT = self.carve(NFC * 6).rearrange("p (c t) -> p c t", c=NFC)
        BUP = self.mkbuf("upt")
        OST = self.ring("ost", 2, 512)
        gc2, gc3 = self.pcol["ng"] + (l * 4 + 2) * 8, self.pcol["ng"] + (l * 4 + 3) * 8
        wcol, bcol, scol = self.pcol[("fwd", l)], self.pcol[("fbd", l)], self.pcol[("stf", l)]
        P = self.PARAMS
        if sg == 0:
            self.memset(HT[:, :, 0:2], 0.0, [BHC])
        else:
            self.cp(HT[:, :, 0:2], self.FCARRY[l], [self.BCARRY], [BHC])
            self.memset(HT[:, :, 2 + NPR:HW].rearrange("p k (s q) -> p k s q", s=2)[:, :, :, 0:2], 0.0, [BH[3]])

        def out_fn(k, t0, nn, samp):
            if samp:
                return HT[:, k, 2 + NPR:HW].rearrange("p (s q) -> p s q", s=2)[:, :, 2:66]
            return HT[:, k, 2 + t0:2 + t0 + nn]

        self.norm_h(sg, gc2, out_fn, BH)
        if sg == 0:
            self.cp(self.FCARRY[l], HT[:, :, NPR:NPR + 2], [BH[2]], [self.BCARRY])
        wup = d["ffn_w_up"][l].rearrange("(k p) c -> p k c", p=128)
        wdn = d["ffn_w_down"][l].rearrange("(k p) c -> p k c", p=128)
        def load_pair(i):
            wv, wb = self.wslot(16)
            self.load_w(wv[:, 0:8, :], wup[:, :, i * 128:(i + 1) * 128], wb)
            self.load_w(wv[:, 8:16, :], wup[:, :, DFF + i * 128:DFF + (i + 1) * 128], wb)
            return wv, wb

        def load_dn(half, oc):
            wv, wb = self.wslot(11)
            self.load_w(wv, wdn[:, half * 11:(half + 1) * 11, oc * 128:(oc + 1) * 128], wb)
            return wv, wb
        stream = []
        for half in range(2):
            stream += [("up", half * 11 + ii) for ii in range(11)] + [("dn", half, oc) for oc in range(8)]
        AHEAD = 2
        loaded = []

        def issue(n):
            while len(loaded) < min(n, len(stream)):
                it = stream[len(loaded)]
                loaded.append(load_pair(it[1]) if it[0] == "up" else load_dn(it[1], it[2]))
        pos = [0]

        def take():
            issue(pos[0] + 1 + AHEAD)
            r = loaded[pos[0]]
            pos[0] += 1
            return r
        for half in range(2):
            for ii in range(11):
                i = half * 11 + ii
                wv, wb = take()
                for (w, t0, nn, samp) in self.windows(sg):
                    res = []
                    hr = [BH[w]] if samp else [BH[w], (BHC if w == 0 else BH[w - 1])]
                    for gu in range(2):
                        c = i + 22 * gu
                        ps, pb = self.PS.next()
                        if samp:
                            c0, N = 2 + NPR, 132
                        else:
                            c0, N = t0, nn + 2
                        for k in range(8):
                            self.mm(ps[:, 0:N], wv[:, gu * 8 + k, :], HT[:, k, c0:c0 + N], k == 0, k == 7, [wb] + hr, [pb])
                        if samp:
                            pv3 = ps[:, 0:132].rearrange("p (s q) -> p s q", s=2)
                            stv = P[:, scol + c:scol + c + 4 * NFC].rearrange("p (s c) -> p s c", s=4)[:, :, 0]
                            self.cp(pv3[:, :, 0:2], stv.rearrange("p (s r) -> p s r", s=2), [self.BP, pb], [pb])
                            a2, a1, a0 = pv3[:, :, 2:66], pv3[:, :, 1:65], pv3[:, :, 0:64]
                        else:
                            a2, a1, a0 = ps[:, 2:nn + 2], ps[:, 1:nn + 1], ps[:, 0:nn]
                        tring = TG if gu == 0 else TU
                        tb_, tbb = tring.next()
                        tv = tb_[:, 0:nn]
                        if samp:
                            tv = tv.rearrange("p (s q) -> p s q", s=2)
                        w0 = P[:, wcol + c:wcol + c + 1]
                        w1 = P[:, wcol + NFC + c:wcol + NFC + c + 1]
                        w2 = P[:, wcol + 2 * NFC + c:wcol + 2 * NFC + c + 1]
                        self.act(tv, a2, AF.Identity, [pb, self.BP], [tbb], bias=P[:, bcol + c:bcol + c + 1], scale=w2)
                        self.stt(tv, a1, w1, tv, ALU.mult, ALU.add, [pb, tbb, self.BP], [tbb])
                        self.stt(tv, a0, w0, tv, ALU.mult, ALU.add, [pb, tbb, self.BP], [tbb])
                        if sg == 1 and samp:
                            self.act(UPT[:, c, 2:6].rearrange("p (s r) -> p s r", s=2), pv3[:, :, 64:66], AF.Identity, [pb], [BUP])
                        elif sg == 1 and w == 2:
                            self.act(UPT[:, c, 0:2], ps[:, nn:nn + 2], AF.Identity, [pb], [BUP])
                        res.append((tv, tbb))
                    (tg, tgb), (tu, tub) = res
                    self.act(tg, tg, AF.Gelu_apprx_tanh, [tgb], [tgb])
                    dst = ACTH[:, ii, t0:t0 + nn]
                    if samp:
                        dst = dst.rearrange("p (s q) -> p s q", s=2)
                    self.tt(dst, tg, tu, ALU.mult, [tgb, tub], [BA[w]], eng="pool")
            for oc in range(8):
                wv, wb = take()
                for (w, t0, nn, samp) in self.windows(sg):
                    ps, pb = self.PS.next()
                    for k in range(11):
                        self.mm(ps[:, 0:nn], wv[:, k, :], ACTH[:, k, t0:t0 + nn], k == 0, k == 10, [wb, BA[w]], [pb])
                    if half == 0:
                        self.act(Y[:, oc, t0:t0 + nn], ps[:, 0:nn], AF.Identity, [pb], [BY[w]])
                    else:
                        self.tt(Y[:, oc, t0:t0 + nn], Y[:, oc, t0:t0 + nn], ps[:, 0:nn], ALU.add, [pb, BY[w]], [BY[w]])
        for (w, t0, nn, samp) in self.windows(sg):
            self.pending.append((w, (lambda w=w, t0=t0, nn=nn: self.resid(w, t0, nn, Y[:, :, t0:t0 + nn], BY[w], gc3))))
        if sg == 1:
            for cb in range(11):
                ps, pb = self.PS.next()
                for cc in range(4):
                    c = cb * 4 + cc
                    self.tr(ps[0:6, cc * 128:(cc + 1) * 128], UPT[:, c, :], [BUP, self.BC], [pb])
                oa, ob = OST.next()
                self.act(oa[0:6, :], ps[0:6, :], AF.Identity, [pb], [ob])
                self.finals.append(self.dma(d["pffn"][l][:, cb * 512:(cb + 1) * 512], oa[0:2, :], [ob], []))
                self.finals.append(self.dma(d["sffn"][l].rearrange("s r c -> (s r) c")[:, cb * 512:(cb + 1) * 512], oa[2:6, :], [ob], []))

    def attn(self, l, j, sg):
        self.scratch_reset()
        d = self.d
        P = self.PARAMS
        HW = 128 + 1280
        HT = self.carve_bf(8 * HW).rearrange("p (k t) -> p k t", k=8)
        ht_rng = self.last_rng
        BH = self.mkbufs(["ah0", "ah1", "ah2", "ah3", "ahc"])
        BHC = BH[4]

        def hb(c0, c1):
            out = []
            if c0 < 128:
                out.append(BHC)
            for w in range(4):
                a, b = 128 + w * WIN, 128 + min((w + 1) * WIN, 1280)
                if c0 < b and a < c1 and not (w == 3 and sg == 0):
                    out.append(BH[w])
            return out

        QT = self.carve_bf(8 * 1280).rearrange("p (k t) -> p k t", k=8)
        qt_rng = self.last_rng
        BQ = self.mkbufs(["aq0", "aq1", "aq2", "aq3"])
        kv0 = self.aoff
        KT = self.carve_bf(2 * HW).rearrange("p (k t) -> p k t", k=2)
        BK = self.mkbuf("a_kt")
        a0 = self.aoff
        VA = self.carve_bf(11 * 256).rearrange("p (a c) -> p a c", a=11)
        VB = self.carve_bf(11 * 256).rearrange("p (a c) -> p a c", a=11)
        BV = self.mkbuf("a_v", (a0, self.aoff))
        WKV = self.carve_bf(8 * 512).rearrange("p (k c) -> p k c", k=8)
        wkv_rng = self.last_rng
        BWKV = self.mkbuf("a_wkv")
        BKVB = self.carve(512)
        BBKV = self.mkbuf("a_bkv")
        KVO = self.ring("kvo", 1, 512)
        CK = self.ring("ckr", 2, 256)
        a0 = self.aoff
        KTC = self.carve_bf(2 * 2 * 128).rearrange("p (s k t) -> p s k t", s=2, k=2)
        VC = self.carve_bf(2 * 256).rearrange("p (s c) -> p s c", s=2)
        BCACHE = self.mkbuf("a_cache", (a0, self.aoff))
        a0 = self.aoff
        EBT = [self.carve(1024).rearrange("p (h j q) -> p h j q", h=2, j=8) for _ in range(2)]
        ESKR = self.carve_bf(1024).rearrange("p (h j q) -> p h j q", h=2, j=8)
        BEB = self.mkbuf("a_eb", (a0, self.aoff))
        OHT = self.ring("oht", 2, 4 * 96)
        a0 = self.aoff
        TAB = self.carve(16)
        SNK = self.carve(16)
        TABH = self.carve_bf(16)
        TABT = self.carve(16)
        BTAB = self.mkbuf("a_tab", (a0, self.aoff))
        e0 = self.aoff
        ET = self.ring("et", 3, 512)
        PT = self.ring("pt", 4, 512, bf16=True)
        DEN = self.ring("den", 2, 256)
        e1 = self.aoff
        assert e1 - e0 == 8 * WIN
        Y = self.ARENA[:, e0:e1].rearrange("p (k t) -> p k t", k=8)
        WOA = self.carve_bf(6 * 8 * 128).rearrange("p (o k c) -> p o k c", o=6, k=8)
        BWOA = self.mkbuf("a_woa")
        gc0, gc1 = self.pcol["ng"] + (l * 4 + 0) * 8, self.pcol["ng"] + (l * 4 + 1) * 8

        wo_e = d["attn_w_o"][j]
        for oc in range(6):
            for hf in range(2):
                for grp in range(2):
                    h0 = (0, 8)[grp] if hf == 0 else (4, 12)[grp]
                    src = wo_e[h0 * 64:(h0 + 4) * 64, oc * 128:(oc + 1) * 128].rearrange("(j p) c -> p j c", p=64)
                    self.dma(WOA[hf * 64:(hf + 1) * 64, oc, grp * 4:(grp + 1) * 4, :], src, [], [BWOA], q="pool")
        if sg == 0:
            self.memset(HT[:, :, 0:128], 0.0, [BHC])
        else:
            self.cp(HT[:, :, 0:128], self.ACARRY[j], [self.BCARRY], [BHC])
        self.norm_h(sg, gc0, lambda k, t0, nn, samp: HT[:, k, 128 + t0:128 + t0 + nn], BH)
        if sg == 0:
            self.cp(self.ACARRY[j], HT[:, :, NPR:NPR + 128], hb(NPR, NPR + 128), [self.BCARRY])

        self.dma(TAB[0:32, :], d["rel_bias_table"][:, :], [], [BTAB])
        self.dma(TAB[32:64, :], d["rel_bias_table"][:, :], [], [BTAB])
        self.dma(SNK, d["attn_sinks"][j].partition_broadcast(128), [], [BTAB])
        self.cp(TABH[0:64, :], TAB[0:64, :], [BTAB], [BTAB])
        self.tt(TABT[32:64, :], TAB[32:64, :], TABH[32:64, :], ALU.subtract, [BTAB], [BTAB])
        self.cp(TABH[32:64, :], TABT[32:64, :], [BTAB], [BTAB])
        for q4 in range(16):
            oa_, ob = OHT.next()
            oa = oa_.bitcast(BF16).rearrange("p (q j) -> p q j", q=4)
            self.dma(oa[0:64, :, :], d["oh"][:, q4 * 4:(q4 + 1) * 4, :], [], [ob])
            if q4 % 4 == 0:
                psf, pbf = self.PS.next()
                pso, pbo = self.PS.next()
            for qi in range(4):
                qq = (q4 % 4) * 4 + qi
                self.mm(psf[:, qq * 16:(qq + 1) * 16], oa[0:64, qi, 0:128], TABH[0:64, :], True, True, [ob, BTAB], [pbf])
                self.mm(pso[0:64, qq * 16:(qq + 1) * 16], oa[0:64, qi, 128:192], TABH[0:64, :], True, True, [ob, BTAB], [pbo])
            if q4 % 4 == 3:
                q0 = (q4 // 4) * 16
                for (src, pbx, dst, npart) in ((psf, pbf, EBT[0], 128), (pso, pbo, EBT[1], 64)):
                    sv_ = src[0:npart, 0:256].rearrange("p (q h) -> p h q", h=16)
                    for hf in range(2):
                        for grp in range(2):
                            h0 = (0, 8)[grp] if hf == 0 else (4, 12)[grp]
                            self.act(dst[0:npart, hf, grp * 4:(grp + 1) * 4, q0:q0 + 16], sv_[:, h0:h0 + 4, :], AF.Exp, [pbx], [BEB])
        for hf in range(2):
            for grp in range(2):
                h0 = (0, 8)[grp] if hf == 0 else (4, 12)[grp]
                self.act(ESKR[0:1, hf, grp * 4:(grp + 1) * 4, :], SNK[0:1, h0:h0 + 4].unsqueeze(2).to_broadcast([1, 4, 64]), AF.Exp, [BTAB], [BEB])

        wq = d["attn_w_qkv"][j].rearrange("(k p) c -> p k c", p=128)
        bqc, bkc, boc = self.pcol[("bq", j)], self.pcol[("bk", j)], self.pcol[("bo", j)]
        for jq in range(8):
            wv, wb = self.wslot(8)
            lo, up = q_lo(jq), q_up(jq)
            self.load_w(wv[:, :, 0:64], wq[:, :, lo * 64:(lo + 1) * 64], wb)
            self.load_w(wv[:, :, 64:128], wq[:, :, up * 64:(up + 1) * 64], wb)
            for (w, t0, nn, samp) in self.windows(sg):
                ps, pb = self.PS.next()
                for k in range(8):
                    self.mm(ps[:, 0:nn], wv[:, k, :], HT[:, k, 128 + t0:128 + t0 + nn], k == 0, k == 7, [wb, BH[w]], [pb])
                self.act(QT[:, jq, t0:t0 + nn], ps[:, 0:nn], AF.Identity, [pb, self.BP], [BQ[w]], bias=P[:, bqc + jq:bqc + jq + 1])
        for jk in range(2):
            wv, wb = self.wslot(8)
            self.load_w(wv, wq[:, :, 1024 + jk * 128:1024 + (jk + 1) * 128], wb)
            for (c0, nn) in [(0, 128)] + [(128 + t0, nn) for (w, t0, nn, samp) in self.windows(sg)]:
                ps, pb = self.PS.next()
                for k in range(8):
                    self.mm(ps[:, 0:nn], wv[:, k, :], HT[:, k, c0:c0 + nn], k == 0, k == 7, [wb] + hb(c0, c0 + nn), [pb])
                self.act(KT[:, jk, c0:c0 + nn], ps[:, 0:nn], AF.Identity, [pb, self.BP], [BK], bias=P[:, bkc + jk:bkc + jk + 1])
        self.dma(WKV, wq[:, :, 1024:1536], [], [BWKV], q="pool")
        self.dma(BKVB, d["attn_b_qkv"][j, 1024:1536].partition_broadcast(128), [], [BBKV])
        na = 10 + (1 if sg == 1 else 0)
        tiles = [("A", a, a * 128, 128) for a in range(na)] + [("B", b, 64 + b * 128, 128) for b in range(10)]
        if sg == 1:
            tiles.append(("B", 10, 64 + 10 * 128, 64))
        for (kind, idx, c0, m) in tiles:
            ps, pb = self.PS.next()
            hr = hb(c0, min(c0 + m, 128 + self.nsg(sg)))
            for k in range(8):
                self.mm(ps[0:m, :], HT[:, k, c0:c0 + m], WKV[:, k, :], k == 0, k == 7, hr + [BWKV], [pb])
            dst = (VA if kind == "A" else VB)[0:m, idx, :]
            self.tt(dst, ps[0:m, 256:512], BKVB[0:m, 256:512], ALU.add, [pb, BBKV], [BV])
            if sg == 1 and kind == "A" and idx in (9, 10):
                oa, ob = KVO.next()
                self.tt(oa, ps, BKVB, ALU.add, [pb, BBKV], [ob])
                dk, dv = (d["pk"], d["pv"]) if idx == 9 else (d["sk"], d["sv"])
                self.finals.append(self.dma(dk[j], oa[:, 0:256], [ob], []))
                self.finals.append(self.dma(dv[j], oa[:, 256:512], [ob], []))
        if sg == 1:
            for s in range(2):
                ca, cb_ = CK.next()
                self.dma(ca, d["ck"][j, s], [], [cb_])
                ps, pb = self.PS.next()
                for jk in range(2):
                    self.tr(ps[:, jk * 128:(jk + 1) * 128], ca[:, jk * 128:(jk + 1) * 128], [cb_, self.BC], [pb])
                self.act(KTC[:, s, :, :], ps[:, 0:256].rearrange("p (k t) -> p k t", k=2), AF.Identity, [pb], [BCACHE])
                self.dma(VC[:, s, :], d["cv"][j, s], [], [BCACHE], q="pool")

        WOB = WKV.rearrange("p k c -> p (k c)")[:, 0:2 * 8 * 128].rearrange("p (o k c) -> p o k c", o=2, k=8)
        BWOB = self.mkbuf("a_wob", wkv_rng)
        wo_ = d["attn_w_o"][j]

        def wo_ap(oc):
            return (WOA[:, oc], BWOA) if oc < 6 else (WOB[:, oc - 6], BWOB)

        OT = HT
        BO = self.mkbufs(["ao0", "ao1", "ao2", "ao3"], ht_rng)
        items = [("p", c) for c in range(18)] + ([("s", 0), ("s", 1)] if sg == 1 else [])
        work = []
        for (typ, c) in items:
            if typ == "p":
                qc0 = c * 64
                w = c // 6
                gc = c + 18 * sg
                pieces = []
                if gc >= 2:
                    vfull = VA[:, c // 2, :] if c % 2 == 0 else VB[:, (c - 1) // 2, :]
                    pieces.append((c * 64, 128, vfull, 0, BV))
                    vown = VA[0:64, 1 + c // 2, :] if c % 2 == 0 else VB[0:64, (c + 1) // 2, :]
                    pieces.append((128 + c * 64, 64, vown, 1, BV))
                elif gc == 1:
                    pieces.append((64, 128, VB[:, 0, :], 0, BV, True))
                    pieces.append((128 + c * 64, 64, VB[0:64, 1, :], 1, BV))
                else:
                    pieces.append((128, 64, VA[0:64, 1, :], 1, BV))
            else:
                qc0 = NPR + c * 64
                w = 3
                vown = VA[0:64, 10, :] if c == 0 else VB[0:64, 10, :]
                pieces = [(None, 128, VC[:, c, :], 0, BCACHE), (128 + NPR + c * 64, 64, vown, 1, BV)]
            for kv in range(4):
                work.append((c, qc0, w, pieces, kv))

        def stage_a(it):
            (c, qc0, w, pieces, kv) = it
            hf, jk, j0 = kv % 2, kv // 2, (kv // 2) * 4
            rows = slice(hf * 64, hf * 64 + 64)
            rhs_q = QT[rows, j0:j0 + 4, qc0:qc0 + 64]
            pss, pbs = self.PS.next()
            et, eb = ET.next()
            pt, ptb = PT.next()
            offs = []
            off = 0
            for pc in pieces:
                (kc0, nk, vap, bt, vbuf) = pc[0:5]
                if kc0 is None:
                    lk, rk = KTC[rows, c, jk, :], [BCACHE]
                else:
                    lk, rk = KT[rows, jk, kc0:kc0 + nk], [BK]
                self.mm(pss[0:nk, off:off + 256], lk, rhs_q, True, True, rk + [BQ[w]], [pbs])
                offs.append(off)
                off += 256
            for pi, pc in enumerate(pieces):
                (kc0, nk, vap, bt, vbuf) = pc[0:5]
                o_ = offs[pi]
                self.act(et[0:nk, o_:o_ + 256], pss[0:nk, o_:o_ + 256], AF.Exp, [pbs], [eb], scale=0.125)
                self.tt(pt[0:nk, o_:o_ + 256].rearrange("p (j q) -> p j q", j=4),
                        et[0:nk, o_:o_ + 256].rearrange("p (j q) -> p j q", j=4),
                        EBT[bt][0:nk, hf, j0:j0 + 4, :], ALU.mult, [eb, BEB], [ptb], eng="pool")
                if len(pc) > 5:
                    self.memset(pt[0:64, o_:o_ + 256], 0.0, [ptb], eng="pool")
            return (pt, ptb, offs)

        def stage_b(it, st):
            (c, qc0, w, pieces, kv) = it
            (pt, ptb, offs) = st
            hf, jk, j0 = kv % 2, kv // 2, (kv // 2) * 4
            rows = slice(hf * 64, hf * 64 + 64)
            pso, pbo = self.PS.next()
            np_ = len(pieces)
            for pi, pc in enumerate(pieces):
                (kc0, nk, vap, bt, vbuf) = pc[0:5]
                o_ = offs[pi]
                self.mm(pso[:, 0:256], vap[:, jk * 128:(jk + 1) * 128], pt[0:nk, o_:o_ + 256], pi == 0, pi == np_ - 1, [vbuf, ptb], [pbo])
            for pi, pc in enumerate(pieces):
                (kc0, nk, vap, bt, vbuf) = pc[0:5]
                o_ = offs[pi]
                self.mm(pso[:, 256:512], self.ONES1[0:nk, :], pt[0:nk, o_:o_ + 256], pi == 0, False, [self.BC, ptb], [pbo])
            self.mm(pso[:, 256:512], self.ONES1[0:1, :], ESKR[0:1, hf, j0:j0 + 4, :], False, True, [self.BC, BEB], [pbo])
            dn, dnb = DEN.next()
            self.recip(dn[rows, :], pso[rows, 256:512], [pbo], [dnb])
            self.tt(OT[rows, j0:j0 + 4, qc0:qc0 + 64], pso[rows, 0:256].rearrange("p (j q) -> p j q", j=4),
                    dn[rows, :].rearrange("p (j q) -> p j q", j=4), ALU.mult, [pbo, dnb], [BO[w]])

        DEPTH = 3
        sts = {}
        for i in range(min(DEPTH, len(work))):
            sts[i] = stage_a(work[i])
        for i in range(len(work)):
            stage_b(work[i], sts.pop(i))
            if i + DEPTH < len(work):
                sts[i + DEPTH] = stage_a(work[i + DEPTH])

        for oc in range(6, 8):
            dst, dbuf = wo_ap(oc)
            for hf in range(2):
                for grp in range(2):
                    h0 = (0, 8)[grp] if hf == 0 else (4, 12)[grp]
                    src = wo_[h0 * 64:(h0 + 4) * 64, oc * 128:(oc + 1) * 128].rearrange("(j p) c -> p j c", p=64)
                    self.dma(dst[hf * 64:(hf + 1) * 64, grp * 4:(grp + 1) * 4, :], src, [], [dbuf], q="pool")
        BY = self.mkbuf("a_y", (e0, e1))
        Y2 = self.ARENA[:, kv0:kv0 + 8 * WIN].rearrange("p (k t) -> p k t", k=8)
        BY2 = self.mkbuf("a_y2", (kv0, kv0 + 8 * WIN))
        Ys = [(Y, BY), (Y2, BY2)]
        for wi_, (w, t0, n, samp) in enumerate(self.windows(sg)):
            Yw, BYw = Ys[wi_ % 2]
            for oc in range(8):
                wo_t, wo_b = wo_ap(oc)
                ps, pb = self.PS.next()
                for k in range(8):
                    self.mm(ps[:, 0:n], wo_t[:, k, :], OT[:, k, t0:t0 + n], k == 0, k == 7, [wo_b, BO[w]], [pb])
                self.act(Yw[:, oc, 0:n], ps[:, 0:n], AF.Identity, [pb, self.BP], [BYw], bias=P[:, boc + oc:boc + oc + 1])
            self.resid(w, t0, n, Yw[:, :, 0:n], BYw, gc1)

    def convmod(self, l, sg):
        self.scratch_reset()
        d = self.d
        P = self.PARAMS
        HT = self.carve_bf(8 * 1280).rearrange("p (k t) -> p k t", k=8)
        ht_rng = self.last_rng
        BH = self.mkbufs(["ch0", "ch1", "ch2", "ch3"])
        GW = 30 + NPR + 2 * 94
        GLU = self.carve_bf(8 * GW).rearrange("p (k t) -> p k t", k=8)
        glu_rng = self.last_rng
        BG = self.mkbufs(["cg0", "cg1", "cg2", "cg3", "cgc"])
        BGC = BG[4]
        ZB = self.carve_bf(8 * 1280).rearrange("p (k t) -> p k t", k=8)
        BZ = self.mkbufs(["cz0", "cz1", "cz2", "cz3"])
        a0 = self.aoff
        S1 = self.carve(1280)
        S2 = self.carve(1280)
        BS = self.mkbufs(["cs0", "cs1", "cs2", "cs3"], (a0, self.aoff))
        DWR = self.ring("dw", 2, 31 * 128, bf16=True)
        ZSQ = self.ring("zsq", 3, WIN, bf16=True)
        SIG = self.ring("sig", 2, WIN)
        TMP = self.ring("ctmp", 2, WIN)
        GT = self.carve(8 * 96).rearrange("p (k t) -> p k t", k=8)
        BGT = self.mkbuf("c_gt")
        GOST = self.ring("gost", 1, 1024)
        Y = self.carve(8 * WIN).rearrange("p (k t) -> p k t", k=8)
        BY = self.mkbuf("c_y")
        gc0, gc1 = self.pcol["ng"] + (l * 4 + 0) * 8, self.pcol["ng"] + (l * 4 + 1) * 8
        cb1, cwd, cbd, clg, clb, cb2, stc = (self.pcol[k] for k in ("cb1", "cwd", "cbd", "clg", "clb", "cb2", "stc"))
        self.norm_h(sg, gc0, lambda k, t0, nn, samp: HT[:, k, t0:t0 + nn], BH)
        if sg == 0:
            self.memset(GLU[:, :, 0:30], 0.0, [BGC])
        else:
            self.cp(GLU[:, :, 0:30], self.CCARRY, [self.BCARRY], [BGC])
            for s in range(2):
                src = P[:, stc + s * 240:stc + (s + 1) * 240].rearrange("p (r k) -> p k r", k=8)
                self.cp(GLU[:, :, 30 + NPR + s * 94:30 + NPR + s * 94 + 30], src, [self.BP], [BG[3]])
        w1 = d["conv_w_pw1"][0].rearrange("(k p) c -> p k c", p=128)
        for jc in range(8):
            wv, wb = self.wslot(16)
            self.load_w(wv[:, 0:8, :], w1[:, :, jc * 128:(jc + 1) * 128], wb)
            self.load_w(wv[:, 8:16, :], w1[:, :, 1024 + jc * 128:1024 + (jc + 1) * 128], wb)
            for (w, t0, nn, samp) in self.windows(sg):
                psa, pba = self.PS.next()
                psg, pbg = self.PS.next()
                for k in range(8):
                    self.mm(psa[:, 0:nn], wv[:, k, :], HT[:, k, t0:t0 + nn], k == 0, k == 7, [wb, BH[w]], [pba])
                for k in range(8):
                    self.mm(psg[:, 0:nn], wv[:, 8 + k, :], HT[:, k, t0:t0 + nn], k == 0, k == 7, [wb, BH[w]], [pbg])
                sg_, sgb = SIG.next()
                self.act(sg_[:, 0:nn], psg[:, 0:nn], AF.Sigmoid, [pbg, self.BP], [sgb], bias=P[:, cb1 + 8 + jc:cb1 + 9 + jc])
                ba = P[:, cb1 + jc:cb1 + jc + 1]
                if samp:
                    dst = GLU[:, jc, 30 + NPR:GW].rearrange("p (s q) -> p s q", s=2)[:, :, 30:94]
                    a3 = psa[:, 0:128].rearrange("p (s q) -> p s q", s=2)
                    s3 = sg_[:, 0:128].rearrange("p (s q) -> p s q", s=2)
                    self.stt(dst, a3, ba, s3, ALU.add, ALU.mult, [pba, sgb, self.BP], [BG[3]])
                    self.stt(GT[:, jc, 32:96].rearrange("p (s q) -> p s q", s=2)[:, :, 0:30], a3[:, :, 34:64], ba, s3[:, :, 34:64],
                             ALU.add, ALU.mult, [pba, sgb, self.BP], [BGT])
                else:
                    self.stt(GLU[:, jc, 30 + t0:30 + t0 + nn], psa[:, 0:nn], ba, sg_[:, 0:nn], ALU.add, ALU.mult, [pba, sgb, self.BP], [BG[w]])
                    if sg == 1 and w == 2:
                        self.stt(GT[:, jc, 0:30], psa[:, nn - 30:nn], ba, sg_[:, nn - 30:nn], ALU.add, ALU.mult, [pba, sgb, self.BP], [BGT])
        if sg == 0:
            self.cp(self.CCARRY, GLU[:, :, NPR:NPR + 30], [BG[2]], [self.BCARRY])
        else:
            for seg in range(3):
                c0 = 0 if seg == 0 else 32 * seg
                oa, ob = GOST.next()
                for h in range(2):
                    ps, pb = self.PS.next()
                    for kk in range(4):
                        self.tr(ps[0:30, kk * 128:(kk + 1) * 128], GT[:, h * 4 + kk, c0:c0 + 30], [BGT, self.BC], [pb])
                    self.act(oa[0:30, h * 512:(h + 1) * 512], ps[0:30, :], AF.Identity, [pb], [ob])
                dst = d["pconv"] if seg == 0 else d["sconv"][seg - 1]
                self.finals.append(self.dma(dst[:, :], oa[0:30, :], [ob], []))
        C = HT
        BCt = self.mkbufs(["cc0", "cc1", "cc2", "cc3"], ht_rng)
        pend_stats = None

        def emit_stats(jc, w, t0, nn, zq, zqb):
            ps1, pb1 = self.PS.next()
            self.mm(ps1[:, 0:nn], self.ONESM, ZB[:, jc, t0:t0 + nn], True, True, [self.BC, BZ[w]], [pb1])
            ps2, pb2 = self.PS.next()
            self.mm(ps2[:, 0:nn], self.ONESM, zq[:, 0:nn], True, True, [self.BC, zqb], [pb2])
            if jc == 0:
                self.cp(S1[:, t0:t0 + nn], ps1[:, 0:nn], [pb1], [BS[w]])
                self.cp(S2[:, t0:t0 + nn], ps2[:, 0:nn], [pb2], [BS[w]])
            else:
                self.tt(S1[:, t0:t0 + nn], S1[:, t0:t0 + nn], ps1[:, 0:nn], ALU.add, [pb1, BS[w]], [BS[w]])
                self.tt(S2[:, t0:t0 + nn], S2[:, t0:t0 + nn], ps2[:, 0:nn], ALU.add, [pb2, BS[w]], [BS[w]])
        for jc in range(8):
            dw_, dwb = DWR.next()
            dw = dw_.rearrange("p (k c) -> p k c", k=31)
            wk = P[:, cwd + jc:cwd + jc + 31 * 8].rearrange("p (k j) -> p k j", j=8)[:, :, 0:1]
            self.tt(dw, self.IDENT.unsqueeze(1).to_broadcast([128, 31, 128]), wk.to_broadcast([128, 31, 128]), ALU.mult,
                    [self.BC, self.BP], [dwb])
            for (w, t0, nn, samp) in self.windows(sg):
                ps, pb = self.PS.next()
                gr = [BG[3]] if samp else [BG[w], (BGC if w == 0 else BG[w - 1])]
                for k in range(31):
                    if samp:
                        rhs = GLU[:, jc, 30 + NPR:GW].rearrange("p (s q) -> p s q", s=2)[:, :, k:k + 64]
                    else:
                        rhs = GLU[:, jc, t0 + k:t0 + k + nn]
                    self.mm(ps[:, 0:nn], dw[:, k, :], rhs, k == 0, k == 30, [dwb] + gr, [pb])
                bd = P[:, cbd + jc:cbd + jc + 1]
                self.act(ZB[:, jc, t0:t0 + nn], ps[:, 0:nn], AF.Identity, [pb, self.BP], [BZ[w]], bias=bd)
                zq, zqb = ZSQ.next()
                self.act(zq[:, 0:nn], ps[:, 0:nn], AF.Square, [pb, self.BP], [zqb], bias=bd)
                if pend_stats is not None:
                    emit_stats(*pend_stats)
                pend_stats = (jc, w, t0, nn, zq, zqb)
        emit_stats(*pend_stats)
        for (w, t0, nn, samp) in self.windows(sg):
            tm, tmb = TMP.next()
            self.tt(tm[:, 0:nn], S1[:, t0:t0 + nn], S1[:, t0:t0 + nn], ALU.mult, [BS[w]], [tmb])
            self.tt(S2[:, t0:t0 + nn], S2[:, t0:t0 + nn], tm[:, 0:nn], ALU.subtract, [BS[w], tmb], [BS[w]])
            self.act(S2[:, t0:t0 + nn], S2[:, t0:t0 + nn], AF.Sqrt, [BS[w], self.BC], [BS[w]], bias=self.EPS[:, 1:2])
            self.recip(S2[:, t0:t0 + nn], S2[:, t0:t0 + nn], [BS[w]], [BS[w]])
            for jc in range(8):
                tm, tmb = TMP.next()
                self.tt(tm[:, 0:nn], ZB[:, jc, t0:t0 + nn], S1[:, t0:t0 + nn], ALU.subtract, [BZ[w], BS[w]], [tmb])
                self.tt(tm[:, 0:nn], tm[:, 0:nn], S2[:, t0:t0 + nn], ALU.mult, [tmb, BS[w]], [tmb])
                self.act(C[:, jc, t0:t0 + nn], tm[:, 0:nn], AF.Silu, [tmb, self.BP], [BCt[w]],
                         bias=P[:, clb + jc:clb + jc + 1], scale=P[:, clg + jc:clg + jc + 1])
        w2 = d["conv_w_pw2"][0].rearrange("(k p) c -> p k c", p=128)
        assert glu_rng[1] - glu_rng[0] >= 8 * WIN
        Y2 = self.ARENA[:, glu_rng[0]:glu_rng[0] + 8 * WIN].rearrange("p (k t) -> p k t", k=8)
        BY2 = self.mkbuf("c_y2", (glu_rng[0], glu_rng[0] + 8 * WIN))
        self.out_linear(sg, 8, lambda k, t0, nn: C[:, k, t0:t0 + nn],
                        lambda dst, oc, wb: self.load_w(dst, w2[:, :, oc * 128:(oc + 1) * 128], wb),
                        cb2, gc1, BCt, [(Y, BY), (Y2, BY2)], None)

    def cmlp(self, l, sg):
        self.flush()
        self.flushed_ops = []
        self.scratch_reset()
        d = self.d
        P = self.PARAMS
        WV = self.carve_bf(8 * 2048).rearrange("p (k c) -> p k c", k=8)
        BWV = self.mkbuf("m_wv")
        a0 = self.aoff
        LNG = self.carve(2048)
        LNB = self.carve(2048)
        BVB = self.carve(2048)
        BLN = self.mkbuf("m_ln", (a0, self.aoff))
        HT = self.carve_bf(8 * WIN).rearrange("p (k t) -> p k t", k=8)
        BH = self.mkbuf("m_ht")
        U = self.carve_bf(16 * WIN).rearrange("p (k t) -> p k t", k=16)
        BU = self.mkbuf("m_u")
        VT = self.carve_bf(3 * 2048).rearrange("p (a c) -> p a c", a=3)
        BVT = self.mkbufs(["m_vt0", "m_vt1", "m_vt2"])
        VR = self.carve(2048)
        BVR = self.mkbuf("m_vr")
        a0 = self.aoff
        STAT = self.carve(4 * 6).rearrange("p (a b) -> p a b", a=4)
        MV = self.carve(4)
        BST = self.mkbuf("m_st", (a0, self.aoff))
        WSIN = self.carve(4 * 128).rearrange("p (g j) -> p g j", g=4)
        wsin_rng = self.last_rng
        BWSI = self.mkbuf("m_wsi")
        WST = self.carve_bf(2 * 4 * 128).rearrange("p (v g i) -> p v g i", v=2, g=4)
        BWS = self.mkbuf("m_ws")
        a0 = self.aoff
        BSR = self.carve(4 * 128).rearrange("p (g i) -> p g i", g=4)
        BSH = self.carve_bf(2 * 4 * 128).rearrange("p (v g i) -> p v g i", v=2, g=4)
        BBS = self.mkbuf("m_bs", (a0, self.aoff))
        Y = self.carve(8 * WIN).rearrange("p (k t) -> p k t", k=8)
        BY = self.mkbuf("m_y")
        gc0, gc1 = self.pcol["ng"] + (l * 4 + 0) * 8, self.pcol["ng"] + (l * 4 + 1) * 8
        mbu, mbo = self.pcol["mbu"], self.pcol["mbo"]
        wi = d["cmlp_w_in"][0].rearrange("(k p) c -> p k c", p=128)
        wo_ = d["cmlp_w_out"][0].rearrange("(k p) c -> p k c", p=128)
        for cb in range(4):
            self.dma(WV[:, :, cb * 512:(cb + 1) * 512], wi[:, :, 2048 + cb * 512:2048 + (cb + 1) * 512], [], [BWV], q="pool")
        self.dma(LNG, d["cmlp_ln_g"][0].partition_broadcast(128), [], [BLN])
        self.dma(LNB, d["cmlp_ln_b"][0].partition_broadcast(128), [], [BLN])
        self.dma(BVB, d["cmlp_b_in"][0, 2048:4096].partition_broadcast(128), [], [BLN])
        ws = d["cmlp_w_s"][0]
        bs = d["cmlp_b_s"][0]
        for v in range(2 if sg == 1 else 1):
            if v == 0:
                self.dma(WSIN, ws.rearrange("g i j -> i g j"), [BWSI], [BWSI])
            else:
                self.memset(WSIN, 0.0, [BWSI])
                self.dma(WSIN[0:64, :, 0:64], ws[:, 0:64, 0:64].rearrange("g i j -> i g j"), [BWSI], [BWSI])
                self.dma(WSIN[64:128, :, 64:128], ws[:, 0:64, 0:64].rearrange("g i j -> i g j"), [BWSI], [BWSI])
            ps, pb = self.PS.next()
            for g in range(4):
                self.tr(ps[:, g * 128:(g + 1) * 128], WSIN[:, g, :], [BWSI, self.BC], [pb])
            self.act(WST[:, v, :, :], ps.rearrange("p (g i) -> p g i", g=4), AF.Identity, [pb], [BWS])
            if v == 0:
                self.memset(WST[64:128, 0, :, 0:64], 0.0, [BWS])

        BSL = self.ARENA[:, wsin_rng[0]:wsin_rng[1]].bitcast(BF16).rearrange("p (v g i) -> p v g i", v=2, g=4)
        BBL = self.mkbuf("m_bsl", wsin_rng)
        for v in range(2 if sg == 1 else 1):
            if v == 0:
                self.dma(BSR[0:1, :, :], bs.rearrange("g i -> (g i)").rearrange("(o g i) -> o g i", o=1, g=4), [BBS], [BBS])
            else:
                self.dma(BSR[0:1, :, 0:64], bs[:, 0:64].unsqueeze(0), [BBS], [BBS])
                self.dma(BSR[0:1, :, 64:128], bs[:, 0:64].unsqueeze(0), [BBS], [BBS])
            self.cp(BSH[0:1, v], BSR[0:1], [BBS], [BBS])
            self.tt(BSR[0:1], BSR[0:1], BSH[0:1, v], ALU.subtract, [BBS], [BBS])
            self.cp(BSL[0:1, v], BSR[0:1], [BBS], [BBL])
        def make_ht(wn):
            (w_, t0_, nn_, samp_) = wn
            rs, rb = self.rstd_of(self.X[:, :, t0_:t0_ + nn_], nn_, [self.BX[w_]])
            for k in range(8):
                self.stt(HT[:, k, 0:nn_], self.X[:, k, t0_:t0_ + nn_], P[:, gc0 + k:gc0 + k + 1], rs[:, 0:nn_], ALU.mult, ALU.mult,
                         [self.BX[w_], rb, self.BP], [BH])
        wlist = self.windows(sg)
        make_ht(wlist[0])
        for widx, (w, t0, nn, samp) in enumerate(wlist):
            var = 1 if samp else 0
            def u_chunk(jc):
                wv, wb = self.wslot(8)
                self.load_w(wv, wi[:, :, jc * 128:(jc + 1) * 128], wb)
                ps, pb = self.PS.next()
                for k in range(8):
                    self.mm(ps[:, 0:nn], wv[:, k, :], HT[:, k, 0:nn], k == 0, k == 7, [wb, BH], [pb])
                self.act(U[:, jc, 0:nn], ps[:, 0:nn], AF.Gelu_apprx_tanh, [pb, self.BP], [BU], bias=P[:, mbu + jc:mbu + jc + 1])

            def v_tile(t):
                for cb in range(4):
                    ps, pb = self.PS.next()
                    for k in range(8):
                        self.mm(ps, HT[:, k, t * 128:(t + 1) * 128], WV[:, k, cb * 512:(cb + 1) * 512], k == 0, k == 7, [BH, BWV], [pb])
                    vs = VR[:, cb * 512:(cb + 1) * 512]
                    self.tt(vs, ps, BVB[:, cb * 512:(cb + 1) * 512], ALU.add, [pb, BLN], [BVR])
                    self.act(vs, vs, AF.Gelu_apprx_tanh, [BVR], [BVR])
                    self.S.op("dve", (lambda o, i: lambda e: e.bn_stats(out=o, in_=i))(STAT[:, cb, :], vs), [BVR], [BST])
                self.S.op("dve", lambda e: e.bn_aggr(out=MV[:, 0:2], in_=STAT), [BST], [BST])
                self.act(MV[:, 2:3], MV[:, 1:2], AF.Sqrt, [BST, self.BC], [BST], bias=self.EPS[:, 1:2])
                self.recip(MV[:, 2:3], MV[:, 2:3], [BST], [BST])
                self.stt(MV[:, 3:4], MV[:, 0:1], -1.0, MV[:, 2:3], ALU.mult, ALU.mult, [BST], [BST])
                self.act(VR, VR, AF.Identity, [BVR, BST], [BVR], bias=MV[:, 3:4], scale=MV[:, 2:3])
                self.tt(VR, VR, LNG, ALU.mult, [BVR, BLN], [BVR])
                if samp:
                    self.tt(VR, VR, LNB, ALU.add, [BVR, BLN], [BVR])
                    self.cp(VT[:, t, :], VR, [BVR], [BVT[t]])
                    self.finals.append(self.dma(d["scv"][:, :], VR, [BVR], []))
                else:
                    self.tt(VT[:, t, :], VR, LNB, ALU.add, [BVR, BLN], [BVT[t]])

            def spatial(t):
                for g in range(4):
                    ps, pb = self.PS.next()
                    for cc in range(4):
                        ch = g * 4 + cc
                        self.mm(ps[:, cc * 128:(cc + 1) * 128], VT[:, t, ch * 128:(ch + 1) * 128], WST[:, var, g, :], True, False, [BVT[t], BWS], [pb])
                        self.mm(ps[:, cc * 128:(cc + 1) * 128], self.ONES1[0:1, :], BSH[0:1, var, g, :], False, False, [self.BC, BBS], [pb])
                        self.mm(ps[:, cc * 128:(cc + 1) * 128], self.ONES1[0:1, :], BSL[0:1, var, g, :], False, True, [self.BC, BBL], [pb])
                    uu = U[:, g * 4:(g + 1) * 4, t * 128:(t + 1) * 128]
                    self.tt(uu, ps.rearrange("p (c i) -> p c i", c=4), uu, ALU.mult, [pb, BU], [BU])

            nt = nn // 128
            sched_u = {0: range(0, 6), 1: range(6, 11), 2: range(11, 16)} if nt == 3 else {0: range(0, 16)}
            for t in range(nt):
                v_tile(t)
                for jc in sched_u[t]:
                    u_chunk(jc)
            for t in range(nt):
                spatial(t)
            if widx + 1 < len(wlist):
                make_ht(wlist[widx + 1])
            for oc in range(8):
                wv, wb = self.wslot(16)
                self.load_w(wv, wo_[:, :, oc * 128:(oc + 1) * 128], wb)
                ps, pb = self.PS.next()
                for k in range(16):
                    self.mm(ps[:, 0:nn], wv[:, k, :], U[:, k, 0:nn], k == 0, k == 15, [wb, BU], [pb])
                self.act(Y[:, oc, 0:nn], ps[:, 0:nn], AF.Identity, [pb, self.BP], [BY], bias=P[:, mbo + oc:mbo + oc + 1])
            self.resid(w, t0, nn, Y[:, :, 0:nn], BY, gc1)


_PROG = None
_OH = None


def _get_prog():
    global _PROG, _OH
    if _PROG is None:
        p = Prog()
        p.build()
        _PROG = p
        _OH = onehot2_const()
    return _PROG


WEIGHT_KEYS = ["rel_bias_table", "norm_gain", "attn_w_qkv", "attn_b_qkv", "attn_w_o", "attn_b_o", "attn_sinks",
               "conv_w_pw1", "conv_b_pw1", "conv_w_dw", "conv_b_dw", "conv_ln_g", "conv_ln_b", "conv_w_pw2",
               "conv_b_pw2", "cmlp_w_in", "cmlp_b_in", "cmlp_ln_g", "cmlp_ln_b", "cmlp_w_s", "cmlp_b_s",
               "cmlp_w_out", "cmlp_b_out", "ffn_w_up", "ffn_w_dw", "ffn_b_dw", "ffn_w_down"]


def kernel(**inputs):
    p = _get_prog()
    f = lambda a: np.ascontiguousarray(np.asarray(a, dtype=np.float32))
    xp, xs = f(inputs["x_prompt"]), f(inputs["x_sample"])
    cak, cav = f(inputs["cache_attn_k"]), f(inputs["cache_attn_v"])
    stc, stf = f(inputs["state_conv"]), f(inputs["state_ffn_conv"])
    wts = {k: f(inputs[k]) for k in WEIGHT_KEYS}
    in_maps = []
    for c in range(8):
        b, hf = c // 2, c % 2
        s0 = 0 if hf == 0 else 4096 - TP
        m = dict(wts)
        m["xin"] = np.ascontiguousarray(np.concatenate([xp[b, s0:s0 + TP], xs[2 * c], xs[2 * c + 1]], axis=0))
        m["ck"] = np.ascontiguousarray(cak[:, 2 * c:2 * c + 2].reshape(2, 2, 128, 256))
        m["cv"] = np.ascontiguousarray(cav[:, 2 * c:2 * c + 2].reshape(2, 2, 128, 256))
        m["stc"] = np.ascontiguousarray(stc[0, 2 * c:2 * c + 2])
        m["stf"] = np.ascontiguousarray(stf[:, 2 * c:2 * c + 2])
        m["oh"] = _OH
        in_maps.append(m)
    res = run_bass_kernel_spmd(p.nc, in_maps, core_ids=list(range(8)))
    R = res.results
    y_prompt = np.zeros((4, 4096, 1024), np.float32)
    y_sample = np.zeros((16, 64, 1024), np.float32)
    p_k = np.zeros((2, 4, 128, 4, 64), np.float32)
    p_v = np.zeros((2, 4, 128, 4, 64), np.float32)
    p_conv = np.zeros((1, 4, 30, 1024), np.float32)
    p_ffn = np.zeros((4, 4, 2, 5632), np.float32)
    s_k = np.zeros((2, 16, 64, 4, 64), np.float32)
    s_v = np.zeros((2, 16, 64, 4, 64), np.float32)
    s_conv = np.zeros((1, 16, 30, 1024), np.float32)
    s_cv = np.zeros((1, 16, 64, 2048), np.float32)
    s_ffn = np.zeros((4, 16, 2, 5632), np.float32)
    for c in range(8):
        b, hf = c // 2, c % 2
        r = R[c]
        yo = np.asarray(r["yout"])
        if hf == 0:
            y_prompt[b, 0:TP] = yo[0:TP]
        else:
            y_prompt[b, TP:4096] = yo[2 * TP - 4096:TP]
            p_k[:, b] = np.asarray(r["pk"]).reshape(2, 128, 4, 64)
            p_v[:, b] = np.asarray(r["pv"]).reshape(2, 128, 4, 64)
            p_conv[0, b] = np.asarray(r["pconv"])
            p_ffn[:, b] = np.asarray(r["pffn"])
        y_sample[2 * c] = yo[TP:TP + 64]
        y_sample[2 * c + 1] = yo[TP + 64:TP + 128]
        s_k[:, 2 * c:2 * c + 2] = np.asarray(r["sk"]).reshape(2, 2, 64, 4, 64)
        s_v[:, 2 * c:2 * c + 2] = np.asarray(r["sv"]).reshape(2, 2, 64, 4, 64)
        s_conv[0, 2 * c:2 * c + 2] = np.asarray(r["sconv"])
        s_cv[0, 2 * c:2 * c + 2] = np.asarray(r["scv"]).reshape(2, 64, 2048)
        s_ffn[:, 2 * c:2 * c + 2] = np.asarray(r["sffn"])
    return (y_prompt, y_sample, p_k, p_v, p_conv, p_ffn, s_k, s_v, s_conv, s_cv, s_ffn)
```

```python
import contextlib
import numpy as np
import concourse.bass as bass
import concourse.mybir as mybir
from concourse.bass_utils import run_bass_kernel_spmd

F32 = mybir.dt.float32
BF16 = mybir.dt.bfloat16
AF = mybir.ActivationFunctionType
ALU = mybir.AluOpType

ENGS = ("pe", "act", "dve", "pool", "sp")
NPR = 1152
WIN = 384
TP = 2304
NTOK = 2432
DFF = 2816
NFC = 44


class Buf:
    __slots__ = ("name", "w", "r")

    def __init__(self, name, fence=None):
        self.name = name
        self.w = None
        self.r = list(fence) if fence else []


class Op:
    __slots__ = ("eng", "fn", "deps", "dma", "has_dep", "tok", "id")


class Sched:
    def __init__(self, nc, n_dma_sems=24):
        self.nc = nc
        self.ops = []
        self.per_eng = {e: [] for e in ENGS}
        self.n_dma_sems = n_dma_sems

    def op(self, eng, fn, reads=(), writes=(), dma=False):
        o = Op()
        o.eng, o.fn, o.dma, o.has_dep, o.tok, o.id = eng, fn, dma, False, None, len(self.ops)
        deps = set()
        for b in reads:
            if b.w is not None:
                deps.add(b.w)
        for b in writes:
            if b.w is not None:
                deps.add(b.w)
            deps.update(b.r)
        if eng == "pe":
            deps = {dd for dd in deps if self.ops[dd].eng != "pe"}
        o.deps = deps
        for b in reads:
            b.r.append(o.id)
        for b in writes:
            b.w = o.id
            b.r = []
        self.ops.append(o)
        self.per_eng[eng].append(o)
        return o

    def emit(self, final_wait_ops=()):
        nc, ops = self.nc, self.ops
        for o in ops:
            for d in o.deps:
                ops[d].has_dep = True
        for o in final_wait_ops:
            o.has_dep = True
        with contextlib.ExitStack() as st:
            esem = {e: st.enter_context(nc.semaphore("s_" + e)) for e in ENGS}
            dsems = {e: [st.enter_context(nc.semaphore("d_%s_%d" % (e, i)))
                         for i in range(self.n_dma_sems)] for e in ("sp", "pool")}
            ecnt = {e: 0 for e in ENGS}
            dcnt = {e: [0] * self.n_dma_sems for e in dsems}
            drr = {e: 0 for e in dsems}
            for e in ENGS:
                for o in self.per_eng[e]:
                    if not o.has_dep:
                        continue
                    if o.dma:
                        j = drr[e]
                        drr[e] = (j + 1) % self.n_dma_sems
                        prev = dcnt[e][j]
                        dcnt[e][j] += 16
                        o.tok = (dsems[e][j], dcnt[e][j], prev)
                    else:
                        ecnt[e] += 1
                        o.tok = (esem[e], ecnt[e], None)
            block = st.enter_context(nc.Block())

            def run(e, eng):
                waited = {}
                for o in self.per_eng[e]:
                    need = {}
                    for d in o.deps:
                        sem, val, _ = ops[d].tok
                        k = id(sem)
                        if k not in need or need[k][1] < val:
                            need[k] = (sem, val)
                    if o.dma and o.tok is not None and o.tok[2]:
                        sem, _, prev = o.tok
                        k = id(sem)
                        if k not in need or need[k][1] < prev:
                            need[k] = (sem, prev)
                    for k, (sem, val) in need.items():
                        if waited.get(k, 0) >= val:
                            continue
                        eng.wait_ge(sem, val)
                        waited[k] = val
                    ins = o.fn(eng)
                    if o.tok is not None:
                        ins.then_inc(o.tok[0], 16 if o.dma else 1)
                if e == "sp":
                    for o in final_wait_ops:
                        sem, val, _ = o.tok
                        eng.wait_ge(sem, val)

            @block.tensor
            def _(eng):
                run("pe", eng)

            @block.scalar
            def _(eng):
                run("act", eng)

            @block.vector
            def _(eng):
                run("dve", eng)

            @block.gpsimd
            def _(eng):
                run("pool", eng)

            @block.sync
            def _(eng):
                run("sp", eng)


class Ring:
    def __init__(self, name, aps, bufs):
        self.aps = aps
        self.bufs = bufs
        self.i = 0

    def next(self):
        j = self.i
        self.i = (j + 1) % len(self.aps)
        return self.aps[j], self.bufs[j]


def q_lo(j):
    return j if j < 4 else 8 + (j - 4)


def q_up(j):
    return 4 + j if j < 4 else 12 + (j - 4)


def t5_bucket_np(rel):
    half, max_exact = 16, 8
    n = np.abs(rel)
    log_ratio = np.log(np.maximum(n, 1).astype(np.float32) / np.float32(max_exact)) / np.float32(np.log(128 / max_exact))
    large = np.minimum(max_exact + (log_ratio * np.float32(half - max_exact)).astype(np.int32), half - 1)
    return np.where(rel > 0, half, 0) + np.where(n < max_exact, n, large)


def onehot2_const():
    import ml_dtypes
    oh = onehot_const()
    return np.ascontiguousarray(np.concatenate([oh, oh], axis=0).astype(ml_dtypes.bfloat16))


def onehot_const():
    q = np.arange(64)[:, None]
    j = np.arange(192)[None, :]
    bk = t5_bucket_np((j - 128) - q)
    oh = np.zeros((32, 64, 192), np.float32)
    for b in range(32):
        oh[b] = (bk == b)
    return oh


class Prog:
    def __init__(self):
        self.nc = nc = bass.Bass("TRN2", target_bir_lowering=False)
        self.S = Sched(nc)
        self.st = contextlib.ExitStack()
        self.d = {}
        self.outs = []
        self.live = []
        self.phase = 0


    def din(self, name, shape):
        self.d[name] = self.nc.dram_tensor(name, list(shape), F32, kind="ExternalInput").ap()
        return self.d[name]

    def dout(self, name, shape):
        self.d[name] = self.nc.dram_tensor(name, list(shape), F32, kind="ExternalOutput").ap()
        return self.d[name]

    def mm(self, out, lhsT, rhs, start, stop, R, W):
        return self.S.op("pe", lambda e: e.matmul(out, lhsT=lhsT, rhs=rhs, start=start, stop=stop), R, W)

    def tr(self, out, in_, R, W):
        ident = self.IDENT[0:in_.shape[0], 0:in_.shape[0]]
        return self.S.op("pe", lambda e: e.transpose(out, in_, ident), R, W)

    def act(self, out, in_, func, R, W, bias=None, scale=1.0):
        if bias is None:
            return self.S.op("act", lambda e: e.activation(out=out, in_=in_, func=func, scale=scale), R, W)
        return self.S.op("act", lambda e: e.activation(out=out, in_=in_, func=func, bias=bias, scale=scale), R, W)

    def stt(self, out, in0, scalar, in1, op0, op1, R, W, eng="dve"):
        return self.S.op(eng, lambda e: e.scalar_tensor_tensor(out=out, in0=in0, scalar=scalar, in1=in1, op0=op0, op1=op1), R, W)

    def tt(self, out, in0, in1, op, R, W, eng="dve"):
        return self.S.op(eng, lambda e: e.tensor_tensor(out=out, in0=in0, in1=in1, op=op), R, W)

    def cp(self, out, in_, R, W, eng="dve"):
        return self.S.op(eng, lambda e: e.tensor_copy(out=out, in_=in_), R, W)

    def recip(self, out, in_, R, W):
        return self.S.op("dve", lambda e: e.reciprocal(out=out, in_=in_), R, W)

    def memset(self, ap, val, W, eng="dve"):
        return self.S.op(eng, lambda e: e.memset(ap, val), (), W)

    def dma(self, out, in_, R, W, q="sp"):
        return self.S.op(q, lambda e: e.dma_start(out=out, in_=in_), R, W, dma=True)

    def carve(self, nwords):
        a = self.aoff
        self.aoff += nwords
        assert self.aoff <= self.NA, ("arena overflow", self.aoff, self.NA)
        self.last_rng = (a, a + nwords)
        return self.ARENA[:, a:a + nwords]

    def carve_bf(self, nel):
        return self.carve((nel + 1) // 2).bitcast(BF16)[:, 0:nel]

    def mkbufs(self, names, rng=None):
        rng = rng or self.last_rng
        fence = []
        keep = []
        for (a, b, bf, ph) in self.live:
            if a < rng[1] and rng[0] < b:
                if bf.w is not None:
                    fence.append(bf.w)
                fence.extend(bf.r)
                if ph < self.phase and rng[0] <= a and b <= rng[1]:
                    continue
            keep.append((a, b, bf, ph))
        self.live = keep
        out = [Buf(n, fence) for n in names]
        for bf in out:
            self.live.append((rng[0], rng[1], bf, self.phase))
        return out

    def mkbuf(self, name, rng=None):
        return self.mkbufs([name], rng)[0]

    def ring(self, name, n, nwords, bf16=False):
        aps, bufs = [], []
        for i in range(n):
            ap = self.carve_bf(nwords) if bf16 else self.carve(nwords)
            aps.append(ap)
            bufs.append(self.mkbuf("%s%d" % (name, i)))
        return Ring(name, aps, bufs)

    def scratch_reset(self):
        self.aoff = self.scratch_base
        self.phase += 1

    def windows(self, sg):
        w = [(i, i * WIN, WIN, False) for i in range(3)]
        if sg == 1:
            w.append((3, NPR, 128, True))
        return w

    def nsg(self, sg):
        return NPR + (128 if sg == 1 else 0)

    def build(self):
        nc, S, st = self.nc, self.S, self.st
        din, dout = self.din, self.dout
        xin = din("xin", (NTOK, 1024))
        ck = din("ck", (2, 2, 128, 256))
        cv = din("cv", (2, 2, 128, 256))
        stc = din("stc", (2, 30, 1024))
        stf = din("stf", (4, 2, 2, 5632))
        self.d["oh"] = nc.dram_tensor("oh", [64, 64, 192], BF16, kind="ExternalInput").ap()
        relt = din("rel_bias_table", (32, 16))
        ng = din("norm_gain", (4, 4, 1024))
        wqkv = din("attn_w_qkv", (2, 1024, 1536))
        bqkv = din("attn_b_qkv", (2, 1536))
        wo = din("attn_w_o", (2, 1024, 1024))
        bo = din("attn_b_o", (2, 1024))
        sinks = din("attn_sinks", (2, 16))
        cw1 = din("conv_w_pw1", (1, 1024, 2048))
        cb1 = din("conv_b_pw1", (1, 2048))
        cwd = din("conv_w_dw", (1, 31, 1024))
        cbd = din("conv_b_dw", (1, 1024))
        clg = din("conv_ln_g", (1, 1024))
        clb = din("conv_ln_b", (1, 1024))
        cw2 = din("conv_w_pw2", (1, 1024, 1024))
        cb2 = din("conv_b_pw2", (1, 1024))
        mwi = din("cmlp_w_in", (1, 1024, 4096))
        mbi = din("cmlp_b_in", (1, 4096))
        mlg = din("cmlp_ln_g", (1, 2048))
        mlb = din("cmlp_ln_b", (1, 2048))
        mws = din("cmlp_w_s", (1, 4, 128, 128))
        mbs = din("cmlp_b_s", (1, 4, 128))
        mwo = din("cmlp_w_out", (1, 2048, 1024))
        mbo = din("cmlp_b_out", (1, 1024))
        fwu = din("ffn_w_up", (4, 1024, 5632))
        fwd = din("ffn_w_dw", (4, 3, 5632))
        fbd = din("ffn_b_dw", (4, 5632))
        fwdn = din("ffn_w_down", (4, 2816, 1024))
        yout = dout("yout", (NTOK, 1024))
        pk = dout("pk", (2, 128, 256))
        pv = dout("pv", (2, 128, 256))
        sk = dout("sk", (2, 128, 256))
        sv = dout("sv", (2, 128, 256))
        pconv = dout("pconv", (30, 1024))
        sconv = dout("sconv", (2, 30, 1024))
        pffn = dout("pffn", (4, 2, 5632))
        sffn = dout("sffn", (4, 2, 2, 5632))
        scv = dout("scv", (128, 2048))

        self.NA = 207 * 256 - 64
        self.ARENA = st.enter_context(nc.sbuf_tensor("arena", [128, self.NA], F32))
        self.PSUM = st.enter_context(nc.psum_tensor("psum", [128, 8, 512], F32))
        self.aoff = 0
        self.PS = Ring("ps", [self.PSUM[:, i, :] for i in range(8)], [Buf("ps%d" % i) for i in range(8)])

        self.X = self.carve(8 * 1280).rearrange("p (k t) -> p k t", k=8)
        self.BX = self.mkbufs(["X0", "X1", "X2", "X3"])
        self.IDENT = self.carve(128)
        self.BC = self.mkbuf("consts")
        self.ONESM = self.carve_bf(128)
        self.ONES1 = self.carve_bf(128)
        self.ONESF = self.carve(128)
        self.EPS = self.carve(2)
        plist = []

        def chunks(ap1d):
            return ap1d.rearrange("(c p) -> c p", p=128)

        pcol = {}
        ncol = 0

        def addp(key, ap2d):
            nonlocal ncol
            pcol[key] = ncol
            plist.append((ncol, ap2d))
            ncol += ap2d.shape[0]

        addp("ng", chunks(ng.rearrange("a b c -> (a b c)")))
        self.bq_special = []
        for j in range(2):
            pcol[("bq", j)] = ncol
            bq16 = bqkv[j, 0:1024].rearrange("(h d) -> h d", d=64)
            self.bq_special.append((ncol, bq16))
            ncol += 8
            addp(("bk", j), chunks(bqkv[j, 1024:1280]))
            addp(("bo", j), chunks(bo[j]))
        addp("cb1", chunks(cb1[0]))
        addp("cwd", chunks(cwd[0].rearrange("k c -> (k c)")))
        addp("cbd", chunks(cbd[0]))
        addp("clg", chunks(clg[0]))
        addp("clb", chunks(clb[0]))
        addp("cb2", chunks(cb2[0]))
        addp("mbu", chunks(mbi[0, 0:2048]))
        addp("mbo", chunks(mbo[0]))
        for l in range(4):
            addp(("fwd", l), chunks(fwd[l].rearrange("k c -> (k c)")))
            addp(("fbd", l), chunks(fbd[l]))
            addp(("stf", l), chunks(stf[l].rearrange("s r c -> (s r c)")))
        addp("stc", chunks(stc.rearrange("s r c -> (s r c)")))
        self.pcol = pcol
        npad = ((ncol + 127) // 128) * 128
        self.PARAMS = self.carve(npad)
        self.BP = self.mkbuf("params")
        a0 = self.aoff
        self.ACARRY = [self.carve_bf(8 * 128).rearrange("p (k t) -> p k t", k=8) for _ in range(2)]
        self.CCARRY = self.carve_bf(8 * 30).rearrange("p (k t) -> p k t", k=8)
        self.FCARRY = [self.carve_bf(8 * 2).rearrange("p (k t) -> p k t", k=8) for _ in range(4)]
        self.BCARRY = self.mkbuf("carry", (a0, self.aoff))
        self.NSLOT = 5
        self.SLOTW = 1408
        self.RW = self.ring("rw", self.NSLOT, self.SLOTW)
        sq = self.carve_bf(8 * WIN).rearrange("p (k t) -> p k t", k=8)
        self.SQ = Ring("sq", [sq], [self.mkbuf("sq")])
        self.RS = self.ring("rs", 2, WIN)
        self.scratch_base = self.aoff

        S.op("pool", lambda e: e.memset(self.IDENT, 1.0), (), [self.BC])
        S.op("pool", lambda e: e.affine_select(out=self.IDENT, in_=self.IDENT, pattern=[[-1, 128]],
                                               compare_op=ALU.is_equal, fill=0.0, base=0, channel_multiplier=1),
             [self.BC], [self.BC])
        self.memset(self.ONESM, 1.0 / 1024.0, [self.BC])
        self.memset(self.ONES1, 1.0, [self.BC])
        self.memset(self.ONESF, 1.0, [self.BC])
        self.memset(self.EPS[:, 0:1], 1e-6, [self.BC])
        self.memset(self.EPS[:, 1:2], 1e-5, [self.BC])

        self.scratch_reset()
        stg = self.ring("pstg", 3, 128)
        for t0 in range(0, npad, 128):
            sap, sb = stg.next()
            self.memset(sap, 0.0, [sb])
            for (c0, ap2d) in plist:
                n = ap2d.shape[0]
                lo, hi = max(c0, t0), min(c0 + n, t0 + 128)
                if lo < hi:
                    self.dma(sap[lo - t0:hi - t0, :], ap2d[lo - c0:hi - c0, :], [], [sb])
            for (c0, bq16) in self.bq_special:
                if t0 <= c0 < t0 + 128:
                    assert c0 + 8 <= t0 + 128
                    r = c0 - t0
                    self.dma(sap[r:r + 4, 0:64], bq16[0:4, :], [], [sb])
                    self.dma(sap[r:r + 4, 64:128], bq16[4:8, :], [], [sb])
                    self.dma(sap[r + 4:r + 8, 0:64], bq16[8:12, :], [], [sb])
                    self.dma(sap[r + 4:r + 8, 64:128], bq16[12:16, :], [], [sb])
            ps, pb = self.PS.next()
            self.tr(ps[:, 0:128], sap, [sb, self.BC], [pb])
            self.act(self.PARAMS[:, t0:t0 + 128], ps[:, 0:128], AF.Identity, [pb], [self.BP])

        finals = []
        self.finals = finals
        for sg in range(2):
            self.load_x(sg)
            for l in range(4):
                kind, j = l % 3, l // 3
                if kind == 0:
                    self.attn(l, j, sg)
                elif kind == 1:
                    self.convmod(l, sg)
                else:
                    self.cmlp(l, sg)
                self.ffn(l, sg)
            self.store_y(sg)
        S.emit(final_wait_ops=finals)
        return nc

    def load_x(self, sg):
        self.scratch_reset()
        xin = self.d["xin"]
        self.aoff = self.NA - 3 * 1024
        stg = self.ring("xstg", 3, 1024)
        self.aoff = self.scratch_base
        n = self.nsg(sg)
        for t in range(n // 128):
            r0 = sg * NPR + t * 128 if t < 9 else TP
            w = t // 3
            sap, sb = stg.next()
            self.dma(sap, xin[r0:r0 + 128, :], [], [sb])
            for h in range(2):
                ps, pb = self.PS.next()
                for kk in range(4):
                    k = h * 4 + kk
                    self.tr(ps[:, kk * 128:(kk + 1) * 128], sap[:, k * 128:(k + 1) * 128], [sb, self.BC], [pb])
                self.act(self.X[:, h * 4:(h + 1) * 4, t * 128:(t + 1) * 128],
                         ps.rearrange("p (a b) -> p a b", a=4), AF.Identity, [pb], [self.BX[w]])

    def store_y(self, sg):
        self.scratch_reset()
        yout = self.d["yout"]
        self.aoff = self.NA - 3 * 1024
        stg = self.ring("ystg", 3, 1024)
        self.aoff = self.scratch_base
        n = self.nsg(sg)
        for t in range(n // 128):
            r0 = sg * NPR + t * 128 if t < 9 else TP
            w = t // 3
            sap, sb = stg.next()
            for h in range(2):
                ps, pb = self.PS.next()
                for kk in range(4):
                    k = h * 4 + kk
                    self.tr(ps[:, kk * 128:(kk + 1) * 128], self.X[:, k, t * 128:(t + 1) * 128], [self.BX[w], self.BC], [pb])
                self.act(sap[:, h * 512:(h + 1) * 512], ps, AF.Identity, [pb], [sb])
            self.finals.append(self.dma(yout[r0:r0 + 128, :], sap, [sb], []))

    def rstd_of(self, src3, n, R):
        sq, sqb = self.SQ.next()
        self.act(sq[:, :, 0:n], src3, AF.Square, R, [sqb])
        ps, pb = self.PS.next()
        for k in range(8):
            self.mm(ps[:, 0:n], self.ONESM, sq[:, k, 0:n], k == 0, k == 7, [sqb, self.BC], [pb])
        rs, rb = self.RS.next()
        self.act(rs[:, 0:n], ps[:, 0:n], AF.Sqrt, [pb, self.BC], [rb], bias=self.EPS[:, 0:1])
        self.recip(rs[:, 0:n], rs[:, 0:n], [rb], [rb])
        return rs, rb

    def norm_h(self, sg, gcol, out_fn, BH):
        for (w, t0, n, samp) in self.windows(sg):
            rs, rb = self.rstd_of(self.X[:, :, t0:t0 + n], n, [self.BX[w]])
            for k in range(8):
                dst = out_fn(k, t0, n, samp)
                xin_, rin = self.X[:, k, t0:t0 + n], rs[:, 0:n]
                if len(dst.shape) == 3:
                    xin_ = xin_.rearrange("p (s q) -> p s q", s=2)
                    rin = rin.rearrange("p (s q) -> p s q", s=2)
                self.stt(dst, xin_, self.PARAMS[:, gcol + k:gcol + k + 1], rin, ALU.mult, ALU.mult,
                         [self.BX[w], rb, self.BP], [BH[w]])

    def resid(self, w, t0, n, Y3, BY, gcol):
        rs, rb = self.rstd_of(Y3, n, [BY])
        for k in range(8):
            self.stt(Y3[:, k, :], Y3[:, k, :], self.PARAMS[:, gcol + k:gcol + k + 1], rs[:, 0:n], ALU.mult, ALU.mult,
                     [BY, rb, self.BP], [BY])
        self.tt(self.X[:, :, t0:t0 + n], self.X[:, :, t0:t0 + n], Y3, ALU.add, [BY, self.BX[w]], [self.BX[w]])

    def wslot(self, kc):
        sl, sb = self.RW.next()
        v = sl.bitcast(BF16)[:, 0:kc * 128].rearrange("p (k c) -> p k c", k=kc)
        return v, sb

    def load_w(self, dst, src, sb):
        self.dma(dst, src, [], [sb], q="pool")

    def out_linear(self, sg, KC, rhs_fn, wload, bcol, gcol, RB, Y, BY):
        Ys = Y if isinstance(Y, list) else [(Y, BY)]
        for wi_, (w, t0, n, samp) in enumerate(self.windows(sg)):
            Yw, BYw = Ys[wi_ % len(Ys)]
            for oc in range(8):
                wv, wb = self.wslot(KC)
                wload(wv, oc, wb)
                ps, pb = self.PS.next()
                for k in range(KC):
                    self.mm(ps[:, 0:n], wv[:, k, :], rhs_fn(k, t0, n), k == 0, k == KC - 1, [wb, RB[w]], [pb])
                self.act(Yw[:, oc, 0:n], ps[:, 0:n], AF.Identity, [pb, self.BP], [BYw],
                         bias=self.PARAMS[:, bcol + oc:bcol + oc + 1])
            self.resid(w, t0, n, Yw[:, :, 0:n], BYw, gcol)

    def ffn(self, l, sg):
        self.scratch_reset()
        d = self.d
        HW = 2 + NPR + 132
        HT = self.carve_bf(8 * HW).rearrange("p (k t) -> p k t", k=8)
        BH = self.mkbufs(["fh0", "fh1", "fh2", "fh3", "fhc"])
        BHC = BH[4]
        ACTH = self.carve_bf(11 * 1280).rearrange("p (k t) -> p k t", k=11)
        BA = self.mkbufs(["fa0", "fa1", "fa2", "fa3"])
        Y = self.carve(8 * 1280).rearrange("p (k t) -> p k t", k=8)
        BY = self.mkbufs(["fy0", "fy1", "fy2", "fy3"])
        TG = self.ring("tg", 4, WIN)
        TU = self.ring("tu", 4, WIN)
        UPT = self.carve(NFC * 6).rearrange("p (c t) -> p c t", c=NFC)
        BUP = self.mkbuf("upt")
        OST = self.ring("ost", 2, 512)
        gc2, gc3 = self.pcol["ng"] + (l * 4 + 2) * 8, self.pcol["ng"] + (l * 4 + 3) * 8
        wcol, bcol, scol = self.pcol[("fwd", l)], self.pcol[("fbd", l)], self.pcol[("stf", l)]
        P = self.PARAMS
        if sg == 0:
            self.memset(HT[:, :, 0:2], 0.0, [BHC])
        else:
            self.cp(HT[:, :, 0:2], self.FCARRY[l], [self.BCARRY], [BHC])
            self.memset(HT[:, :, 2 + NPR:HW].rearrange("p k (s q) -> p k s q", s=2)[:, :, :, 0:2], 0.0, [BH[3]])

        def out_fn(k, t0, nn, samp):
            if samp:
                return HT[:, k, 2 + NPR:HW].rearrange("p (s q) -> p s q", s=2)[:, :, 2:66]
            return HT[:, k, 2 + t0:2 + t0 + nn]

        self.norm_h(sg, gc2, out_fn, BH)
        if sg == 0:
            self.cp(self.FCARRY[l], HT[:, :, NPR:NPR + 2], [BH[2]], [self.BCARRY])
        wup = d["ffn_w_up"][l].rearrange("(k p) c -> p k c", p=128)
        wdn = d["ffn_w_down"][l].rearrange("(k p) c -> p k c", p=128)
        def load_pair(i):
            wv, wb = self.wslot(16)
            self.load_w(wv[:, 0:8, :], wup[:, :, i * 128:(i + 1) * 128], wb)
            self.load_w(wv[:, 8:16, :], wup[:, :, DFF + i * 128:DFF + (i + 1) * 128], wb)
            return wv, wb

        def load_dn(half, oc):
            wv, wb = self.wslot(11)
            self.load_w(wv, wdn[:, half * 11:(half + 1) * 11, oc * 128:(oc + 1) * 128], wb)
            return wv, wb
        stream = []
        for half in range(2):
            stream += [("up", half * 11 + ii) for ii in range(11)] + [("dn", half, oc) for oc in range(8)]
        AHEAD = 3
        loaded = []

        def issue(n):
            while len(loaded) < min(n, len(stream)):
                it = stream[len(loaded)]
                loaded.append(load_pair(it[1]) if it[0] == "up" else load_dn(it[1], it[2]))
        pos = [0]

        def take():
            issue(pos[0] + 1 + AHEAD)
            r = loaded[pos[0]]
            pos[0] += 1
            return r
        for half in range(2):
            for ii in range(11):
                i = half * 11 + ii
                wv, wb = take()
                for (w, t0, nn, samp) in self.windows(sg):
                    res = []
                    hr = [BH[w]] if samp else [BH[w], (BHC if w == 0 else BH[w - 1])]
                    for gu in range(2):
                        c = i + 22 * gu
                        ps, pb = self.PS.next()
                        if samp:
                            c0, N = 2 + NPR, 132
                        else:
                            c0, N = t0, nn + 2
                        for k in range(8):
                            self.mm(ps[:, 0:N], wv[:, gu * 8 + k, :], HT[:, k, c0:c0 + N], k == 0, k == 7, [wb] + hr, [pb])
                        if samp:
                            pv3 = ps[:, 0:132].rearrange("p (s q) -> p s q", s=2)
                            stv = P[:, scol + c:scol + c + 4 * NFC].rearrange("p (s c) -> p s c", s=4)[:, :, 0]
                            self.cp(pv3[:, :, 0:2], stv.rearrange("p (s r) -> p s r", s=2), [self.BP, pb], [pb])
                            a2, a1, a0 = pv3[:, :, 2:66], pv3[:, :, 1:65], pv3[:, :, 0:64]
                        else:
                            a2, a1, a0 = ps[:, 2:nn + 2], ps[:, 1:nn + 1], ps[:, 0:nn]
                        tring = TG if gu == 0 else TU
                        tb_, tbb = tring.next()
                        tv = tb_[:, 0:nn]
                        if samp:
                            tv = tv.rearrange("p (s q) -> p s q", s=2)
                        w0 = P[:, wcol + c:wcol + c + 1]
                        w1 = P[:, wcol + NFC + c:wcol + NFC + c + 1]
                        w2 = P[:, wcol + 2 * NFC + c:wcol + 2 * NFC + c + 1]
                        self.act(tv, a2, AF.Identity, [pb, self.BP], [tbb], bias=P[:, bcol + c:bcol + c + 1], scale=w2)
                        self.stt(tv, a1, w1, tv, ALU.mult, ALU.add, [pb, tbb, self.BP], [tbb])
                        self.stt(tv, a0, w0, tv, ALU.mult, ALU.add, [pb, tbb, self.BP], [tbb])
                        if sg == 1 and samp:
                            self.act(UPT[:, c, 2:6].rearrange("p (s r) -> p s r", s=2), pv3[:, :, 64:66], AF.Identity, [pb], [BUP])
                        elif sg == 1 and w == 2:
                            self.act(UPT[:, c, 0:2], ps[:, nn:nn + 2], AF.Identity, [pb], [BUP])
                        res.append((tv, tbb))
                    (tg, tgb), (tu, tub) = res
                    self.act(tg, tg, AF.Gelu_apprx_tanh, [tgb], [tgb])
                    dst = ACTH[:, ii, t0:t0 + nn]
                    if samp:
                        dst = dst.rearrange("p (s q) -> p s q", s=2)
                    self.tt(dst, tg, tu, ALU.mult, [tgb, tub], [BA[w]], eng="pool")
            for oc in range(8):
                wv, wb = take()
                for (w, t0, nn, samp) in self.windows(sg):
                    ps, pb = self.PS.next()
                    for k in range(11):
                        self.mm(ps[:, 0:nn], wv[:, k, :], ACTH[:, k, t0:t0 + nn], k == 0, k == 10, [wb, BA[w]], [pb])
                    if half == 0:
                        self.act(Y[:, oc, t0:t0 + nn], ps[:, 0:nn], AF.Identity, [pb], [BY[w]])
                    else:
                        self.tt(Y[:, oc, t0:t0 + nn], Y[:, oc, t0:t0 + nn], ps[:, 0:nn], ALU.add, [pb, BY[w]], [BY[w]])
        for (w, t0, nn, samp) in self.windows(sg):
            self.resid(w, t0, nn, Y[:, :, t0:t0 + nn], BY[w], gc3)
        if sg == 1:
            for cb in range(11):
                ps, pb = self.PS.next()
                for cc in range(4):
                    c = cb * 4 + cc
                    self.tr(ps[0:6, cc * 128:(cc + 1) * 128], UPT[:, c, :], [BUP, self.BC], [pb])
                oa, ob = OST.next()
                self.act(oa[0:6, :], ps[0:6, :], AF.Identity, [pb], [ob])
                self.finals.append(self.dma(d["pffn"][l][:, cb * 512:(cb + 1) * 512], oa[0:2, :], [ob], []))
                self.finals.append(self.dma(d["sffn"][l].rearrange("s r c -> (s r) c")[:, cb * 512:(cb + 1) * 512], oa[2:6, :], [ob], []))

    def attn(self, l, j, sg):
        self.scratch_reset()
        d = self.d
        P = self.PARAMS
        HW = 128 + 1280
        HT = self.carve_bf(8 * HW).rearrange("p (k t) -> p k t", k=8)
        ht_rng = self.last_rng
        BH = self.mkbufs(["ah0", "ah1", "ah2", "ah3", "ahc"])
        BHC = BH[4]

        def hb(c0, c1):
            out = []
            if c0 < 128:
                out.append(BHC)
            for w in range(4):
                a, b = 128 + w * WIN, 128 + min((w + 1) * WIN, 1280)
                if c0 < b and a < c1 and not (w == 3 and sg == 0):
                    out.append(BH[w])
            return out

        QT = self.carve_bf(8 * 1280).rearrange("p (k t) -> p k t", k=8)
        qt_rng = self.last_rng
        BQ = self.mkbufs(["aq0", "aq1", "aq2", "aq3"])
        kv0 = self.aoff
        KT = self.carve_bf(2 * HW).rearrange("p (k t) -> p k t", k=2)
        BK = self.mkbuf("a_kt")
        a0 = self.aoff
        VA = self.carve_bf(11 * 256).rearrange("p (a c) -> p a c", a=11)
        VB = self.carve_bf(11 * 256).rearrange("p (a c) -> p a c", a=11)
        BV = self.mkbuf("a_v", (a0, self.aoff))
        WKV = self.carve_bf(8 * 512).rearrange("p (k c) -> p k c", k=8)
        wkv_rng = self.last_rng
        BWKV = self.mkbuf("a_wkv")
        BKVB = self.carve(512)
        BBKV = self.mkbuf("a_bkv")
        KVO = self.ring("kvo", 1, 512)
        CK = self.ring("ckr", 2, 256)
        a0 = self.aoff
        KTC = self.carve_bf(2 * 2 * 128).rearrange("p (s k t) -> p s k t", s=2, k=2)
        VC = self.carve_bf(2 * 256).rearrange("p (s c) -> p s c", s=2)
        BCACHE = self.mkbuf("a_cache", (a0, self.aoff))
        a0 = self.aoff
        EBT = [self.carve(1024).rearrange("p (h j q) -> p h j q", h=2, j=8) for _ in range(2)]
        ESKR = self.carve_bf(1024).rearrange("p (h j q) -> p h j q", h=2, j=8)
        BEB = self.mkbuf("a_eb", (a0, self.aoff))
        OHT = self.ring("oht", 2, 4 * 96)
        a0 = self.aoff
        TAB = self.carve(16)
        SNK = self.carve(16)
        TABH = self.carve_bf(16)
        TABT = self.carve(16)
        BTAB = self.mkbuf("a_tab", (a0, self.aoff))
        e0 = self.aoff
        ET = self.ring("et", 3, 512)
        PT = self.ring("pt", 4, 512, bf16=True)
        DEN = self.ring("den", 2, 256)
        e1 = self.aoff
        assert e1 - e0 == 8 * WIN
        Y = self.ARENA[:, e0:e1].rearrange("p (k t) -> p k t", k=8)
        WOA = self.carve_bf(6 * 8 * 128).rearrange("p (o k c) -> p o k c", o=6, k=8)
        BWOA = self.mkbuf("a_woa")
        gc0, gc1 = self.pcol["ng"] + (l * 4 + 0) * 8, self.pcol["ng"] + (l * 4 + 1) * 8

        wo_e = d["attn_w_o"][j]
        for oc in range(6):
            for hf in range(2):
                for grp in range(2):
                    h0 = (0, 8)[grp] if hf == 0 else (4, 12)[grp]
                    src = wo_e[h0 * 64:(h0 + 4) * 64, oc * 128:(oc + 1) * 128].rearrange("(j p) c -> p j c", p=64)
                    self.dma(WOA[hf * 64:(hf + 1) * 64, oc, grp * 4:(grp + 1) * 4, :], src, [], [BWOA], q="pool")
        if sg == 0:
            self.memset(HT[:, :, 0:128], 0.0, [BHC])
        else:
            self.cp(HT[:, :, 0:128], self.ACARRY[j], [self.BCARRY], [BHC])
        self.norm_h(sg, gc0, lambda k, t0, nn, samp: HT[:, k, 128 + t0:128 + t0 + nn], BH)
        if sg == 0:
            self.cp(self.ACARRY[j], HT[:, :, NPR:NPR + 128], hb(NPR, NPR + 128), [self.BCARRY])

        self.dma(TAB[0:32, :], d["rel_bias_table"][:, :], [], [BTAB])
        self.dma(TAB[32:64, :], d["rel_bias_table"][:, :], [], [BTAB])
        self.dma(SNK, d["attn_sinks"][j].partition_broadcast(128), [], [BTAB])
        self.cp(TABH[0:64, :], TAB[0:64, :], [BTAB], [BTAB])
        self.tt(TABT[32:64, :], TAB[32:64, :], TABH[32:64, :], ALU.subtract, [BTAB], [BTAB])
        self.cp(TABH[32:64, :], TABT[32:64, :], [BTAB], [BTAB])
        for q4 in range(16):
            oa_, ob = OHT.next()
            oa = oa_.bitcast(BF16).rearrange("p (q j) -> p q j", q=4)
            self.dma(oa[0:64, :, :], d["oh"][:, q4 * 4:(q4 + 1) * 4, :], [], [ob])
            if q4 % 4 == 0:
                psf, pbf = self.PS.next()
                pso, pbo = self.PS.next()
            for qi in range(4):
                qq = (q4 % 4) * 4 + qi
                self.mm(psf[:, qq * 16:(qq + 1) * 16], oa[0:64, qi, 0:128], TABH[0:64, :], True, True, [ob, BTAB], [pbf])
                self.mm(pso[0:64, qq * 16:(qq + 1) * 16], oa[0:64, qi, 128:192], TABH[0:64, :], True, True, [ob, BTAB], [pbo])
            if q4 % 4 == 3:
                q0 = (q4 // 4) * 16
                for (src, pbx, dst, npart) in ((psf, pbf, EBT[0], 128), (pso, pbo, EBT[1], 64)):
                    sv_ = src[0:npart, 0:256].rearrange("p (q h) -> p h q", h=16)
                    for hf in range(2):
                        for grp in range(2):
                            h0 = (0, 8)[grp] if hf == 0 else (4, 12)[grp]
                            self.act(dst[0:npart, hf, grp * 4:(grp + 1) * 4, q0:q0 + 16], sv_[:, h0:h0 + 4, :], AF.Exp, [pbx], [BEB])
        for hf in range(2):
            for grp in range(2):
                h0 = (0, 8)[grp] if hf == 0 else (4, 12)[grp]
                self.act(ESKR[0:1, hf, grp * 4:(grp + 1) * 4, :], SNK[0:1, h0:h0 + 4].unsqueeze(2).to_broadcast([1, 4, 64]), AF.Exp, [BTAB], [BEB])

        wq = d["attn_w_qkv"][j].rearrange("(k p) c -> p k c", p=128)
        bqc, bkc, boc = self.pcol[("bq", j)], self.pcol[("bk", j)], self.pcol[("bo", j)]
        for jq in range(8):
            wv, wb = self.wslot(8)
            lo, up = q_lo(jq), q_up(jq)
            self.load_w(wv[:, :, 0:64], wq[:, :, lo * 64:(lo + 1) * 64], wb)
            self.load_w(wv[:, :, 64:128], wq[:, :, up * 64:(up + 1) * 64], wb)
            for (w, t0, nn, samp) in self.windows(sg):
                ps, pb = self.PS.next()
                for k in range(8):
                    self.mm(ps[:, 0:nn], wv[:, k, :], HT[:, k, 128 + t0:128 + t0 + nn], k == 0, k == 7, [wb, BH[w]], [pb])
                self.act(QT[:, jq, t0:t0 + nn], ps[:, 0:nn], AF.Identity, [pb, self.BP], [BQ[w]], bias=P[:, bqc + jq:bqc + jq + 1])
        for jk in range(2):
            wv, wb = self.wslot(8)
            self.load_w(wv, wq[:, :, 1024 + jk * 128:1024 + (jk + 1) * 128], wb)
            for (c0, nn) in [(0, 128)] + [(128 + t0, nn) for (w, t0, nn, samp) in self.windows(sg)]:
                ps, pb = self.PS.next()
                for k in range(8):
                    self.mm(ps[:, 0:nn], wv[:, k, :], HT[:, k, c0:c0 + nn], k == 0, k == 7, [wb] + hb(c0, c0 + nn), [pb])
                self.act(KT[:, jk, c0:c0 + nn], ps[:, 0:nn], AF.Identity, [pb, self.BP], [BK], bias=P[:, bkc + jk:bkc + jk + 1])
        self.dma(WKV, wq[:, :, 1024:1536], [], [BWKV], q="pool")
        self.dma(BKVB, d["attn_b_qkv"][j, 1024:1536].partition_broadcast(128), [], [BBKV])
        na = 10 + (1 if sg == 1 else 0)
        tiles = [("A", a, a * 128, 128) for a in range(na)] + [("B", b, 64 + b * 128, 128) for b in range(10)]
        if sg == 1:
            tiles.append(("B", 10, 64 + 10 * 128, 64))
        for (kind, idx, c0, m) in tiles:
            ps, pb = self.PS.next()
            hr = hb(c0, min(c0 + m, 128 + self.nsg(sg)))
            for k in range(8):
                self.mm(ps[0:m, :], HT[:, k, c0:c0 + m], WKV[:, k, :], k == 0, k == 7, hr + [BWKV], [pb])
            dst = (VA if kind == "A" else VB)[0:m, idx, :]
            self.tt(dst, ps[0:m, 256:512], BKVB[0:m, 256:512], ALU.add, [pb, BBKV], [BV])
            if sg == 1 and kind == "A" and idx in (9, 10):
                oa, ob = KVO.next()
                self.tt(oa, ps, BKVB, ALU.add, [pb, BBKV], [ob])
                dk, dv = (d["pk"], d["pv"]) if idx == 9 else (d["sk"], d["sv"])
                self.finals.append(self.dma(dk[j], oa[:, 0:256], [ob], []))
                self.finals.append(self.dma(dv[j], oa[:, 256:512], [ob], []))
        if sg == 1:
            for s in range(2):
                ca, cb_ = CK.next()
                self.dma(ca, d["ck"][j, s], [], [cb_])
                ps, pb = self.PS.next()
                for jk in range(2):
                    self.tr(ps[:, jk * 128:(jk + 1) * 128], ca[:, jk * 128:(jk + 1) * 128], [cb_, self.BC], [pb])
                self.act(KTC[:, s, :, :], ps[:, 0:256].rearrange("p (k t) -> p k t", k=2), AF.Identity, [pb], [BCACHE])
                self.dma(VC[:, s, :], d["cv"][j, s], [], [BCACHE], q="pool")

        WOB = WKV.rearrange("p k c -> p (k c)")[:, 0:2 * 8 * 128].rearrange("p (o k c) -> p o k c", o=2, k=8)
        BWOB = self.mkbuf("a_wob", wkv_rng)
        wo_ = d["attn_w_o"][j]

        def wo_ap(oc):
            return (WOA[:, oc], BWOA) if oc < 6 else (WOB[:, oc - 6], BWOB)

        OT = HT
        BO = self.mkbufs(["ao0", "ao1", "ao2", "ao3"], ht_rng)
        items = [("p", c) for c in range(18)] + ([("s", 0), ("s", 1)] if sg == 1 else [])
        work = []
        for (typ, c) in items:
            if typ == "p":
                qc0 = c * 64
                w = c // 6
                gc = c + 18 * sg
                pieces = []
                if gc >= 2:
                    vfull = VA[:, c // 2, :] if c % 2 == 0 else VB[:, (c - 1) // 2, :]
                    pieces.append((c * 64, 128, vfull, 0, BV))
                    vown = VA[0:64, 1 + c // 2, :] if c % 2 == 0 else VB[0:64, (c + 1) // 2, :]
                    pieces.append((128 + c * 64, 64, vown, 1, BV))
                elif gc == 1:
                    pieces.append((64, 128, VB[:, 0, :], 0, BV, True))
                    pieces.append((128 + c * 64, 64, VB[0:64, 1, :], 1, BV))
                else:
                    pieces.append((128, 64, VA[0:64, 1, :], 1, BV))
            else:
                qc0 = NPR + c * 64
                w = 3
                vown = VA[0:64, 10, :] if c == 0 else VB[0:64, 10, :]
                pieces = [(None, 128, VC[:, c, :], 0, BCACHE), (128 + NPR + c * 64, 64, vown, 1, BV)]
            for kv in range(4):
                work.append((c, qc0, w, pieces, kv))

        def stage_a(it):
            (c, qc0, w, pieces, kv) = it
            hf, jk, j0 = kv % 2, kv // 2, (kv // 2) * 4
            rows = slice(hf * 64, hf * 64 + 64)
            rhs_q = QT[rows, j0:j0 + 4, qc0:qc0 + 64]
            pss, pbs = self.PS.next()
            et, eb = ET.next()
            pt, ptb = PT.next()
            offs = []
            off = 0
            for pc in pieces:
                (kc0, nk, vap, bt, vbuf) = pc[0:5]
                if kc0 is None:
                    lk, rk = KTC[rows, c, jk, :], [BCACHE]
                else:
                    lk, rk = KT[rows, jk, kc0:kc0 + nk], [BK]
                self.mm(pss[0:nk, off:off + 256], lk, rhs_q, True, True, rk + [BQ[w]], [pbs])
                offs.append(off)
                off += 256
            for pi, pc in enumerate(pieces):
                (kc0, nk, vap, bt, vbuf) = pc[0:5]
                o_ = offs[pi]
                self.act(et[0:nk, o_:o_ + 256], pss[0:nk, o_:o_ + 256], AF.Exp, [pbs], [eb], scale=0.125)
                self.tt(pt[0:nk, o_:o_ + 256].rearrange("p (j q) -> p j q", j=4),
                        et[0:nk, o_:o_ + 256].rearrange("p (j q) -> p j q", j=4),
                        EBT[bt][0:nk, hf, j0:j0 + 4, :], ALU.mult, [eb, BEB], [ptb], eng="pool")
                if len(pc) > 5:
                    self.memset(pt[0:64, o_:o_ + 256], 0.0, [ptb], eng="pool")
            return (pt, ptb, offs)

        def stage_b(it, st):
            (c, qc0, w, pieces, kv) = it
            (pt, ptb, offs) = st
            hf, jk, j0 = kv % 2, kv // 2, (kv // 2) * 4
            rows = slice(hf * 64, hf * 64 + 64)
            pso, pbo = self.PS.next()
            np_ = len(pieces)
            for pi, pc in enumerate(pieces):
                (kc0, nk, vap, bt, vbuf) = pc[0:5]
                o_ = offs[pi]
                self.mm(pso[:, 0:256], vap[:, jk * 128:(jk + 1) * 128], pt[0:nk, o_:o_ + 256], pi == 0, pi == np_ - 1, [vbuf, ptb], [pbo])
            for pi, pc in enumerate(pieces):
                (kc0, nk, vap, bt, vbuf) = pc[0:5]
                o_ = offs[pi]
                self.mm(pso[:, 256:512], self.ONES1[0:nk, :], pt[0:nk, o_:o_ + 256], pi == 0, False, [self.BC, ptb], [pbo])
            self.mm(pso[:, 256:512], self.ONES1[0:1, :], ESKR[0:1, hf, j0:j0 + 4, :], False, True, [self.BC, BEB], [pbo])
            dn, dnb = DEN.next()
            self.recip(dn[rows, :], pso[rows, 256:512], [pbo], [dnb])
            self.tt(OT[rows, j0:j0 + 4, qc0:qc0 + 64], pso[rows, 0:256].rearrange("p (j q) -> p j q", j=4),
                    dn[rows, :].rearrange("p (j q) -> p j q", j=4), ALU.mult, [pbo, dnb], [BO[w]])

        DEPTH = 3
        sts = {}
        for i in range(min(DEPTH, len(work))):
            sts[i] = stage_a(work[i])
        for i in range(len(work)):
            stage_b(work[i], sts.pop(i))
            if i + DEPTH < len(work):
                sts[i + DEPTH] = stage_a(work[i + DEPTH])

        for oc in range(6, 8):
            dst, dbuf = wo_ap(oc)
            for hf in range(2):
                for grp in range(2):
                    h0 = (0, 8)[grp] if hf == 0 else (4, 12)[grp]
                    src = wo_[h0 * 64:(h0 + 4) * 64, oc * 128:(oc + 1) * 128].rearrange("(j p) c -> p j c", p=64)
                    self.dma(dst[hf * 64:(hf + 1) * 64, grp * 4:(grp + 1) * 4, :], src, [], [dbuf], q="pool")
        BY = self.mkbuf("a_y", (e0, e1))
        Y2 = self.ARENA[:, kv0:kv0 + 8 * WIN].rearrange("p (k t) -> p k t", k=8)
        BY2 = self.mkbuf("a_y2", (kv0, kv0 + 8 * WIN))
        Ys = [(Y, BY), (Y2, BY2)]
        for wi_, (w, t0, n, samp) in enumerate(self.windows(sg)):
            Yw, BYw = Ys[wi_ % 2]
            for oc in range(8):
                wo_t, wo_b = wo_ap(oc)
                ps, pb = self.PS.next()
                for k in range(8):
                    self.mm(ps[:, 0:n], wo_t[:, k, :], OT[:, k, t0:t0 + n], k == 0, k == 7, [wo_b, BO[w]], [pb])
                self.act(Yw[:, oc, 0:n], ps[:, 0:n], AF.Identity, [pb, self.BP], [BYw], bias=P[:, boc + oc:boc + oc + 1])
            self.resid(w, t0, n, Yw[:, :, 0:n], BYw, gc1)

    def convmod(self, l, sg):
        self.scratch_reset()
        d = self.d
        P = self.PARAMS
        HT = self.carve_bf(8 * 1280).rearrange("p (k t) -> p k t", k=8)
        ht_rng = self.last_rng
        BH = self.mkbufs(["ch0", "ch1", "ch2", "ch3"])
        GW = 30 + NPR + 2 * 94
        GLU = self.carve_bf(8 * GW).rearrange("p (k t) -> p k t", k=8)
        glu_rng = self.last_rng
        BG = self.mkbufs(["cg0", "cg1", "cg2", "cg3", "cgc"])
        BGC = BG[4]
        ZB = self.carve_bf(8 * 1280).rearrange("p (k t) -> p k t", k=8)
        BZ = self.mkbufs(["cz0", "cz1", "cz2", "cz3"])
        a0 = self.aoff
        S1 = self.carve(1280)
        S2 = self.carve(1280)
        BS = self.mkbufs(["cs0", "cs1", "cs2", "cs3"], (a0, self.aoff))
        DWR = self.ring("dw", 2, 31 * 128, bf16=True)
        ZSQ = self.ring("zsq", 3, WIN, bf16=True)
        SIG = self.ring("sig", 2, WIN)
        TMP = self.ring("ctmp", 2, WIN)
        GT = self.carve(8 * 96).rearrange("p (k t) -> p k t", k=8)
        BGT = self.mkbuf("c_gt")
        GOST = self.ring("gost", 1, 1024)
        Y = self.carve(8 * WIN).rearrange("p (k t) -> p k t", k=8)
        BY = self.mkbuf("c_y")
        gc0, gc1 = self.pcol["ng"] + (l * 4 + 0) * 8, self.pcol["ng"] + (l * 4 + 1) * 8
        cb1, cwd, cbd, clg, clb, cb2, stc = (self.pcol[k] for k in ("cb1", "cwd", "cbd", "clg", "clb", "cb2", "stc"))
        self.norm_h(sg, gc0, lambda k, t0, nn, samp: HT[:, k, t0:t0 + nn], BH)
        if sg == 0:
            self.memset(GLU[:, :, 0:30], 0.0, [BGC])
        else:
            self.cp(GLU[:, :, 0:30], self.CCARRY, [self.BCARRY], [BGC])
            for s in range(2):
                src = P[:, stc + s * 240:stc + (s + 1) * 240].rearrange("p (r k) -> p k r", k=8)
                self.cp(GLU[:, :, 30 + NPR + s * 94:30 + NPR + s * 94 + 30], src, [self.BP], [BG[3]])
        w1 = d["conv_w_pw1"][0].rearrange("(k p) c -> p k c", p=128)
        for jc in range(8):
            wv, wb = self.wslot(16)
            self.load_w(wv[:, 0:8, :], w1[:, :, jc * 128:(jc + 1) * 128], wb)
            self.load_w(wv[:, 8:16, :], w1[:, :, 1024 + jc * 128:1024 + (jc + 1) * 128], wb)
            for (w, t0, nn, samp) in self.windows(sg):
                psa, pba = self.PS.next()
                psg, pbg = self.PS.next()
                for k in range(8):
                    self.mm(psa[:, 0:nn], wv[:, k, :], HT[:, k, t0:t0 + nn], k == 0, k == 7, [wb, BH[w]], [pba])
                for k in range(8):
                    self.mm(psg[:, 0:nn], wv[:, 8 + k, :], HT[:, k, t0:t0 + nn], k == 0, k == 7, [wb, BH[w]], [pbg])
                sg_, sgb = SIG.next()
                self.act(sg_[:, 0:nn], psg[:, 0:nn], AF.Sigmoid, [pbg, self.BP], [sgb], bias=P[:, cb1 + 8 + jc:cb1 + 9 + jc])
                ba = P[:, cb1 + jc:cb1 + jc + 1]
                if samp:
                    dst = GLU[:, jc, 30 + NPR:GW].rearrange("p (s q) -> p s q", s=2)[:, :, 30:94]
                    a3 = psa[:, 0:128].rearrange("p (s q) -> p s q", s=2)
                    s3 = sg_[:, 0:128].rearrange("p (s q) -> p s q", s=2)
                    self.stt(dst, a3, ba, s3, ALU.add, ALU.mult, [pba, sgb, self.BP], [BG[3]])
                    self.stt(GT[:, jc, 32:96].rearrange("p (s q) -> p s q", s=2)[:, :, 0:30], a3[:, :, 34:64], ba, s3[:, :, 34:64],
                             ALU.add, ALU.mult, [pba, sgb, self.BP], [BGT])
                else:
                    self.stt(GLU[:, jc, 30 + t0:30 + t0 + nn], psa[:, 0:nn], ba, sg_[:, 0:nn], ALU.add, ALU.mult, [pba, sgb, self.BP], [BG[w]])
                    if sg == 1 and w == 2:
                        self.stt(GT[:, jc, 0:30], psa[:, nn - 30:nn], ba, sg_[:, nn - 30:nn], ALU.add, ALU.mult, [pba, sgb, self.BP], [BGT])
        if sg == 0:
            self.cp(self.CCARRY, GLU[:, :, NPR:NPR + 30], [BG[2]], [self.BCARRY])
        else:
            for seg in range(3):
                c0 = 0 if seg == 0 else 32 * seg
                oa, ob = GOST.next()
                for h in range(2):
                    ps, pb = self.PS.next()
                    for kk in range(4):
                        self.tr(ps[0:30, kk * 128:(kk + 1) * 128], GT[:, h * 4 + kk, c0:c0 + 30], [BGT, self.BC], [pb])
                    self.act(oa[0:30, h * 512:(h + 1) * 512], ps[0:30, :], AF.Identity, [pb], [ob])
                dst = d["pconv"] if seg == 0 else d["sconv"][seg - 1]
                self.finals.append(self.dma(dst[:, :], oa[0:30, :], [ob], []))
        C = HT
        BCt = self.mkbufs(["cc0", "cc1", "cc2", "cc3"], ht_rng)
        pend_stats = None

        def emit_stats(jc, w, t0, nn, zq, zqb):
            ps1, pb1 = self.PS.next()
            self.mm(ps1[:, 0:nn], self.ONESM, ZB[:, jc, t0:t0 + nn], True, True, [self.BC, BZ[w]], [pb1])
            ps2, pb2 = self.PS.next()
            self.mm(ps2[:, 0:nn], self.ONESM, zq[:, 0:nn], True, True, [self.BC, zqb], [pb2])
            if jc == 0:
                self.cp(S1[:, t0:t0 + nn], ps1[:, 0:nn], [pb1], [BS[w]])
                self.cp(S2[:, t0:t0 + nn], ps2[:, 0:nn], [pb2], [BS[w]])
            else:
                self.tt(S1[:, t0:t0 + nn], S1[:, t0:t0 + nn], ps1[:, 0:nn], ALU.add, [pb1, BS[w]], [BS[w]])
                self.tt(S2[:, t0:t0 + nn], S2[:, t0:t0 + nn], ps2[:, 0:nn], ALU.add, [pb2, BS[w]], [BS[w]])
        for jc in range(8):
            dw_, dwb = DWR.next()
            dw = dw_.rearrange("p (k c) -> p k c", k=31)
            wk = P[:, cwd + jc:cwd + jc + 31 * 8].rearrange("p (k j) -> p k j", j=8)[:, :, 0:1]
            self.tt(dw, self.IDENT.unsqueeze(1).to_broadcast([128, 31, 128]), wk.to_broadcast([128, 31, 128]), ALU.mult,
                    [self.BC, self.BP], [dwb])
            for (w, t0, nn, samp) in self.windows(sg):
                ps, pb = self.PS.next()
                gr = [BG[3]] if samp else [BG[w], (BGC if w == 0 else BG[w - 1])]
                for k in range(31):
                    if samp:
                        rhs = GLU[:, jc, 30 + NPR:GW].rearrange("p (s q) -> p s q", s=2)[:, :, k:k + 64]
                    else:
                        rhs = GLU[:, jc, t0 + k:t0 + k + nn]
                    self.mm(ps[:, 0:nn], dw[:, k, :], rhs, k == 0, k == 30, [dwb] + gr, [pb])
                bd = P[:, cbd + jc:cbd + jc + 1]
                self.act(ZB[:, jc, t0:t0 + nn], ps[:, 0:nn], AF.Identity, [pb, self.BP], [BZ[w]], bias=bd)
                zq, zqb = ZSQ.next()
                self.act(zq[:, 0:nn], ps[:, 0:nn], AF.Square, [pb, self.BP], [zqb], bias=bd)
                if pend_stats is not None:
                    emit_stats(*pend_stats)
                pend_stats = (jc, w, t0, nn, zq, zqb)
        emit_stats(*pend_stats)
        for (w, t0, nn, samp) in self.windows(sg):
            tm, tmb = TMP.next()
            self.tt(tm[:, 0:nn], S1[:, t0:t0 + nn], S1[:, t0:t0 + nn], ALU.mult, [BS[w]], [tmb])
            self.tt(S2[:, t0:t0 + nn], S2[:, t0:t0 + nn], tm[:, 0:nn], ALU.subtract, [BS[w], tmb], [BS[w]])
            self.act(S2[:, t0:t0 + nn], S2[:, t0:t0 + nn], AF.Sqrt, [BS[w], self.BC], [BS[w]], bias=self.EPS[:, 1:2])
            self.recip(S2[:, t0:t0 + nn], S2[:, t0:t0 + nn], [BS[w]], [BS[w]])
            for jc in range(8):
                tm, tmb = TMP.next()
                self.tt(tm[:, 0:nn], ZB[:, jc, t0:t0 + nn], S1[:, t0:t0 + nn], ALU.subtract, [BZ[w], BS[w]], [tmb])
                self.tt(tm[:, 0:nn], tm[:, 0:nn], S2[:, t0:t0 + nn], ALU.mult, [tmb, BS[w]], [tmb])
                self.act(C[:, jc, t0:t0 + nn], tm[:, 0:nn], AF.Silu, [tmb, self.BP], [BCt[w]],
                         bias=P[:, clb + jc:clb + jc + 1], scale=P[:, clg + jc:clg + jc + 1])
        w2 = d["conv_w_pw2"][0].rearrange("(k p) c -> p k c", p=128)
        assert glu_rng[1] - glu_rng[0] >= 8 * WIN
        Y2 = self.ARENA[:, glu_rng[0]:glu_rng[0] + 8 * WIN].rearrange("p (k t) -> p k t", k=8)
        BY2 = self.mkbuf("c_y2", (glu_rng[0], glu_rng[0] + 8 * WIN))
        self.out_linear(sg, 8, lambda k, t0, nn: C[:, k, t0:t0 + nn],
                        lambda dst, oc, wb: self.load_w(dst, w2[:, :, oc * 128:(oc + 1) * 128], wb),
                        cb2, gc1, BCt, [(Y, BY), (Y2, BY2)], None)

    def cmlp(self, l, sg):
        self.scratch_reset()
        d = self.d
        P = self.PARAMS
        WV = self.carve_bf(8 * 2048).rearrange("p (k c) -> p k c", k=8)
        BWV = self.mkbuf("m_wv")
        a0 = self.aoff
        LNG = self.carve(2048)
        LNB = self.carve(2048)
        BVB = self.carve(2048)
        BLN = self.mkbuf("m_ln", (a0, self.aoff))
        HT = self.carve_bf(8 * WIN).rearrange("p (k t) -> p k t", k=8)
        BH = self.mkbuf("m_ht")
        U = self.carve_bf(16 * WIN).rearrange("p (k t) -> p k t", k=16)
        BU = self.mkbuf("m_u")
        VT = self.carve_bf(3 * 2048).rearrange("p (a c) -> p a c", a=3)
        BVT = self.mkbufs(["m_vt0", "m_vt1", "m_vt2"])
        VR = self.carve(2048)
        BVR = self.mkbuf("m_vr")
        a0 = self.aoff
        STAT = self.carve(4 * 6).rearrange("p (a b) -> p a b", a=4)
        MV = self.carve(4)
        BST = self.mkbuf("m_st", (a0, self.aoff))
        WSIN = self.carve(4 * 128).rearrange("p (g j) -> p g j", g=4)
        wsin_rng = self.last_rng
        BWSI = self.mkbuf("m_wsi")
        WST = self.carve_bf(2 * 4 * 128).rearrange("p (v g i) -> p v g i", v=2, g=4)
        BWS = self.mkbuf("m_ws")
        a0 = self.aoff
        BSR = self.carve(4 * 128).rearrange("p (g i) -> p g i", g=4)
        BSH = self.carve_bf(2 * 4 * 128).rearrange("p (v g i) -> p v g i", v=2, g=4)
        BBS = self.mkbuf("m_bs", (a0, self.aoff))
        Y = self.carve(8 * WIN).rearrange("p (k t) -> p k t", k=8)
        BY = self.mkbuf("m_y")
        gc0, gc1 = self.pcol["ng"] + (l * 4 + 0) * 8, self.pcol["ng"] + (l * 4 + 1) * 8
        mbu, mbo = self.pcol["mbu"], self.pcol["mbo"]
        wi = d["cmlp_w_in"][0].rearrange("(k p) c -> p k c", p=128)
        wo_ = d["cmlp_w_out"][0].rearrange("(k p) c -> p k c", p=128)
        for cb in range(4):
            self.dma(WV[:, :, cb * 512:(cb + 1) * 512], wi[:, :, 2048 + cb * 512:2048 + (cb + 1) * 512], [], [BWV], q="pool")
        self.dma(LNG, d["cmlp_ln_g"][0].partition_broadcast(128), [], [BLN])
        self.dma(LNB, d["cmlp_ln_b"][0].partition_broadcast(128), [], [BLN])
        self.dma(BVB, d["cmlp_b_in"][0, 2048:4096].partition_broadcast(128), [], [BLN])
        ws = d["cmlp_w_s"][0]
        bs = d["cmlp_b_s"][0]
        for v in range(2 if sg == 1 else 1):
            if v == 0:
                self.dma(WSIN, ws.rearrange("g i j -> i g j"), [BWSI], [BWSI])
            else:
                self.memset(WSIN, 0.0, [BWSI])
                self.dma(WSIN[0:64, :, 0:64], ws[:, 0:64, 0:64].rearrange("g i j -> i g j"), [BWSI], [BWSI])
                self.dma(WSIN[64:128, :, 64:128], ws[:, 0:64, 0:64].rearrange("g i j -> i g j"), [BWSI], [BWSI])
            ps, pb = self.PS.next()
            for g in range(4):
                self.tr(ps[:, g * 128:(g + 1) * 128], WSIN[:, g, :], [BWSI, self.BC], [pb])
            self.act(WST[:, v, :, :], ps.rearrange("p (g i) -> p g i", g=4), AF.Identity, [pb], [BWS])
            if v == 0:
                self.memset(WST[64:128, 0, :, 0:64], 0.0, [BWS])

        BSL = self.ARENA[:, wsin_rng[0]:wsin_rng[1]].bitcast(BF16).rearrange("p (v g i) -> p v g i", v=2, g=4)
        BBL = self.mkbuf("m_bsl", wsin_rng)
        for v in range(2 if sg == 1 else 1):
            if v == 0:
                self.dma(BSR[0:1, :, :], bs.rearrange("g i -> (g i)").rearrange("(o g i) -> o g i", o=1, g=4), [BBS], [BBS])
            else:
                self.dma(BSR[0:1, :, 0:64], bs[:, 0:64].unsqueeze(0), [BBS], [BBS])
                self.dma(BSR[0:1, :, 64:128], bs[:, 0:64].unsqueeze(0), [BBS], [BBS])
            self.cp(BSH[0:1, v], BSR[0:1], [BBS], [BBS])
            self.tt(BSR[0:1], BSR[0:1], BSH[0:1, v], ALU.subtract, [BBS], [BBS])
            self.cp(BSL[0:1, v], BSR[0:1], [BBS], [BBL])
        def make_ht(wn):
            (w_, t0_, nn_, samp_) = wn
            rs, rb = self.rstd_of(self.X[:, :, t0_:t0_ + nn_], nn_, [self.BX[w_]])
            for k in range(8):
                self.stt(HT[:, k, 0:nn_], self.X[:, k, t0_:t0_ + nn_], P[:, gc0 + k:gc0 + k + 1], rs[:, 0:nn_], ALU.mult, ALU.mult,
                         [self.BX[w_], rb, self.BP], [BH])
        wlist = self.windows(sg)
        make_ht(wlist[0])
        for widx, (w, t0, nn, samp) in enumerate(wlist):
            var = 1 if samp else 0
            def u_chunk(jc):
                wv, wb = self.wslot(8)
                self.load_w(wv, wi[:, :, jc * 128:(jc + 1) * 128], wb)
                ps, pb = self.PS.next()
                for k in range(8):
                    self.mm(ps[:, 0:nn], wv[:, k, :], HT[:, k, 0:nn], k == 0, k == 7, [wb, BH], [pb])
                self.act(U[:, jc, 0:nn], ps[:, 0:nn], AF.Gelu_apprx_tanh, [pb, self.BP], [BU], bias=P[:, mbu + jc:mbu + jc + 1])

            def v_tile(t):
                for cb in range(4):
                    ps, pb = self.PS.next()
                    for k in range(8):
                        self.mm(ps, HT[:, k, t * 128:(t + 1) * 128], WV[:, k, cb * 512:(cb + 1) * 512], k == 0, k == 7, [BH, BWV], [pb])
                    vs = VR[:, cb * 512:(cb + 1) * 512]
                    self.tt(vs, ps, BVB[:, cb * 512:(cb + 1) * 512], ALU.add, [pb, BLN], [BVR])
                    self.act(vs, vs, AF.Gelu_apprx_tanh, [BVR], [BVR])
                    self.S.op("dve", (lambda o, i: lambda e: e.bn_stats(out=o, in_=i))(STAT[:, cb, :], vs), [BVR], [BST])
                self.S.op("dve", lambda e: e.bn_aggr(out=MV[:, 0:2], in_=STAT), [BST], [BST])
                self.act(MV[:, 2:3], MV[:, 1:2], AF.Sqrt, [BST, self.BC], [BST], bias=self.EPS[:, 1:2])
                self.recip(MV[:, 2:3], MV[:, 2:3], [BST], [BST])
                self.stt(MV[:, 3:4], MV[:, 0:1], -1.0, MV[:, 2:3], ALU.mult, ALU.mult, [BST], [BST])
                self.act(VR, VR, AF.Identity, [BVR, BST], [BVR], bias=MV[:, 3:4], scale=MV[:, 2:3])
                self.tt(VR, VR, LNG, ALU.mult, [BVR, BLN], [BVR])
                if samp:
                    self.tt(VR, VR, LNB, ALU.add, [BVR, BLN], [BVR])
                    self.cp(VT[:, t, :], VR, [BVR], [BVT[t]])
                    self.finals.append(self.dma(d["scv"][:, :], VR, [BVR], []))
                else:
                    self.tt(VT[:, t, :], VR, LNB, ALU.add, [BVR, BLN], [BVT[t]])

            def spatial(t):
                for g in range(4):
                    ps, pb = self.PS.next()
                    for cc in range(4):
                        ch = g * 4 + cc
                        self.mm(ps[:, cc * 128:(cc + 1) * 128], VT[:, t, ch * 128:(ch + 1) * 128], WST[:, var, g, :], True, False, [BVT[t], BWS], [pb])
                        self.mm(ps[:, cc * 128:(cc + 1) * 128], self.ONES1[0:1, :], BSH[0:1, var, g, :], False, False, [self.BC, BBS], [pb])
                        self.mm(ps[:, cc * 128:(cc + 1) * 128], self.ONES1[0:1, :], BSL[0:1, var, g, :], False, True, [self.BC, BBL], [pb])
                    uu = U[:, g * 4:(g + 1) * 4, t * 128:(t + 1) * 128]
                    self.tt(uu, ps.rearrange("p (c i) -> p c i", c=4), uu, ALU.mult, [pb, BU], [BU])

            nt = nn // 128
            sched_u = {0: range(0, 6), 1: range(6, 11), 2: range(11, 16)} if nt == 3 else {0: range(0, 16)}
            for t in range(nt):
                v_tile(t)
                for jc in sched_u[t]:
                    u_chunk(jc)
            for t in range(nt):
                spatial(t)
            if widx + 1 < len(wlist):
                make_ht(wlist[widx + 1])
            for oc in range(8):
                wv, wb = self.wslot(16)
                self.load_w(wv, wo_[:, :, oc * 128:(oc + 1) * 128], wb)
                ps, pb = self.PS.next()
                for k in range(16):
                    self.mm(ps[:, 0:nn], wv[:, k, :], U[:, k, 0:nn], k == 0, k == 15, [wb, BU], [pb])
                self.act(Y[:, oc, 0:nn], ps[:, 0:nn], AF.Identity, [pb, self.BP], [BY], bias=P[:, mbo + oc:mbo + oc + 1])
            self.resid(w, t0, nn, Y[:, :, 0:nn], BY, gc1)


_PROG = None
_OH = None


def _get_prog():
    global _PROG, _OH
    if _PROG is None:
        p = Prog()
        p.build()
        _PROG = p
        _OH = onehot2_const()
    return _PROG


WEIGHT_KEYS = ["rel_bias_table", "norm_gain", "attn_w_qkv", "attn_b_qkv", "attn_w_o", "attn_b_o", "attn_sinks",
               "conv_w_pw1", "conv_b_pw1", "conv_w_dw", "conv_b_dw", "conv_ln_g", "conv_ln_b", "conv_w_pw2",
               "conv_b_pw2", "cmlp_w_in", "cmlp_b_in", "cmlp_ln_g", "cmlp_ln_b", "cmlp_w_s", "cmlp_b_s",
               "cmlp_w_out", "cmlp_b_out", "ffn_w_up", "ffn_w_dw", "ffn_b_dw", "ffn_w_down"]


def kernel(**inputs):
    p = _get_prog()
    f = lambda a: np.ascontiguousarray(np.asarray(a, dtype=np.float32))
    xp, xs = f(inputs["x_prompt"]), f(inputs["x_sample"])
    cak, cav = f(inputs["cache_attn_k"]), f(inputs["cache_attn_v"])
    stc, stf = f(inputs["state_conv"]), f(inputs["state_ffn_conv"])
    wts = {k: f(inputs[k]) for k in WEIGHT_KEYS}
    in_maps = []
    for c in range(8):
        b, hf = c // 2, c % 2
        s0 = 0 if hf == 0 else 4096 - TP
        m = dict(wts)
        m["xin"] = np.ascontiguousarray(np.concatenate([xp[b, s0:s0 + TP], xs[2 * c], xs[2 * c + 1]], axis=0))
        m["ck"] = np.ascontiguousarray(cak[:, 2 * c:2 * c + 2].reshape(2, 2, 128, 256))
        m["cv"] = np.ascontiguousarray(cav[:, 2 * c:2 * c + 2].reshape(2, 2, 128, 256))
        m["stc"] = np.ascontiguousarray(stc[0, 2 * c:2 * c + 2])
        m["stf"] = np.ascontiguousarray(stf[:, 2 * c:2 * c + 2])
        m["oh"] = _OH
        in_maps.append(m)
    res = run_bass_kernel_spmd(p.nc, in_maps, core_ids=list(range(8)))
    R = res.results
    y_prompt = np.zeros((4, 4096, 1024), np.float32)
    y_sample = np.zeros((16, 64, 1024), np.float32)
    p_k = np.zeros((2, 4, 128, 4, 64), np.float32)
    p_v = np.zeros((2, 4, 128, 4, 64), np.float32)
    p_conv = np.zeros((1, 4, 30, 1024), np.float32)
    p_ffn = np.zeros((4, 4, 2, 5632), np.float32)
    s_k = np.zeros((2, 16, 64, 4, 64), np.float32)
    s_v = np.zeros((2, 16, 64, 4, 64), np.float32)
    s_conv = np.zeros((1, 16, 30, 1024), np.float32)
    s_cv = np.zeros((1, 16, 64, 2048), np.float32)
    s_ffn = np.zeros((4, 16, 2, 5632), np.float32)
    for c in range(8):
        b, hf = c // 2, c % 2
        r = R[c]
        yo = np.asarray(r["yout"])
        if hf == 0:
            y_prompt[b, 0:TP] = yo[0:TP]
        else:
            y_prompt[b, TP:4096] = yo[2 * TP - 4096:TP]
            p_k[:, b] = np.asarray(r["pk"]).reshape(2, 128, 4, 64)
            p_v[:, b] = np.asarray(r["pv"]).reshape(2, 128, 4, 64)
            p_conv[0, b] = np.asarray(r["pconv"])
            p_ffn[:, b] = np.asarray(r["pffn"])
        y_sample[2 * c] = yo[TP:TP + 64]
        y_sample[2 * c + 1] = yo[TP + 64:TP + 128]
        s_k[:, 2 * c:2 * c + 2] = np.asarray(r["sk"]).reshape(2, 2, 64, 4, 64)
        s_v[:, 2 * c:2 * c + 2] = np.asarray(r["sv"]).reshape(2, 2, 64, 4, 64)
        s_conv[0, 2 * c:2 * c + 2] = np.asarray(r["sconv"])
        s_cv[0, 2 * c:2 * c + 2] = np.asarray(r["scv"]).reshape(2, 64, 2048)
        s_ffn[:, 2 * c:2 * c + 2] = np.asarray(r["sffn"])
    return (y_prompt, y_sample, p_k, p_v, p_conv, p_ffn, s_k, s_v, s_conv, s_cv, s_ffn)
```

```python
import contextlib
import numpy as np
import concourse.bass as bass
import concourse.mybir as mybir
from concourse.bass_utils import run_bass_kernel_spmd

F32 = mybir.dt.float32
BF16 = mybir.dt.bfloat16
AF = mybir.ActivationFunctionType
ALU = mybir.AluOpType

ENGS = ("pe", "act", "dve", "pool", "sp")
NPR = 1152
WIN = 384
TP = 2304
NTOK = 2432
DFF = 2816
NFC = 44


class Buf:
    __slots__ = ("name", "w", "r")

    def __init__(self, name, fence=None):
        self.name = name
        self.w = None
        self.r = list(fence) if fence else []


class Op:
    __slots__ = ("eng", "fn", "deps", "dma", "has_dep", "tok", "id")


class Sched:
    def __init__(self, nc, n_dma_sems=24):
        self.nc = nc
        self.ops = []
        self.per_eng = {e: [] for e in ENGS}
        self.n_dma_sems = n_dma_sems

    def op(self, eng, fn, reads=(), writes=(), dma=False):
        o = Op()
        o.eng, o.fn, o.dma, o.has_dep, o.tok, o.id = eng, fn, dma, False, None, len(self.ops)
        deps = set()
        for b in reads:
            if b.w is not None:
                deps.add(b.w)
        for b in writes:
            if b.w is not None:
                deps.add(b.w)
            deps.update(b.r)
        if eng == "pe":
            deps = {dd for dd in deps if self.ops[dd].eng != "pe"}
        o.deps = deps
        for b in reads:
            b.r.append(o.id)
        for b in writes:
            b.w = o.id
            b.r = []
        self.ops.append(o)
        self.per_eng[eng].append(o)
        return o

    def emit(self, final_wait_ops=()):
        nc, ops = self.nc, self.ops
        for o in ops:
            for d in o.deps:
                ops[d].has_dep = True
        for o in final_wait_ops:
            o.has_dep = True
        with contextlib.ExitStack() as st:
            esem = {e: st.enter_context(nc.semaphore("s_" + e)) for e in ENGS}
            dsems = {e: [st.enter_context(nc.semaphore("d_%s_%d" % (e, i)))
                         for i in range(self.n_dma_sems)] for e in ("sp", "pool")}
            ecnt = {e: 0 for e in ENGS}
            dcnt = {e: [0] * self.n_dma_sems for e in dsems}
            drr = {e: 0 for e in dsems}
            for e in ENGS:
                for o in self.per_eng[e]:
                    if not o.has_dep:
                        continue
                    if o.dma:
                        j = drr[e]
                        drr[e] = (j + 1) % self.n_dma_sems
                        prev = dcnt[e][j]
                        dcnt[e][j] += 16
                        o.tok = (dsems[e][j], dcnt[e][j], prev)
                    else:
                        ecnt[e] += 1
                        o.tok = (esem[e], ecnt[e], None)
            block = st.enter_context(nc.Block())

            def run(e, eng):
                waited = {}
                for o in self.per_eng[e]:
                    need = {}
                    for d in o.deps:
                        sem, val, _ = ops[d].tok
                        k = id(sem)
                        if k not in need or need[k][1] < val:
                            need[k] = (sem, val)
                    if o.dma and o.tok is not None and o.tok[2]:
                        sem, _, prev = o.tok
                        k = id(sem)
                        if k not in need or need[k][1] < prev:
                            need[k] = (sem, prev)
                    for k, (sem, val) in need.items():
                        if waited.get(k, 0) >= val:
                            continue
                        eng.wait_ge(sem, val)
                        waited[k] = val
                    ins = o.fn(eng)
                    if o.tok is not None:
                        ins.then_inc(o.tok[0], 16 if o.dma else 1)
                if e == "sp":
                    for o in final_wait_ops:
                        sem, val, _ = o.tok
                        eng.wait_ge(sem, val)

            @block.tensor
            def _(eng):
                run("pe", eng)

            @block.scalar
            def _(eng):
                run("act", eng)

            @block.vector
            def _(eng):
                run("dve", eng)

            @block.gpsimd
            def _(eng):
                run("pool", eng)

            @block.sync
            def _(eng):
                run("sp", eng)


class Ring:
    def __init__(self, name, aps, bufs):
        self.aps = aps
        self.bufs = bufs
        self.i = 0

    def next(self):
        j = self.i
        self.i = (j + 1) % len(self.aps)
        return self.aps[j], self.bufs[j]


def q_lo(j):
    return j if j < 4 else 8 + (j - 4)


def q_up(j):
    return 4 + j if j < 4 else 12 + (j - 4)


def t5_bucket_np(rel):
    half, max_exact = 16, 8
    n = np.abs(rel)
    log_ratio = np.log(np.maximum(n, 1).astype(np.float32) / np.float32(max_exact)) / np.float32(np.log(128 / max_exact))
    large = np.minimum(max_exact + (log_ratio * np.float32(half - max_exact)).astype(np.int32), half - 1)
    return np.where(rel > 0, half, 0) + np.where(n < max_exact, n, large)


def onehot2_const():
    import ml_dtypes
    oh = onehot_const()
    return np.ascontiguousarray(np.concatenate([oh, oh], axis=0).astype(ml_dtypes.bfloat16))


def onehot_const():
    q = np.arange(64)[:, None]
    j = np.arange(192)[None, :]
    bk = t5_bucket_np((j - 128) - q)
    oh = np.zeros((32, 64, 192), np.float32)
    for b in range(32):
        oh[b] = (bk == b)
    return oh


class Prog:
    def __init__(self):
        self.nc = nc = bass.Bass("TRN2", target_bir_lowering=False)
        self.S = Sched(nc)
        self.st = contextlib.ExitStack()
        self.d = {}
        self.outs = []
        self.live = []
        self.phase = 0


    def din(self, name, shape):
        self.d[name] = self.nc.dram_tensor(name, list(shape), F32, kind="ExternalInput").ap()
        return self.d[name]

    def dout(self, name, shape):
        self.d[name] = self.nc.dram_tensor(name, list(shape), F32, kind="ExternalOutput").ap()
        return self.d[name]

    def mm(self, out, lhsT, rhs, start, stop, R, W):
        return self.S.op("pe", lambda e: e.matmul(out, lhsT=lhsT, rhs=rhs, start=start, stop=stop), R, W)

    def tr(self, out, in_, R, W):
        ident = self.IDENT[0:in_.shape[0], 0:in_.shape[0]]
        return self.S.op("pe", lambda e: e.transpose(out, in_, ident), R, W)

    def act(self, out, in_, func, R, W, bias=None, scale=1.0):
        if bias is None:
            return self.S.op("act", lambda e: e.activation(out=out, in_=in_, func=func, scale=scale), R, W)
        return self.S.op("act", lambda e: e.activation(out=out, in_=in_, func=func, bias=bias, scale=scale), R, W)

    def stt(self, out, in0, scalar, in1, op0, op1, R, W, eng="dve"):
        return self.S.op(eng, lambda e: e.scalar_tensor_tensor(out=out, in0=in0, scalar=scalar, in1=in1, op0=op0, op1=op1), R, W)

    def tt(self, out, in0, in1, op, R, W, eng="dve"):
        return self.S.op(eng, lambda e: e.tensor_tensor(out=out, in0=in0, in1=in1, op=op), R, W)

    def cp(self, out, in_, R, W, eng="dve"):
        return self.S.op(eng, lambda e: e.tensor_copy(out=out, in_=in_), R, W)

    def recip(self, out, in_, R, W):
        return self.S.op("dve", lambda e: e.reciprocal(out=out, in_=in_), R, W)

    def memset(self, ap, val, W, eng="dve"):
        return self.S.op(eng, lambda e: e.memset(ap, val), (), W)

    def dma(self, out, in_, R, W, q="sp"):
        return self.S.op(q, lambda e: e.dma_start(out=out, in_=in_), R, W, dma=True)

    def carve(self, nwords):
        a = self.aoff
        self.aoff += nwords
        assert self.aoff <= self.NA, ("arena overflow", self.aoff, self.NA)
        self.last_rng = (a, a + nwords)
        return self.ARENA[:, a:a + nwords]

    def carve_bf(self, nel):
        return self.carve((nel + 1) // 2).bitcast(BF16)[:, 0:nel]

    def mkbufs(self, names, rng=None):
        rng = rng or self.last_rng
        fence = []
        keep = []
        for (a, b, bf, ph) in self.live:
            if a < rng[1] and rng[0] < b:
                if bf.w is not None:
                    fence.append(bf.w)
                fence.extend(bf.r)
                if ph < self.phase and rng[0] <= a and b <= rng[1]:
                    continue
            keep.append((a, b, bf, ph))
        self.live = keep
        out = [Buf(n, fence) for n in names]
        for bf in out:
            self.live.append((rng[0], rng[1], bf, self.phase))
        return out

    def mkbuf(self, name, rng=None):
        return self.mkbufs([name], rng)[0]

    def ring(self, name, n, nwords, bf16=False):
        aps, bufs = [], []
        for i in range(n):
            ap = self.carve_bf(nwords) if bf16 else self.carve(nwords)
            aps.append(ap)
            bufs.append(self.mkbuf("%s%d" % (name, i)))
        return Ring(name, aps, bufs)

    def scratch_reset(self):
        self.aoff = self.scratch_base
        self.phase += 1

    def windows(self, sg):
        w = [(i, i * WIN, WIN, False) for i in range(3)]
        if sg == 1:
            w.append((3, NPR, 128, True))
        return w

    def nsg(self, sg):
        return NPR + (128 if sg == 1 else 0)

    def build(self):
        nc, S, st = self.nc, self.S, self.st
        din, dout = self.din, self.dout
        xin = din("xin", (NTOK, 1024))
        ck = din("ck", (2, 2, 128, 256))
        cv = din("cv", (2, 2, 128, 256))
        stc = din("stc", (2, 30, 1024))
        stf = din("stf", (4, 2, 2, 5632))
        self.d["oh"] = nc.dram_tensor("oh", [64, 64, 192], BF16, kind="ExternalInput").ap()
        relt = din("rel_bias_table", (32, 16))
        ng = din("norm_gain", (4, 4, 1024))
        wqkv = din("attn_w_qkv", (2, 1024, 1536))
        bqkv = din("attn_b_qkv", (2, 1536))
        wo = din("attn_w_o", (2, 1024, 1024))
        bo = din("attn_b_o", (2, 1024))
        sinks = din("attn_sinks", (2, 16))
        cw1 = din("conv_w_pw1", (1, 1024, 2048))
        cb1 = din("conv_b_pw1", (1, 2048))
        cwd = din("conv_w_dw", (1, 31, 1024))
        cbd = din("conv_b_dw", (1, 1024))
        clg = din("conv_ln_g", (1, 1024))
        clb = din("conv_ln_b", (1, 1024))
        cw2 = din("conv_w_pw2", (1, 1024, 1024))
        cb2 = din("conv_b_pw2", (1, 1024))
        mwi = din("cmlp_w_in", (1, 1024, 4096))
        mbi = din("cmlp_b_in", (1, 4096))
        mlg = din("cmlp_ln_g", (1, 2048))
        mlb = din("cmlp_ln_b", (1, 2048))
        mws = din("cmlp_w_s", (1, 4, 128, 128))
        mbs = din("cmlp_b_s", (1, 4, 128))
        mwo = din("cmlp_w_out", (1, 2048, 1024))
        mbo = din("cmlp_b_out", (1, 1024))
        fwu = din("ffn_w_up", (4, 1024, 5632))
        fwd = din("ffn_w_dw", (4, 3, 5632))
        fbd = din("ffn_b_dw", (4, 5632))
        fwdn = din("ffn_w_down", (4, 2816, 1024))
        yout = dout("yout", (NTOK, 1024))
        pk = dout("pk", (2, 128, 256))
        pv = dout("pv", (2, 128, 256))
        sk = dout("sk", (2, 128, 256))
        sv = dout("sv", (2, 128, 256))
        pconv = dout("pconv", (30, 1024))
        sconv = dout("sconv", (2, 30, 1024))
        pffn = dout("pffn", (4, 2, 5632))
        sffn = dout("sffn", (4, 2, 2, 5632))
        scv = dout("scv", (128, 2048))

        self.NA = 207 * 256 - 64
        self.ARENA = st.enter_context(nc.sbuf_tensor("arena", [128, self.NA], F32))
        self.PSUM = st.enter_context(nc.psum_tensor("psum", [128, 8, 512], F32))
        self.aoff = 0
        self.PS = Ring("ps", [self.PSUM[:, i, :] for i in range(8)], [Buf("ps%d" % i) for i in range(8)])

        self.X = self.carve(8 * 1280).rearrange("p (k t) -> p k t", k=8)
        self.BX = self.mkbufs(["X0", "X1", "X2", "X3"])
        self.IDENT = self.carve(128)
        self.BC = self.mkbuf("consts")
        self.ONESM = self.carve_bf(128)
        self.ONES1 = self.carve_bf(128)
        self.ONESF = self.carve(128)
        self.EPS = self.carve(2)
        plist = []

        def chunks(ap1d):
            return ap1d.rearrange("(c p) -> c p", p=128)

        pcol = {}
        ncol = 0

        def addp(key, ap2d):
            nonlocal ncol
            pcol[key] = ncol
            plist.append((ncol, ap2d))
            ncol += ap2d.shape[0]

        addp("ng", chunks(ng.rearrange("a b c -> (a b c)")))
        self.bq_special = []
        for j in range(2):
            pcol[("bq", j)] = ncol
            bq16 = bqkv[j, 0:1024].rearrange("(h d) -> h d", d=64)
            self.bq_special.append((ncol, bq16))
            ncol += 8
            addp(("bk", j), chunks(bqkv[j, 1024:1280]))
            addp(("bo", j), chunks(bo[j]))
        addp("cb1", chunks(cb1[0]))
        addp("cwd", chunks(cwd[0].rearrange("k c -> (k c)")))
        addp("cbd", chunks(cbd[0]))
        addp("clg", chunks(clg[0]))
        addp("clb", chunks(clb[0]))
        addp("cb2", chunks(cb2[0]))
        addp("mbu", chunks(mbi[0, 0:2048]))
        addp("mbo", chunks(mbo[0]))
        for l in range(4):
            addp(("fwd", l), chunks(fwd[l].rearrange("k c -> (k c)")))
            addp(("fbd", l), chunks(fbd[l]))
            addp(("stf", l), chunks(stf[l].rearrange("s r c -> (s r c)")))
        addp("stc", chunks(stc.rearrange("s r c -> (s r c)")))
        self.pcol = pcol
        npad = ((ncol + 127) // 128) * 128
        self.PARAMS = self.carve(npad)
        self.BP = self.mkbuf("params")
        a0 = self.aoff
        self.ACARRY = [self.carve_bf(8 * 128).rearrange("p (k t) -> p k t", k=8) for _ in range(2)]
        self.CCARRY = self.carve_bf(8 * 30).rearrange("p (k t) -> p k t", k=8)
        self.FCARRY = [self.carve_bf(8 * 2).rearrange("p (k t) -> p k t", k=8) for _ in range(4)]
        self.BCARRY = self.mkbuf("carry", (a0, self.aoff))
        self.NSLOT = 5
        self.SLOTW = 1408
        self.RW = self.ring("rw", self.NSLOT, self.SLOTW)
        sq = self.carve_bf(8 * WIN).rearrange("p (k t) -> p k t", k=8)
        self.SQ = Ring("sq", [sq], [self.mkbuf("sq")])
        self.RS = self.ring("rs", 2, WIN)
        self.scratch_base = self.aoff

        S.op("pool", lambda e: e.memset(self.IDENT, 1.0), (), [self.BC])
        S.op("pool", lambda e: e.affine_select(out=self.IDENT, in_=self.IDENT, pattern=[[-1, 128]],
                                               compare_op=ALU.is_equal, fill=0.0, base=0, channel_multiplier=1),
             [self.BC], [self.BC])
        self.memset(self.ONESM, 1.0 / 1024.0, [self.BC])
        self.memset(self.ONES1, 1.0, [self.BC])
        self.memset(self.ONESF, 1.0, [self.BC])
        self.memset(self.EPS[:, 0:1], 1e-6, [self.BC])
        self.memset(self.EPS[:, 1:2], 1e-5, [self.BC])

        self.scratch_reset()
        stg = self.ring("pstg", 3, 128)
        for t0 in range(0, npad, 128):
            sap, sb = stg.next()
            self.memset(sap, 0.0, [sb])
            for (c0, ap2d) in plist:
                n = ap2d.shape[0]
                lo, hi = max(c0, t0), min(c0 + n, t0 + 128)
                if lo < hi:
                    self.dma(sap[lo - t0:hi - t0, :], ap2d[lo - c0:hi - c0, :], [], [sb])
            for (c0, bq16) in self.bq_special:
                if t0 <= c0 < t0 + 128:
                    assert c0 + 8 <= t0 + 128
                    r = c0 - t0
                    self.dma(sap[r:r + 4, 0:64], bq16[0:4, :], [], [sb])
                    self.dma(sap[r:r + 4, 64:128], bq16[4:8, :], [], [sb])
                    self.dma(sap[r + 4:r + 8, 0:64], bq16[8:12, :], [], [sb])
                    self.dma(sap[r + 4:r + 8, 64:128], bq16[12:16, :], [], [sb])
            ps, pb = self.PS.next()
            self.tr(ps[:, 0:128], sap, [sb, self.BC], [pb])
            self.act(self.PARAMS[:, t0:t0 + 128], ps[:, 0:128], AF.Identity, [pb], [self.BP])

        finals = []
        self.finals = finals
        for sg in range(2):
            self.load_x(sg)
            for l in range(4):
                kind, j = l % 3, l // 3
                if kind == 0:
                    self.attn(l, j, sg)
                elif kind == 1:
                    self.convmod(l, sg)
                else:
                    self.cmlp(l, sg)
                self.ffn(l, sg)
            self.store_y(sg)
        S.emit(final_wait_ops=finals)
        return nc

    def load_x(self, sg):
        self.scratch_reset()
        xin = self.d["xin"]
        stg = self.ring("xstg", 3, 1024)
        n = self.nsg(sg)
        for t in range(n // 128):
            r0 = sg * NPR + t * 128 if t < 9 else TP
            w = t // 3
            sap, sb = stg.next()
            self.dma(sap, xin[r0:r0 + 128, :], [], [sb])
            for h in range(2):
                ps, pb = self.PS.next()
                for kk in range(4):
                    k = h * 4 + kk
                    self.tr(ps[:, kk * 128:(kk + 1) * 128], sap[:, k * 128:(k + 1) * 128], [sb, self.BC], [pb])
                self.act(self.X[:, h * 4:(h + 1) * 4, t * 128:(t + 1) * 128],
                         ps.rearrange("p (a b) -> p a b", a=4), AF.Identity, [pb], [self.BX[w]])

    def store_y(self, sg):
        self.scratch_reset()
        yout = self.d["yout"]
        stg = self.ring("ystg", 3, 1024)
        n = self.nsg(sg)
        for t in range(n // 128):
            r0 = sg * NPR + t * 128 if t < 9 else TP
            w = t // 3
            sap, sb = stg.next()
            for h in range(2):
                ps, pb = self.PS.next()
                for kk in range(4):
                    k = h * 4 + kk
                    self.tr(ps[:, kk * 128:(kk + 1) * 128], self.X[:, k, t * 128:(t + 1) * 128], [self.BX[w], self.BC], [pb])
                self.act(sap[:, h * 512:(h + 1) * 512], ps, AF.Identity, [pb], [sb])
            self.finals.append(self.dma(yout[r0:r0 + 128, :], sap, [sb], []))

    def rstd_of(self, src3, n, R):
        sq, sqb = self.SQ.next()
        self.act(sq[:, :, 0:n], src3, AF.Square, R, [sqb])
        ps, pb = self.PS.next()
        for k in range(8):
            self.mm(ps[:, 0:n], self.ONESM, sq[:, k, 0:n], k == 0, k == 7, [sqb, self.BC], [pb])
        rs, rb = self.RS.next()
        self.act(rs[:, 0:n], ps[:, 0:n], AF.Sqrt, [pb, self.BC], [rb], bias=self.EPS[:, 0:1])
        self.recip(rs[:, 0:n], rs[:, 0:n], [rb], [rb])
        return rs, rb

    def norm_h(self, sg, gcol, out_fn, BH):
        for (w, t0, n, samp) in self.windows(sg):
            rs, rb = self.rstd_of(self.X[:, :, t0:t0 + n], n, [self.BX[w]])
            for k in range(8):
                dst = out_fn(k, t0, n, samp)
                xin_, rin = self.X[:, k, t0:t0 + n], rs[:, 0:n]
                if len(dst.shape) == 3:
                    xin_ = xin_.rearrange("p (s q) -> p s q", s=2)
                    rin = rin.rearrange("p (s q) -> p s q", s=2)
                self.stt(dst, xin_, self.PARAMS[:, gcol + k:gcol + k + 1], rin, ALU.mult, ALU.mult,
                         [self.BX[w], rb, self.BP], [BH[w]])

    def resid(self, w, t0, n, Y3, BY, gcol):
        rs, rb = self.rstd_of(Y3, n, [BY])
        for k in range(8):
            self.stt(Y3[:, k, :], Y3[:, k, :], self.PARAMS[:, gcol + k:gcol + k + 1], rs[:, 0:n], ALU.mult, ALU.mult,
                     [BY, rb, self.BP], [BY])
        self.tt(self.X[:, :, t0:t0 + n], self.X[:, :, t0:t0 + n], Y3, ALU.add, [BY, self.BX[w]], [self.BX[w]])

    def wslot(self, kc):
        sl, sb = self.RW.next()
        v = sl.bitcast(BF16)[:, 0:kc * 128].rearrange("p (k c) -> p k c", k=kc)
        return v, sb

    def load_w(self, dst, src, sb):
        self.dma(dst, src, [], [sb], q="pool")

    def out_linear(self, sg, KC, rhs_fn, wload, bcol, gcol, RB, Y, BY):
        Ys = Y if isinstance(Y, list) else [(Y, BY)]
        for wi_, (w, t0, n, samp) in enumerate(self.windows(sg)):
            Yw, BYw = Ys[wi_ % len(Ys)]
            for oc in range(8):
                wv, wb = self.wslot(KC)
                wload(wv, oc, wb)
                ps, pb = self.PS.next()
                for k in range(KC):
                    self.mm(ps[:, 0:n], wv[:, k, :], rhs_fn(k, t0, n), k == 0, k == KC - 1, [wb, RB[w]], [pb])
                self.act(Yw[:, oc, 0:n], ps[:, 0:n], AF.Identity, [pb, self.BP], [BYw],
                         bias=self.PARAMS[:, bcol + oc:bcol + oc + 1])
            self.resid(w, t0, n, Yw[:, :, 0:n], BYw, gcol)

    def ffn(self, l, sg):
        self.scratch_reset()
        d = self.d
        HW = 2 + NPR + 132
        HT = self.carve_bf(8 * HW).rearrange("p (k t) -> p k t", k=8)
        BH = self.mkbufs(["fh0", "fh1", "fh2", "fh3", "fhc"])
        BHC = BH[4]
        ACTH = self.carve_bf(11 * 1280).rearrange("p (k t) -> p k t", k=11)
        BA = self.mkbufs(["fa0", "fa1", "fa2", "fa3"])
        Y = self.carve(8 * 1280).rearrange("p (k t) -> p k t", k=8)
        BY = self.mkbufs(["fy0", "fy1", "fy2", "fy3"])
        TG = self.ring("tg", 3, WIN)
        TU = self.ring("tu", 3, WIN)
        UPT = self.carve(NFC * 6).rearrange("p (c t) -> p c t", c=NFC)
        BUP = self.mkbuf("upt")
        OST = self.ring("ost", 2, 512)
        gc2, gc3 = self.pcol["ng"] + (l * 4 + 2) * 8, self.pcol["ng"] + (l * 4 + 3) * 8
        wcol, bcol, scol = self.pcol[("fwd", l)], self.pcol[("fbd", l)], self.pcol[("stf", l)]
        P = self.PARAMS
        if sg == 0:
            self.memset(HT[:, :, 0:2], 0.0, [BHC])
        else:
            self.cp(HT[:, :, 0:2], self.FCARRY[l], [self.BCARRY], [BHC])
            self.memset(HT[:, :, 2 + NPR:HW].rearrange("p k (s q) -> p k s q", s=2)[:, :, :, 0:2], 0.0, [BH[3]])

        def out_fn(k, t0, nn, samp):
            if samp:
                return HT[:, k, 2 + NPR:HW].rearrange("p (s q) -> p s q", s=2)[:, :, 2:66]
            return HT[:, k, 2 + t0:2 + t0 + nn]

        self.norm_h(sg, gc2, out_fn, BH)
        if sg == 0:
            self.cp(self.FCARRY[l], HT[:, :, NPR:NPR + 2], [BH[2]], [self.BCARRY])
        wup = d["ffn_w_up"][l].rearrange("(k p) c -> p k c", p=128)
        wdn = d["ffn_w_down"][l].rearrange("(k p) c -> p k c", p=128)
        def load_pair(i):
            wv, wb = self.wslot(16)
            self.load_w(wv[:, 0:8, :], wup[:, :, i * 128:(i + 1) * 128], wb)
            self.load_w(wv[:, 8:16, :], wup[:, :, DFF + i * 128:DFF + (i + 1) * 128], wb)
            return wv, wb

        def load_dn(half, oc):
            wv, wb = self.wslot(11)
            self.load_w(wv, wdn[:, half * 11:(half + 1) * 11, oc * 128:(oc + 1) * 128], wb)
            return wv, wb
        stream = []
        for half in range(2):
            stream += [("up", half * 11 + ii) for ii in range(11)] + [("dn", half, oc) for oc in range(8)]
        AHEAD = 2
        loaded = []

        def issue(n):
            while len(loaded) < min(n, len(stream)):
                it = stream[len(loaded)]
                loaded.append(load_pair(it[1]) if it[0] == "up" else load_dn(it[1], it[2]))
        pos = [0]

        def take():
            issue(pos[0] + 1 + AHEAD)
            r = loaded[pos[0]]
            pos[0] += 1
            return r
        for half in range(2):
            for ii in range(11):
                i = half * 11 + ii
                wv, wb = take()
                for (w, t0, nn, samp) in self.windows(sg):
                    res = []
                    hr = [BH[w]] if samp else [BH[w], (BHC if w == 0 else BH[w - 1])]
                    for gu in range(2):
                        c = i + 22 * gu
                        ps, pb = self.PS.next()
                        if samp:
                            c0, N = 2 + NPR, 132
                        else:
                            c0, N = t0, nn + 2
                        for k in range(8):
                            self.mm(ps[:, 0:N], wv[:, gu * 8 + k, :], HT[:, k, c0:c0 + N], k == 0, k == 7, [wb] + hr, [pb])
                        if samp:
                            pv3 = ps[:, 0:132].rearrange("p (s q) -> p s q", s=2)
                            stv = P[:, scol + c:scol + c + 4 * NFC].rearrange("p (s c) -> p s c", s=4)[:, :, 0]
                            self.cp(pv3[:, :, 0:2], stv.rearrange("p (s r) -> p s r", s=2), [self.BP, pb], [pb])
                            a2, a1, a0 = pv3[:, :, 2:66], pv3[:, :, 1:65], pv3[:, :, 0:64]
                        else:
                            a2, a1, a0 = ps[:, 2:nn + 2], ps[:, 1:nn + 1], ps[:, 0:nn]
                        tring = TG if gu == 0 else TU
                        tb_, tbb = tring.next()
                        tv = tb_[:, 0:nn]
                        if samp:
                            tv = tv.rearrange("p (s q) -> p s q", s=2)
                        w0 = P[:, wcol + c:wcol + c + 1]
                        w1 = P[:, wcol + NFC + c:wcol + NFC + c + 1]
                        w2 = P[:, wcol + 2 * NFC + c:wcol + 2 * NFC + c + 1]
                        self.act(tv, a2, AF.Identity, [pb, self.BP], [tbb], bias=P[:, bcol + c:bcol + c + 1], scale=w2)
                        self.stt(tv, a1, w1, tv, ALU.mult, ALU.add, [pb, tbb, self.BP], [tbb])
                        self.stt(tv, a0, w0, tv, ALU.mult, ALU.add, [pb, tbb, self.BP], [tbb])
                        if sg == 1 and samp:
                            self.act(UPT[:, c, 2:6].rearrange("p (s r) -> p s r", s=2), pv3[:, :, 64:66], AF.Identity, [pb], [BUP])
                        elif sg == 1 and w == 2:
                            self.act(UPT[:, c, 0:2], ps[:, nn:nn + 2], AF.Identity, [pb], [BUP])
                        res.append((tv, tbb))
                    (tg, tgb), (tu, tub) = res
                    self.act(tg, tg, AF.Gelu_apprx_tanh, [tgb], [tgb])
                    dst = ACTH[:, ii, t0:t0 + nn]
                    if samp:
                        dst = dst.rearrange("p (s q) -> p s q", s=2)
                    self.tt(dst, tg, tu, ALU.mult, [tgb, tub], [BA[w]], eng="pool")
            for oc in range(8):
                wv, wb = take()
                for (w, t0, nn, samp) in self.windows(sg):
                    ps, pb = self.PS.next()
                    for k in range(11):
                        self.mm(ps[:, 0:nn], wv[:, k, :], ACTH[:, k, t0:t0 + nn], k == 0, k == 10, [wb, BA[w]], [pb])
                    if half == 0:
                        self.act(Y[:, oc, t0:t0 + nn], ps[:, 0:nn], AF.Identity, [pb], [BY[w]])
                    else:
                        self.tt(Y[:, oc, t0:t0 + nn], Y[:, oc, t0:t0 + nn], ps[:, 0:nn], ALU.add, [pb, BY[w]], [BY[w]])
        for (w, t0, nn, samp) in self.windows(sg):
            self.resid(w, t0, nn, Y[:, :, t0:t0 + nn], BY[w], gc3)
        if sg == 1:
            for cb in range(11):
                ps, pb = self.PS.next()
                for cc in range(4):
                    c = cb * 4 + cc
                    self.tr(ps[0:6, cc * 128:(cc + 1) * 128], UPT[:, c, :], [BUP, self.BC], [pb])
                oa, ob = OST.next()
                self.act(oa[0:6, :], ps[0:6, :], AF.Identity, [pb], [ob])
                self.finals.append(self.dma(d["pffn"][l][:, cb * 512:(cb + 1) * 512], oa[0:2, :], [ob], []))
                self.finals.append(self.dma(d["sffn"][l].rearrange("s r c -> (s r) c")[:, cb * 512:(cb + 1) * 512], oa[2:6, :], [ob], []))

    def attn(self, l, j, sg):
        self.scratch_reset()
        d = self.d
        P = self.PARAMS
        HW = 128 + 1280
        HT = self.carve_bf(8 * HW).rearrange("p (k t) -> p k t", k=8)
        ht_rng = self.last_rng
        BH = self.mkbufs(["ah0", "ah1", "ah2", "ah3", "ahc"])
        BHC = BH[4]

        def hb(c0, c1):
            out = []
            if c0 < 128:
                out.append(BHC)
            for w in range(4):
                a, b = 128 + w * WIN, 128 + min((w + 1) * WIN, 1280)
                if c0 < b and a < c1 and not (w == 3 and sg == 0):
                    out.append(BH[w])
            return out

        QT = self.carve_bf(8 * 1280).rearrange("p (k t) -> p k t", k=8)
        qt_rng = self.last_rng
        BQ = self.mkbufs(["aq0", "aq1", "aq2", "aq3"])
        kv0 = self.aoff
        KT = self.carve_bf(2 * HW).rearrange("p (k t) -> p k t", k=2)
        BK = self.mkbuf("a_kt")
        a0 = self.aoff
        VA = self.carve_bf(11 * 256).rearrange("p (a c) -> p a c", a=11)
        VB = self.carve_bf(11 * 256).rearrange("p (a c) -> p a c", a=11)
        BV = self.mkbuf("a_v", (a0, self.aoff))
        WKV = self.carve_bf(8 * 512).rearrange("p (k c) -> p k c", k=8)
        wkv_rng = self.last_rng
        BWKV = self.mkbuf("a_wkv")
        BKVB = self.carve(512)
        BBKV = self.mkbuf("a_bkv")
        KVO = self.ring("kvo", 1, 512)
        CK = self.ring("ckr", 2, 256)
        a0 = self.aoff
        KTC = self.carve_bf(2 * 2 * 128).rearrange("p (s k t) -> p s k t", s=2, k=2)
        VC = self.carve_bf(2 * 256).rearrange("p (s c) -> p s c", s=2)
        BCACHE = self.mkbuf("a_cache", (a0, self.aoff))
        a0 = self.aoff
        EBT = [self.carve(1024).rearrange("p (h j q) -> p h j q", h=2, j=8) for _ in range(2)]
        ESKR = self.carve_bf(1024).rearrange("p (h j q) -> p h j q", h=2, j=8)
        BEB = self.mkbuf("a_eb", (a0, self.aoff))
        OHT = self.ring("oht", 2, 4 * 96)
        a0 = self.aoff
        TAB = self.carve(16)
        SNK = self.carve(16)
        TABH = self.carve_bf(16)
        TABT = self.carve(16)
        BTAB = self.mkbuf("a_tab", (a0, self.aoff))
        e0 = self.aoff
        ET = self.ring("et", 3, 512)
        PT = self.ring("pt", 4, 512, bf16=True)
        DEN = self.ring("den", 2, 256)
        e1 = self.aoff
        assert e1 - e0 == 8 * WIN
        Y = self.ARENA[:, e0:e1].rearrange("p (k t) -> p k t", k=8)
        WOA = self.carve_bf(6 * 8 * 128).rearrange("p (o k c) -> p o k c", o=6, k=8)
        BWOA = self.mkbuf("a_woa")
        gc0, gc1 = self.pcol["ng"] + (l * 4 + 0) * 8, self.pcol["ng"] + (l * 4 + 1) * 8

        wo_e = d["attn_w_o"][j]
        for oc in range(6):
            for hf in range(2):
                for grp in range(2):
                    h0 = (0, 8)[grp] if hf == 0 else (4, 12)[grp]
                    src = wo_e[h0 * 64:(h0 + 4) * 64, oc * 128:(oc + 1) * 128].rearrange("(j p) c -> p j c", p=64)
                    self.dma(WOA[hf * 64:(hf + 1) * 64, oc, grp * 4:(grp + 1) * 4, :], src, [], [BWOA], q="pool")
        if sg == 0:
            self.memset(HT[:, :, 0:128], 0.0, [BHC])
        else:
            self.cp(HT[:, :, 0:128], self.ACARRY[j], [self.BCARRY], [BHC])
        self.norm_h(sg, gc0, lambda k, t0, nn, samp: HT[:, k, 128 + t0:128 + t0 + nn], BH)
        if sg == 0:
            self.cp(self.ACARRY[j], HT[:, :, NPR:NPR + 128], hb(NPR, NPR + 128), [self.BCARRY])

        self.dma(TAB[0:32, :], d["rel_bias_table"][:, :], [], [BTAB])
        self.dma(TAB[32:64, :], d["rel_bias_table"][:, :], [], [BTAB])
        self.dma(SNK, d["attn_sinks"][j].partition_broadcast(128), [], [BTAB])
        self.cp(TABH[0:64, :], TAB[0:64, :], [BTAB], [BTAB])
        self.tt(TABT[32:64, :], TAB[32:64, :], TABH[32:64, :], ALU.subtract, [BTAB], [BTAB])
        self.cp(TABH[32:64, :], TABT[32:64, :], [BTAB], [BTAB])
        for q4 in range(16):
            oa_, ob = OHT.next()
            oa = oa_.bitcast(BF16).rearrange("p (q j) -> p q j", q=4)
            self.dma(oa[0:64, :, :], d["oh"][:, q4 * 4:(q4 + 1) * 4, :], [], [ob])
            if q4 % 4 == 0:
                psf, pbf = self.PS.next()
                pso, pbo = self.PS.next()
            for qi in range(4):
                qq = (q4 % 4) * 4 + qi
                self.mm(psf[:, qq * 16:(qq + 1) * 16], oa[0:64, qi, 0:128], TABH[0:64, :], True, True, [ob, BTAB], [pbf])
                self.mm(pso[0:64, qq * 16:(qq + 1) * 16], oa[0:64, qi, 128:192], TABH[0:64, :], True, True, [ob, BTAB], [pbo])
            if q4 % 4 == 3:
                q0 = (q4 // 4) * 16
                for (src, pbx, dst, npart) in ((psf, pbf, EBT[0], 128), (pso, pbo, EBT[1], 64)):
                    sv_ = src[0:npart, 0:256].rearrange("p (q h) -> p h q", h=16)
                    for hf in range(2):
                        for grp in range(2):
                            h0 = (0, 8)[grp] if hf == 0 else (4, 12)[grp]
                            self.act(dst[0:npart, hf, grp * 4:(grp + 1) * 4, q0:q0 + 16], sv_[:, h0:h0 + 4, :], AF.Exp, [pbx], [BEB])
        for hf in range(2):
            for grp in range(2):
                h0 = (0, 8)[grp] if hf == 0 else (4, 12)[grp]
                self.act(ESKR[64:65, hf, grp * 4:(grp + 1) * 4, :], SNK[64:65, h0:h0 + 4].unsqueeze(2).to_broadcast([1, 4, 64]), AF.Exp, [BTAB], [BEB])

        wq = d["attn_w_qkv"][j].rearrange("(k p) c -> p k c", p=128)
        bqc, bkc, boc = self.pcol[("bq", j)], self.pcol[("bk", j)], self.pcol[("bo", j)]
        for jq in range(8):
            wv, wb = self.wslot(8)
            lo, up = q_lo(jq), q_up(jq)
            self.load_w(wv[:, :, 0:64], wq[:, :, lo * 64:(lo + 1) * 64], wb)
            self.load_w(wv[:, :, 64:128], wq[:, :, up * 64:(up + 1) * 64], wb)
            for (w, t0, nn, samp) in self.windows(sg):
                ps, pb = self.PS.next()
                for k in range(8):
                    self.mm(ps[:, 0:nn], wv[:, k, :], HT[:, k, 128 + t0:128 + t0 + nn], k == 0, k == 7, [wb, BH[w]], [pb])
                self.act(QT[:, jq, t0:t0 + nn], ps[:, 0:nn], AF.Identity, [pb, self.BP], [BQ[w]], bias=P[:, bqc + jq:bqc + jq + 1])
        for jk in range(2):
            wv, wb = self.wslot(8)
            self.load_w(wv, wq[:, :, 1024 + jk * 128:1024 + (jk + 1) * 128], wb)
            for (c0, nn) in [(0, 128)] + [(128 + t0, nn) for (w, t0, nn, samp) in self.windows(sg)]:
                ps, pb = self.PS.next()
                for k in range(8):
                    self.mm(ps[:, 0:nn], wv[:, k, :], HT[:, k, c0:c0 + nn], k == 0, k == 7, [wb] + hb(c0, c0 + nn), [pb])
                self.act(KT[:, jk, c0:c0 + nn], ps[:, 0:nn], AF.Identity, [pb, self.BP], [BK], bias=P[:, bkc + jk:bkc + jk + 1])
        self.dma(WKV, wq[:, :, 1024:1536], [], [BWKV], q="pool")
        self.dma(BKVB, d["attn_b_qkv"][j, 1024:1536].partition_broadcast(128), [], [BBKV])
        na = 10 + (1 if sg == 1 else 0)
        tiles = [("A", a, a * 128, 128) for a in range(na)] + [("B", b, 64 + b * 128, 128) for b in range(10)]
        if sg == 1:
            tiles.append(("B", 10, 64 + 10 * 128, 64))
        for (kind, idx, c0, m) in tiles:
            ps, pb = self.PS.next()
            hr = hb(c0, min(c0 + m, 128 + self.nsg(sg)))
            for k in range(8):
                self.mm(ps[0:m, :], HT[:, k, c0:c0 + m], WKV[:, k, :], k == 0, k == 7, hr + [BWKV], [pb])
            dst = (VA if kind == "A" else VB)[0:m, idx, :]
            self.tt(dst, ps[0:m, 256:512], BKVB[0:m, 256:512], ALU.add, [pb, BBKV], [BV])
            if sg == 1 and kind == "A" and idx in (9, 10):
                oa, ob = KVO.next()
                self.tt(oa, ps, BKVB, ALU.add, [pb, BBKV], [ob])
                dk, dv = (d["pk"], d["pv"]) if idx == 9 else (d["sk"], d["sv"])
                self.finals.append(self.dma(dk[j], oa[:, 0:256], [ob], []))
                self.finals.append(self.dma(dv[j], oa[:, 256:512], [ob], []))
        if sg == 1:
            for s in range(2):
                ca, cb_ = CK.next()
                self.dma(ca, d["ck"][j, s], [], [cb_])
                ps, pb = self.PS.next()
                for jk in range(2):
                    self.tr(ps[:, jk * 128:(jk + 1) * 128], ca[:, jk * 128:(jk + 1) * 128], [cb_, self.BC], [pb])
                self.act(KTC[:, s, :, :], ps[:, 0:256].rearrange("p (k t) -> p k t", k=2), AF.Identity, [pb], [BCACHE])
                self.dma(VC[:, s, :], d["cv"][j, s], [], [BCACHE], q="pool")

        WOB = WKV.rearrange("p k c -> p (k c)")[:, 0:2 * 8 * 128].rearrange("p (o k c) -> p o k c", o=2, k=8)
        BWOB = self.mkbuf("a_wob", wkv_rng)
        wo_ = d["attn_w_o"][j]

        def wo_ap(oc):
            return (WOA[:, oc], BWOA) if oc < 6 else (WOB[:, oc - 6], BWOB)

        OT = HT
        BO = self.mkbufs(["ao0", "ao1", "ao2", "ao3"], ht_rng)
        items = [("p", c) for c in range(18)] + ([("s", 0), ("s", 1)] if sg == 1 else [])
        work = []
        for (typ, c) in items:
            if typ == "p":
                qc0 = c * 64
                w = c // 6
                gc = c + 18 * sg
                pieces = []
                if gc >= 2:
                    vfull = VA[:, c // 2, :] if c % 2 == 0 else VB[:, (c - 1) // 2, :]
                    pieces.append((c * 64, 128, vfull, 0, BV))
                    vown = VA[0:64, 1 + c // 2, :] if c % 2 == 0 else VB[0:64, (c + 1) // 2, :]
                    pieces.append((128 + c * 64, 64, vown, 1, BV))
                elif gc == 1:
                    pieces.append((64, 128, VB[:, 0, :], 0, BV, True))
                    pieces.append((128 + c * 64, 64, VB[0:64, 1, :], 1, BV))
                else:
                    pieces.append((128, 64, VA[0:64, 1, :], 1, BV))
            else:
                qc0 = NPR + c * 64
                w = 3
                vown = VA[0:64, 10, :] if c == 0 else VB[0:64, 10, :]
                pieces = [(None, 128, VC[:, c, :], 0, BCACHE), (128 + NPR + c * 64, 64, vown, 1, BV)]
            for kv in range(4):
                work.append((c, qc0, w, pieces, kv))

        def stage_a(it):
            (c, qc0, w, pieces, kv) = it
            hf, jk, j0 = kv % 2, kv // 2, (kv // 2) * 4
            rows = slice(hf * 64, hf * 64 + 64)
            rhs_q = QT[rows, j0:j0 + 4, qc0:qc0 + 64]
            pss, pbs = self.PS.next()
            et, eb = ET.next()
            pt, ptb = PT.next()
            offs = []
            off = 0
            for pc in pieces:
                (kc0, nk, vap, bt, vbuf) = pc[0:5]
                if kc0 is None:
                    lk, rk = KTC[rows, c, jk, :], [BCACHE]
                else:
                    lk, rk = KT[rows, jk, kc0:kc0 + nk], [BK]
                self.mm(pss[0:nk, off:off + 256], lk, rhs_q, True, True, rk + [BQ[w]], [pbs])
                offs.append(off)
                off += 256
            for pi, pc in enumerate(pieces):
                (kc0, nk, vap, bt, vbuf) = pc[0:5]
                o_ = offs[pi]
                self.act(et[0:nk, o_:o_ + 256], pss[0:nk, o_:o_ + 256], AF.Exp, [pbs], [eb], scale=0.125)
                self.tt(pt[0:nk, o_:o_ + 256].rearrange("p (j q) -> p j q", j=4),
                        et[0:nk, o_:o_ + 256].rearrange("p (j q) -> p j q", j=4),
                        EBT[bt][0:nk, hf, j0:j0 + 4, :], ALU.mult, [eb, BEB], [ptb], eng="pool")
                if len(pc) > 5:
                    self.memset(pt[0:64, o_:o_ + 256], 0.0, [ptb], eng="pool")
            return (pt, ptb, offs)

        def stage_b(it, st):
            (c, qc0, w, pieces, kv) = it
            (pt, ptb, offs) = st
            hf, jk, j0 = kv % 2, kv // 2, (kv // 2) * 4
            rows = slice(hf * 64, hf * 64 + 64)
            pso, pbo = self.PS.next()
            np_ = len(pieces)
            for pi, pc in enumerate(pieces):
                (kc0, nk, vap, bt, vbuf) = pc[0:5]
                o_ = offs[pi]
                self.mm(pso[:, 0:256], vap[:, jk * 128:(jk + 1) * 128], pt[0:nk, o_:o_ + 256], pi == 0, pi == np_ - 1, [vbuf, ptb], [pbo])
            for pi, pc in enumerate(pieces):
                (kc0, nk, vap, bt, vbuf) = pc[0:5]
                o_ = offs[pi]
                nks = nk + 1 if nk == 64 else nk
                self.mm(pso[:, 256:512], self.ONES1[0:nks, :], pt[0:nks, o_:o_ + 256], pi == 0, pi == np_ - 1, [self.BC, ptb], [pbo])
            dn, dnb = DEN.next()
            self.recip(dn[rows, :], pso[rows, 256:512], [pbo], [dnb])
            self.tt(OT[rows, j0:j0 + 4, qc0:qc0 + 64], pso[rows, 0:256].rearrange("p (j q) -> p j q", j=4),
                    dn[rows, :].rearrange("p (j q) -> p j q", j=4), ALU.mult, [pbo, dnb], [BO[w]])

        assert len(PT.aps) == 4 and PT.i == 0
        for s4 in range(4):
            hf4, j04 = s4 % 2, (s4 // 2) * 4
            for o4 in (0, 256):
                self.cp(PT.aps[s4][64:65, o4:o4 + 256].rearrange("p (j q) -> p j q", j=4), ESKR[64:65, hf4, j04:j04 + 4, :],
                        [BEB], [PT.bufs[s4]], eng="pool")
        DEPTH = 3
        sts = {}
        for i in range(min(DEPTH, len(work))):
            sts[i] = stage_a(work[i])
        for i in range(len(work)):
            stage_b(work[i], sts.pop(i))
            if i + DEPTH < len(work):
                sts[i + DEPTH] = stage_a(work[i + DEPTH])

        for oc in range(6, 8):
            dst, dbuf = wo_ap(oc)
            for hf in range(2):
                for grp in range(2):
                    h0 = (0, 8)[grp] if hf == 0 else (4, 12)[grp]
                    src = wo_[h0 * 64:(h0 + 4) * 64, oc * 128:(oc + 1) * 128].rearrange("(j p) c -> p j c", p=64)
                    self.dma(dst[hf * 64:(hf + 1) * 64, grp * 4:(grp + 1) * 4, :], src, [], [dbuf], q="pool")
        BY = self.mkbuf("a_y", (e0, e1))
        Y2 = self.ARENA[:, kv0:kv0 + 8 * WIN].rearrange("p (k t) -> p k t", k=8)
        BY2 = self.mkbuf("a_y2", (kv0, kv0 + 8 * WIN))
        Ys = [(Y, BY), (Y2, BY2)]
        for wi_, (w, t0, n, samp) in enumerate(self.windows(sg)):
            Yw, BYw = Ys[wi_ % 2]
            for oc in range(8):
                wo_t, wo_b = wo_ap(oc)
                ps, pb = self.PS.next()
                for k in range(8):
                    self.mm(ps[:, 0:n], wo_t[:, k, :], OT[:, k, t0:t0 + n], k == 0, k == 7, [wo_b, BO[w]], [pb])
                self.act(Yw[:, oc, 0:n], ps[:, 0:n], AF.Identity, [pb, self.BP], [BYw], bias=P[:, boc + oc:boc + oc + 1])
            self.resid(w, t0, n, Yw[:, :, 0:n], BYw, gc1)

    def convmod(self, l, sg):
        self.scratch_reset()
        d = self.d
        P = self.PARAMS
        HT = self.carve_bf(8 * 1280).rearrange("p (k t) -> p k t", k=8)
        ht_rng = self.last_rng
        BH = self.mkbufs(["ch0", "ch1", "ch2", "ch3"])
        GW = 30 + NPR + 2 * 94
        GLU = self.carve_bf(8 * GW).rearrange("p (k t) -> p k t", k=8)
        glu_rng = self.last_rng
        BG = self.mkbufs(["cg0", "cg1", "cg2", "cg3", "cgc"])
        BGC = BG[4]
        ZB = self.carve_bf(8 * 1280).rearrange("p (k t) -> p k t", k=8)
        BZ = self.mkbufs(["cz0", "cz1", "cz2", "cz3"])
        a0 = self.aoff
        S1 = self.carve(1280)
        S2 = self.carve(1280)
        BS = self.mkbufs(["cs0", "cs1", "cs2", "cs3"], (a0, self.aoff))
        DWR = self.ring("dw", 2, 31 * 128, bf16=True)
        ZSQ = self.ring("zsq", 3, WIN, bf16=True)
        SIG = self.ring("sig", 2, WIN)
        TMP = self.ring("ctmp", 2, WIN)
        GT = self.carve(8 * 96).rearrange("p (k t) -> p k t", k=8)
        BGT = self.mkbuf("c_gt")
        GOST = self.ring("gost", 1, 1024)
        Y = self.carve(8 * WIN).rearrange("p (k t) -> p k t", k=8)
        BY = self.mkbuf("c_y")
        gc0, gc1 = self.pcol["ng"] + (l * 4 + 0) * 8, self.pcol["ng"] + (l * 4 + 1) * 8
        cb1, cwd, cbd, clg, clb, cb2, stc = (self.pcol[k] for k in ("cb1", "cwd", "cbd", "clg", "clb", "cb2", "stc"))
        self.norm_h(sg, gc0, lambda k, t0, nn, samp: HT[:, k, t0:t0 + nn], BH)
        if sg == 0:
            self.memset(GLU[:, :, 0:30], 0.0, [BGC])
        else:
            self.cp(GLU[:, :, 0:30], self.CCARRY, [self.BCARRY], [BGC])
            for s in range(2):
                src = P[:, stc + s * 240:stc + (s + 1) * 240].rearrange("p (r k) -> p k r", k=8)
                self.cp(GLU[:, :, 30 + NPR + s * 94:30 + NPR + s * 94 + 30], src, [self.BP], [BG[3]])
        w1 = d["conv_w_pw1"][0].rearrange("(k p) c -> p k c", p=128)
        for jc in range(8):
            wv, wb = self.wslot(16)
            self.load_w(wv[:, 0:8, :], w1[:, :, jc * 128:(jc + 1) * 128], wb)
            self.load_w(wv[:, 8:16, :], w1[:, :, 1024 + jc * 128:1024 + (jc + 1) * 128], wb)
            for (w, t0, nn, samp) in self.windows(sg):
                psa, pba = self.PS.next()
                psg, pbg = self.PS.next()
                for k in range(8):
                    self.mm(psa[:, 0:nn], wv[:, k, :], HT[:, k, t0:t0 + nn], k == 0, k == 7, [wb, BH[w]], [pba])
                for k in range(8):
                    self.mm(psg[:, 0:nn], wv[:, 8 + k, :], HT[:, k, t0:t0 + nn], k == 0, k == 7, [wb, BH[w]], [pbg])
                sg_, sgb = SIG.next()
                self.act(sg_[:, 0:nn], psg[:, 0:nn], AF.Sigmoid, [pbg, self.BP], [sgb], bias=P[:, cb1 + 8 + jc:cb1 + 9 + jc])
                ba = P[:, cb1 + jc:cb1 + jc + 1]
                if samp:
                    dst = GLU[:, jc, 30 + NPR:GW].rearrange("p (s q) -> p s q", s=2)[:, :, 30:94]
                    a3 = psa[:, 0:128].rearrange("p (s q) -> p s q", s=2)
                    s3 = sg_[:, 0:128].rearrange("p (s q) -> p s q", s=2)
                    self.stt(dst, a3, ba, s3, ALU.add, ALU.mult, [pba, sgb, self.BP], [BG[3]])
                    self.stt(GT[:, jc, 32:96].rearrange("p (s q) -> p s q", s=2)[:, :, 0:30], a3[:, :, 34:64], ba, s3[:, :, 34:64],
                             ALU.add, ALU.mult, [pba, sgb, self.BP], [BGT])
                else:
                    self.stt(GLU[:, jc, 30 + t0:30 + t0 + nn], psa[:, 0:nn], ba, sg_[:, 0:nn], ALU.add, ALU.mult, [pba, sgb, self.BP], [BG[w]])
                    if sg == 1 and w == 2:
                        self.stt(GT[:, jc, 0:30], psa[:, nn - 30:nn], ba, sg_[:, nn - 30:nn], ALU.add, ALU.mult, [pba, sgb, self.BP], [BGT])
        if sg == 0:
            self.cp(self.CCARRY, GLU[:, :, NPR:NPR + 30], [BG[2]], [self.BCARRY])
        else:
            for seg in range(3):
                c0 = 0 if seg == 0 else 32 * seg
                oa, ob = GOST.next()
                for h in range(2):
                    ps, pb = self.PS.next()
                    for kk in range(4):
                        self.tr(ps[0:30, kk * 128:(kk + 1) * 128], GT[:, h * 4 + kk, c0:c0 + 30], [BGT, self.BC], [pb])
                    self.act(oa[0:30, h * 512:(h + 1) * 512], ps[0:30, :], AF.Identity, [pb], [ob])
                dst = d["pconv"] if seg == 0 else d["sconv"][seg - 1]
                self.finals.append(self.dma(dst[:, :], oa[0:30, :], [ob], []))
        C = HT
        BCt = self.mkbufs(["cc0", "cc1", "cc2", "cc3"], ht_rng)
        pend_stats = None

        def emit_stats(jc, w, t0, nn, zq, zqb):
            ps1, pb1 = self.PS.next()
            self.mm(ps1[:, 0:nn], self.ONESM, ZB[:, jc, t0:t0 + nn], True, True, [self.BC, BZ[w]], [pb1])
            ps2, pb2 = self.PS.next()
            self.mm(ps2[:, 0:nn], self.ONESM, zq[:, 0:nn], True, True, [self.BC, zqb], [pb2])
            if jc == 0:
                self.cp(S1[:, t0:t0 + nn], ps1[:, 0:nn], [pb1], [BS[w]])
                self.cp(S2[:, t0:t0 + nn], ps2[:, 0:nn], [pb2], [BS[w]])
            else:
                self.tt(S1[:, t0:t0 + nn], S1[:, t0:t0 + nn], ps1[:, 0:nn], ALU.add, [pb1, BS[w]], [BS[w]])
                self.tt(S2[:, t0:t0 + nn], S2[:, t0:t0 + nn], ps2[:, 0:nn], ALU.add, [pb2, BS[w]], [BS[w]])
        for jc in range(8):
            dw_, dwb = DWR.next()
            dw = dw_.rearrange("p (k c) -> p k c", k=31)
            wk = P[:, cwd + jc:cwd + jc + 31 * 8].rearrange("p (k j) -> p k j", j=8)[:, :, 0:1]
            self.tt(dw, self.IDENT.unsqueeze(1).to_broadcast([128, 31, 128]), wk.to_broadcast([128, 31, 128]), ALU.mult,
                    [self.BC, self.BP], [dwb])
            for (w, t0, nn, samp) in self.windows(sg):
                ps, pb = self.PS.next()
                gr = [BG[3]] if samp else [BG[w], (BGC if w == 0 else BG[w - 1])]
                for k in range(31):
                    if samp:
                        rhs = GLU[:, jc, 30 + NPR:GW].rearrange("p (s q) -> p s q", s=2)[:, :, k:k + 64]
                    else:
                        rhs = GLU[:, jc, t0 + k:t0 + k + nn]
                    self.mm(ps[:, 0:nn], dw[:, k, :], rhs, k == 0, k == 30, [dwb] + gr, [pb])
                bd = P[:, cbd + jc:cbd + jc + 1]
                self.act(ZB[:, jc, t0:t0 + nn], ps[:, 0:nn], AF.Identity, [pb, self.BP], [BZ[w]], bias=bd)
                zq, zqb = ZSQ.next()
                self.act(zq[:, 0:nn], ps[:, 0:nn], AF.Square, [pb, self.BP], [zqb], bias=bd)
                if pend_stats is not None:
                    emit_stats(*pend_stats)
                pend_stats = (jc, w, t0, nn, zq, zqb)
        emit_stats(*pend_stats)
        for (w, t0, nn, samp) in self.windows(sg):
            tm, tmb = TMP.next()
            self.tt(tm[:, 0:nn], S1[:, t0:t0 + nn], S1[:, t0:t0 + nn], ALU.mult, [BS[w]], [tmb])
            self.tt(S2[:, t0:t0 + nn], S2[:, t0:t0 + nn], tm[:, 0:nn], ALU.subtract, [BS[w], tmb], [BS[w]])
            self.act(S2[:, t0:t0 + nn], S2[:, t0:t0 + nn], AF.Sqrt, [BS[w], self.BC], [BS[w]], bias=self.EPS[:, 1:2])
            self.recip(S2[:, t0:t0 + nn], S2[:, t0:t0 + nn], [BS[w]], [BS[w]])
            for jc in range(8):
                tm, tmb = TMP.next()
                self.tt(tm[:, 0:nn], ZB[:, jc, t0:t0 + nn], S1[:, t0:t0 + nn], ALU.subtract, [BZ[w], BS[w]], [tmb])
                self.tt(tm[:, 0:nn], tm[:, 0:nn], S2[:, t0:t0 + nn], ALU.mult, [tmb, BS[w]], [tmb])
                self.act(C[:, jc, t0:t0 + nn], tm[:, 0:nn], AF.Silu, [tmb, self.BP], [BCt[w]],
                         bias=P[:, clb + jc:clb + jc + 1], scale=P[:, clg + jc:clg + jc + 1])
        w2 = d["conv_w_pw2"][0].rearrange("(k p) c -> p k c", p=128)
        assert glu_rng[1] - glu_rng[0] >= 8 * WIN
        Y2 = self.ARENA[:, glu_rng[0]:glu_rng[0] + 8 * WIN].rearrange("p (k t) -> p k t", k=8)
        BY2 = self.mkbuf("c_y2", (glu_rng[0], glu_rng[0] + 8 * WIN))
        self.out_linear(sg, 8, lambda k, t0, nn: C[:, k, t0:t0 + nn],
                        lambda dst, oc, wb: self.load_w(dst, w2[:, :, oc * 128:(oc + 1) * 128], wb),
                        cb2, gc1, BCt, [(Y, BY), (Y2, BY2)], None)

    def cmlp(self, l, sg):
        self.scratch_reset()
        d = self.d
        P = self.PARAMS
        WV = self.carve_bf(8 * 2048).rearrange("p (k c) -> p k c", k=8)
        BWV = self.mkbuf("m_wv")
        a0 = self.aoff
        LNG = self.carve(2048)
        LNB = self.carve(2048)
        BVB = self.carve(2048)
        BLN = self.mkbuf("m_ln", (a0, self.aoff))
        HT = self.carve_bf(8 * WIN).rearrange("p (k t) -> p k t", k=8)
        BH = self.mkbuf("m_ht")
        U = self.carve_bf(16 * WIN).rearrange("p (k t) -> p k t", k=16)
        BU = self.mkbuf("m_u")
        VT = self.carve_bf(3 * 2048).rearrange("p (a c) -> p a c", a=3)
        BVT = self.mkbufs(["m_vt0", "m_vt1", "m_vt2"])
        VR = self.carve(2048)
        BVR = self.mkbuf("m_vr")
        a0 = self.aoff
        STAT = self.carve(4 * 6).rearrange("p (a b) -> p a b", a=4)
        MV = self.carve(4)
        BST = self.mkbuf("m_st", (a0, self.aoff))
        WSIN = self.carve(4 * 128).rearrange("p (g j) -> p g j", g=4)
        wsin_rng = self.last_rng
        BWSI = self.mkbuf("m_wsi")
        WST = self.carve_bf(2 * 4 * 128).rearrange("p (v g i) -> p v g i", v=2, g=4)
        BWS = self.mkbuf("m_ws")
        a0 = self.aoff
        BSR = self.carve(4 * 128).rearrange("p (g i) -> p g i", g=4)
        BSH = self.carve_bf(2 * 4 * 128).rearrange("p (v g i) -> p v g i", v=2, g=4)
        BBS = self.mkbuf("m_bs", (a0, self.aoff))
        Y = self.carve(8 * WIN).rearrange("p (k t) -> p k t", k=8)
        BY = self.mkbuf("m_y")
        gc0, gc1 = self.pcol["ng"] + (l * 4 + 0) * 8, self.pcol["ng"] + (l * 4 + 1) * 8
        mbu, mbo = self.pcol["mbu"], self.pcol["mbo"]
        wi = d["cmlp_w_in"][0].rearrange("(k p) c -> p k c", p=128)
        wo_ = d["cmlp_w_out"][0].rearrange("(k p) c -> p k c", p=128)
        for cb in range(4):
            self.dma(WV[:, :, cb * 512:(cb + 1) * 512], wi[:, :, 2048 + cb * 512:2048 + (cb + 1) * 512], [], [BWV], q="pool")
        self.dma(LNG, d["cmlp_ln_g"][0].partition_broadcast(128), [], [BLN])
        self.dma(LNB, d["cmlp_ln_b"][0].partition_broadcast(128), [], [BLN])
        self.dma(BVB, d["cmlp_b_in"][0, 2048:4096].partition_broadcast(128), [], [BLN])
        ws = d["cmlp_w_s"][0]
        bs = d["cmlp_b_s"][0]
        for v in range(2 if sg == 1 else 1):
            if v == 0:
                self.dma(WSIN, ws.rearrange("g i j -> i g j"), [BWSI], [BWSI])
            else:
                self.memset(WSIN, 0.0, [BWSI])
                self.dma(WSIN[0:64, :, 0:64], ws[:, 0:64, 0:64].rearrange("g i j -> i g j"), [BWSI], [BWSI])
                self.dma(WSIN[64:128, :, 64:128], ws[:, 0:64, 0:64].rearrange("g i j -> i g j"), [BWSI], [BWSI])
            ps, pb = self.PS.next()
            for g in range(4):
                self.tr(ps[:, g * 128:(g + 1) * 128], WSIN[:, g, :], [BWSI, self.BC], [pb])
            self.act(WST[:, v, :, :], ps.rearrange("p (g i) -> p g i", g=4), AF.Identity, [pb], [BWS])
            if v == 0:
                self.memset(WST[64:128, 0, :, 0:64], 0.0, [BWS])

        BSL = self.ARENA[:, wsin_rng[0]:wsin_rng[1]].bitcast(BF16).rearrange("p (v g i) -> p v g i", v=2, g=4)
        BBL = self.mkbuf("m_bsl", wsin_rng)
        for v in range(2 if sg == 1 else 1):
            if v == 0:
                self.dma(BSR[0:1, :, :], bs.rearrange("g i -> (g i)").rearrange("(o g i) -> o g i", o=1, g=4), [BBS], [BBS])
            else:
                self.dma(BSR[0:1, :, 0:64], bs[:, 0:64].unsqueeze(0), [BBS], [BBS])
                self.dma(BSR[0:1, :, 64:128], bs[:, 0:64].unsqueeze(0), [BBS], [BBS])
            self.cp(BSH[0:1, v], BSR[0:1], [BBS], [BBS])
            self.tt(BSR[0:1], BSR[0:1], BSH[0:1, v], ALU.subtract, [BBS], [BBS])
            self.cp(BSL[0:1, v], BSR[0:1], [BBS], [BBL])
        def make_ht(wn):
            (w_, t0_, nn_, samp_) = wn
            rs, rb = self.rstd_of(self.X[:, :, t0_:t0_ + nn_], nn_, [self.BX[w_]])
            for k in range(8):
                self.stt(HT[:, k, 0:nn_], self.X[:, k, t0_:t0_ + nn_], P[:, gc0 + k:gc0 + k + 1], rs[:, 0:nn_], ALU.mult, ALU.mult,
                         [self.BX[w_], rb, self.BP], [BH])
        wlist = self.windows(sg)
        make_ht(wlist[0])
        for widx, (w, t0, nn, samp) in enumerate(wlist):
            var = 1 if samp else 0
            def u_chunk(jc):
                wv, wb = self.wslot(8)
                self.load_w(wv, wi[:, :, jc * 128:(jc + 1) * 128], wb)
                ps, pb = self.PS.next()
                for k in range(8):
                    self.mm(ps[:, 0:nn], wv[:, k, :], HT[:, k, 0:nn], k == 0, k == 7, [wb, BH], [pb])
                self.act(U[:, jc, 0:nn], ps[:, 0:nn], AF.Gelu_apprx_tanh, [pb, self.BP], [BU], bias=P[:, mbu + jc:mbu + jc + 1])

            def v_tile(t):
                for cb in range(4):
                    ps, pb = self.PS.next()
                    for k in range(8):
                        self.mm(ps, HT[:, k, t * 128:(t + 1) * 128], WV[:, k, cb * 512:(cb + 1) * 512], k == 0, k == 7, [BH, BWV], [pb])
                    vs = VR[:, cb * 512:(cb + 1) * 512]
                    self.tt(vs, ps, BVB[:, cb * 512:(cb + 1) * 512], ALU.add, [pb, BLN], [BVR])
                    self.act(vs, vs, AF.Gelu_apprx_tanh, [BVR], [BVR])
                    self.S.op("dve", (lambda o, i: lambda e: e.bn_stats(out=o, in_=i))(STAT[:, cb, :], vs), [BVR], [BST])
                self.S.op("dve", lambda e: e.bn_aggr(out=MV[:, 0:2], in_=STAT), [BST], [BST])
                self.act(MV[:, 2:3], MV[:, 1:2], AF.Sqrt, [BST, self.BC], [BST], bias=self.EPS[:, 1:2])
                self.recip(MV[:, 2:3], MV[:, 2:3], [BST], [BST])
                self.stt(MV[:, 3:4], MV[:, 0:1], -1.0, MV[:, 2:3], ALU.mult, ALU.mult, [BST], [BST])
                self.act(VR, VR, AF.Identity, [BVR, BST], [BVR], bias=MV[:, 3:4], scale=MV[:, 2:3])
                self.tt(VR, VR, LNG, ALU.mult, [BVR, BLN], [BVR])
                if samp:
                    self.tt(VR, VR, LNB, ALU.add, [BVR, BLN], [BVR])
                    self.cp(VT[:, t, :], VR, [BVR], [BVT[t]])
                    self.finals.append(self.dma(d["scv"][:, :], VR, [BVR], []))
                else:
                    self.tt(VT[:, t, :], VR, LNB, ALU.add, [BVR, BLN], [BVT[t]])

            def spatial(t):
                for g in range(4):
                    ps, pb = self.PS.next()
                    for cc in range(4):
                        ch = g * 4 + cc
                        self.mm(ps[:, cc * 128:(cc + 1) * 128], VT[:, t, ch * 128:(ch + 1) * 128], WST[:, var, g, :], True, False, [BVT[t], BWS], [pb])
                        self.mm(ps[:, cc * 128:(cc + 1) * 128], self.ONES1[0:1, :], BSH[0:1, var, g, :], False, False, [self.BC, BBS], [pb])
                        self.mm(ps[:, cc * 128:(cc + 1) * 128], self.ONES1[0:1, :], BSL[0:1, var, g, :], False, True, [self.BC, BBL], [pb])
                    uu = U[:, g * 4:(g + 1) * 4, t * 128:(t + 1) * 128]
                    self.tt(uu, ps.rearrange("p (c i) -> p c i", c=4), uu, ALU.mult, [pb, BU], [BU])

            nt = nn // 128
            sched_u = {0: range(0, 6), 1: range(6, 11), 2: range(11, 16)} if nt == 3 else {0: range(0, 16)}
            for t in range(nt):
                v_tile(t)
                for jc in sched_u[t]:
                    u_chunk(jc)
            for t in range(nt):
                spatial(t)
            if widx + 1 < len(wlist):
                make_ht(wlist[widx + 1])
            for oc in range(8):
                wv, wb = self.wslot(16)
                self.load_w(wv, wo_[:, :, oc * 128:(oc + 1) * 128], wb)
                ps, pb = self.PS.next()
                for k in range(16):
                    self.mm(ps[:, 0:nn], wv[:, k, :], U[:, k, 0:nn], k == 0, k == 15, [wb, BU], [pb])
                self.act(Y[:, oc, 0:nn], ps[:, 0:nn], AF.Identity, [pb, self.BP], [BY], bias=P[:, mbo + oc:mbo + oc + 1])
            self.resid(w, t0, nn, Y[:, :, 0:nn], BY, gc1)


_PROG = None
_OH = None


def _get_prog():
    global _PROG, _OH
    if _PROG is None:
        p = Prog()
        p.build()
        _PROG = p
        _OH = onehot2_const()
    return _PROG


WEIGHT_KEYS = ["rel_bias_table", "norm_gain", "attn_w_qkv", "attn_b_qkv", "attn_w_o", "attn_b_o", "attn_sinks",
               "conv_w_pw1", "conv_b_pw1", "conv_w_dw", "conv_b_dw", "conv_ln_g", "conv_ln_b", "conv_w_pw2",
               "conv_b_pw2", "cmlp_w_in", "cmlp_b_in", "cmlp_ln_g", "cmlp_ln_b", "cmlp_w_s", "cmlp_b_s",
               "cmlp_w_out", "cmlp_b_out", "ffn_w_up", "ffn_w_dw", "ffn_b_dw", "ffn_w_down"]


def kernel(**inputs):
    p = _get_prog()
    f = lambda a: np.ascontiguousarray(np.asarray(a, dtype=np.float32))
    xp, xs = f(inputs["x_prompt"]), f(inputs["x_sample"])
    cak, cav = f(inputs["cache_attn_k"]), f(inputs["cache_attn_v"])
    stc, stf = f(inputs["state_conv"]), f(inputs["state_ffn_conv"])
    wts = {k: f(inputs[k]) for k in WEIGHT_KEYS}
    in_maps = []
    for c in range(8):
        b, hf = c // 2, c % 2
        s0 = 0 if hf == 0 else 4096 - TP
        m = dict(wts)
        m["xin"] = np.ascontiguousarray(np.concatenate([xp[b, s0:s0 + TP], xs[2 * c], xs[2 * c + 1]], axis=0))
        m["ck"] = np.ascontiguousarray(cak[:, 2 * c:2 * c + 2].reshape(2, 2, 128, 256))
        m["cv"] = np.ascontiguousarray(cav[:, 2 * c:2 * c + 2].reshape(2, 2, 128, 256))
        m["stc"] = np.ascontiguousarray(stc[0, 2 * c:2 * c + 2])
        m["stf"] = np.ascontiguousarray(stf[:, 2 * c:2 * c + 2])
        m["oh"] = _OH
        in_maps.append(m)
    res = run_bass_kernel_spmd(p.nc, in_maps, core_ids=list(range(8)))
    R = res.results
    y_prompt = np.zeros((4, 4096, 1024), np.float32)
    y_sample = np.zeros((16, 64, 1024), np.float32)
    p_k = np.zeros((2, 4, 128, 4, 64), np.float32)
    p_v = np.zeros((2, 4, 128, 4, 64), np.float32)
    p_conv = np.zeros((1, 4, 30, 1024), np.float32)
    p_ffn = np.zeros((4, 4, 2, 5632), np.float32)
    s_k = np.zeros((2, 16, 64, 4, 64), np.float32)
    s_v = np.zeros((2, 16, 64, 4, 64), np.float32)
    s_conv = np.zeros((1, 16, 30, 1024), np.float32)
    s_cv = np.zeros((1, 16, 64, 2048), np.float32)
    s_ffn = np.zeros((4, 16, 2, 5632), np.float32)
    for c in range(8):
        b, hf = c // 2, c % 2
        r = R[c]
        yo = np.asarray(r["yout"])
        if hf == 0:
            y_prompt[b, 0:TP] = yo[0:TP]
        else:
            y_prompt[b, TP:4096] = yo[2 * TP - 4096:TP]
            p_k[:, b] = np.asarray(r["pk"]).reshape(2, 128, 4, 64)
            p_v[:, b] = np.asarray(r["pv"]).reshape(2, 128, 4, 64)
            p_conv[0, b] = np.asarray(r["pconv"])
            p_ffn[:, b] = np.asarray(r["pffn"])
        y_sample[2 * c] = yo[TP:TP + 64]
        y_sample[2 * c + 1] = yo[TP + 64:TP + 128]
        s_k[:, 2 * c:2 * c + 2] = np.asarray(r["sk"]).reshape(2, 2, 64, 4, 64)
        s_v[:, 2 * c:2 * c + 2] = np.asarray(r["sv"]).reshape(2, 2, 64, 4, 64)
        s_conv[0, 2 * c:2 * c + 2] = np.asarray(r["sconv"])
        s_cv[0, 2 * c:2 * c + 2] = np.asarray(r["scv"]).reshape(2, 64, 2048)
        s_ffn[:, 2 * c:2 * c + 2] = np.asarray(r["sffn"])
    return (y_prompt, y_sample, p_k, p_v, p_conv, p_ffn, s_k, s_v, s_conv, s_cv, s_ffn)
```

```python
import contextlib
import numpy as np
import concourse.bass as bass
import concourse.mybir as mybir
from concourse.bass_utils import run_bass_kernel_spmd

F32 = mybir.dt.float32
BF16 = mybir.dt.bfloat16
AF = mybir.ActivationFunctionType
ALU = mybir.AluOpType

ENGS = ("pe", "act", "dve", "pool", "sp")
NPR = 1152
WIN = 384
TP = 2304
NTOK = 2432
DFF = 2816
NFC = 44


class Buf:
    __slots__ = ("name", "w", "r")

    def __init__(self, name, fence=None):
        self.name = name
        self.w = None
        self.r = list(fence) if fence else []


class Op:
    __slots__ = ("eng", "fn", "deps", "dma", "has_dep", "tok", "id")


class Sched:
    def __init__(self, nc, n_dma_sems=24):
        self.nc = nc
        self.ops = []
        self.per_eng = {e: [] for e in ENGS}
        self.n_dma_sems = n_dma_sems

    def op(self, eng, fn, reads=(), writes=(), dma=False):
        o = Op()
        o.eng, o.fn, o.dma, o.has_dep, o.tok, o.id = eng, fn, dma, False, None, len(self.ops)
        deps = set()
        for b in reads:
            if b.w is not None:
                deps.add(b.w)
        for b in writes:
            if b.w is not None:
                deps.add(b.w)
            deps.update(b.r)
        if eng == "pe":
            deps = {dd for dd in deps if self.ops[dd].eng != "pe"}
        o.deps = deps
        for b in reads:
            b.r.append(o.id)
        for b in writes:
            b.w = o.id
            b.r = []
        self.ops.append(o)
        self.per_eng[eng].append(o)
        return o

    def emit(self, final_wait_ops=()):
        nc, ops = self.nc, self.ops
        for o in ops:
            for d in o.deps:
                ops[d].has_dep = True
        for o in final_wait_ops:
            o.has_dep = True
        with contextlib.ExitStack() as st:
            esem = {e: st.enter_context(nc.semaphore("s_" + e)) for e in ENGS}
            dsems = {e: [st.enter_context(nc.semaphore("d_%s_%d" % (e, i)))
                         for i in range(self.n_dma_sems)] for e in ("sp", "pool")}
            ecnt = {e: 0 for e in ENGS}
            dcnt = {e: [0] * self.n_dma_sems for e in dsems}
            drr = {e: 0 for e in dsems}
            for e in ENGS:
                for o in self.per_eng[e]:
                    if not o.has_dep:
                        continue
                    if o.dma:
                        j = drr[e]
                        drr[e] = (j + 1) % self.n_dma_sems
                        prev = dcnt[e][j]
                        dcnt[e][j] += 16
                        o.tok = (dsems[e][j], dcnt[e][j], prev)
                    else:
                        ecnt[e] += 1
                        o.tok = (esem[e], ecnt[e], None)
            block = st.enter_context(nc.Block())

            def run(e, eng):
                waited = {}
                for o in self.per_eng[e]:
                    need = {}
                    for d in o.deps:
                        sem, val, _ = ops[d].tok
                        k = id(sem)
                        if k not in need or need[k][1] < val:
                            need[k] = (sem, val)
                    if o.dma and o.tok is not None and o.tok[2]:
                        sem, _, prev = o.tok
                        k = id(sem)
                        if k not in need or need[k][1] < prev:
                            need[k] = (sem, prev)
                    for k, (sem, val) in need.items():
                        if waited.get(k, 0) >= val:
                            continue
                        eng.wait_ge(sem, val)
                        waited[k] = val
                    ins = o.fn(eng)
                    if o.tok is not None:
                        ins.then_inc(o.tok[0], 16 if o.dma else 1)
                if e == "sp":
                    for o in final_wait_ops:
                        sem, val, _ = o.tok
                        eng.wait_ge(sem, val)

            @block.tensor
            def _(eng):
                run("pe", eng)

            @block.scalar
            def _(eng):
                run("act", eng)

            @block.vector
            def _(eng):
                run("dve", eng)

            @block.gpsimd
            def _(eng):
                run("pool", eng)

            @block.sync
            def _(eng):
                run("sp", eng)


class Ring:
    def __init__(self, name, aps, bufs):
        self.aps = aps
        self.bufs = bufs
        self.i = 0

    def next(self):
        j = self.i
        self.i = (j + 1) % len(self.aps)
        return self.aps[j], self.bufs[j]


def q_lo(j):
    return j if j < 4 else 8 + (j - 4)


def q_up(j):
    return 4 + j if j < 4 else 12 + (j - 4)


def t5_bucket_np(rel):
    half, max_exact = 16, 8
    n = np.abs(rel)
    log_ratio = np.log(np.maximum(n, 1).astype(np.float32) / np.float32(max_exact)) / np.float32(np.log(128 / max_exact))
    large = np.minimum(max_exact + (log_ratio * np.float32(half - max_exact)).astype(np.int32), half - 1)
    return np.where(rel > 0, half, 0) + np.where(n < max_exact, n, large)


def onehot2_const():
    import ml_dtypes
    oh = onehot_const()
    return np.ascontiguousarray(np.concatenate([oh, oh], axis=0).astype(ml_dtypes.bfloat16))


def onehot_const():
    q = np.arange(64)[:, None]
    j = np.arange(192)[None, :]
    bk = t5_bucket_np((j - 128) - q)
    oh = np.zeros((32, 64, 192), np.float32)
    for b in range(32):
        oh[b] = (bk == b)
    return oh


class Prog:
    def __init__(self):
        self.nc = nc = bass.Bass("TRN2", target_bir_lowering=False)
        self.S = Sched(nc)
        self.st = contextlib.ExitStack()
        self.d = {}
        self.outs = []
        self.live = []
        self.phase = 0


    def din(self, name, shape):
        self.d[name] = self.nc.dram_tensor(name, list(shape), F32, kind="ExternalInput").ap()
        return self.d[name]

    def dout(self, name, shape):
        self.d[name] = self.nc.dram_tensor(name, list(shape), F32, kind="ExternalOutput").ap()
        return self.d[name]

    def mm(self, out, lhsT, rhs, start, stop, R, W):
        return self.S.op("pe", lambda e: e.matmul(out, lhsT=lhsT, rhs=rhs, start=start, stop=stop), R, W)

    def tr(self, out, in_, R, W):
        ident = self.IDENT[0:in_.shape[0], 0:in_.shape[0]]
        return self.S.op("pe", lambda e: e.transpose(out, in_, ident), R, W)

    def act(self, out, in_, func, R, W, bias=None, scale=1.0):
        if bias is None:
            return self.S.op("act", lambda e: e.activation(out=out, in_=in_, func=func, scale=scale), R, W)
        return self.S.op("act", lambda e: e.activation(out=out, in_=in_, func=func, bias=bias, scale=scale), R, W)

    def stt(self, out, in0, scalar, in1, op0, op1, R, W, eng="dve"):
        return self.S.op(eng, lambda e: e.scalar_tensor_tensor(out=out, in0=in0, scalar=scalar, in1=in1, op0=op0, op1=op1), R, W)

    def tt(self, out, in0, in1, op, R, W, eng="dve"):
        return self.S.op(eng, lambda e: e.tensor_tensor(out=out, in0=in0, in1=in1, op=op), R, W)

    def cp(self, out, in_, R, W, eng="dve"):
        return self.S.op(eng, lambda e: e.tensor_copy(out=out, in_=in_), R, W)

    def recip(self, out, in_, R, W):
        return self.S.op("dve", lambda e: e.reciprocal(out=out, in_=in_), R, W)

    def memset(self, ap, val, W, eng="dve"):
        return self.S.op(eng, lambda e: e.memset(ap, val), (), W)

    def dma(self, out, in_, R, W, q="sp"):
        return self.S.op(q, lambda e: e.dma_start(out=out, in_=in_), R, W, dma=True)

    def carve(self, nwords):
        a = self.aoff
        self.aoff += nwords
        assert self.aoff <= self.NA, ("arena overflow", self.aoff, self.NA)
        self.last_rng = (a, a + nwords)
        return self.ARENA[:, a:a + nwords]

    def carve_bf(self, nel):
        return self.carve((nel + 1) // 2).bitcast(BF16)[:, 0:nel]

    def mkbufs(self, names, rng=None):
        rng = rng or self.last_rng
        fence = []
        keep = []
        for (a, b, bf, ph) in self.live:
            if a < rng[1] and rng[0] < b:
                if bf.w is not None:
                    fence.append(bf.w)
                fence.extend(bf.r)
                if ph < self.phase and rng[0] <= a and b <= rng[1]:
                    continue
            keep.append((a, b, bf, ph))
        self.live = keep
        out = [Buf(n, fence) for n in names]
        for bf in out:
            self.live.append((rng[0], rng[1], bf, self.phase))
        return out

    def mkbuf(self, name, rng=None):
        return self.mkbufs([name], rng)[0]

    def ring(self, name, n, nwords, bf16=False):
        aps, bufs = [], []
        for i in range(n):
            ap = self.carve_bf(nwords) if bf16 else self.carve(nwords)
            aps.append(ap)
            bufs.append(self.mkbuf("%s%d" % (name, i)))
        return Ring(name, aps, bufs)

    def scratch_reset(self):
        self.aoff = self.scratch_base
        self.phase += 1

    def windows(self, sg):
        w = [(i, i * WIN, WIN, False) for i in range(3)]
        if sg == 1:
            w.append((3, NPR, 128, True))
        return w

    def nsg(self, sg):
        return NPR + (128 if sg == 1 else 0)

    def build(self):
        nc, S, st = self.nc, self.S, self.st
        din, dout = self.din, self.dout
        xin = din("xin", (NTOK, 1024))
        ck = din("ck", (2, 2, 128, 256))
        cv = din("cv", (2, 2, 128, 256))
        stc = din("stc", (2, 30, 1024))
        stf = din("stf", (4, 2, 2, 5632))
        self.d["oh"] = nc.dram_tensor("oh", [64, 64, 192], BF16, kind="ExternalInput").ap()
        relt = din("rel_bias_table", (32, 16))
        ng = din("norm_gain", (4, 4, 1024))
        wqkv = din("attn_w_qkv", (2, 1024, 1536))
        bqkv = din("attn_b_qkv", (2, 1536))
        wo = din("attn_w_o", (2, 1024, 1024))
        bo = din("attn_b_o", (2, 1024))
        sinks = din("attn_sinks", (2, 16))
        cw1 = din("conv_w_pw1", (1, 1024, 2048))
        cb1 = din("conv_b_pw1", (1, 2048))
        cwd = din("conv_w_dw", (1, 31, 1024))
        cbd = din("conv_b_dw", (1, 1024))
        clg = din("conv_ln_g", (1, 1024))
        clb = din("conv_ln_b", (1, 1024))
        cw2 = din("conv_w_pw2", (1, 1024, 1024))
        cb2 = din("conv_b_pw2", (1, 1024))
        mwi = din("cmlp_w_in", (1, 1024, 4096))
        mbi = din("cmlp_b_in", (1, 4096))
        mlg = din("cmlp_ln_g", (1, 2048))
        mlb = din("cmlp_ln_b", (1, 2048))
        mws = din("cmlp_w_s", (1, 4, 128, 128))
        mbs = din("cmlp_b_s", (1, 4, 128))
        mwo = din("cmlp_w_out", (1, 2048, 1024))
        mbo = din("cmlp_b_out", (1, 1024))
        fwu = din("ffn_w_up", (4, 1024, 5632))
        fwd = din("ffn_w_dw", (4, 3, 5632))
        fbd = din("ffn_b_dw", (4, 5632))
        fwdn = din("ffn_w_down", (4, 2816, 1024))
        yout = dout("yout", (NTOK, 1024))
        pk = dout("pk", (2, 128, 256))
        pv = dout("pv", (2, 128, 256))
        sk = dout("sk", (2, 128, 256))
        sv = dout("sv", (2, 128, 256))
        pconv = dout("pconv", (30, 1024))
        sconv = dout("sconv", (2, 30, 1024))
        pffn = dout("pffn", (4, 2, 5632))
        sffn = dout("sffn", (4, 2, 2, 5632))
        scv = dout("scv", (128, 2048))

        self.NA = 207 * 256 - 64
        self.ARENA = st.enter_context(nc.sbuf_tensor("arena", [128, self.NA], F32))
        self.PSUM = st.enter_context(nc.psum_tensor("psum", [128, 8, 512], F32))
        self.aoff = 0
        self.PS = Ring("ps", [self.PSUM[:, i, :] for i in range(8)], [Buf("ps%d" % i) for i in range(8)])

        self.X = self.carve(8 * 1280).rearrange("p (k t) -> p k t", k=8)
        self.BX = self.mkbufs(["X0", "X1", "X2", "X3"])
        self.IDENT = self.carve(128)
        self.BC = self.mkbuf("consts")
        self.ONESM = self.carve_bf(128)
        self.ONES1 = self.carve_bf(128)
        self.ONESF = self.carve(128)
        self.EPS = self.carve(2)
        plist = []

        def chunks(ap1d):
            return ap1d.rearrange("(c p) -> c p", p=128)

        pcol = {}
        ncol = 0

        def addp(key, ap2d):
            nonlocal ncol
            pcol[key] = ncol
            plist.append((ncol, ap2d))
            ncol += ap2d.shape[0]

        addp("ng", chunks(ng.rearrange("a b c -> (a b c)")))
        self.bq_special = []
        for j in range(2):
            pcol[("bq", j)] = ncol
            bq16 = bqkv[j, 0:1024].rearrange("(h d) -> h d", d=64)
            self.bq_special.append((ncol, bq16))
            ncol += 8
            addp(("bk", j), chunks(bqkv[j, 1024:1280]))
            addp(("bo", j), chunks(bo[j]))
        addp("cb1", chunks(cb1[0]))
        addp("cwd", chunks(cwd[0].rearrange("k c -> (k c)")))
        addp("cbd", chunks(cbd[0]))
        addp("clg", chunks(clg[0]))
        addp("clb", chunks(clb[0]))
        addp("cb2", chunks(cb2[0]))
        addp("mbu", chunks(mbi[0, 0:2048]))
        addp("mbo", chunks(mbo[0]))
        for l in range(4):
            addp(("fwd", l), chunks(fwd[l].rearrange("k c -> (k c)")))
            addp(("fbd", l), chunks(fbd[l]))
            addp(("stf", l), chunks(stf[l].rearrange("s r c -> (s r c)")))
        addp("stc", chunks(stc.rearrange("s r c -> (s r c)")))
        self.pcol = pcol
        npad = ((ncol + 127) // 128) * 128
        self.PARAMS = self.carve(npad)
        self.BP = self.mkbuf("params")
        a0 = self.aoff
        self.ACARRY = [self.carve_bf(8 * 128).rearrange("p (k t) -> p k t", k=8) for _ in range(2)]
        self.CCARRY = self.carve_bf(8 * 30).rearrange("p (k t) -> p k t", k=8)
        self.FCARRY = [self.carve_bf(8 * 2).rearrange("p (k t) -> p k t", k=8) for _ in range(4)]
        self.BCARRY = self.mkbuf("carry", (a0, self.aoff))
        self.NSLOT = 5
        self.SLOTW = 1408
        self.RW = self.ring("rw", self.NSLOT, self.SLOTW)
        sq = self.carve_bf(8 * WIN).rearrange("p (k t) -> p k t", k=8)
        self.SQ = Ring("sq", [sq], [self.mkbuf("sq")])
        self.RS = self.ring("rs", 2, WIN)
        self.scratch_base = self.aoff

        S.op("pool", lambda e: e.memset(self.IDENT, 1.0), (), [self.BC])
        S.op("pool", lambda e: e.affine_select(out=self.IDENT, in_=self.IDENT, pattern=[[-1, 128]],
                                               compare_op=ALU.is_equal, fill=0.0, base=0, channel_multiplier=1),
             [self.BC], [self.BC])
        self.memset(self.ONESM, 1.0 / 1024.0, [self.BC])
        self.memset(self.ONES1, 1.0, [self.BC])
        self.memset(self.ONESF, 1.0, [self.BC])
        self.memset(self.EPS[:, 0:1], 1e-6, [self.BC])
        self.memset(self.EPS[:, 1:2], 1e-5, [self.BC])

        self.scratch_reset()
        stg = self.ring("pstg", 3, 128)
        for t0 in range(0, npad, 128):
            sap, sb = stg.next()
            self.memset(sap, 0.0, [sb])
            for (c0, ap2d) in plist:
                n = ap2d.shape[0]
                lo, hi = max(c0, t0), min(c0 + n, t0 + 128)
                if lo < hi:
                    self.dma(sap[lo - t0:hi - t0, :], ap2d[lo - c0:hi - c0, :], [], [sb])
            for (c0, bq16) in self.bq_special:
                if t0 <= c0 < t0 + 128:
                    assert c0 + 8 <= t0 + 128
                    r = c0 - t0
                    self.dma(sap[r:r + 4, 0:64], bq16[0:4, :], [], [sb])
                    self.dma(sap[r:r + 4, 64:128], bq16[4:8, :], [], [sb])
                    self.dma(sap[r + 4:r + 8, 0:64], bq16[8:12, :], [], [sb])
                    self.dma(sap[r + 4:r + 8, 64:128], bq16[12:16, :], [], [sb])
            ps, pb = self.PS.next()
            self.tr(ps[:, 0:128], sap, [sb, self.BC], [pb])
            self.act(self.PARAMS[:, t0:t0 + 128], ps[:, 0:128], AF.Identity, [pb], [self.BP])

        finals = []
        self.finals = finals
        for sg in range(2):
            self.load_x(sg)
            for l in range(4):
                kind, j = l % 3, l // 3
                if kind == 0:
                    self.attn(l, j, sg)
                elif kind == 1:
                    self.convmod(l, sg)
                else:
                    self.cmlp(l, sg)
                self.ffn(l, sg)
            self.store_y(sg)
        S.emit(final_wait_ops=finals)
        return nc

    def load_x(self, sg):
        self.scratch_reset()
        xin = self.d["xin"]
        stg = self.ring("xstg", 3, 1024)
        n = self.nsg(sg)
        for t in range(n // 128):
            r0 = sg * NPR + t * 128 if t < 9 else TP
            w = t // 3
            sap, sb = stg.next()
            self.dma(sap, xin[r0:r0 + 128, :], [], [sb])
            for h in range(2):
                ps, pb = self.PS.next()
                for kk in range(4):
                    k = h * 4 + kk
                    self.tr(ps[:, kk * 128:(kk + 1) * 128], sap[:, k * 128:(k + 1) * 128], [sb, self.BC], [pb])
                self.act(self.X[:, h * 4:(h + 1) * 4, t * 128:(t + 1) * 128],
                         ps.rearrange("p (a b) -> p a b", a=4), AF.Identity, [pb], [self.BX[w]])

    def store_y(self, sg):
        self.scratch_reset()
        yout = self.d["yout"]
        stg = self.ring("ystg", 3, 1024)
        n = self.nsg(sg)
        for t in range(n // 128):
            r0 = sg * NPR + t * 128 if t < 9 else TP
            w = t // 3
            sap, sb = stg.next()
            for h in range(2):
                ps, pb = self.PS.next()
                for kk in range(4):
                    k = h * 4 + kk
                    self.tr(ps[:, kk * 128:(kk + 1) * 128], self.X[:, k, t * 128:(t + 1) * 128], [self.BX[w], self.BC], [pb])
                self.act(sap[:, h * 512:(h + 1) * 512], ps, AF.Identity, [pb], [sb])
            self.finals.append(self.dma(yout[r0:r0 + 128, :], sap, [sb], []))

    def rstd_of(self, src3, n, R):
        sq, sqb = self.SQ.next()
        self.act(sq[:, :, 0:n], src3, AF.Square, R, [sqb])
        ps, pb = self.PS.next()
        for k in range(8):
            self.mm(ps[:, 0:n], self.ONESM, sq[:, k, 0:n], k == 0, k == 7, [sqb, self.BC], [pb])
        rs, rb = self.RS.next()
        self.act(rs[:, 0:n], ps[:, 0:n], AF.Sqrt, [pb, self.BC], [rb], bias=self.EPS[:, 0:1])
        self.recip(rs[:, 0:n], rs[:, 0:n], [rb], [rb])
        return rs, rb

    def norm_h(self, sg, gcol, out_fn, BH):
        for (w, t0, n, samp) in self.windows(sg):
            rs, rb = self.rstd_of(self.X[:, :, t0:t0 + n], n, [self.BX[w]])
            for k in range(8):
                dst = out_fn(k, t0, n, samp)
                xin_, rin = self.X[:, k, t0:t0 + n], rs[:, 0:n]
                if len(dst.shape) == 3:
                    xin_ = xin_.rearrange("p (s q) -> p s q", s=2)
                    rin = rin.rearrange("p (s q) -> p s q", s=2)
                self.stt(dst, xin_, self.PARAMS[:, gcol + k:gcol + k + 1], rin, ALU.mult, ALU.mult,
                         [self.BX[w], rb, self.BP], [BH[w]])

    def resid(self, w, t0, n, Y3, BY, gcol):
        rs, rb = self.rstd_of(Y3, n, [BY])
        for k in range(8):
            self.stt(Y3[:, k, :], Y3[:, k, :], self.PARAMS[:, gcol + k:gcol + k + 1], rs[:, 0:n], ALU.mult, ALU.mult,
                     [BY, rb, self.BP], [BY])
        self.tt(self.X[:, :, t0:t0 + n], self.X[:, :, t0:t0 + n], Y3, ALU.add, [BY, self.BX[w]], [self.BX[w]])

    def resid_tail(self, w, t0, n, Y3, BY, gcol, psn, pbn):
        rs, rb = self.RS.next()
        self.act(rs[:, 0:n], psn[:, 0:n], AF.Sqrt, [pbn, self.BC], [rb], bias=self.EPS[:, 0:1])
        self.recip(rs[:, 0:n], rs[:, 0:n], [rb], [rb])
        for k in range(8):
            self.stt(Y3[:, k, :], Y3[:, k, :], self.PARAMS[:, gcol + k:gcol + k + 1], rs[:, 0:n], ALU.mult, ALU.mult,
                     [BY, rb, self.BP], [BY])
        self.tt(self.X[:, :, t0:t0 + n], self.X[:, :, t0:t0 + n], Y3, ALU.add, [BY, self.BX[w]], [self.BX[w]])

    def wslot(self, kc):
        sl, sb = self.RW.next()
        v = sl.bitcast(BF16)[:, 0:kc * 128].rearrange("p (k c) -> p k c", k=kc)
        return v, sb

    def load_w(self, dst, src, sb):
        self.dma(dst, src, [], [sb], q="pool")

    def out_linear(self, sg, KC, rhs_fn, wload, bcol, gcol, RB, Y, BY):
        Ys = Y if isinstance(Y, list) else [(Y, BY)]
        for wi_, (w, t0, n, samp) in enumerate(self.windows(sg)):
            Yw, BYw = Ys[wi_ % len(Ys)]
            for oc in range(8):
                wv, wb = self.wslot(KC)
                wload(wv, oc, wb)
                ps, pb = self.PS.next()
                for k in range(KC):
                    self.mm(ps[:, 0:n], wv[:, k, :], rhs_fn(k, t0, n), k == 0, k == KC - 1, [wb, RB[w]], [pb])
                self.act(Yw[:, oc, 0:n], ps[:, 0:n], AF.Identity, [pb, self.BP], [BYw],
                         bias=self.PARAMS[:, bcol + oc:bcol + oc + 1])
            self.resid(w, t0, n, Yw[:, :, 0:n], BYw, gcol)

    def ffn(self, l, sg):
        self.scratch_reset()
        d = self.d
        HW = 2 + NPR + 132
        HT = self.carve_bf(8 * HW).rearrange("p (k t) -> p k t", k=8)
        BH = self.mkbufs(["fh0", "fh1", "fh2", "fh3", "fhc"])
        BHC = BH[4]
        ACTH = self.carve_bf(11 * 1280).rearrange("p (k t) -> p k t", k=11)
        BA = self.mkbufs(["fa0", "fa1", "fa2", "fa3"])
        Y = self.carve(8 * 1280).rearrange("p (k t) -> p k t", k=8)
        BY = self.mkbufs(["fy0", "fy1", "fy2", "fy3"])
        TG = self.ring("tg", 3, WIN)
        TU = self.ring("tu", 3, WIN)
        UPT = self.carve(NFC * 6).rearrange("p (c t) -> p c t", c=NFC)
        BUP = self.mkbuf("upt")
        OST = self.ring("ost", 2, 512)
        gc2, gc3 = self.pcol["ng"] + (l * 4 + 2) * 8, self.pcol["ng"] + (l * 4 + 3) * 8
        wcol, bcol, scol = self.pcol[("fwd", l)], self.pcol[("fbd", l)], self.pcol[("stf", l)]
        P = self.PARAMS
        if sg == 0:
            self.memset(HT[:, :, 0:2], 0.0, [BHC])
        else:
            self.cp(HT[:, :, 0:2], self.FCARRY[l], [self.BCARRY], [BHC])
            self.memset(HT[:, :, 2 + NPR:HW].rearrange("p k (s q) -> p k s q", s=2)[:, :, :, 0:2], 0.0, [BH[3]])

        def out_fn(k, t0, nn, samp):
            if samp:
                return HT[:, k, 2 + NPR:HW].rearrange("p (s q) -> p s q", s=2)[:, :, 2:66]
            return HT[:, k, 2 + t0:2 + t0 + nn]

        self.norm_h(sg, gc2, out_fn, BH)
        if sg == 0:
            self.cp(self.FCARRY[l], HT[:, :, NPR:NPR + 2], [BH[2]], [self.BCARRY])
        wup = d["ffn_w_up"][l].rearrange("(k p) c -> p k c", p=128)
        wdn = d["ffn_w_down"][l].rearrange("(k p) c -> p k c", p=128)
        def load_pair(i):
            wv, wb = self.wslot(16)
            self.load_w(wv[:, 0:8, :], wup[:, :, i * 128:(i + 1) * 128], wb)
            self.load_w(wv[:, 8:16, :], wup[:, :, DFF + i * 128:DFF + (i + 1) * 128], wb)
            return wv, wb

        def load_dn(half, oc):
            wv, wb = self.wslot(11)
            self.load_w(wv, wdn[:, half * 11:(half + 1) * 11, oc * 128:(oc + 1) * 128], wb)
            return wv, wb
        stream = []
        for half in range(2):
            stream += [("up", half * 11 + ii) for ii in range(11)] + [("dn", half, oc) for oc in range(8)]
        AHEAD = 2
        loaded = []

        def issue(n):
            while len(loaded) < min(n, len(stream)):
                it = stream[len(loaded)]
                loaded.append(load_pair(it[1]) if it[0] == "up" else load_dn(it[1], it[2]))
        pos = [0]

        def take():
            issue(pos[0] + 1 + AHEAD)
            r = loaded[pos[0]]
            pos[0] += 1
            return r
        for half in range(2):
            for ii in range(11):
                i = half * 11 + ii
                wv, wb = take()
                for (w, t0, nn, samp) in self.windows(sg):
                    res = []
                    hr = [BH[w]] if samp else [BH[w], (BHC if w == 0 else BH[w - 1])]
                    for gu in range(2):
                        c = i + 22 * gu
                        ps, pb = self.PS.next()
                        if samp:
                            c0, N = 2 + NPR, 132
                        else:
                            c0, N = t0, nn + 2
                        for k in range(8):
                            self.mm(ps[:, 0:N], wv[:, gu * 8 + k, :], HT[:, k, c0:c0 + N], k == 0, k == 7, [wb] + hr, [pb])
                        if samp:
                            pv3 = ps[:, 0:132].rearrange("p (s q) -> p s q", s=2)
                            stv = P[:, scol + c:scol + c + 4 * NFC].rearrange("p (s c) -> p s c", s=4)[:, :, 0]
                            self.cp(pv3[:, :, 0:2], stv.rearrange("p (s r) -> p s r", s=2), [self.BP, pb], [pb])
                            a2, a1, a0 = pv3[:, :, 2:66], pv3[:, :, 1:65], pv3[:, :, 0:64]
                        else:
                            a2, a1, a0 = ps[:, 2:nn + 2], ps[:, 1:nn + 1], ps[:, 0:nn]
                        tring = TG if gu == 0 else TU
                        tb_, tbb = tring.next()
                        tv = tb_[:, 0:nn]
                        if samp:
                            tv = tv.rearrange("p (s q) -> p s q", s=2)
                        w0 = P[:, wcol + c:wcol + c + 1]
                        w1 = P[:, wcol + NFC + c:wcol + NFC + c + 1]
                        w2 = P[:, wcol + 2 * NFC + c:wcol + 2 * NFC + c + 1]
                        self.act(tv, a2, AF.Identity, [pb, self.BP], [tbb], bias=P[:, bcol + c:bcol + c + 1], scale=w2)
                        self.stt(tv, a1, w1, tv, ALU.mult, ALU.add, [pb, tbb, self.BP], [tbb])
                        self.stt(tv, a0, w0, tv, ALU.mult, ALU.add, [pb, tbb, self.BP], [tbb])
                        if sg == 1 and samp:
                            self.act(UPT[:, c, 2:6].rearrange("p (s r) -> p s r", s=2), pv3[:, :, 64:66], AF.Identity, [pb], [BUP])
                        elif sg == 1 and w == 2:
                            self.act(UPT[:, c, 0:2], ps[:, nn:nn + 2], AF.Identity, [pb], [BUP])
                        res.append((tv, tbb))
                    (tg, tgb), (tu, tub) = res
                    self.act(tg, tg, AF.Gelu_apprx_tanh, [tgb], [tgb])
                    dst = ACTH[:, ii, t0:t0 + nn]
                    if samp:
                        dst = dst.rearrange("p (s q) -> p s q", s=2)
                    self.tt(dst, tg, tu, ALU.mult, [tgb, tub], [BA[w]], eng="pool")
            for oc in range(8):
                wv, wb = take()
                for (w, t0, nn, samp) in self.windows(sg):
                    ps, pb = self.PS.next()
                    for k in range(11):
                        self.mm(ps[:, 0:nn], wv[:, k, :], ACTH[:, k, t0:t0 + nn], k == 0, k == 10, [wb, BA[w]], [pb])
                    if half == 0:
                        self.act(Y[:, oc, t0:t0 + nn], ps[:, 0:nn], AF.Identity, [pb], [BY[w]])
                    else:
                        self.tt(Y[:, oc, t0:t0 + nn], Y[:, oc, t0:t0 + nn], ps[:, 0:nn], ALU.add, [pb, BY[w]], [BY[w]])
        for (w, t0, nn, samp) in self.windows(sg):
            self.resid(w, t0, nn, Y[:, :, t0:t0 + nn], BY[w], gc3)
        if sg == 1:
            for cb in range(11):
                ps, pb = self.PS.next()
                for cc in range(4):
                    c = cb * 4 + cc
                    self.tr(ps[0:6, cc * 128:(cc + 1) * 128], UPT[:, c, :], [BUP, self.BC], [pb])
                oa, ob = OST.next()
                self.act(oa[0:6, :], ps[0:6, :], AF.Identity, [pb], [ob])
                self.finals.append(self.dma(d["pffn"][l][:, cb * 512:(cb + 1) * 512], oa[0:2, :], [ob], []))
                self.finals.append(self.dma(d["sffn"][l].rearrange("s r c -> (s r) c")[:, cb * 512:(cb + 1) * 512], oa[2:6, :], [ob], []))

    def attn(self, l, j, sg):
        self.scratch_reset()
        d = self.d
        P = self.PARAMS
        HW = 128 + 1280
        HT = self.carve_bf(8 * HW).rearrange("p (k t) -> p k t", k=8)
        ht_rng = self.last_rng
        BH = self.mkbufs(["ah0", "ah1", "ah2", "ah3", "ahc"])
        BHC = BH[4]

        def hb(c0, c1):
            out = []
            if c0 < 128:
                out.append(BHC)
            for w in range(4):
                a, b = 128 + w * WIN, 128 + min((w + 1) * WIN, 1280)
                if c0 < b and a < c1 and not (w == 3 and sg == 0):
                    out.append(BH[w])
            return out

        QT = self.carve_bf(8 * 1280).rearrange("p (k t) -> p k t", k=8)
        qt_rng = self.last_rng
        BQ = self.mkbufs(["aq0", "aq1", "aq2", "aq3"])
        kv0 = self.aoff
        KT = self.carve_bf(2 * HW).rearrange("p (k t) -> p k t", k=2)
        BK = self.mkbuf("a_kt")
        a0 = self.aoff
        VA = self.carve_bf(11 * 256).rearrange("p (a c) -> p a c", a=11)
        VB = self.carve_bf(11 * 256).rearrange("p (a c) -> p a c", a=11)
        BV = self.mkbuf("a_v", (a0, self.aoff))
        WKV = self.carve_bf(8 * 512).rearrange("p (k c) -> p k c", k=8)
        wkv_rng = self.last_rng
        BWKV = self.mkbuf("a_wkv")
        BKVB = self.carve(512)
        BBKV = self.mkbuf("a_bkv")
        KVO = self.ring("kvo", 1, 512)
        CK = self.ring("ckr", 2, 256)
        a0 = self.aoff
        KTC = self.carve_bf(2 * 2 * 128).rearrange("p (s k t) -> p s k t", s=2, k=2)
        VC = self.carve_bf(2 * 256).rearrange("p (s c) -> p s c", s=2)
        BCACHE = self.mkbuf("a_cache", (a0, self.aoff))
        a0 = self.aoff
        EBT = [self.carve(1024).rearrange("p (h j q) -> p h j q", h=2, j=8) for _ in range(2)]
        ESKR = self.carve_bf(1024).rearrange("p (h j q) -> p h j q", h=2, j=8)
        BEB = self.mkbuf("a_eb", (a0, self.aoff))
        OHT = self.ring("oht", 2, 4 * 96)
        a0 = self.aoff
        TAB = self.carve(16)
        SNK = self.carve(16)
        TABH = self.carve_bf(16)
        TABT = self.carve(16)
        BTAB = self.mkbuf("a_tab", (a0, self.aoff))
        e0 = self.aoff
        ET = self.ring("et", 3, 512)
        PT = self.ring("pt", 4, 512, bf16=True)
        DEN = self.ring("den", 2, 256)
        e1 = self.aoff
        assert e1 - e0 == 8 * WIN
        Y = self.ARENA[:, e0:e1].rearrange("p (k t) -> p k t", k=8)
        WOA = self.carve_bf(6 * 8 * 128).rearrange("p (o k c) -> p o k c", o=6, k=8)
        BWOA = self.mkbuf("a_woa")
        gc0, gc1 = self.pcol["ng"] + (l * 4 + 0) * 8, self.pcol["ng"] + (l * 4 + 1) * 8

        wo_e = d["attn_w_o"][j]
        for oc in range(6):
            for hf in range(2):
                for grp in range(2):
                    h0 = (0, 8)[grp] if hf == 0 else (4, 12)[grp]
                    src = wo_e[h0 * 64:(h0 + 4) * 64, oc * 128:(oc + 1) * 128].rearrange("(j p) c -> p j c", p=64)
                    self.dma(WOA[hf * 64:(hf + 1) * 64, oc, grp * 4:(grp + 1) * 4, :], src, [], [BWOA], q="pool")
        if sg == 0:
            self.memset(HT[:, :, 0:128], 0.0, [BHC])
        else:
            self.cp(HT[:, :, 0:128], self.ACARRY[j], [self.BCARRY], [BHC])
        self.norm_h(sg, gc0, lambda k, t0, nn, samp: HT[:, k, 128 + t0:128 + t0 + nn], BH)
        if sg == 0:
            self.cp(self.ACARRY[j], HT[:, :, NPR:NPR + 128], hb(NPR, NPR + 128), [self.BCARRY])

        self.dma(TAB[0:32, :], d["rel_bias_table"][:, :], [], [BTAB])
        self.dma(TAB[32:64, :], d["rel_bias_table"][:, :], [], [BTAB])
        self.dma(SNK, d["attn_sinks"][j].partition_broadcast(128), [], [BTAB])
        self.cp(TABH[0:64, :], TAB[0:64, :], [BTAB], [BTAB])
        self.tt(TABT[32:64, :], TAB[32:64, :], TABH[32:64, :], ALU.subtract, [BTAB], [BTAB])
        self.cp(TABH[32:64, :], TABT[32:64, :], [BTAB], [BTAB])
        for q4 in range(16):
            oa_, ob = OHT.next()
            oa = oa_.bitcast(BF16).rearrange("p (q j) -> p q j", q=4)
            self.dma(oa[0:64, :, :], d["oh"][:, q4 * 4:(q4 + 1) * 4, :], [], [ob])
            if q4 % 4 == 0:
                psf, pbf = self.PS.next()
                pso, pbo = self.PS.next()
            for qi in range(4):
                qq = (q4 % 4) * 4 + qi
                self.mm(psf[:, qq * 16:(qq + 1) * 16], oa[0:64, qi, 0:128], TABH[0:64, :], True, True, [ob, BTAB], [pbf])
                self.mm(pso[0:64, qq * 16:(qq + 1) * 16], oa[0:64, qi, 128:192], TABH[0:64, :], True, True, [ob, BTAB], [pbo])
            if q4 % 4 == 3:
                q0 = (q4 // 4) * 16
                for (src, pbx, dst, npart) in ((psf, pbf, EBT[0], 128), (pso, pbo, EBT[1], 64)):
                    sv_ = src[0:npart, 0:256].rearrange("p (q h) -> p h q", h=16)
                    for hf in range(2):
                        for grp in range(2):
                            h0 = (0, 8)[grp] if hf == 0 else (4, 12)[grp]
                            self.act(dst[0:npart, hf, grp * 4:(grp + 1) * 4, q0:q0 + 16], sv_[:, h0:h0 + 4, :], AF.Exp, [pbx], [BEB])
        for hf in range(2):
            for grp in range(2):
                h0 = (0, 8)[grp] if hf == 0 else (4, 12)[grp]
                self.act(ESKR[64:65, hf, grp * 4:(grp + 1) * 4, :], SNK[64:65, h0:h0 + 4].unsqueeze(2).to_broadcast([1, 4, 64]), AF.Exp, [BTAB], [BEB])

        wq = d["attn_w_qkv"][j].rearrange("(k p) c -> p k c", p=128)
        bqc, bkc, boc = self.pcol[("bq", j)], self.pcol[("bk", j)], self.pcol[("bo", j)]
        for jq in range(8):
            wv, wb = self.wslot(8)
            lo, up = q_lo(jq), q_up(jq)
            self.load_w(wv[:, :, 0:64], wq[:, :, lo * 64:(lo + 1) * 64], wb)
            self.load_w(wv[:, :, 64:128], wq[:, :, up * 64:(up + 1) * 64], wb)
            for (w, t0, nn, samp) in self.windows(sg):
                ps, pb = self.PS.next()
                for k in range(8):
                    self.mm(ps[:, 0:nn], wv[:, k, :], HT[:, k, 128 + t0:128 + t0 + nn], k == 0, k == 7, [wb, BH[w]], [pb])
                self.act(QT[:, jq, t0:t0 + nn], ps[:, 0:nn], AF.Identity, [pb, self.BP], [BQ[w]], bias=P[:, bqc + jq:bqc + jq + 1])
        for jk in range(2):
            wv, wb = self.wslot(8)
            self.load_w(wv, wq[:, :, 1024 + jk * 128:1024 + (jk + 1) * 128], wb)
            for (c0, nn) in [(0, 128)] + [(128 + t0, nn) for (w, t0, nn, samp) in self.windows(sg)]:
                ps, pb = self.PS.next()
                for k in range(8):
                    self.mm(ps[:, 0:nn], wv[:, k, :], HT[:, k, c0:c0 + nn], k == 0, k == 7, [wb] + hb(c0, c0 + nn), [pb])
                self.act(KT[:, jk, c0:c0 + nn], ps[:, 0:nn], AF.Identity, [pb, self.BP], [BK], bias=P[:, bkc + jk:bkc + jk + 1])
        self.dma(WKV, wq[:, :, 1024:1536], [], [BWKV], q="pool")
        self.dma(BKVB, d["attn_b_qkv"][j, 1024:1536].partition_broadcast(128), [], [BBKV])
        na = 10 + (1 if sg == 1 else 0)
        tiles = [("A", a, a * 128, 128) for a in range(na)] + [("B", b, 64 + b * 128, 128) for b in range(10)]
        if sg == 1:
            tiles.append(("B", 10, 64 + 10 * 128, 64))
        for (kind, idx, c0, m) in tiles:
            ps, pb = self.PS.next()
            hr = hb(c0, min(c0 + m, 128 + self.nsg(sg)))
            for k in range(8):
                self.mm(ps[0:m, :], HT[:, k, c0:c0 + m], WKV[:, k, :], k == 0, k == 7, hr + [BWKV], [pb])
            dst = (VA if kind == "A" else VB)[0:m, idx, :]
            self.tt(dst, ps[0:m, 256:512], BKVB[0:m, 256:512], ALU.add, [pb, BBKV], [BV])
            if sg == 1 and kind == "A" and idx in (9, 10):
                oa, ob = KVO.next()
                self.tt(oa, ps, BKVB, ALU.add, [pb, BBKV], [ob])
                dk, dv = (d["pk"], d["pv"]) if idx == 9 else (d["sk"], d["sv"])
                self.finals.append(self.dma(dk[j], oa[:, 0:256], [ob], []))
                self.finals.append(self.dma(dv[j], oa[:, 256:512], [ob], []))
        if sg == 1:
            for s in range(2):
                ca, cb_ = CK.next()
                self.dma(ca, d["ck"][j, s], [], [cb_])
                ps, pb = self.PS.next()
                for jk in range(2):
                    self.tr(ps[:, jk * 128:(jk + 1) * 128], ca[:, jk * 128:(jk + 1) * 128], [cb_, self.BC], [pb])
                self.act(KTC[:, s, :, :], ps[:, 0:256].rearrange("p (k t) -> p k t", k=2), AF.Identity, [pb], [BCACHE])
                self.dma(VC[:, s, :], d["cv"][j, s], [], [BCACHE], q="pool")

        WOB = WKV.rearrange("p k c -> p (k c)")[:, 0:2 * 8 * 128].rearrange("p (o k c) -> p o k c", o=2, k=8)
        BWOB = self.mkbuf("a_wob", wkv_rng)
        wo_ = d["attn_w_o"][j]

        def wo_ap(oc):
            return (WOA[:, oc], BWOA) if oc < 6 else (WOB[:, oc - 6], BWOB)

        OT = HT
        BO = self.mkbufs(["ao0", "ao1", "ao2", "ao3"], ht_rng)
        items = [("p", c) for c in range(18)] + ([("s", 0), ("s", 1)] if sg == 1 else [])
        work = []
        for (typ, c) in items:
            if typ == "p":
                qc0 = c * 64
                w = c // 6
                gc = c + 18 * sg
                pieces = []
                if gc >= 2:
                    vfull = VA[:, c // 2, :] if c % 2 == 0 else VB[:, (c - 1) // 2, :]
                    pieces.append((c * 64, 128, vfull, 0, BV))
                    vown = VA[0:64, 1 + c // 2, :] if c % 2 == 0 else VB[0:64, (c + 1) // 2, :]
                    pieces.append((128 + c * 64, 64, vown, 1, BV))
                elif gc == 1:
                    pieces.append((64, 128, VB[:, 0, :], 0, BV, True))
                    pieces.append((128 + c * 64, 64, VB[0:64, 1, :], 1, BV))
                else:
                    pieces.append((128, 64, VA[0:64, 1, :], 1, BV))
            else:
                qc0 = NPR + c * 64
                w = 3
                vown = VA[0:64, 10, :] if c == 0 else VB[0:64, 10, :]
                pieces = [(None, 128, VC[:, c, :], 0, BCACHE), (128 + NPR + c * 64, 64, vown, 1, BV)]
            for kv in range(4):
                work.append((c, qc0, w, pieces, kv))

        def stage_a(it):
            (c, qc0, w, pieces, kv) = it
            hf, jk, j0 = kv % 2, kv // 2, (kv // 2) * 4
            rows = slice(hf * 64, hf * 64 + 64)
            rhs_q = QT[rows, j0:j0 + 4, qc0:qc0 + 64]
            pss, pbs = self.PS.next()
            et, eb = ET.next()
            pt, ptb = PT.next()
            offs = []
            off = 0
            for pc in pieces:
                (kc0, nk, vap, bt, vbuf) = pc[0:5]
                if kc0 is None:
                    lk, rk = KTC[rows, c, jk, :], [BCACHE]
                else:
                    lk, rk = KT[rows, jk, kc0:kc0 + nk], [BK]
                self.mm(pss[0:nk, off:off + 256], lk, rhs_q, True, True, rk + [BQ[w]], [pbs])
                offs.append(off)
                off += 256
            for pi, pc in enumerate(pieces):
                (kc0, nk, vap, bt, vbuf) = pc[0:5]
                o_ = offs[pi]
                self.act(et[0:nk, o_:o_ + 256], pss[0:nk, o_:o_ + 256], AF.Exp, [pbs], [eb], scale=0.125)
                self.tt(pt[0:nk, o_:o_ + 256].rearrange("p (j q) -> p j q", j=4),
                        et[0:nk, o_:o_ + 256].rearrange("p (j q) -> p j q", j=4),
                        EBT[bt][0:nk, hf, j0:j0 + 4, :], ALU.mult, [eb, BEB], [ptb], eng="pool")
                if len(pc) > 5:
                    self.memset(pt[0:64, o_:o_ + 256], 0.0, [ptb], eng="pool")
            return (pt, ptb, offs)

        def stage_b(it, st):
            (c, qc0, w, pieces, kv) = it
            (pt, ptb, offs) = st
            hf, jk, j0 = kv % 2, kv // 2, (kv // 2) * 4
            rows = slice(hf * 64, hf * 64 + 64)
            pso, pbo = self.PS.next()
            np_ = len(pieces)
            for pi, pc in enumerate(pieces):
                (kc0, nk, vap, bt, vbuf) = pc[0:5]
                o_ = offs[pi]
                self.mm(pso[:, 0:256], vap[:, jk * 128:(jk + 1) * 128], pt[0:nk, o_:o_ + 256], pi == 0, pi == np_ - 1, [vbuf, ptb], [pbo])
            for pi, pc in enumerate(pieces):
                (kc0, nk, vap, bt, vbuf) = pc[0:5]
                o_ = offs[pi]
                nks = nk + 1 if nk == 64 else nk
                self.mm(pso[:, 256:512], self.ONES1[0:nks, :], pt[0:nks, o_:o_ + 256], pi == 0, pi == np_ - 1, [self.BC, ptb], [pbo])
            dn, dnb = DEN.next()
            self.recip(dn[rows, :], pso[rows, 256:512], [pbo], [dnb])
            self.tt(OT[rows, j0:j0 + 4, qc0:qc0 + 64], pso[rows, 0:256].rearrange("p (j q) -> p j q", j=4),
                    dn[rows, :].rearrange("p (j q) -> p j q", j=4), ALU.mult, [pbo, dnb], [BO[w]])

        assert len(PT.aps) == 4 and PT.i == 0
        for s4 in range(4):
            hf4, j04 = s4 % 2, (s4 // 2) * 4
            for o4 in (0, 256):
                self.cp(PT.aps[s4][64:65, o4:o4 + 256].rearrange("p (j q) -> p j q", j=4), ESKR[64:65, hf4, j04:j04 + 4, :],
                        [BEB], [PT.bufs[s4]], eng="pool")
        DEPTH = 3
        sts = {}
        for i in range(min(DEPTH, len(work))):
            sts[i] = stage_a(work[i])
        for i in range(len(work)):
            stage_b(work[i], sts.pop(i))
            if i + DEPTH < len(work):
                sts[i + DEPTH] = stage_a(work[i + DEPTH])

        for oc in range(6, 8):
            dst, dbuf = wo_ap(oc)
            for hf in range(2):
                for grp in range(2):
                    h0 = (0, 8)[grp] if hf == 0 else (4, 12)[grp]
                    src = wo_[h0 * 64:(h0 + 4) * 64, oc * 128:(oc + 1) * 128].rearrange("(j p) c -> p j c", p=64)
                    self.dma(dst[hf * 64:(hf + 1) * 64, grp * 4:(grp + 1) * 4, :], src, [], [dbuf], q="pool")
        BY = self.mkbuf("a_y", (e0, e1))
        Y2 = self.ARENA[:, kv0:kv0 + 8 * WIN].rearrange("p (k t) -> p k t", k=8)
        BY2 = self.mkbuf("a_y2", (kv0, kv0 + 8 * WIN))
        Ys = [(Y, BY), (Y2, BY2)]
        for wi_, (w, t0, n, samp) in enumerate(self.windows(sg)):
            Yw, BYw = Ys[wi_ % 2]
            sq, sqb = self.SQ.next()
            psn = pbn = None
            pend = None
            for oc in range(8):
                wo_t, wo_b = wo_ap(oc)
                ps, pb = self.PS.next()
                for k in range(8):
                    self.mm(ps[:, 0:n], wo_t[:, k, :], OT[:, k, t0:t0 + n], k == 0, k == 7, [wo_b, BO[w]], [pb])
                if pend is not None:
                    if psn is None:
                        psn, pbn = self.PS.next()
                    self.mm(psn[:, 0:n], self.ONESM, sq[:, pend, 0:n], pend == 0, False, [sqb, self.BC], [pbn])
                bo_ap = P[:, boc + oc:boc + oc + 1]
                self.act(Yw[:, oc, 0:n], ps[:, 0:n], AF.Identity, [pb, self.BP], [BYw], bias=bo_ap)
                self.act(sq[:, oc, 0:n], ps[:, 0:n], AF.Square, [pb, self.BP], [sqb], bias=bo_ap)
                pend = oc
            self.mm(psn[:, 0:n], self.ONESM, sq[:, 7, 0:n], False, True, [sqb, self.BC], [pbn])
            self.resid_tail(w, t0, n, Yw[:, :, 0:n], BYw, gc1, psn, pbn)

    def convmod(self, l, sg):
        self.scratch_reset()
        d = self.d
        P = self.PARAMS
        HT = self.carve_bf(8 * 1280).rearrange("p (k t) -> p k t", k=8)
        ht_rng = self.last_rng
        BH = self.mkbufs(["ch0", "ch1", "ch2", "ch3"])
        GW = 30 + NPR + 2 * 94
        GLU = self.carve_bf(8 * GW).rearrange("p (k t) -> p k t", k=8)
        glu_rng = self.last_rng
        BG = self.mkbufs(["cg0", "cg1", "cg2", "cg3", "cgc"])
        BGC = BG[4]
        ZB = self.carve_bf(8 * 1280).rearrange("p (k t) -> p k t", k=8)
        BZ = self.mkbufs(["cz0", "cz1", "cz2", "cz3"])
        a0 = self.aoff
        S1 = self.carve(1280)
        S2 = self.carve(1280)
        BS = self.mkbufs(["cs0", "cs1", "cs2", "cs3"], (a0, self.aoff))
        DWR = self.ring("dw", 2, 31 * 128, bf16=True)
        ZSQ = self.ring("zsq", 3, WIN, bf16=True)
        SIG = self.ring("sig", 2, WIN)
        TMP = self.ring("ctmp", 2, WIN)
        GT = self.carve(8 * 96).rearrange("p (k t) -> p k t", k=8)
        BGT = self.mkbuf("c_gt")
        GOST = self.ring("gost", 1, 1024)
        Y = self.carve(8 * WIN).rearrange("p (k t) -> p k t", k=8)
        BY = self.mkbuf("c_y")
        gc0, gc1 = self.pcol["ng"] + (l * 4 + 0) * 8, self.pcol["ng"] + (l * 4 + 1) * 8
        cb1, cwd, cbd, clg, clb, cb2, stc = (self.pcol[k] for k in ("cb1", "cwd", "cbd", "clg", "clb", "cb2", "stc"))
        self.norm_h(sg, gc0, lambda k, t0, nn, samp: HT[:, k, t0:t0 + nn], BH)
        if sg == 0:
            self.memset(GLU[:, :, 0:30], 0.0, [BGC])
        else:
            self.cp(GLU[:, :, 0:30], self.CCARRY, [self.BCARRY], [BGC])
            for s in range(2):
                src = P[:, stc + s * 240:stc + (s + 1) * 240].rearrange("p (r k) -> p k r", k=8)
                self.cp(GLU[:, :, 30 + NPR + s * 94:30 + NPR + s * 94 + 30], src, [self.BP], [BG[3]])
        w1 = d["conv_w_pw1"][0].rearrange("(k p) c -> p k c", p=128)
        for jc in range(8):
            wv, wb = self.wslot(16)
            self.load_w(wv[:, 0:8, :], w1[:, :, jc * 128:(jc + 1) * 128], wb)
            self.load_w(wv[:, 8:16, :], w1[:, :, 1024 + jc * 128:1024 + (jc + 1) * 128], wb)
            for (w, t0, nn, samp) in self.windows(sg):
                psa, pba = self.PS.next()
                psg, pbg = self.PS.next()
                for k in range(8):
                    self.mm(psa[:, 0:nn], wv[:, k, :], HT[:, k, t0:t0 + nn], k == 0, k == 7, [wb, BH[w]], [pba])
                for k in range(8):
                    self.mm(psg[:, 0:nn], wv[:, 8 + k, :], HT[:, k, t0:t0 + nn], k == 0, k == 7, [wb, BH[w]], [pbg])
                sg_, sgb = SIG.next()
                self.act(sg_[:, 0:nn], psg[:, 0:nn], AF.Sigmoid, [pbg, self.BP], [sgb], bias=P[:, cb1 + 8 + jc:cb1 + 9 + jc])
                ba = P[:, cb1 + jc:cb1 + jc + 1]
                if samp:
                    dst = GLU[:, jc, 30 + NPR:GW].rearrange("p (s q) -> p s q", s=2)[:, :, 30:94]
                    a3 = psa[:, 0:128].rearrange("p (s q) -> p s q", s=2)
                    s3 = sg_[:, 0:128].rearrange("p (s q) -> p s q", s=2)
                    self.stt(dst, a3, ba, s3, ALU.add, ALU.mult, [pba, sgb, self.BP], [BG[3]])
                    self.stt(GT[:, jc, 32:96].rearrange("p (s q) -> p s q", s=2)[:, :, 0:30], a3[:, :, 34:64], ba, s3[:, :, 34:64],
                             ALU.add, ALU.mult, [pba, sgb, self.BP], [BGT])
                else:
                    self.stt(GLU[:, jc, 30 + t0:30 + t0 + nn], psa[:, 0:nn], ba, sg_[:, 0:nn], ALU.add, ALU.mult, [pba, sgb, self.BP], [BG[w]])
                    if sg == 1 and w == 2:
                        self.stt(GT[:, jc, 0:30], psa[:, nn - 30:nn], ba, sg_[:, nn - 30:nn], ALU.add, ALU.mult, [pba, sgb, self.BP], [BGT])
        if sg == 0:
            self.cp(self.CCARRY, GLU[:, :, NPR:NPR + 30], [BG[2]], [self.BCARRY])
        else:
            for seg in range(3):
                c0 = 0 if seg == 0 else 32 * seg
                oa, ob = GOST.next()
                for h in range(2):
                    ps, pb = self.PS.next()
                    for kk in range(4):
                        self.tr(ps[0:30, kk * 128:(kk + 1) * 128], GT[:, h * 4 + kk, c0:c0 + 30], [BGT, self.BC], [pb])
                    self.act(oa[0:30, h * 512:(h + 1) * 512], ps[0:30, :], AF.Identity, [pb], [ob])
                dst = d["pconv"] if seg == 0 else d["sconv"][seg - 1]
                self.finals.append(self.dma(dst[:, :], oa[0:30, :], [ob], []))
        C = HT
        BCt = self.mkbufs(["cc0", "cc1", "cc2", "cc3"], ht_rng)
        pend_stats = None

        def emit_stats(jc, w, t0, nn, zq, zqb):
            ps1, pb1 = self.PS.next()
            self.mm(ps1[:, 0:nn], self.ONESM, ZB[:, jc, t0:t0 + nn], True, True, [self.BC, BZ[w]], [pb1])
            ps2, pb2 = self.PS.next()
            self.mm(ps2[:, 0:nn], self.ONESM, zq[:, 0:nn], True, True, [self.BC, zqb], [pb2])
            if jc == 0:
                self.cp(S1[:, t0:t0 + nn], ps1[:, 0:nn], [pb1], [BS[w]])
                self.cp(S2[:, t0:t0 + nn], ps2[:, 0:nn], [pb2], [BS[w]])
            else:
                self.tt(S1[:, t0:t0 + nn], S1[:, t0:t0 + nn], ps1[:, 0:nn], ALU.add, [pb1, BS[w]], [BS[w]])
                self.tt(S2[:, t0:t0 + nn], S2[:, t0:t0 + nn], ps2[:, 0:nn], ALU.add, [pb2, BS[w]], [BS[w]])
        for jc in range(8):
            dw_, dwb = DWR.next()
            dw = dw_.rearrange("p (k c) -> p k c", k=31)
            wk = P[:, cwd + jc:cwd + jc + 31 * 8].rearrange("p (k j) -> p k j", j=8)[:, :, 0:1]
            self.tt(dw, self.IDENT.unsqueeze(1).to_broadcast([128, 31, 128]), wk.to_broadcast([128, 31, 128]), ALU.mult,
                    [self.BC, self.BP], [dwb])
            for (w, t0, nn, samp) in self.windows(sg):
                ps, pb = self.PS.next()
                gr = [BG[3]] if samp else [BG[w], (BGC if w == 0 else BG[w - 1])]
                for k in range(31):
                    if samp:
                        rhs = GLU[:, jc, 30 + NPR:GW].rearrange("p (s q) -> p s q", s=2)[:, :, k:k + 64]
                    else:
                        rhs = GLU[:, jc, t0 + k:t0 + k + nn]
                    self.mm(ps[:, 0:nn], dw[:, k, :], rhs, k == 0, k == 30, [dwb] + gr, [pb])
                bd = P[:, cbd + jc:cbd + jc + 1]
                self.act(ZB[:, jc, t0:t0 + nn], ps[:, 0:nn], AF.Identity, [pb, self.BP], [BZ[w]], bias=bd)
                zq, zqb = ZSQ.next()
                self.act(zq[:, 0:nn], ps[:, 0:nn], AF.Square, [pb, self.BP], [zqb], bias=bd)
                if pend_stats is not None:
                    emit_stats(*pend_stats)
                pend_stats = (jc, w, t0, nn, zq, zqb)
        emit_stats(*pend_stats)
        for (w, t0, nn, samp) in self.windows(sg):
            tm, tmb = TMP.next()
            self.tt(tm[:, 0:nn], S1[:, t0:t0 + nn], S1[:, t0:t0 + nn], ALU.mult, [BS[w]], [tmb])
            self.tt(S2[:, t0:t0 + nn], S2[:, t0:t0 + nn], tm[:, 0:nn], ALU.subtract, [BS[w], tmb], [BS[w]])
            self.act(S2[:, t0:t0 + nn], S2[:, t0:t0 + nn], AF.Sqrt, [BS[w], self.BC], [BS[w]], bias=self.EPS[:, 1:2])
            self.recip(S2[:, t0:t0 + nn], S2[:, t0:t0 + nn], [BS[w]], [BS[w]])
            for jc in range(8):
                tm, tmb = TMP.next()
                self.tt(tm[:, 0:nn], ZB[:, jc, t0:t0 + nn], S1[:, t0:t0 + nn], ALU.subtract, [BZ[w], BS[w]], [tmb])
                self.tt(tm[:, 0:nn], tm[:, 0:nn], S2[:, t0:t0 + nn], ALU.mult, [tmb, BS[w]], [tmb])
                self.act(C[:, jc, t0:t0 + nn], tm[:, 0:nn], AF.Silu, [tmb, self.BP], [BCt[w]],
                         bias=P[:, clb + jc:clb + jc + 1], scale=P[:, clg + jc:clg + jc + 1])
        w2 = d["conv_w_pw2"][0].rearrange("(k p) c -> p k c", p=128)
        assert glu_rng[1] - glu_rng[0] >= 8 * WIN
        Y2 = self.ARENA[:, glu_rng[0]:glu_rng[0] + 8 * WIN].rearrange("p (k t) -> p k t", k=8)
        BY2 = self.mkbuf("c_y2", (glu_rng[0], glu_rng[0] + 8 * WIN))
        self.out_linear(sg, 8, lambda k, t0, nn: C[:, k, t0:t0 + nn],
                        lambda dst, oc, wb: self.load_w(dst, w2[:, :, oc * 128:(oc + 1) * 128], wb),
                        cb2, gc1, BCt, [(Y, BY), (Y2, BY2)], None)

    def cmlp(self, l, sg):
        self.scratch_reset()
        d = self.d
        P = self.PARAMS
        WV = self.carve_bf(8 * 2048).rearrange("p (k c) -> p k c", k=8)
        BWV = self.mkbuf("m_wv")
        a0 = self.aoff
        LNG = self.carve(2048)
        LNB = self.carve(2048)
        BVB = self.carve(2048)
        BLN = self.mkbuf("m_ln", (a0, self.aoff))
        HT = self.carve_bf(8 * WIN).rearrange("p (k t) -> p k t", k=8)
        BH = self.mkbuf("m_ht")
        U = self.carve_bf(16 * WIN).rearrange("p (k t) -> p k t", k=16)
        BU = self.mkbuf("m_u")
        VT = self.carve_bf(3 * 2048).rearrange("p (a c) -> p a c", a=3)
        BVT = self.mkbufs(["m_vt0", "m_vt1", "m_vt2"])
        VR = self.carve(2048)
        BVR = self.mkbuf("m_vr")
        a0 = self.aoff
        STAT = self.carve(4 * 6).rearrange("p (a b) -> p a b", a=4)
        MV = self.carve(4)
        BST = self.mkbuf("m_st", (a0, self.aoff))
        WSIN = self.carve(4 * 128).rearrange("p (g j) -> p g j", g=4)
        wsin_rng = self.last_rng
        BWSI = self.mkbuf("m_wsi")
        WST = self.carve_bf(2 * 4 * 128).rearrange("p (v g i) -> p v g i", v=2, g=4)
        BWS = self.mkbuf("m_ws")
        a0 = self.aoff
        BSR = self.carve(4 * 128).rearrange("p (g i) -> p g i", g=4)
        BSH = self.carve_bf(2 * 4 * 128).rearrange("p (v g i) -> p v g i", v=2, g=4)
        BBS = self.mkbuf("m_bs", (a0, self.aoff))
        Y = self.carve(8 * WIN).rearrange("p (k t) -> p k t", k=8)
        BY = self.mkbuf("m_y")
        gc0, gc1 = self.pcol["ng"] + (l * 4 + 0) * 8, self.pcol["ng"] + (l * 4 + 1) * 8
        mbu, mbo = self.pcol["mbu"], self.pcol["mbo"]
        wi = d["cmlp_w_in"][0].rearrange("(k p) c -> p k c", p=128)
        wo_ = d["cmlp_w_out"][0].rearrange("(k p) c -> p k c", p=128)
        for cb in range(4):
            self.dma(WV[:, :, cb * 512:(cb + 1) * 512], wi[:, :, 2048 + cb * 512:2048 + (cb + 1) * 512], [], [BWV], q="pool")
        self.dma(LNG, d["cmlp_ln_g"][0].partition_broadcast(128), [], [BLN])
        self.dma(LNB, d["cmlp_ln_b"][0].partition_broadcast(128), [], [BLN])
        self.dma(BVB, d["cmlp_b_in"][0, 2048:4096].partition_broadcast(128), [], [BLN])
        ws = d["cmlp_w_s"][0]
        bs = d["cmlp_b_s"][0]
        for v in range(2 if sg == 1 else 1):
            if v == 0:
                self.dma(WSIN, ws.rearrange("g i j -> i g j"), [BWSI], [BWSI])
            else:
                self.memset(WSIN, 0.0, [BWSI])
                self.dma(WSIN[0:64, :, 0:64], ws[:, 0:64, 0:64].rearrange("g i j -> i g j"), [BWSI], [BWSI])
                self.dma(WSIN[64:128, :, 64:128], ws[:, 0:64, 0:64].rearrange("g i j -> i g j"), [BWSI], [BWSI])
            ps, pb = self.PS.next()
            for g in range(4):
                self.tr(ps[:, g * 128:(g + 1) * 128], WSIN[:, g, :], [BWSI, self.BC], [pb])
            self.act(WST[:, v, :, :], ps.rearrange("p (g i) -> p g i", g=4), AF.Identity, [pb], [BWS])
            if v == 0:
                self.memset(WST[64:128, 0, :, 0:64], 0.0, [BWS])

        BSL = self.ARENA[:, wsin_rng[0]:wsin_rng[1]].bitcast(BF16).rearrange("p (v g i) -> p v g i", v=2, g=4)
        BBL = self.mkbuf("m_bsl", wsin_rng)
        for v in range(2 if sg == 1 else 1):
            if v == 0:
                self.dma(BSR[0:1, :, :], bs.rearrange("g i -> (g i)").rearrange("(o g i) -> o g i", o=1, g=4), [BBS], [BBS])
            else:
                self.dma(BSR[0:1, :, 0:64], bs[:, 0:64].unsqueeze(0), [BBS], [BBS])
                self.dma(BSR[0:1, :, 64:128], bs[:, 0:64].unsqueeze(0), [BBS], [BBS])
            self.cp(BSH[0:1, v], BSR[0:1], [BBS], [BBS])
            self.tt(BSR[0:1], BSR[0:1], BSH[0:1, v], ALU.subtract, [BBS], [BBS])
            self.cp(BSL[0:1, v], BSR[0:1], [BBS], [BBL])
        def make_ht(wn):
            (w_, t0_, nn_, samp_) = wn
            rs, rb = self.rstd_of(self.X[:, :, t0_:t0_ + nn_], nn_, [self.BX[w_]])
            for k in range(8):
                self.stt(HT[:, k, 0:nn_], self.X[:, k, t0_:t0_ + nn_], P[:, gc0 + k:gc0 + k + 1], rs[:, 0:nn_], ALU.mult, ALU.mult,
                         [self.BX[w_], rb, self.BP], [BH])
        wlist = self.windows(sg)
        make_ht(wlist[0])
        for widx, (w, t0, nn, samp) in enumerate(wlist):
            var = 1 if samp else 0
            def u_chunk(jc):
                wv, wb = self.wslot(8)
                self.load_w(wv, wi[:, :, jc * 128:(jc + 1) * 128], wb)
                ps, pb = self.PS.next()
                for k in range(8):
                    self.mm(ps[:, 0:nn], wv[:, k, :], HT[:, k, 0:nn], k == 0, k == 7, [wb, BH], [pb])
                self.act(U[:, jc, 0:nn], ps[:, 0:nn], AF.Gelu_apprx_tanh, [pb, self.BP], [BU], bias=P[:, mbu + jc:mbu + jc + 1])

            def v_tile(t):
                for cb in range(4):
                    ps, pb = self.PS.next()
                    for k in range(8):
                        self.mm(ps, HT[:, k, t * 128:(t + 1) * 128], WV[:, k, cb * 512:(cb + 1) * 512], k == 0, k == 7, [BH, BWV], [pb])
                    vs = VR[:, cb * 512:(cb + 1) * 512]
                    self.tt(vs, ps, BVB[:, cb * 512:(cb + 1) * 512], ALU.add, [pb, BLN], [BVR])
                    self.act(vs, vs, AF.Gelu_apprx_tanh, [BVR], [BVR])
                    self.S.op("dve", (lambda o, i: lambda e: e.bn_stats(out=o, in_=i))(STAT[:, cb, :], vs), [BVR], [BST])
                self.S.op("dve", lambda e: e.bn_aggr(out=MV[:, 0:2], in_=STAT), [BST], [BST])
                self.act(MV[:, 2:3], MV[:, 1:2], AF.Sqrt, [BST, self.BC], [BST], bias=self.EPS[:, 1:2])
                self.recip(MV[:, 2:3], MV[:, 2:3], [BST], [BST])
                self.stt(MV[:, 3:4], MV[:, 0:1], -1.0, MV[:, 2:3], ALU.mult, ALU.mult, [BST], [BST])
                self.act(VR, VR, AF.Identity, [BVR, BST], [BVR], bias=MV[:, 3:4], scale=MV[:, 2:3])
                self.tt(VR, VR, LNG, ALU.mult, [BVR, BLN], [BVR])
                if samp:
                    self.tt(VR, VR, LNB, ALU.add, [BVR, BLN], [BVR])
                    self.cp(VT[:, t, :], VR, [BVR], [BVT[t]])
                    self.finals.append(self.dma(d["scv"][:, :], VR, [BVR], []))
                else:
                    self.tt(VT[:, t, :], VR, LNB, ALU.add, [BVR, BLN], [BVT[t]])

            def spatial(t):
                for g in range(4):
                    ps, pb = self.PS.next()
                    for cc in range(4):
                        ch = g * 4 + cc
                        self.mm(ps[:, cc * 128:(cc + 1) * 128], VT[:, t, ch * 128:(ch + 1) * 128], WST[:, var, g, :], True, False, [BVT[t], BWS], [pb])
                        self.mm(ps[:, cc * 128:(cc + 1) * 128], self.ONES1[0:1, :], BSH[0:1, var, g, :], False, False, [self.BC, BBS], [pb])
                        self.mm(ps[:, cc * 128:(cc + 1) * 128], self.ONES1[0:1, :], BSL[0:1, var, g, :], False, True, [self.BC, BBL], [pb])
                    uu = U[:, g * 4:(g + 1) * 4, t * 128:(t + 1) * 128]
                    self.tt(uu, ps.rearrange("p (c i) -> p c i", c=4), uu, ALU.mult, [pb, BU], [BU])

            nt = nn // 128
            sched_u = {0: range(0, 6), 1: range(6, 11), 2: range(11, 16)} if nt == 3 else {0: range(0, 16)}
            for t in range(nt):
                v_tile(t)
                for jc in sched_u[t]:
                    u_chunk(jc)
            for t in range(nt):
                spatial(t)
            if widx + 1 < len(wlist):
                make_ht(wlist[widx + 1])
            for oc in range(8):
                wv, wb = self.wslot(16)
                self.load_w(wv, wo_[:, :, oc * 128:(oc + 1) * 128], wb)
                ps, pb = self.PS.next()
                for k in range(16):
                    self.mm(ps[:, 0:nn], wv[:, k, :], U[:, k, 0:nn], k == 0, k == 15, [wb, BU], [pb])
                self.act(Y[:, oc, 0:nn], ps[:, 0:nn], AF.Identity, [pb, self.BP], [BY], bias=P[:, mbo + oc:mbo + oc + 1])
            self.resid(w, t0, nn, Y[:, :, 0:nn], BY, gc1)


_PROG = None
_OH = None


def _get_prog():
    global _PROG, _OH
    if _PROG is None:
        p = Prog()
        p.build()
        _PROG = p
        _OH = onehot2_const()
    return _PROG


WEIGHT_KEYS = ["rel_bias_table", "norm_gain", "attn_w_qkv", "attn_b_qkv", "attn_w_o", "attn_b_o", "attn_sinks",
               "conv_w_pw1", "conv_b_pw1", "conv_w_dw", "conv_b_dw", "conv_ln_g", "conv_ln_b", "conv_w_pw2",
               "conv_b_pw2", "cmlp_w_in", "cmlp_b_in", "cmlp_ln_g", "cmlp_ln_b", "cmlp_w_s", "cmlp_b_s",
               "cmlp_w_out", "cmlp_b_out", "ffn_w_up", "ffn_w_dw", "ffn_b_dw", "ffn_w_down"]


def kernel(**inputs):
    p = _get_prog()
    f = lambda a: np.ascontiguousarray(np.asarray(a, dtype=np.float32))
    xp, xs = f(inputs["x_prompt"]), f(inputs["x_sample"])
    cak, cav = f(inputs["cache_attn_k"]), f(inputs["cache_attn_v"])
    stc, stf = f(inputs["state_conv"]), f(inputs["state_ffn_conv"])
    wts = {k: f(inputs[k]) for k in WEIGHT_KEYS}
    in_maps = []
    for c in range(8):
        b, hf = c // 2, c % 2
        s0 = 0 if hf == 0 else 4096 - TP
        m = dict(wts)
        m["xin"] = np.ascontiguousarray(np.concatenate([xp[b, s0:s0 + TP], xs[2 * c], xs[2 * c + 1]], axis=0))
        m["ck"] = np.ascontiguousarray(cak[:, 2 * c:2 * c + 2].reshape(2, 2, 128, 256))
        m["cv"] = np.ascontiguousarray(cav[:, 2 * c:2 * c + 2].reshape(2, 2, 128, 256))
        m["stc"] = np.ascontiguousarray(stc[0, 2 * c:2 * c + 2])
        m["stf"] = np.ascontiguousarray(stf[:, 2 * c:2 * c + 2])
        m["oh"] = _OH
        in_maps.append(m)
    res = run_bass_kernel_spmd(p.nc, in_maps, core_ids=list(range(8)))
    R = res.results
    y_prompt = np.zeros((4, 4096, 1024), np.float32)
    y_sample = np.zeros((16, 64, 1024), np.float32)
    p_k = np.zeros((2, 4, 128, 4, 64), np.float32)
    p_v = np.zeros((2, 4, 128, 4, 64), np.float32)
    p_conv = np.zeros((1, 4, 30, 1024), np.float32)
    p_ffn = np.zeros((4, 4, 2, 5632), np.float32)
    s_k = np.zeros((2, 16, 64, 4, 64), np.float32)
    s_v = np.zeros((2, 16, 64, 4, 64), np.float32)
    s_conv = np.zeros((1, 16, 30, 1024), np.float32)
    s_cv = np.zeros((1, 16, 64, 2048), np.float32)
    s_ffn = np.zeros((4, 16, 2, 5632), np.float32)
    for c in range(8):
        b, hf = c // 2, c % 2
        r = R[c]
        yo = np.asarray(r["yout"])
        if hf == 0:
            y_prompt[b, 0:TP] = yo[0:TP]
        else:
            y_prompt[b, TP:4096] = yo[2 * TP - 4096:TP]
            p_k[:, b] = np.asarray(r["pk"]).reshape(2, 128, 4, 64)
            p_v[:, b] = np.asarray(r["pv"]).reshape(2, 128, 4, 64)
            p_conv[0, b] = np.asarray(r["pconv"])
            p_ffn[:, b] = np.asarray(r["pffn"])
        y_sample[2 * c] = yo[TP:TP + 64]
        y_sample[2 * c + 1] = yo[TP + 64:TP + 128]
        s_k[:, 2 * c:2 * c + 2] = np.asarray(r["sk"]).reshape(2, 2, 64, 4, 64)
        s_v[:, 2 * c:2 * c + 2] = np.asarray(r["sv"]).reshape(2, 2, 64, 4, 64)
        s_conv[0, 2 * c:2 * c + 2] = np.asarray(r["sconv"])
        s_cv[0, 2 * c:2 * c + 2] = np.asarray(r["scv"]).reshape(2, 64, 2048)
        s_ffn[:, 2 * c:2 * c + 2] = np.asarray(r["sffn"])
    return (y_prompt, y_sample, p_k, p_v, p_conv, p_ffn, s_k, s_v, s_conv, s_cv, s_ffn)
```

```python
import contextlib
import numpy as np
import concourse.bass as bass
import concourse.mybir as mybir
from concourse.bass_utils import run_bass_kernel_spmd

F32 = mybir.dt.float32
BF16 = mybir.dt.bfloat16
AF = mybir.ActivationFunctionType
ALU = mybir.AluOpType

ENGS = ("pe", "act", "dve", "pool", "sp")
NPR = 1152
WIN = 384
TP = 2304
NTOK = 2432
DFF = 2816
NFC = 44


class Buf:
    __slots__ = ("name", "w", "r")

    def __init__(self, name, fence=None):
        self.name = name
        self.w = None
        self.r = list(fence) if fence else []


class Op:
    __slots__ = ("eng", "fn", "deps", "dma", "has_dep", "tok", "id")


class Sched:
    def __init__(self, nc, n_dma_sems=24):
        self.nc = nc
        self.ops = []
        self.per_eng = {e: [] for e in ENGS}
        self.n_dma_sems = n_dma_sems

    def op(self, eng, fn, reads=(), writes=(), dma=False):
        o = Op()
        o.eng, o.fn, o.dma, o.has_dep, o.tok, o.id = eng, fn, dma, False, None, len(self.ops)
        deps = set()
        for b in reads:
            if b.w is not None:
                deps.add(b.w)
        for b in writes:
            if b.w is not None:
                deps.add(b.w)
            deps.update(b.r)
        if eng == "pe":
            deps = {dd for dd in deps if self.ops[dd].eng != "pe"}
        o.deps = deps
        for b in reads:
            b.r.append(o.id)
        for b in writes:
            b.w = o.id
            b.r = []
        self.ops.append(o)
        self.per_eng[eng].append(o)
        return o

    def emit(self, final_wait_ops=()):
        nc, ops = self.nc, self.ops
        for o in ops:
            for d in o.deps:
                ops[d].has_dep = True
        for o in final_wait_ops:
            o.has_dep = True
        with contextlib.ExitStack() as st:
            esem = {e: st.enter_context(nc.semaphore("s_" + e)) for e in ENGS}
            dsems = {e: [st.enter_context(nc.semaphore("d_%s_%d" % (e, i)))
                         for i in range(self.n_dma_sems)] for e in ("sp", "pool")}
            ecnt = {e: 0 for e in ENGS}
            dcnt = {e: [0] * self.n_dma_sems for e in dsems}
            drr = {e: 0 for e in dsems}
            for e in ENGS:
                for o in self.per_eng[e]:
                    if not o.has_dep:
                        continue
                    if o.dma:
                        j = drr[e]
                        drr[e] = (j + 1) % self.n_dma_sems
                        prev = dcnt[e][j]
                        dcnt[e][j] += 16
                        o.tok = (dsems[e][j], dcnt[e][j], prev)
                    else:
                        ecnt[e] += 1
                        o.tok = (esem[e], ecnt[e], None)
            block = st.enter_context(nc.Block())

            def run(e, eng):
                waited = {}
                for o in self.per_eng[e]:
                    need = {}
                    for d in o.deps:
                        sem, val, _ = ops[d].tok
                        k = id(sem)
                        if k not in need or need[k][1] < val:
                            need[k] = (sem, val)
                    if o.dma and o.tok is not None and o.tok[2]:
                        sem, _, prev = o.tok
                        k = id(sem)
                        if k not in need or need[k][1] < prev:
                            need[k] = (sem, prev)
                    for k, (sem, val) in need.items():
                        if waited.get(k, 0) >= val:
                            continue
                        eng.wait_ge(sem, val)
                        waited[k] = val
                    ins = o.fn(eng)
                    if o.tok is not None:
                        ins.then_inc(o.tok[0], 16 if o.dma else 1)
                if e == "sp":
                    for o in final_wait_ops:
                        sem, val, _ = o.tok
                        eng.wait_ge(sem, val)

            @block.tensor
            def _(eng):
                run("pe", eng)

            @block.scalar
            def _(eng):
                run("act", eng)

            @block.vector
            def _(eng):
                run("dve", eng)

            @block.gpsimd
            def _(eng):
                run("pool", eng)

            @block.sync
            def _(eng):
                run("sp", eng)


class Ring:
    def __init__(self, name, aps, bufs):
        self.aps = aps
        self.bufs = bufs
        self.i = 0

    def next(self):
        j = self.i
        self.i = (j + 1) % len(self.aps)
        return self.aps[j], self.bufs[j]


def q_lo(j):
    return j if j < 4 else 8 + (j - 4)


def q_up(j):
    return 4 + j if j < 4 else 12 + (j - 4)


def t5_bucket_np(rel):
    half, max_exact = 16, 8
    n = np.abs(rel)
    log_ratio = np.log(np.maximum(n, 1).astype(np.float32) / np.float32(max_exact)) / np.float32(np.log(128 / max_exact))
    large = np.minimum(max_exact + (log_ratio * np.float32(half - max_exact)).astype(np.int32), half - 1)
    return np.where(rel > 0, half, 0) + np.where(n < max_exact, n, large)


def onehot2_const():
    import ml_dtypes
    oh = onehot_const()
    return np.ascontiguousarray(np.concatenate([oh, oh], axis=0).astype(ml_dtypes.bfloat16))


def onehot_const():
    q = np.arange(64)[:, None]
    j = np.arange(192)[None, :]
    bk = t5_bucket_np((j - 128) - q)
    oh = np.zeros((32, 64, 192), np.float32)
    for b in range(32):
        oh[b] = (bk == b)
    return oh


class Prog:
    def __init__(self):
        self.nc = nc = bass.Bass("TRN2", target_bir_lowering=False)
        self.S = Sched(nc)
        self.st = contextlib.ExitStack()
        self.d = {}
        self.outs = []
        self.live = []
        self.phase = 0


    def din(self, name, shape):
        self.d[name] = self.nc.dram_tensor(name, list(shape), F32, kind="ExternalInput").ap()
        return self.d[name]

    def dout(self, name, shape):
        self.d[name] = self.nc.dram_tensor(name, list(shape), F32, kind="ExternalOutput").ap()
        return self.d[name]

    def mm(self, out, lhsT, rhs, start, stop, R, W):
        return self.S.op("pe", lambda e: e.matmul(out, lhsT=lhsT, rhs=rhs, start=start, stop=stop), R, W)

    def tr(self, out, in_, R, W):
        ident = self.IDENT[0:in_.shape[0], 0:in_.shape[0]]
        return self.S.op("pe", lambda e: e.transpose(out, in_, ident), R, W)

    def act(self, out, in_, func, R, W, bias=None, scale=1.0):
        if bias is None:
            return self.S.op("act", lambda e: e.activation(out=out, in_=in_, func=func, scale=scale), R, W)
        return self.S.op("act", lambda e: e.activation(out=out, in_=in_, func=func, bias=bias, scale=scale), R, W)

    def stt(self, out, in0, scalar, in1, op0, op1, R, W, eng="dve"):
        return self.S.op(eng, lambda e: e.scalar_tensor_tensor(out=out, in0=in0, scalar=scalar, in1=in1, op0=op0, op1=op1), R, W)

    def tt(self, out, in0, in1, op, R, W, eng="dve"):
        return self.S.op(eng, lambda e: e.tensor_tensor(out=out, in0=in0, in1=in1, op=op), R, W)

    def cp(self, out, in_, R, W, eng="dve"):
        return self.S.op(eng, lambda e: e.tensor_copy(out=out, in_=in_), R, W)

    def recip(self, out, in_, R, W):
        return self.S.op("dve", lambda e: e.reciprocal(out=out, in_=in_), R, W)

    def memset(self, ap, val, W, eng="dve"):
        return self.S.op(eng, lambda e: e.memset(ap, val), (), W)

    def dma(self, out, in_, R, W, q="sp"):
        return self.S.op(q, lambda e: e.dma_start(out=out, in_=in_), R, W, dma=True)

    def carve(self, nwords):
        a = self.aoff
        self.aoff += nwords
        assert self.aoff <= self.NA, ("arena overflow", self.aoff, self.NA)
        self.last_rng = (a, a + nwords)
        return self.ARENA[:, a:a + nwords]

    def carve_bf(self, nel):
        return self.carve((nel + 1) // 2).bitcast(BF16)[:, 0:nel]

    def mkbufs(self, names, rng=None):
        rng = rng or self.last_rng
        fence = []
        keep = []
        for (a, b, bf, ph) in self.live:
            if a < rng[1] and rng[0] < b:
                if bf.w is not None:
                    fence.append(bf.w)
                fence.extend(bf.r)
                if ph < self.phase and rng[0] <= a and b <= rng[1]:
                    continue
            keep.append((a, b, bf, ph))
        self.live = keep
        out = [Buf(n, fence) for n in names]
        for bf in out:
            self.live.append((rng[0], rng[1], bf, self.phase))
        return out

    def mkbuf(self, name, rng=None):
        return self.mkbufs([name], rng)[0]

    def ring(self, name, n, nwords, bf16=False):
        aps, bufs = [], []
        for i in range(n):
            ap = self.carve_bf(nwords) if bf16 else self.carve(nwords)
            aps.append(ap)
            bufs.append(self.mkbuf("%s%d" % (name, i)))
        return Ring(name, aps, bufs)

    def scratch_reset(self):
        self.aoff = self.scratch_base
        self.phase += 1

    def windows(self, sg):
        w = [(i, i * WIN, WIN, False) for i in range(3)]
        if sg == 1:
            w.append((3, NPR, 128, True))
        return w

    def nsg(self, sg):
        return NPR + (128 if sg == 1 else 0)

    def build(self):
        nc, S, st = self.nc, self.S, self.st
        din, dout = self.din, self.dout
        xin = din("xin", (NTOK, 1024))
        ck = din("ck", (2, 2, 128, 256))
        cv = din("cv", (2, 2, 128, 256))
        stc = din("stc", (2, 30, 1024))
        stf = din("stf", (4, 2, 2, 5632))
        self.d["oh"] = nc.dram_tensor("oh", [64, 64, 192], BF16, kind="ExternalInput").ap()
        relt = din("rel_bias_table", (32, 16))
        ng = din("norm_gain", (4, 4, 1024))
        wqkv = din("attn_w_qkv", (2, 1024, 1536))
        bqkv = din("attn_b_qkv", (2, 1536))
        wo = din("attn_w_o", (2, 1024, 1024))
        bo = din("attn_b_o", (2, 1024))
        sinks = din("attn_sinks", (2, 16))
        cw1 = din("conv_w_pw1", (1, 1024, 2048))
        cb1 = din("conv_b_pw1", (1, 2048))
        cwd = din("conv_w_dw", (1, 31, 1024))
        cbd = din("conv_b_dw", (1, 1024))
        clg = din("conv_ln_g", (1, 1024))
        clb = din("conv_ln_b", (1, 1024))
        cw2 = din("conv_w_pw2", (1, 1024, 1024))
        cb2 = din("conv_b_pw2", (1, 1024))
        mwi = din("cmlp_w_in", (1, 1024, 4096))
        mbi = din("cmlp_b_in", (1, 4096))
        mlg = din("cmlp_ln_g", (1, 2048))
        mlb = din("cmlp_ln_b", (1, 2048))
        mws = din("cmlp_w_s", (1, 4, 128, 128))
        mbs = din("cmlp_b_s", (1, 4, 128))
        mwo = din("cmlp_w_out", (1, 2048, 1024))
        mbo = din("cmlp_b_out", (1, 1024))
        fwu = din("ffn_w_up", (4, 1024, 5632))
        fwd = din("ffn_w_dw", (4, 3, 5632))
        fbd = din("ffn_b_dw", (4, 5632))
        fwdn = din("ffn_w_down", (4, 2816, 1024))
        yout = dout("yout", (NTOK, 1024))
        pk = dout("pk", (2, 128, 256))
        pv = dout("pv", (2, 128, 256))
        sk = dout("sk", (2, 128, 256))
        sv = dout("sv", (2, 128, 256))
        pconv = dout("pconv", (30, 1024))
        sconv = dout("sconv", (2, 30, 1024))
        pffn = dout("pffn", (4, 2, 5632))
        sffn = dout("sffn", (4, 2, 2, 5632))
        scv = dout("scv", (128, 2048))

        self.NA = 207 * 256 - 64
        self.ARENA = st.enter_context(nc.sbuf_tensor("arena", [128, self.NA], F32))
        self.PSUM = st.enter_context(nc.psum_tensor("psum", [128, 8, 512], F32))
        self.aoff = 0
        self.PS = Ring("ps", [self.PSUM[:, i, :] for i in range(8)], [Buf("ps%d" % i) for i in range(8)])

        self.X = self.carve(8 * 1280).rearrange("p (k t) -> p k t", k=8)
        self.BX = self.mkbufs(["X0", "X1", "X2", "X3"])
        self.IDENT = self.carve(128)
        self.BC = self.mkbuf("consts")
        self.ONESM = self.carve_bf(128)
        self.ONES1 = self.carve_bf(128)
        self.ONESF = self.carve(128)
        self.EPS = self.carve(2)
        plist = []

        def chunks(ap1d):
            return ap1d.rearrange("(c p) -> c p", p=128)

        pcol = {}
        ncol = 0

        def addp(key, ap2d):
            nonlocal ncol
            pcol[key] = ncol
            plist.append((ncol, ap2d))
            ncol += ap2d.shape[0]

        addp("ng", chunks(ng.rearrange("a b c -> (a b c)")))
        self.bq_special = []
        for j in range(2):
            pcol[("bq", j)] = ncol
            bq16 = bqkv[j, 0:1024].rearrange("(h d) -> h d", d=64)
            self.bq_special.append((ncol, bq16))
            ncol += 8
            addp(("bk", j), chunks(bqkv[j, 1024:1280]))
            addp(("bo", j), chunks(bo[j]))
        addp("cb1", chunks(cb1[0]))
        addp("cwd", chunks(cwd[0].rearrange("k c -> (k c)")))
        addp("cbd", chunks(cbd[0]))
        addp("clg", chunks(clg[0]))
        addp("clb", chunks(clb[0]))
        addp("cb2", chunks(cb2[0]))
        addp("mbu", chunks(mbi[0, 0:2048]))
        addp("mbo", chunks(mbo[0]))
        for l in range(4):
            addp(("fwd", l), chunks(fwd[l].rearrange("k c -> (k c)")))
            addp(("fbd", l), chunks(fbd[l]))
            addp(("stf", l), chunks(stf[l].rearrange("s r c -> (s r c)")))
        addp("stc", chunks(stc.rearrange("s r c -> (s r c)")))
        self.pcol = pcol
        npad = ((ncol + 127) // 128) * 128
        self.PARAMS = self.carve(npad)
        self.BP = self.mkbuf("params")
        a0 = self.aoff
        self.ACARRY = [self.carve_bf(8 * 128).rearrange("p (k t) -> p k t", k=8) for _ in range(2)]
        self.CCARRY = self.carve_bf(8 * 30).rearrange("p (k t) -> p k t", k=8)
        self.FCARRY = [self.carve_bf(8 * 2).rearrange("p (k t) -> p k t", k=8) for _ in range(4)]
        self.BCARRY = self.mkbuf("carry", (a0, self.aoff))
        self.NSLOT = 5
        self.SLOTW = 1408
        self.RW = self.ring("rw", self.NSLOT, self.SLOTW)
        sq = self.carve_bf(8 * WIN).rearrange("p (k t) -> p k t", k=8)
        self.SQ = Ring("sq", [sq], [self.mkbuf("sq")])
        self.RS = self.ring("rs", 2, WIN)
        self.scratch_base = self.aoff

        S.op("pool", lambda e: e.memset(self.IDENT, 1.0), (), [self.BC])
        S.op("pool", lambda e: e.affine_select(out=self.IDENT, in_=self.IDENT, pattern=[[-1, 128]],
                                               compare_op=ALU.is_equal, fill=0.0, base=0, channel_multiplier=1),
             [self.BC], [self.BC])
        self.memset(self.ONESM, 1.0 / 1024.0, [self.BC])
        self.memset(self.ONES1, 1.0, [self.BC])
        self.memset(self.ONESF, 1.0, [self.BC])
        self.memset(self.EPS[:, 0:1], 1e-6, [self.BC])
        self.memset(self.EPS[:, 1:2], 1e-5, [self.BC])

        self.scratch_reset()
        stg = self.ring("pstg", 3, 128)
        for t0 in range(0, npad, 128):
            sap, sb = stg.next()
            self.memset(sap, 0.0, [sb])
            for (c0, ap2d) in plist:
                n = ap2d.shape[0]
                lo, hi = max(c0, t0), min(c0 + n, t0 + 128)
                if lo < hi:
                    self.dma(sap[lo - t0:hi - t0, :], ap2d[lo - c0:hi - c0, :], [], [sb])
            for (c0, bq16) in self.bq_special:
                if t0 <= c0 < t0 + 128:
                    assert c0 + 8 <= t0 + 128
                    r = c0 - t0
                    self.dma(sap[r:r + 4, 0:64], bq16[0:4, :], [], [sb])
                    self.dma(sap[r:r + 4, 64:128], bq16[4:8, :], [], [sb])
                    self.dma(sap[r + 4:r + 8, 0:64], bq16[8:12, :], [], [sb])
                    self.dma(sap[r + 4:r + 8, 64:128], bq16[12:16, :], [], [sb])
            ps, pb = self.PS.next()
            self.tr(ps[:, 0:128], sap, [sb, self.BC], [pb])
            self.act(self.PARAMS[:, t0:t0 + 128], ps[:, 0:128], AF.Identity, [pb], [self.BP])

        finals = []
        self.finals = finals
        for sg in range(2):
            self.load_x(sg)
            for l in range(4):
                kind, j = l % 3, l // 3
                if kind == 0:
                    self.attn(l, j, sg)
                elif kind == 1:
                    self.convmod(l, sg)
                else:
                    self.cmlp(l, sg)
                self.ffn(l, sg)
            self.store_y(sg)
        S.emit(final_wait_ops=finals)
        return nc

    def load_x(self, sg):
        self.scratch_reset()
        xin = self.d["xin"]
        stg = self.ring("xstg", 3, 1024)
        n = self.nsg(sg)
        for t in range(n // 128):
            r0 = sg * NPR + t * 128 if t < 9 else TP
            w = t // 3
            sap, sb = stg.next()
            self.dma(sap, xin[r0:r0 + 128, :], [], [sb])
            for h in range(2):
                ps, pb = self.PS.next()
                for kk in range(4):
                    k = h * 4 + kk
                    self.tr(ps[:, kk * 128:(kk + 1) * 128], sap[:, k * 128:(k + 1) * 128], [sb, self.BC], [pb])
                self.act(self.X[:, h * 4:(h + 1) * 4, t * 128:(t + 1) * 128],
                         ps.rearrange("p (a b) -> p a b", a=4), AF.Identity, [pb], [self.BX[w]])

    def store_y(self, sg):
        self.scratch_reset()
        yout = self.d["yout"]
        stg = self.ring("ystg", 3, 1024)
        n = self.nsg(sg)
        for t in range(n // 128):
            r0 = sg * NPR + t * 128 if t < 9 else TP
            w = t // 3
            sap, sb = stg.next()
            for h in range(2):
                ps, pb = self.PS.next()
                for kk in range(4):
                    k = h * 4 + kk
                    self.tr(ps[:, kk * 128:(kk + 1) * 128], self.X[:, k, t * 128:(t + 1) * 128], [self.BX[w], self.BC], [pb])
                self.act(sap[:, h * 512:(h + 1) * 512], ps, AF.Identity, [pb], [sb])
            self.finals.append(self.dma(yout[r0:r0 + 128, :], sap, [sb], []))

    def rstd_of(self, src3, n, R):
        sq, sqb = self.SQ.next()
        self.act(sq[:, :, 0:n], src3, AF.Square, R, [sqb])
        ps, pb = self.PS.next()
        for k in range(8):
            self.mm(ps[:, 0:n], self.ONESM, sq[:, k, 0:n], k == 0, k == 7, [sqb, self.BC], [pb])
        rs, rb = self.RS.next()
        self.act(rs[:, 0:n], ps[:, 0:n], AF.Sqrt, [pb, self.BC], [rb], bias=self.EPS[:, 0:1])
        self.recip(rs[:, 0:n], rs[:, 0:n], [rb], [rb])
        return rs, rb

    def norm_h(self, sg, gcol, out_fn, BH):
        for (w, t0, n, samp) in self.windows(sg):
            rs, rb = self.rstd_of(self.X[:, :, t0:t0 + n], n, [self.BX[w]])
            for k in range(8):
                dst = out_fn(k, t0, n, samp)
                xin_, rin = self.X[:, k, t0:t0 + n], rs[:, 0:n]
                if len(dst.shape) == 3:
                    xin_ = xin_.rearrange("p (s q) -> p s q", s=2)
                    rin = rin.rearrange("p (s q) -> p s q", s=2)
                self.stt(dst, xin_, self.PARAMS[:, gcol + k:gcol + k + 1], rin, ALU.mult, ALU.mult,
                         [self.BX[w], rb, self.BP], [BH[w]])

    def resid(self, w, t0, n, Y3, BY, gcol):
        rs, rb = self.rstd_of(Y3, n, [BY])
        for k in range(8):
            self.stt(Y3[:, k, :], Y3[:, k, :], self.PARAMS[:, gcol + k:gcol + k + 1], rs[:, 0:n], ALU.mult, ALU.mult,
                     [BY, rb, self.BP], [BY])
        self.tt(self.X[:, :, t0:t0 + n], self.X[:, :, t0:t0 + n], Y3, ALU.add, [BY, self.BX[w]], [self.BX[w]])

    def resid_tail(self, w, t0, n, Y3, BY, gcol, psn, pbn):
        rs, rb = self.RS.next()
        self.act(rs[:, 0:n], psn[:, 0:n], AF.Sqrt, [pbn, self.BC], [rb], bias=self.EPS[:, 0:1])
        self.recip(rs[:, 0:n], rs[:, 0:n], [rb], [rb])
        for k in range(8):
            self.stt(Y3[:, k, :], Y3[:, k, :], self.PARAMS[:, gcol + k:gcol + k + 1], rs[:, 0:n], ALU.mult, ALU.mult,
                     [BY, rb, self.BP], [BY])
        self.tt(self.X[:, :, t0:t0 + n], self.X[:, :, t0:t0 + n], Y3, ALU.add, [BY, self.BX[w]], [self.BX[w]])

    def wslot(self, kc):
        sl, sb = self.RW.next()
        v = sl.bitcast(BF16)[:, 0:kc * 128].rearrange("p (k c) -> p k c", k=kc)
        return v, sb

    def load_w(self, dst, src, sb):
        self.dma(dst, src, [], [sb], q="pool")

    def out_linear(self, sg, KC, rhs_fn, wload, bcol, gcol, RB, Y, BY):
        Ys = Y if isinstance(Y, list) else [(Y, BY)]
        for wi_, (w, t0, n, samp) in enumerate(self.windows(sg)):
            Yw, BYw = Ys[wi_ % len(Ys)]
            sq, sqb = self.SQ.next()
            psn = pbn = None
            pend = None
            for oc in range(8):
                wv, wb = self.wslot(KC)
                wload(wv, oc, wb)
                ps, pb = self.PS.next()
                for k in range(KC):
                    self.mm(ps[:, 0:n], wv[:, k, :], rhs_fn(k, t0, n), k == 0, k == KC - 1, [wb, RB[w]], [pb])
                if pend is not None:
                    if psn is None:
                        psn, pbn = self.PS.next()
                    self.mm(psn[:, 0:n], self.ONESM, sq[:, pend, 0:n], pend == 0, False, [sqb, self.BC], [pbn])
                b_ap = self.PARAMS[:, bcol + oc:bcol + oc + 1]
                self.act(Yw[:, oc, 0:n], ps[:, 0:n], AF.Identity, [pb, self.BP], [BYw], bias=b_ap)
                self.act(sq[:, oc, 0:n], ps[:, 0:n], AF.Square, [pb, self.BP], [sqb], bias=b_ap)
                pend = oc
            self.mm(psn[:, 0:n], self.ONESM, sq[:, 7, 0:n], False, True, [sqb, self.BC], [pbn])
            self.resid_tail(w, t0, n, Yw[:, :, 0:n], BYw, gcol, psn, pbn)

    def ffn(self, l, sg):
        self.scratch_reset()
        d = self.d
        HW = 2 + NPR + 132
        HT = self.carve_bf(8 * HW).rearrange("p (k t) -> p k t", k=8)
        BH = self.mkbufs(["fh0", "fh1", "fh2", "fh3", "fhc"])
        BHC = BH[4]
        ACTH = self.carve_bf(11 * 1280).rearrange("p (k t) -> p k t", k=11)
        BA = self.mkbufs(["fa0", "fa1", "fa2", "fa3"])
        Y = self.carve(8 * 1280).rearrange("p (k t) -> p k t", k=8)
        BY = self.mkbufs(["fy0", "fy1", "fy2", "fy3"])
        TG = self.ring("tg", 3, WIN)
        TU = self.ring("tu", 3, WIN)
        UPT = self.carve(NFC * 6).rearrange("p (c t) -> p c t", c=NFC)
        BUP = self.mkbuf("upt")
        OST = self.ring("ost", 2, 512)
        gc2, gc3 = self.pcol["ng"] + (l * 4 + 2) * 8, self.pcol["ng"] + (l * 4 + 3) * 8
        wcol, bcol, scol = self.pcol[("fwd", l)], self.pcol[("fbd", l)], self.pcol[("stf", l)]
        P = self.PARAMS
        if sg == 0:
            self.memset(HT[:, :, 0:2], 0.0, [BHC])
        else:
            self.cp(HT[:, :, 0:2], self.FCARRY[l], [self.BCARRY], [BHC])
            self.memset(HT[:, :, 2 + NPR:HW].rearrange("p k (s q) -> p k s q", s=2)[:, :, :, 0:2], 0.0, [BH[3]])

        def out_fn(k, t0, nn, samp):
            if samp:
                return HT[:, k, 2 + NPR:HW].rearrange("p (s q) -> p s q", s=2)[:, :, 2:66]
            return HT[:, k, 2 + t0:2 + t0 + nn]

        self.norm_h(sg, gc2, out_fn, BH)
        if sg == 0:
            self.cp(self.FCARRY[l], HT[:, :, NPR:NPR + 2], [BH[2]], [self.BCARRY])
        wup = d["ffn_w_up"][l].rearrange("(k p) c -> p k c", p=128)
        wdn = d["ffn_w_down"][l].rearrange("(k p) c -> p k c", p=128)
        def load_pair(i):
            wv, wb = self.wslot(16)
            self.load_w(wv[:, 0:8, :], wup[:, :, i * 128:(i + 1) * 128], wb)
            self.load_w(wv[:, 8:16, :], wup[:, :, DFF + i * 128:DFF + (i + 1) * 128], wb)
            return wv, wb

        def load_dn(half, oc):
            wv, wb = self.wslot(11)
            self.load_w(wv, wdn[:, half * 11:(half + 1) * 11, oc * 128:(oc + 1) * 128], wb)
            return wv, wb
        stream = []
        for half in range(2):
            stream += [("up", half * 11 + ii) for ii in range(11)] + [("dn", half, oc) for oc in range(8)]
        AHEAD = 2
        loaded = []

        def issue(n):
            while len(loaded) < min(n, len(stream)):
                it = stream[len(loaded)]
                loaded.append(load_pair(it[1]) if it[0] == "up" else load_dn(it[1], it[2]))
        pos = [0]

        def take():
            issue(pos[0] + 1 + AHEAD)
            r = loaded[pos[0]]
            pos[0] += 1
            return r
        for half in range(2):
            for ii in range(11):
                i = half * 11 + ii
                wv, wb = take()
                for (w, t0, nn, samp) in self.windows(sg):
                    res = []
                    hr = [BH[w]] if samp else [BH[w], (BHC if w == 0 else BH[w - 1])]
                    for gu in range(2):
                        c = i + 22 * gu
                        ps, pb = self.PS.next()
                        if samp:
                            c0, N = 2 + NPR, 132
                        else:
                            c0, N = t0, nn + 2
                        for k in range(8):
                            self.mm(ps[:, 0:N], wv[:, gu * 8 + k, :], HT[:, k, c0:c0 + N], k == 0, k == 7, [wb] + hr, [pb])
                        if samp:
                            pv3 = ps[:, 0:132].rearrange("p (s q) -> p s q", s=2)
                            stv = P[:, scol + c:scol + c + 4 * NFC].rearrange("p (s c) -> p s c", s=4)[:, :, 0]
                            self.cp(pv3[:, :, 0:2], stv.rearrange("p (s r) -> p s r", s=2), [self.BP, pb], [pb])
                            a2, a1, a0 = pv3[:, :, 2:66], pv3[:, :, 1:65], pv3[:, :, 0:64]
                        else:
                            a2, a1, a0 = ps[:, 2:nn + 2], ps[:, 1:nn + 1], ps[:, 0:nn]
                        tring = TG if gu == 0 else TU
                        tb_, tbb = tring.next()
                        tv = tb_[:, 0:nn]
                        if samp:
                            tv = tv.rearrange("p (s q) -> p s q", s=2)
                        w0 = P[:, wcol + c:wcol + c + 1]
                        w1 = P[:, wcol + NFC + c:wcol + NFC + c + 1]
                        w2 = P[:, wcol + 2 * NFC + c:wcol + 2 * NFC + c + 1]
                        self.act(tv, a2, AF.Identity, [pb, self.BP], [tbb], bias=P[:, bcol + c:bcol + c + 1], scale=w2)
                        self.stt(tv, a1, w1, tv, ALU.mult, ALU.add, [pb, tbb, self.BP], [tbb])
                        self.stt(tv, a0, w0, tv, ALU.mult, ALU.add, [pb, tbb, self.BP], [tbb])
                        if sg == 1 and samp:
                            self.act(UPT[:, c, 2:6].rearrange("p (s r) -> p s r", s=2), pv3[:, :, 64:66], AF.Identity, [pb], [BUP])
                        elif sg == 1 and w == 2:
                            self.act(UPT[:, c, 0:2], ps[:, nn:nn + 2], AF.Identity, [pb], [BUP])
                        res.append((tv, tbb))
                    (tg, tgb), (tu, tub) = res
                    self.act(tg, tg, AF.Gelu_apprx_tanh, [tgb], [tgb])
                    dst = ACTH[:, ii, t0:t0 + nn]
                    if samp:
                        dst = dst.rearrange("p (s q) -> p s q", s=2)
                    self.tt(dst, tg, tu, ALU.mult, [tgb, tub], [BA[w]], eng="pool")
            for oc in range(8):
                wv, wb = take()
                for (w, t0, nn, samp) in self.windows(sg):
                    ps, pb = self.PS.next()
                    for k in range(11):
                        self.mm(ps[:, 0:nn], wv[:, k, :], ACTH[:, k, t0:t0 + nn], k == 0, k == 10, [wb, BA[w]], [pb])
                    if half == 0:
                        self.act(Y[:, oc, t0:t0 + nn], ps[:, 0:nn], AF.Identity, [pb], [BY[w]])
                    else:
                        self.tt(Y[:, oc, t0:t0 + nn], Y[:, oc, t0:t0 + nn], ps[:, 0:nn], ALU.add, [pb, BY[w]], [BY[w]])
        for (w, t0, nn, samp) in self.windows(sg):
            self.resid(w, t0, nn, Y[:, :, t0:t0 + nn], BY[w], gc3)
        if sg == 1:
            for cb in range(11):
                ps, pb = self.PS.next()
                for cc in range(4):
                    c = cb * 4 + cc
                    self.tr(ps[0:6, cc * 128:(cc + 1) * 128], UPT[:, c, :], [BUP, self.BC], [pb])
                oa, ob = OST.next()
                self.act(oa[0:6, :], ps[0:6, :], AF.Identity, [pb], [ob])
                self.finals.append(self.dma(d["pffn"][l][:, cb * 512:(cb + 1) * 512], oa[0:2, :], [ob], []))
                self.finals.append(self.dma(d["sffn"][l].rearrange("s r c -> (s r) c")[:, cb * 512:(cb + 1) * 512], oa[2:6, :], [ob], []))

    def attn(self, l, j, sg):
        self.scratch_reset()
        d = self.d
        P = self.PARAMS
        HW = 128 + 1280
        HT = self.carve_bf(8 * HW).rearrange("p (k t) -> p k t", k=8)
        ht_rng = self.last_rng
        BH = self.mkbufs(["ah0", "ah1", "ah2", "ah3", "ahc"])
        BHC = BH[4]

        def hb(c0, c1):
            out = []
            if c0 < 128:
                out.append(BHC)
            for w in range(4):
                a, b = 128 + w * WIN, 128 + min((w + 1) * WIN, 1280)
                if c0 < b and a < c1 and not (w == 3 and sg == 0):
                    out.append(BH[w])
            return out

        QT = self.carve_bf(8 * 1280).rearrange("p (k t) -> p k t", k=8)
        qt_rng = self.last_rng
        BQ = self.mkbufs(["aq0", "aq1", "aq2", "aq3"])
        kv0 = self.aoff
        KT = self.carve_bf(2 * HW).rearrange("p (k t) -> p k t", k=2)
        BK = self.mkbuf("a_kt")
        a0 = self.aoff
        VA = self.carve_bf(11 * 256).rearrange("p (a c) -> p a c", a=11)
        VB = self.carve_bf(11 * 256).rearrange("p (a c) -> p a c", a=11)
        BV = self.mkbuf("a_v", (a0, self.aoff))
        WKV = self.carve_bf(8 * 512).rearrange("p (k c) -> p k c", k=8)
        wkv_rng = self.last_rng
        BWKV = self.mkbuf("a_wkv")
        BKVB = self.carve(512)
        BBKV = self.mkbuf("a_bkv")
        KVO = self.ring("kvo", 1, 512)
        CK = self.ring("ckr", 2, 256)
        a0 = self.aoff
        KTC = self.carve_bf(2 * 2 * 128).rearrange("p (s k t) -> p s k t", s=2, k=2)
        VC = self.carve_bf(2 * 256).rearrange("p (s c) -> p s c", s=2)
        BCACHE = self.mkbuf("a_cache", (a0, self.aoff))
        a0 = self.aoff
        EBT = [self.carve(1024).rearrange("p (h j q) -> p h j q", h=2, j=8) for _ in range(2)]
        ESKR = self.carve_bf(1024).rearrange("p (h j q) -> p h j q", h=2, j=8)
        BEB = self.mkbuf("a_eb", (a0, self.aoff))
        OHT = self.ring("oht", 2, 4 * 96)
        a0 = self.aoff
        TAB = self.carve(16)
        SNK = self.carve(16)
        TABH = self.carve_bf(16)
        TABT = self.carve(16)
        BTAB = self.mkbuf("a_tab", (a0, self.aoff))
        e0 = self.aoff
        ET = self.ring("et", 3, 512)
        PT = self.ring("pt", 4, 512, bf16=True)
        DEN = self.ring("den", 2, 256)
        e1 = self.aoff
        assert e1 - e0 == 8 * WIN
        Y = self.ARENA[:, e0:e1].rearrange("p (k t) -> p k t", k=8)
        WOA = self.carve_bf(6 * 8 * 128).rearrange("p (o k c) -> p o k c", o=6, k=8)
        BWOA = self.mkbuf("a_woa")
        gc0, gc1 = self.pcol["ng"] + (l * 4 + 0) * 8, self.pcol["ng"] + (l * 4 + 1) * 8

        wo_e = d["attn_w_o"][j]
        for oc in range(6):
            for hf in range(2):
                for grp in range(2):
                    h0 = (0, 8)[grp] if hf == 0 else (4, 12)[grp]
                    src = wo_e[h0 * 64:(h0 + 4) * 64, oc * 128:(oc + 1) * 128].rearrange("(j p) c -> p j c", p=64)
                    self.dma(WOA[hf * 64:(hf + 1) * 64, oc, grp * 4:(grp + 1) * 4, :], src, [], [BWOA], q="pool")
        if sg == 0:
            self.memset(HT[:, :, 0:128], 0.0, [BHC])
        else:
            self.cp(HT[:, :, 0:128], self.ACARRY[j], [self.BCARRY], [BHC])
        self.norm_h(sg, gc0, lambda k, t0, nn, samp: HT[:, k, 128 + t0:128 + t0 + nn], BH)
        if sg == 0:
            self.cp(self.ACARRY[j], HT[:, :, NPR:NPR + 128], hb(NPR, NPR + 128), [self.BCARRY])

        self.dma(TAB[0:32, :], d["rel_bias_table"][:, :], [], [BTAB])
        self.dma(TAB[32:64, :], d["rel_bias_table"][:, :], [], [BTAB])
        self.dma(SNK, d["attn_sinks"][j].partition_broadcast(128), [], [BTAB])
        self.cp(TABH[0:64, :], TAB[0:64, :], [BTAB], [BTAB])
        self.tt(TABT[32:64, :], TAB[32:64, :], TABH[32:64, :], ALU.subtract, [BTAB], [BTAB])
        self.cp(TABH[32:64, :], TABT[32:64, :], [BTAB], [BTAB])
        for q4 in range(16):
            oa_, ob = OHT.next()
            oa = oa_.bitcast(BF16).rearrange("p (q j) -> p q j", q=4)
            self.dma(oa[0:64, :, :], d["oh"][:, q4 * 4:(q4 + 1) * 4, :], [], [ob])
            if q4 % 4 == 0:
                psf, pbf = self.PS.next()
                pso, pbo = self.PS.next()
            for qi in range(4):
                qq = (q4 % 4) * 4 + qi
                self.mm(psf[:, qq * 16:(qq + 1) * 16], oa[0:64, qi, 0:128], TABH[0:64, :], True, True, [ob, BTAB], [pbf])
                self.mm(pso[0:64, qq * 16:(qq + 1) * 16], oa[0:64, qi, 128:192], TABH[0:64, :], True, True, [ob, BTAB], [pbo])
            if q4 % 4 == 3:
                q0 = (q4 // 4) * 16
                for (src, pbx, dst, npart) in ((psf, pbf, EBT[0], 128), (pso, pbo, EBT[1], 64)):
                    sv_ = src[0:npart, 0:256].rearrange("p (q h) -> p h q", h=16)
                    for hf in range(2):
                        for grp in range(2):
                            h0 = (0, 8)[grp] if hf == 0 else (4, 12)[grp]
                            self.act(dst[0:npart, hf, grp * 4:(grp + 1) * 4, q0:q0 + 16], sv_[:, h0:h0 + 4, :], AF.Exp, [pbx], [BEB])
        for hf in range(2):
            for grp in range(2):
                h0 = (0, 8)[grp] if hf == 0 else (4, 12)[grp]
                self.act(ESKR[64:65, hf, grp * 4:(grp + 1) * 4, :], SNK[64:65, h0:h0 + 4].unsqueeze(2).to_broadcast([1, 4, 64]), AF.Exp, [BTAB], [BEB])

        wq = d["attn_w_qkv"][j].rearrange("(k p) c -> p k c", p=128)
        bqc, bkc, boc = self.pcol[("bq", j)], self.pcol[("bk", j)], self.pcol[("bo", j)]
        for jq in range(8):
            wv, wb = self.wslot(8)
            lo, up = q_lo(jq), q_up(jq)
            self.load_w(wv[:, :, 0:64], wq[:, :, lo * 64:(lo + 1) * 64], wb)
            self.load_w(wv[:, :, 64:128], wq[:, :, up * 64:(up + 1) * 64], wb)
            for (w, t0, nn, samp) in self.windows(sg):
                ps, pb = self.PS.next()
                for k in range(8):
                    self.mm(ps[:, 0:nn], wv[:, k, :], HT[:, k, 128 + t0:128 + t0 + nn], k == 0, k == 7, [wb, BH[w]], [pb])
                self.act(QT[:, jq, t0:t0 + nn], ps[:, 0:nn], AF.Identity, [pb, self.BP], [BQ[w]], bias=P[:, bqc + jq:bqc + jq + 1])
        for jk in range(2):
            wv, wb = self.wslot(8)
            self.load_w(wv, wq[:, :, 1024 + jk * 128:1024 + (jk + 1) * 128], wb)
            for (c0, nn) in [(0, 128)] + [(128 + t0, nn) for (w, t0, nn, samp) in self.windows(sg)]:
                ps, pb = self.PS.next()
                for k in range(8):
                    self.mm(ps[:, 0:nn], wv[:, k, :], HT[:, k, c0:c0 + nn], k == 0, k == 7, [wb] + hb(c0, c0 + nn), [pb])
                self.act(KT[:, jk, c0:c0 + nn], ps[:, 0:nn], AF.Identity, [pb, self.BP], [BK], bias=P[:, bkc + jk:bkc + jk + 1])
        self.dma(WKV, wq[:, :, 1024:1536], [], [BWKV], q="pool")
        self.dma(BKVB, d["attn_b_qkv"][j, 1024:1536].partition_broadcast(128), [], [BBKV])
        na = 10 + (1 if sg == 1 else 0)
        tiles = [("A", a, a * 128, 128) for a in range(na)] + [("B", b, 64 + b * 128, 128) for b in range(10)]
        if sg == 1:
            tiles.append(("B", 10, 64 + 10 * 128, 64))
        for (kind, idx, c0, m) in tiles:
            ps, pb = self.PS.next()
            hr = hb(c0, min(c0 + m, 128 + self.nsg(sg)))
            for k in range(8):
                self.mm(ps[0:m, :], HT[:, k, c0:c0 + m], WKV[:, k, :], k == 0, k == 7, hr + [BWKV], [pb])
            dst = (VA if kind == "A" else VB)[0:m, idx, :]
            self.tt(dst, ps[0:m, 256:512], BKVB[0:m, 256:512], ALU.add, [pb, BBKV], [BV])
            if sg == 1 and kind == "A" and idx in (9, 10):
                oa, ob = KVO.next()
                self.tt(oa, ps, BKVB, ALU.add, [pb, BBKV], [ob])
                dk, dv = (d["pk"], d["pv"]) if idx == 9 else (d["sk"], d["sv"])
                self.finals.append(self.dma(dk[j], oa[:, 0:256], [ob], []))
                self.finals.append(self.dma(dv[j], oa[:, 256:512], [ob], []))
        if sg == 1:
            for s in range(2):
                ca, cb_ = CK.next()
                self.dma(ca, d["ck"][j, s], [], [cb_])
                ps, pb = self.PS.next()
                for jk in range(2):
                    self.tr(ps[:, jk * 128:(jk + 1) * 128], ca[:, jk * 128:(jk + 1) * 128], [cb_, self.BC], [pb])
                self.act(KTC[:, s, :, :], ps[:, 0:256].rearrange("p (k t) -> p k t", k=2), AF.Identity, [pb], [BCACHE])
                self.dma(VC[:, s, :], d["cv"][j, s], [], [BCACHE], q="pool")

        WOB = WKV.rearrange("p k c -> p (k c)")[:, 0:2 * 8 * 128].rearrange("p (o k c) -> p o k c", o=2, k=8)
        BWOB = self.mkbuf("a_wob", wkv_rng)
        wo_ = d["attn_w_o"][j]

        def wo_ap(oc):
            return (WOA[:, oc], BWOA) if oc < 6 else (WOB[:, oc - 6], BWOB)

        OT = HT
        BO = self.mkbufs(["ao0", "ao1", "ao2", "ao3"], ht_rng)
        items = [("p", c) for c in range(18)] + ([("s", 0), ("s", 1)] if sg == 1 else [])
        work = []
        for (typ, c) in items:
            if typ == "p":
                qc0 = c * 64
                w = c // 6
                gc = c + 18 * sg
                pieces = []
                if gc >= 2:
                    vfull = VA[:, c // 2, :] if c % 2 == 0 else VB[:, (c - 1) // 2, :]
                    pieces.append((c * 64, 128, vfull, 0, BV))
                    vown = VA[0:64, 1 + c // 2, :] if c % 2 == 0 else VB[0:64, (c + 1) // 2, :]
                    pieces.append((128 + c * 64, 64, vown, 1, BV))
                elif gc == 1:
                    pieces.append((64, 128, VB[:, 0, :], 0, BV, True))
                    pieces.append((128 + c * 64, 64, VB[0:64, 1, :], 1, BV))
                else:
                    pieces.append((128, 64, VA[0:64, 1, :], 1, BV))
            else:
                qc0 = NPR + c * 64
                w = 3
                vown = VA[0:64, 10, :] if c == 0 else VB[0:64, 10, :]
                pieces = [(None, 128, VC[:, c, :], 0, BCACHE), (128 + NPR + c * 64, 64, vown, 1, BV)]
            for kv in range(4):
                work.append((c, qc0, w, pieces, kv))

        def stage_a(it):
            (c, qc0, w, pieces, kv) = it
            hf, jk, j0 = kv % 2, kv // 2, (kv // 2) * 4
            rows = slice(hf * 64, hf * 64 + 64)
            rhs_q = QT[rows, j0:j0 + 4, qc0:qc0 + 64]
            pss, pbs = self.PS.next()
            et, eb = ET.next()
            pt, ptb = PT.next()
            offs = []
            off = 0
            for pc in pieces:
                (kc0, nk, vap, bt, vbuf) = pc[0:5]
                if kc0 is None:
                    lk, rk = KTC[rows, c, jk, :], [BCACHE]
                else:
                    lk, rk = KT[rows, jk, kc0:kc0 + nk], [BK]
                self.mm(pss[0:nk, off:off + 256], lk, rhs_q, True, True, rk + [BQ[w]], [pbs])
                offs.append(off)
                off += 256
            for pi, pc in enumerate(pieces):
                (kc0, nk, vap, bt, vbuf) = pc[0:5]
                o_ = offs[pi]
                self.act(et[0:nk, o_:o_ + 256], pss[0:nk, o_:o_ + 256], AF.Exp, [pbs], [eb], scale=0.125)
                self.tt(pt[0:nk, o_:o_ + 256].rearrange("p (j q) -> p j q", j=4),
                        et[0:nk, o_:o_ + 256].rearrange("p (j q) -> p j q", j=4),
                        EBT[bt][0:nk, hf, j0:j0 + 4, :], ALU.mult, [eb, BEB], [ptb], eng="pool")
                if len(pc) > 5:
                    self.memset(pt[0:64, o_:o_ + 256], 0.0, [ptb], eng="pool")
            return (pt, ptb, offs)

        def stage_b(it, st):
            (c, qc0, w, pieces, kv) = it
            (pt, ptb, offs) = st
            hf, jk, j0 = kv % 2, kv // 2, (kv // 2) * 4
            rows = slice(hf * 64, hf * 64 + 64)
            pso, pbo = self.PS.next()
            np_ = len(pieces)
            for pi, pc in enumerate(pieces):
                (kc0, nk, vap, bt, vbuf) = pc[0:5]
                o_ = offs[pi]
                self.mm(pso[:, 0:256], vap[:, jk * 128:(jk + 1) * 128], pt[0:nk, o_:o_ + 256], pi == 0, pi == np_ - 1, [vbuf, ptb], [pbo])
            for pi, pc in enumerate(pieces):
                (kc0, nk, vap, bt, vbuf) = pc[0:5]
                o_ = offs[pi]
                nks = nk + 1 if nk == 64 else nk
                self.mm(pso[:, 256:512], self.ONES1[0:nks, :], pt[0:nks, o_:o_ + 256], pi == 0, pi == np_ - 1, [self.BC, ptb], [pbo])
            dn, dnb = DEN.next()
            self.recip(dn[rows, :], pso[rows, 256:512], [pbo], [dnb])
            self.tt(OT[rows, j0:j0 + 4, qc0:qc0 + 64], pso[rows, 0:256].rearrange("p (j q) -> p j q", j=4),
                    dn[rows, :].rearrange("p (j q) -> p j q", j=4), ALU.mult, [pbo, dnb], [BO[w]])

        assert len(PT.aps) == 4 and PT.i == 0
        for s4 in range(4):
            hf4, j04 = s4 % 2, (s4 // 2) * 4
            for o4 in (0, 256):
                self.cp(PT.aps[s4][64:65, o4:o4 + 256].rearrange("p (j q) -> p j q", j=4), ESKR[64:65, hf4, j04:j04 + 4, :],
                        [BEB], [PT.bufs[s4]], eng="pool")
        DEPTH = 3
        sts = {}
        for i in range(min(DEPTH, len(work))):
            sts[i] = stage_a(work[i])
        for i in range(len(work)):
            stage_b(work[i], sts.pop(i))
            if i + DEPTH < len(work):
                sts[i + DEPTH] = stage_a(work[i + DEPTH])

        for oc in range(6, 8):
            dst, dbuf = wo_ap(oc)
            for hf in range(2):
                for grp in range(2):
                    h0 = (0, 8)[grp] if hf == 0 else (4, 12)[grp]
                    src = wo_[h0 * 64:(h0 + 4) * 64, oc * 128:(oc + 1) * 128].rearrange("(j p) c -> p j c", p=64)
                    self.dma(dst[hf * 64:(hf + 1) * 64, grp * 4:(grp + 1) * 4, :], src, [], [dbuf], q="pool")
        BY = self.mkbuf("a_y", (e0, e1))
        Y2 = self.ARENA[:, kv0:kv0 + 8 * WIN].rearrange("p (k t) -> p k t", k=8)
        BY2 = self.mkbuf("a_y2", (kv0, kv0 + 8 * WIN))
        Ys = [(Y, BY), (Y2, BY2)]
        for wi_, (w, t0, n, samp) in enumerate(self.windows(sg)):
            Yw, BYw = Ys[wi_ % 2]
            sq, sqb = self.SQ.next()
            psn = pbn = None
            pend = None
            for oc in range(8):
                wo_t, wo_b = wo_ap(oc)
                ps, pb = self.PS.next()
                for k in range(8):
                    self.mm(ps[:, 0:n], wo_t[:, k, :], OT[:, k, t0:t0 + n], k == 0, k == 7, [wo_b, BO[w]], [pb])
                if pend is not None:
                    if psn is None:
                        psn, pbn = self.PS.next()
                    self.mm(psn[:, 0:n], self.ONESM, sq[:, pend, 0:n], pend == 0, False, [sqb, self.BC], [pbn])
                bo_ap = P[:, boc + oc:boc + oc + 1]
                self.act(Yw[:, oc, 0:n], ps[:, 0:n], AF.Identity, [pb, self.BP], [BYw], bias=bo_ap)
                self.act(sq[:, oc, 0:n], ps[:, 0:n], AF.Square, [pb, self.BP], [sqb], bias=bo_ap)
                pend = oc
            self.mm(psn[:, 0:n], self.ONESM, sq[:, 7, 0:n], False, True, [sqb, self.BC], [pbn])
            self.resid_tail(w, t0, n, Yw[:, :, 0:n], BYw, gc1, psn, pbn)

    def convmod(self, l, sg):
        self.scratch_reset()
        d = self.d
        P = self.PARAMS
        HT = self.carve_bf(8 * 1280).rearrange("p (k t) -> p k t", k=8)
        ht_rng = self.last_rng
        BH = self.mkbufs(["ch0", "ch1", "ch2", "ch3"])
        GW = 30 + NPR + 2 * 94
        GLU = self.carve_bf(8 * GW).rearrange("p (k t) -> p k t", k=8)
        glu_rng = self.last_rng
        BG = self.mkbufs(["cg0", "cg1", "cg2", "cg3", "cgc"])
        BGC = BG[4]
        ZB = self.carve_bf(8 * 1280).rearrange("p (k t) -> p k t", k=8)
        BZ = self.mkbufs(["cz0", "cz1", "cz2", "cz3"])
        a0 = self.aoff
        S1 = self.carve(1280)
        S2 = self.carve(1280)
        BS = self.mkbufs(["cs0", "cs1", "cs2", "cs3"], (a0, self.aoff))
        DWR = self.ring("dw", 2, 31 * 128, bf16=True)
        ZSQ = self.ring("zsq", 3, WIN, bf16=True)
        SIG = self.ring("sig", 2, WIN)
        TMP = self.ring("ctmp", 2, WIN)
        GT = self.carve(8 * 96).rearrange("p (k t) -> p k t", k=8)
        BGT = self.mkbuf("c_gt")
        GOST = self.ring("gost", 1, 1024)
        Y = self.carve(8 * WIN).rearrange("p (k t) -> p k t", k=8)
        BY = self.mkbuf("c_y")
        gc0, gc1 = self.pcol["ng"] + (l * 4 + 0) * 8, self.pcol["ng"] + (l * 4 + 1) * 8
        cb1, cwd, cbd, clg, clb, cb2, stc = (self.pcol[k] for k in ("cb1", "cwd", "cbd", "clg", "clb", "cb2", "stc"))
        self.norm_h(sg, gc0, lambda k, t0, nn, samp: HT[:, k, t0:t0 + nn], BH)
        if sg == 0:
            self.memset(GLU[:, :, 0:30], 0.0, [BGC])
        else:
            self.cp(GLU[:, :, 0:30], self.CCARRY, [self.BCARRY], [BGC])
            for s in range(2):
                src = P[:, stc + s * 240:stc + (s + 1) * 240].rearrange("p (r k) -> p k r", k=8)
                self.cp(GLU[:, :, 30 + NPR + s * 94:30 + NPR + s * 94 + 30], src, [self.BP], [BG[3]])
        w1 = d["conv_w_pw1"][0].rearrange("(k p) c -> p k c", p=128)
        for jc in range(8):
            wv, wb = self.wslot(16)
            self.load_w(wv[:, 0:8, :], w1[:, :, jc * 128:(jc + 1) * 128], wb)
            self.load_w(wv[:, 8:16, :], w1[:, :, 1024 + jc * 128:1024 + (jc + 1) * 128], wb)
            for (w, t0, nn, samp) in self.windows(sg):
                psa, pba = self.PS.next()
                psg, pbg = self.PS.next()
                for k in range(8):
                    self.mm(psa[:, 0:nn], wv[:, k, :], HT[:, k, t0:t0 + nn], k == 0, k == 7, [wb, BH[w]], [pba])
                for k in range(8):
                    self.mm(psg[:, 0:nn], wv[:, 8 + k, :], HT[:, k, t0:t0 + nn], k == 0, k == 7, [wb, BH[w]], [pbg])
                sg_, sgb = SIG.next()
                self.act(sg_[:, 0:nn], psg[:, 0:nn], AF.Sigmoid, [pbg, self.BP], [sgb], bias=P[:, cb1 + 8 + jc:cb1 + 9 + jc])
                ba = P[:, cb1 + jc:cb1 + jc + 1]
                if samp:
                    dst = GLU[:, jc, 30 + NPR:GW].rearrange("p (s q) -> p s q", s=2)[:, :, 30:94]
                    a3 = psa[:, 0:128].rearrange("p (s q) -> p s q", s=2)
                    s3 = sg_[:, 0:128].rearrange("p (s q) -> p s q", s=2)
                    self.stt(dst, a3, ba, s3, ALU.add, ALU.mult, [pba, sgb, self.BP], [BG[3]])
                    self.stt(GT[:, jc, 32:96].rearrange("p (s q) -> p s q", s=2)[:, :, 0:30], a3[:, :, 34:64], ba, s3[:, :, 34:64],
                             ALU.add, ALU.mult, [pba, sgb, self.BP], [BGT])
                else:
                    self.stt(GLU[:, jc, 30 + t0:30 + t0 + nn], psa[:, 0:nn], ba, sg_[:, 0:nn], ALU.add, ALU.mult, [pba, sgb, self.BP], [BG[w]])
                    if sg == 1 and w == 2:
                        self.stt(GT[:, jc, 0:30], psa[:, nn - 30:nn], ba, sg_[:, nn - 30:nn], ALU.add, ALU.mult, [pba, sgb, self.BP], [BGT])
        if sg == 0:
            self.cp(self.CCARRY, GLU[:, :, NPR:NPR + 30], [BG[2]], [self.BCARRY])
        else:
            for seg in range(3):
                c0 = 0 if seg == 0 else 32 * seg
                oa, ob = GOST.next()
                for h in range(2):
                    ps, pb = self.PS.next()
                    for kk in range(4):
                        self.tr(ps[0:30, kk * 128:(kk + 1) * 128], GT[:, h * 4 + kk, c0:c0 + 30], [BGT, self.BC], [pb])
                    self.act(oa[0:30, h * 512:(h + 1) * 512], ps[0:30, :], AF.Identity, [pb], [ob])
                dst = d["pconv"] if seg == 0 else d["sconv"][seg - 1]
                self.finals.append(self.dma(dst[:, :], oa[0:30, :], [ob], []))
        C = HT
        BCt = self.mkbufs(["cc0", "cc1", "cc2", "cc3"], ht_rng)
        pend_stats = None

        def emit_stats(jc, w, t0, nn, zq, zqb):
            ps1, pb1 = self.PS.next()
            self.mm(ps1[:, 0:nn], self.ONESM, ZB[:, jc, t0:t0 + nn], True, True, [self.BC, BZ[w]], [pb1])
            ps2, pb2 = self.PS.next()
            self.mm(ps2[:, 0:nn], self.ONESM, zq[:, 0:nn], True, True, [self.BC, zqb], [pb2])
            if jc == 0:
                self.cp(S1[:, t0:t0 + nn], ps1[:, 0:nn], [pb1], [BS[w]])
                self.cp(S2[:, t0:t0 + nn], ps2[:, 0:nn], [pb2], [BS[w]])
            else:
                self.tt(S1[:, t0:t0 + nn], S1[:, t0:t0 + nn], ps1[:, 0:nn], ALU.add, [pb1, BS[w]], [BS[w]])
                self.tt(S2[:, t0:t0 + nn], S2[:, t0:t0 + nn], ps2[:, 0:nn], ALU.add, [pb2, BS[w]], [BS[w]])
        for jc in range(8):
            dw_, dwb = DWR.next()
            dw = dw_.rearrange("p (k c) -> p k c", k=31)
            wk = P[:, cwd + jc:cwd + jc + 31 * 8].rearrange("p (k j) -> p k j", j=8)[:, :, 0:1]
            self.tt(dw, self.IDENT.unsqueeze(1).to_broadcast([128, 31, 128]), wk.to_broadcast([128, 31, 128]), ALU.mult,
                    [self.BC, self.BP], [dwb])
            for (w, t0, nn, samp) in self.windows(sg):
                ps, pb = self.PS.next()
                gr = [BG[3]] if samp else [BG[w], (BGC if w == 0 else BG[w - 1])]
                for k in range(31):
                    if samp:
                        rhs = GLU[:, jc, 30 + NPR:GW].rearrange("p (s q) -> p s q", s=2)[:, :, k:k + 64]
                    else:
                        rhs = GLU[:, jc, t0 + k:t0 + k + nn]
                    self.mm(ps[:, 0:nn], dw[:, k, :], rhs, k == 0, k == 30, [dwb] + gr, [pb])
                bd = P[:, cbd + jc:cbd + jc + 1]
                self.act(ZB[:, jc, t0:t0 + nn], ps[:, 0:nn], AF.Identity, [pb, self.BP], [BZ[w]], bias=bd)
                zq, zqb = ZSQ.next()
                self.act(zq[:, 0:nn], ps[:, 0:nn], AF.Square, [pb, self.BP], [zqb], bias=bd)
                if pend_stats is not None:
                    emit_stats(*pend_stats)
                pend_stats = (jc, w, t0, nn, zq, zqb)
        emit_stats(*pend_stats)
        for (w, t0, nn, samp) in self.windows(sg):
            tm, tmb = TMP.next()
            self.tt(tm[:, 0:nn], S1[:, t0:t0 + nn], S1[:, t0:t0 + nn], ALU.mult, [BS[w]], [tmb])
            self.tt(S2[:, t0:t0 + nn], S2[:, t0:t0 + nn], tm[:, 0:nn], ALU.subtract, [BS[w], tmb], [BS[w]])
            self.act(S2[:, t0:t0 + nn], S2[:, t0:t0 + nn], AF.Sqrt, [BS[w], self.BC], [BS[w]], bias=self.EPS[:, 1:2])
            self.recip(S2[:, t0:t0 + nn], S2[:, t0:t0 + nn], [BS[w]], [BS[w]])
            for jc in range(8):
                tm, tmb = TMP.next()
                self.tt(tm[:, 0:nn], ZB[:, jc, t0:t0 + nn], S1[:, t0:t0 + nn], ALU.subtract, [BZ[w], BS[w]], [tmb])
                self.tt(tm[:, 0:nn], tm[:, 0:nn], S2[:, t0:t0 + nn], ALU.mult, [tmb, BS[w]], [tmb])
                self.act(C[:, jc, t0:t0 + nn], tm[:, 0:nn], AF.Silu, [tmb, self.BP], [BCt[w]],
                         bias=P[:, clb + jc:clb + jc + 1], scale=P[:, clg + jc:clg + jc + 1])
        w2 = d["conv_w_pw2"][0].rearrange("(k p) c -> p k c", p=128)
        assert glu_rng[1] - glu_rng[0] >= 8 * WIN
        Y2 = self.ARENA[:, glu_rng[0]:glu_rng[0] + 8 * WIN].rearrange("p (k t) -> p k t", k=8)
        BY2 = self.mkbuf("c_y2", (glu_rng[0], glu_rng[0] + 8 * WIN))
        self.out_linear(sg, 8, lambda k, t0, nn: C[:, k, t0:t0 + nn],
                        lambda dst, oc, wb: self.load_w(dst, w2[:, :, oc * 128:(oc + 1) * 128], wb),
                        cb2, gc1, BCt, [(Y, BY), (Y2, BY2)], None)

    def cmlp(self, l, sg):
        self.scratch_reset()
        d = self.d
        P = self.PARAMS
        WV = self.carve_bf(8 * 2048).rearrange("p (k c) -> p k c", k=8)
        BWV = self.mkbuf("m_wv")
        a0 = self.aoff
        LNG = self.carve(2048)
        LNB = self.carve(2048)
        BVB = self.carve(2048)
        BLN = self.mkbuf("m_ln", (a0, self.aoff))
        HT = self.carve_bf(8 * WIN).rearrange("p (k t) -> p k t", k=8)
        BH = self.mkbuf("m_ht")
        U = self.carve_bf(16 * WIN).rearrange("p (k t) -> p k t", k=16)
        BU = self.mkbuf("m_u")
        VT = self.carve_bf(3 * 2048).rearrange("p (a c) -> p a c", a=3)
        BVT = self.mkbufs(["m_vt0", "m_vt1", "m_vt2"])
        VR = self.carve(2048)
        BVR = self.mkbuf("m_vr")
        a0 = self.aoff
        STAT = self.carve(4 * 6).rearrange("p (a b) -> p a b", a=4)
        MV = self.carve(4)
        BST = self.mkbuf("m_st", (a0, self.aoff))
        WSIN = self.carve(4 * 128).rearrange("p (g j) -> p g j", g=4)
        wsin_rng = self.last_rng
        BWSI = self.mkbuf("m_wsi")
        WST = self.carve_bf(2 * 4 * 128).rearrange("p (v g i) -> p v g i", v=2, g=4)
        BWS = self.mkbuf("m_ws")
        a0 = self.aoff
        BSR = self.carve(4 * 128).rearrange("p (g i) -> p g i", g=4)
        BSH = self.carve_bf(2 * 4 * 128).rearrange("p (v g i) -> p v g i", v=2, g=4)
        BBS = self.mkbuf("m_bs", (a0, self.aoff))
        Y = self.carve(8 * WIN).rearrange("p (k t) -> p k t", k=8)
        BY = self.mkbuf("m_y")
        gc0, gc1 = self.pcol["ng"] + (l * 4 + 0) * 8, self.pcol["ng"] + (l * 4 + 1) * 8
        mbu, mbo = self.pcol["mbu"], self.pcol["mbo"]
        wi = d["cmlp_w_in"][0].rearrange("(k p) c -> p k c", p=128)
        wo_ = d["cmlp_w_out"][0].rearrange("(k p) c -> p k c", p=128)
        for cb in range(4):
            self.dma(WV[:, :, cb * 512:(cb + 1) * 512], wi[:, :, 2048 + cb * 512:2048 + (cb + 1) * 512], [], [BWV], q="pool")
        self.dma(LNG, d["cmlp_ln_g"][0].partition_broadcast(128), [], [BLN])
        self.dma(LNB, d["cmlp_ln_b"][0].partition_broadcast(128), [], [BLN])
        self.dma(BVB, d["cmlp_b_in"][0, 2048:4096].partition_broadcast(128), [], [BLN])
        ws = d["cmlp_w_s"][0]
        bs = d["cmlp_b_s"][0]
        for v in range(2 if sg == 1 else 1):
            if v == 0:
                self.dma(WSIN, ws.rearrange("g i j -> i g j"), [BWSI], [BWSI])
            else:
                self.memset(WSIN, 0.0, [BWSI])
                self.dma(WSIN[0:64, :, 0:64], ws[:, 0:64, 0:64].rearrange("g i j -> i g j"), [BWSI], [BWSI])
                self.dma(WSIN[64:128, :, 64:128], ws[:, 0:64, 0:64].rearrange("g i j -> i g j"), [BWSI], [BWSI])
            ps, pb = self.PS.next()
            for g in range(4):
                self.tr(ps[:, g * 128:(g + 1) * 128], WSIN[:, g, :], [BWSI, self.BC], [pb])
            self.act(WST[:, v, :, :], ps.rearrange("p (g i) -> p g i", g=4), AF.Identity, [pb], [BWS])
            if v == 0:
                self.memset(WST[64:128, 0, :, 0:64], 0.0, [BWS])

        BSL = self.ARENA[:, wsin_rng[0]:wsin_rng[1]].bitcast(BF16).rearrange("p (v g i) -> p v g i", v=2, g=4)
        BBL = self.mkbuf("m_bsl", wsin_rng)
        for v in range(2 if sg == 1 else 1):
            if v == 0:
                self.dma(BSR[0:1, :, :], bs.rearrange("g i -> (g i)").rearrange("(o g i) -> o g i", o=1, g=4), [BBS], [BBS])
            else:
                self.dma(BSR[0:1, :, 0:64], bs[:, 0:64].unsqueeze(0), [BBS], [BBS])
                self.dma(BSR[0:1, :, 64:128], bs[:, 0:64].unsqueeze(0), [BBS], [BBS])
            self.cp(BSH[0:1, v], BSR[0:1], [BBS], [BBS])
            self.tt(BSR[0:1], BSR[0:1], BSH[0:1, v], ALU.subtract, [BBS], [BBS])
            self.cp(BSL[0:1, v], BSR[0:1], [BBS], [BBL])
        def make_ht(wn):
            (w_, t0_, nn_, samp_) = wn
            rs, rb = self.rstd_of(self.X[:, :, t0_:t0_ + nn_], nn_, [self.BX[w_]])
            for k in range(8):
                self.stt(HT[:, k, 0:nn_], self.X[:, k, t0_:t0_ + nn_], P[:, gc0 + k:gc0 + k + 1], rs[:, 0:nn_], ALU.mult, ALU.mult,
                         [self.BX[w_], rb, self.BP], [BH])
        wlist = self.windows(sg)
        make_ht(wlist[0])
        for widx, (w, t0, nn, samp) in enumerate(wlist):
            var = 1 if samp else 0
            def u_chunk(jc):
                wv, wb = self.wslot(8)
                self.load_w(wv, wi[:, :, jc * 128:(jc + 1) * 128], wb)
                ps, pb = self.PS.next()
                for k in range(8):
                    self.mm(ps[:, 0:nn], wv[:, k, :], HT[:, k, 0:nn], k == 0, k == 7, [wb, BH], [pb])
                self.act(U[:, jc, 0:nn], ps[:, 0:nn], AF.Gelu_apprx_tanh, [pb, self.BP], [BU], bias=P[:, mbu + jc:mbu + jc + 1])

            def v_tile(t):
                for cb in range(4):
                    ps, pb = self.PS.next()
                    for k in range(8):
                        self.mm(ps, HT[:, k, t * 128:(t + 1) * 128], WV[:, k, cb * 512:(cb + 1) * 512], k == 0, k == 7, [BH, BWV], [pb])
                    vs = VR[:, cb * 512:(cb + 1) * 512]
                    self.tt(vs, ps, BVB[:, cb * 512:(cb + 1) * 512], ALU.add, [pb, BLN], [BVR])
                    self.act(vs, vs, AF.Gelu_apprx_tanh, [BVR], [BVR])
                    self.S.op("dve", (lambda o, i: lambda e: e.bn_stats(out=o, in_=i))(STAT[:, cb, :], vs), [BVR], [BST])
                self.S.op("dve", lambda e: e.bn_aggr(out=MV[:, 0:2], in_=STAT), [BST], [BST])
                self.act(MV[:, 2:3], MV[:, 1:2], AF.Sqrt, [BST, self.BC], [BST], bias=self.EPS[:, 1:2])
                self.recip(MV[:, 2:3], MV[:, 2:3], [BST], [BST])
                self.stt(MV[:, 3:4], MV[:, 0:1], -1.0, MV[:, 2:3], ALU.mult, ALU.mult, [BST], [BST])
                self.act(VR, VR, AF.Identity, [BVR, BST], [BVR], bias=MV[:, 3:4], scale=MV[:, 2:3])
                self.tt(VR, VR, LNG, ALU.mult, [BVR, BLN], [BVR])
                if samp:
                    self.tt(VR, VR, LNB, ALU.add, [BVR, BLN], [BVR])
                    self.cp(VT[:, t, :], VR, [BVR], [BVT[t]])
                    self.finals.append(self.dma(d["scv"][:, :], VR, [BVR], []))
                else:
                    self.tt(VT[:, t, :], VR, LNB, ALU.add, [BVR, BLN], [BVT[t]])

            def spatial(t):
                for g in range(4):
                    ps, pb = self.PS.next()
                    for cc in range(4):
                        ch = g * 4 + cc
                        self.mm(ps[:, cc * 128:(cc + 1) * 128], VT[:, t, ch * 128:(ch + 1) * 128], WST[:, var, g, :], True, False, [BVT[t], BWS], [pb])
                        self.mm(ps[:, cc * 128:(cc + 1) * 128], self.ONES1[0:1, :], BSH[0:1, var, g, :], False, False, [self.BC, BBS], [pb])
                        self.mm(ps[:, cc * 128:(cc + 1) * 128], self.ONES1[0:1, :], BSL[0:1, var, g, :], False, True, [self.BC, BBL], [pb])
                    uu = U[:, g * 4:(g + 1) * 4, t * 128:(t + 1) * 128]
                    self.tt(uu, ps.rearrange("p (c i) -> p c i", c=4), uu, ALU.mult, [pb, BU], [BU])

            nt = nn // 128
            sched_u = {0: range(0, 6), 1: range(6, 11), 2: range(11, 16)} if nt == 3 else {0: range(0, 16)}
            for t in range(nt):
                v_tile(t)
                for jc in sched_u[t]:
                    u_chunk(jc)
            for t in range(nt):
                spatial(t)
            if widx + 1 < len(wlist):
                make_ht(wlist[widx + 1])
            sq, sqb = self.SQ.next()
            psn = pbn = None
            pend = None
            for oc in range(8):
                wv, wb = self.wslot(16)
                self.load_w(wv, wo_[:, :, oc * 128:(oc + 1) * 128], wb)
                ps, pb = self.PS.next()
                for k in range(16):
                    self.mm(ps[:, 0:nn], wv[:, k, :], U[:, k, 0:nn], k == 0, k == 15, [wb, BU], [pb])
                if pend is not None:
                    if psn is None:
                        psn, pbn = self.PS.next()
                    self.mm(psn[:, 0:nn], self.ONESM, sq[:, pend, 0:nn], pend == 0, False, [sqb, self.BC], [pbn])
                b_ap = P[:, mbo + oc:mbo + oc + 1]
                self.act(Y[:, oc, 0:nn], ps[:, 0:nn], AF.Identity, [pb, self.BP], [BY], bias=b_ap)
                self.act(sq[:, oc, 0:nn], ps[:, 0:nn], AF.Square, [pb, self.BP], [sqb], bias=b_ap)
                pend = oc
            self.mm(psn[:, 0:nn], self.ONESM, sq[:, 7, 0:nn], False, True, [sqb, self.BC], [pbn])
            self.resid_tail(w, t0, nn, Y[:, :, 0:nn], BY, gc1, psn, pbn)


_PROG = None
_OH = None


def _get_prog():
    global _PROG, _OH
    if _PROG is None:
        p = Prog()
        p.build()
        _PROG = p
        _OH = onehot2_const()
    return _PROG


WEIGHT_KEYS = ["rel_bias_table", "norm_gain", "attn_w_qkv", "attn_b_qkv", "attn_w_o", "attn_b_o", "attn_sinks",
               "conv_w_pw1", "conv_b_pw1", "conv_w_dw", "conv_b_dw", "conv_ln_g", "conv_ln_b", "conv_w_pw2",
               "conv_b_pw2", "cmlp_w_in", "cmlp_b_in", "cmlp_ln_g", "cmlp_ln_b", "cmlp_w_s", "cmlp_b_s",
               "cmlp_w_out", "cmlp_b_out", "ffn_w_up", "ffn_w_dw", "ffn_b_dw", "ffn_w_down"]


def kernel(**inputs):
    p = _get_prog()
    f = lambda a: np.ascontiguousarray(np.asarray(a, dtype=np.float32))
    xp, xs = f(inputs["x_prompt"]), f(inputs["x_sample"])
    cak, cav = f(inputs["cache_attn_k"]), f(inputs["cache_attn_v"])
    stc, stf = f(inputs["state_conv"]), f(inputs["state_ffn_conv"])
    wts = {k: f(inputs[k]) for k in WEIGHT_KEYS}
    in_maps = []
    for c in range(8):
        b, hf = c // 2, c % 2
        s0 = 0 if hf == 0 else 4096 - TP
        m = dict(wts)
        m["xin"] = np.ascontiguousarray(np.concatenate([xp[b, s0:s0 + TP], xs[2 * c], xs[2 * c + 1]], axis=0))
        m["ck"] = np.ascontiguousarray(cak[:, 2 * c:2 * c + 2].reshape(2, 2, 128, 256))
        m["cv"] = np.ascontiguousarray(cav[:, 2 * c:2 * c + 2].reshape(2, 2, 128, 256))
        m["stc"] = np.ascontiguousarray(stc[0, 2 * c:2 * c + 2])
        m["stf"] = np.ascontiguousarray(stf[:, 2 * c:2 * c + 2])
        m["oh"] = _OH
        in_maps.append(m)
    res = run_bass_kernel_spmd(p.nc, in_maps, core_ids=list(range(8)))
    R = res.results
    y_prompt = np.zeros((4, 4096, 1024), np.float32)
    y_sample = np.zeros((16, 64, 1024), np.float32)
    p_k = np.zeros((2, 4, 128, 4, 64), np.float32)
    p_v = np.zeros((2, 4, 128, 4, 64), np.float32)
    p_conv = np.zeros((1, 4, 30, 1024), np.float32)
    p_ffn = np.zeros((4, 4, 2, 5632), np.float32)
    s_k = np.zeros((2, 16, 64, 4, 64), np.float32)
    s_v = np.zeros((2, 16, 64, 4, 64), np.float32)
    s_conv = np.zeros((1, 16, 30, 1024), np.float32)
    s_cv = np.zeros((1, 16, 64, 2048), np.float32)
    s_ffn = np.zeros((4, 16, 2, 5632), np.float32)
    for c in range(8):
        b, hf = c // 2, c % 2
        r = R[c]
        yo = np.asarray(r["yout"])
        if hf == 0:
            y_prompt[b, 0:TP] = yo[0:TP]
        else:
            y_prompt[b, TP:4096] = yo[2 * TP - 4096:TP]
            p_k[:, b] = np.asarray(r["pk"]).reshape(2, 128, 4, 64)
            p_v[:, b] = np.asarray(r["pv"]).reshape(2, 128, 4, 64)
            p_conv[0, b] = np.asarray(r["pconv"])
            p_ffn[:, b] = np.asarray(r["pffn"])
        y_sample[2 * c] = yo[TP:TP + 64]
        y_sample[2 * c + 1] = yo[TP + 64:TP + 128]
        s_k[:, 2 * c:2 * c + 2] = np.asarray(r["sk"]).reshape(2, 2, 64, 4, 64)
        s_v[:, 2 * c:2 * c + 2] = np.asarray(r["sv"]).reshape(2, 2, 64, 4, 64)
        s_conv[0, 2 * c:2 * c + 2] = np.asarray(r["sconv"])
        s_cv[0, 2 * c:2 * c + 2] = np.asarray(r["scv"]).reshape(2, 64, 2048)
        s_ffn[:, 2 * c:2 * c + 2] = np.asarray(r["sffn"])
    return (y_prompt, y_sample, p_k, p_v, p_conv, p_ffn, s_k, s_v, s_conv, s_cv, s_ffn)
```

```python
import contextlib
import numpy as np
import concourse.bass as bass
import concourse.mybir as mybir
from concourse.bass_utils import run_bass_kernel_spmd

F32 = mybir.dt.float32
BF16 = mybir.dt.bfloat16
AF = mybir.ActivationFunctionType
ALU = mybir.AluOpType

ENGS = ("pe", "act", "dve", "pool", "sp")
NPR = 1152
WIN = 384
TP = 2304
NTOK = 2432
DFF = 2816
NFC = 44


class Buf:
    __slots__ = ("name", "w", "r")

    def __init__(self, name, fence=None):
        self.name = name
        self.w = None
        self.r = list(fence) if fence else []


class Op:
    __slots__ = ("eng", "fn", "deps", "dma", "has_dep", "tok", "id")


class Sched:
    def __init__(self, nc, n_dma_sems=24):
        self.nc = nc
        self.ops = []
        self.per_eng = {e: [] for e in ENGS}
        self.n_dma_sems = n_dma_sems

    def op(self, eng, fn, reads=(), writes=(), dma=False):
        o = Op()
        o.eng, o.fn, o.dma, o.has_dep, o.tok, o.id = eng, fn, dma, False, None, len(self.ops)
        deps = set()
        for b in reads:
            if b.w is not None:
                deps.add(b.w)
        for b in writes:
            if b.w is not None:
                deps.add(b.w)
            deps.update(b.r)
        if eng == "pe":
            deps = {dd for dd in deps if self.ops[dd].eng != "pe"}
        o.deps = deps
        for b in reads:
            b.r.append(o.id)
        for b in writes:
            b.w = o.id
            b.r = []
        self.ops.append(o)
        self.per_eng[eng].append(o)
        return o

    def emit(self, final_wait_ops=()):
        nc, ops = self.nc, self.ops
        for o in ops:
            for d in o.deps:
                ops[d].has_dep = True
        for o in final_wait_ops:
            o.has_dep = True
        with contextlib.ExitStack() as st:
            esem = {e: st.enter_context(nc.semaphore("s_" + e)) for e in ENGS}
            dsems = {e: [st.enter_context(nc.semaphore("d_%s_%d" % (e, i)))
                         for i in range(self.n_dma_sems)] for e in ("sp", "pool")}
            ecnt = {e: 0 for e in ENGS}
            dcnt = {e: [0] * self.n_dma_sems for e in dsems}
            drr = {e: 0 for e in dsems}
            for e in ENGS:
                for o in self.per_eng[e]:
                    if not o.has_dep:
                        continue
                    if o.dma:
                        j = drr[e]
                        drr[e] = (j + 1) % self.n_dma_sems
                        prev = dcnt[e][j]
                        dcnt[e][j] += 16
                        o.tok = (dsems[e][j], dcnt[e][j], prev)
                    else:
                        ecnt[e] += 1
                        o.tok = (esem[e], ecnt[e], None)
            block = st.enter_context(nc.Block())

            def run(e, eng):
                waited = {}
                for o in self.per_eng[e]:
                    need = {}
                    for d in o.deps:
                        sem, val, _ = ops[d].tok
                        k = id(sem)
                        if k not in need or need[k][1] < val:
                            need[k] = (sem, val)
                    if o.dma and o.tok is not None and o.tok[2]:
                        sem, _, prev = o.tok
                        k = id(sem)
                        if k not in need or need[k][1] < prev:
                            need[k] = (sem, prev)
                    for k, (sem, val) in need.items():
                        if waited.get(k, 0) >= val:
                            continue
                        eng.wait_ge(sem, val)
                        waited[k] = val
                    ins = o.fn(eng)
                    if o.tok is not None:
                        ins.then_inc(o.tok[0], 16 if o.dma else 1)
                if e == "sp":
                    for o in final_wait_ops:
                        sem, val, _ = o.tok
                        eng.wait_ge(sem, val)

            @block.tensor
            def _(eng):
                run("pe", eng)

            @block.scalar
            def _(eng):
                run("act", eng)

            @block.vector
            def _(eng):
                run("dve", eng)

            @block.gpsimd
            def _(eng):
                run("pool", eng)

            @block.sync
            def _(eng):
                run("sp", eng)


class Ring:
    def __init__(self, name, aps, bufs):
        self.aps = aps
        self.bufs = bufs
        self.i = 0

    def next(self):
        j = self.i
        self.i = (j + 1) % len(self.aps)
        return self.aps[j], self.bufs[j]


def q_lo(j):
    return j if j < 4 else 8 + (j - 4)


def q_up(j):
    return 4 + j if j < 4 else 12 + (j - 4)


def t5_bucket_np(rel):
    half, max_exact = 16, 8
    n = np.abs(rel)
    log_ratio = np.log(np.maximum(n, 1).astype(np.float32) / np.float32(max_exact)) / np.float32(np.log(128 / max_exact))
    large = np.minimum(max_exact + (log_ratio * np.float32(half - max_exact)).astype(np.int32), half - 1)
    return np.where(rel > 0, half, 0) + np.where(n < max_exact, n, large)


def onehot2_const():
    import ml_dtypes
    oh = onehot_const()
    return np.ascontiguousarray(np.concatenate([oh, oh], axis=0).astype(ml_dtypes.bfloat16))


def onehot_const():
    q = np.arange(64)[:, None]
    j = np.arange(192)[None, :]
    bk = t5_bucket_np((j - 128) - q)
    oh = np.zeros((32, 64, 192), np.float32)
    for b in range(32):
        oh[b] = (bk == b)
    return oh


class Prog:
    def __init__(self):
        self.nc = nc = bass.Bass("TRN2", target_bir_lowering=False)
        self.S = Sched(nc)
        self.st = contextlib.ExitStack()
        self.d = {}
        self.outs = []
        self.live = []
        self.phase = 0


    def din(self, name, shape):
        self.d[name] = self.nc.dram_tensor(name, list(shape), F32, kind="ExternalInput").ap()
        return self.d[name]

    def dout(self, name, shape):
        self.d[name] = self.nc.dram_tensor(name, list(shape), F32, kind="ExternalOutput").ap()
        return self.d[name]

    def mm(self, out, lhsT, rhs, start, stop, R, W):
        return self.S.op("pe", lambda e: e.matmul(out, lhsT=lhsT, rhs=rhs, start=start, stop=stop), R, W)

    def tr(self, out, in_, R, W):
        ident = self.IDENT[0:in_.shape[0], 0:in_.shape[0]]
        return self.S.op("pe", lambda e: e.transpose(out, in_, ident), R, W)

    def act(self, out, in_, func, R, W, bias=None, scale=1.0):
        if bias is None:
            return self.S.op("act", lambda e: e.activation(out=out, in_=in_, func=func, scale=scale), R, W)
        return self.S.op("act", lambda e: e.activation(out=out, in_=in_, func=func, bias=bias, scale=scale), R, W)

    def stt(self, out, in0, scalar, in1, op0, op1, R, W, eng="dve"):
        return self.S.op(eng, lambda e: e.scalar_tensor_tensor(out=out, in0=in0, scalar=scalar, in1=in1, op0=op0, op1=op1), R, W)

    def tt(self, out, in0, in1, op, R, W, eng="dve"):
        return self.S.op(eng, lambda e: e.tensor_tensor(out=out, in0=in0, in1=in1, op=op), R, W)

    def cp(self, out, in_, R, W, eng="dve"):
        return self.S.op(eng, lambda e: e.tensor_copy(out=out, in_=in_), R, W)

    def recip(self, out, in_, R, W):
        return self.S.op("dve", lambda e: e.reciprocal(out=out, in_=in_), R, W)

    def memset(self, ap, val, W, eng="dve"):
        return self.S.op(eng, lambda e: e.memset(ap, val), (), W)

    def dma(self, out, in_, R, W, q="sp"):
        return self.S.op(q, lambda e: e.dma_start(out=out, in_=in_), R, W, dma=True)

    def carve(self, nwords):
        a = self.aoff
        self.aoff += nwords
        assert self.aoff <= self.NA, ("arena overflow", self.aoff, self.NA)
        self.last_rng = (a, a + nwords)
        return self.ARENA[:, a:a + nwords]

    def carve_bf(self, nel):
        return self.carve((nel + 1) // 2).bitcast(BF16)[:, 0:nel]

    def mkbufs(self, names, rng=None):
        rng = rng or self.last_rng
        fence = []
        keep = []
        for (a, b, bf, ph) in self.live:
            if a < rng[1] and rng[0] < b:
                if bf.w is not None:
                    fence.append(bf.w)
                fence.extend(bf.r)
                if ph < self.phase and rng[0] <= a and b <= rng[1]:
                    continue
            keep.append((a, b, bf, ph))
        self.live = keep
        out = [Buf(n, fence) for n in names]
        for bf in out:
            self.live.append((rng[0], rng[1], bf, self.phase))
        return out

    def mkbuf(self, name, rng=None):
        return self.mkbufs([name], rng)[0]

    def ring(self, name, n, nwords, bf16=False):
        aps, bufs = [], []
        for i in range(n):
            ap = self.carve_bf(nwords) if bf16 else self.carve(nwords)
            aps.append(ap)
            bufs.append(self.mkbuf("%s%d" % (name, i)))
        return Ring(name, aps, bufs)

    def scratch_reset(self):
        self.aoff = self.scratch_base
        self.phase += 1

    def windows(self, sg):
        w = [(i, i * WIN, WIN, False) for i in range(3)]
        if sg == 1:
            w.append((3, NPR, 128, True))
        return w

    def nsg(self, sg):
        return NPR + (128 if sg == 1 else 0)

    def build(self):
        nc, S, st = self.nc, self.S, self.st
        din, dout = self.din, self.dout
        xin = din("xin", (NTOK, 1024))
        ck = din("ck", (2, 2, 128, 256))
        cv = din("cv", (2, 2, 128, 256))
        stc = din("stc", (2, 30, 1024))
        stf = din("stf", (4, 2, 2, 5632))
        self.d["oh"] = nc.dram_tensor("oh", [64, 64, 192], BF16, kind="ExternalInput").ap()
        relt = din("rel_bias_table", (32, 16))
        ng = din("norm_gain", (4, 4, 1024))
        wqkv = din("attn_w_qkv", (2, 1024, 1536))
        bqkv = din("attn_b_qkv", (2, 1536))
        wo = din("attn_w_o", (2, 1024, 1024))
        bo = din("attn_b_o", (2, 1024))
        sinks = din("attn_sinks", (2, 16))
        cw1 = din("conv_w_pw1", (1, 1024, 2048))
        cb1 = din("conv_b_pw1", (1, 2048))
        cwd = din("conv_w_dw", (1, 31, 1024))
        cbd = din("conv_b_dw", (1, 1024))
        clg = din("conv_ln_g", (1, 1024))
        clb = din("conv_ln_b", (1, 1024))
        cw2 = din("conv_w_pw2", (1, 1024, 1024))
        cb2 = din("conv_b_pw2", (1, 1024))
        mwi = din("cmlp_w_in", (1, 1024, 4096))
        mbi = din("cmlp_b_in", (1, 4096))
        mlg = din("cmlp_ln_g", (1, 2048))
        mlb = din("cmlp_ln_b", (1, 2048))
        mws = din("cmlp_w_s", (1, 4, 128, 128))
        mbs = din("cmlp_b_s", (1, 4, 128))
        mwo = din("cmlp_w_out", (1, 2048, 1024))
        mbo = din("cmlp_b_out", (1, 1024))
        fwu = din("ffn_w_up", (4, 1024, 5632))
        fwd = din("ffn_w_dw", (4, 3, 5632))
        fbd = din("ffn_b_dw", (4, 5632))
        fwdn = din("ffn_w_down", (4, 2816, 1024))
        yout = dout("yout", (NTOK, 1024))
        pk = dout("pk", (2, 128, 256))
        pv = dout("pv", (2, 128, 256))
        sk = dout("sk", (2, 128, 256))
        sv = dout("sv", (2, 128, 256))
        pconv = dout("pconv", (30, 1024))
        sconv = dout("sconv", (2, 30, 1024))
        pffn = dout("pffn", (4, 2, 5632))
        sffn = dout("sffn", (4, 2, 2, 5632))
        scv = dout("scv", (128, 2048))

        self.NA = 207 * 256 - 64
        self.ARENA = st.enter_context(nc.sbuf_tensor("arena", [128, self.NA], F32))
        self.PSUM = st.enter_context(nc.psum_tensor("psum", [128, 8, 512], F32))
        self.aoff = 0
        self.PS = Ring("ps", [self.PSUM[:, i, :] for i in range(8)], [Buf("ps%d" % i) for i in range(8)])

        self.X = self.carve(8 * 1280).rearrange("p (k t) -> p k t", k=8)
        self.BX = self.mkbufs(["X0", "X1", "X2", "X3"])
        self.IDENT = self.carve(128)
        self.BC = self.mkbuf("consts")
        self.ONESM = self.carve_bf(128)
        self.ONES1 = self.carve_bf(128)
        self.ONESF = self.carve(128)
        self.EPS = self.carve(2)
        plist = []

        def chunks(ap1d):
            return ap1d.rearrange("(c p) -> c p", p=128)

        pcol = {}
        ncol = 0

        def addp(key, ap2d):
            nonlocal ncol
            pcol[key] = ncol
            plist.append((ncol, ap2d))
            ncol += ap2d.shape[0]

        addp("ng", chunks(ng.rearrange("a b c -> (a b c)")))
        self.bq_special = []
        for j in range(2):
            pcol[("bq", j)] = ncol
            bq16 = bqkv[j, 0:1024].rearrange("(h d) -> h d", d=64)
            self.bq_special.append((ncol, bq16))
            ncol += 8
            addp(("bk", j), chunks(bqkv[j, 1024:1280]))
            addp(("bo", j), chunks(bo[j]))
        addp("cb1", chunks(cb1[0]))
        addp("cwd", chunks(cwd[0].rearrange("k c -> (k c)")))
        addp("cbd", chunks(cbd[0]))
        addp("clg", chunks(clg[0]))
        addp("clb", chunks(clb[0]))
        addp("cb2", chunks(cb2[0]))
        addp("mbu", chunks(mbi[0, 0:2048]))
        addp("mbo", chunks(mbo[0]))
        for l in range(4):
            addp(("fwd", l), chunks(fwd[l].rearrange("k c -> (k c)")))
            addp(("fbd", l), chunks(fbd[l]))
            addp(("stf", l), chunks(stf[l].rearrange("s r c -> (s r c)")))
        addp("stc", chunks(stc.rearrange("s r c -> (s r c)")))
        self.pcol = pcol
        npad = ((ncol + 127) // 128) * 128
        self.PARAMS = self.carve(npad)
        self.BP = self.mkbuf("params")
        a0 = self.aoff
        self.ACARRY = [self.carve_bf(8 * 128).rearrange("p (k t) -> p k t", k=8) for _ in range(2)]
        self.CCARRY = self.carve_bf(8 * 30).rearrange("p (k t) -> p k t", k=8)
        self.FCARRY = [self.carve_bf(8 * 2).rearrange("p (k t) -> p k t", k=8) for _ in range(4)]
        self.BCARRY = self.mkbuf("carry", (a0, self.aoff))
        self.NSLOT = 5
        self.SLOTW = 1408
        self.RW = self.ring("rw", self.NSLOT, self.SLOTW)
        sq = self.carve_bf(8 * WIN).rearrange("p (k t) -> p k t", k=8)
        self.SQ = Ring("sq", [sq], [self.mkbuf("sq")])
        self.RS = self.ring("rs", 2, WIN)
        self.scratch_base = self.aoff

        S.op("pool", lambda e: e.memset(self.IDENT, 1.0), (), [self.BC])
        S.op("pool", lambda e: e.affine_select(out=self.IDENT, in_=self.IDENT, pattern=[[-1, 128]],
                                               compare_op=ALU.is_equal, fill=0.0, base=0, channel_multiplier=1),
             [self.BC], [self.BC])
        self.memset(self.ONESM, 1.0 / 1024.0, [self.BC])
        self.memset(self.ONES1, 1.0, [self.BC])
        self.memset(self.ONESF, 1.0, [self.BC])
        self.memset(self.EPS[:, 0:1], 1e-6, [self.BC])
        self.memset(self.EPS[:, 1:2], 1e-5, [self.BC])

        self.scratch_reset()
        stg = self.ring("pstg", 3, 128)
        for t0 in range(0, npad, 128):
            sap, sb = stg.next()
            self.memset(sap, 0.0, [sb])
            for (c0, ap2d) in plist:
                n = ap2d.shape[0]
                lo, hi = max(c0, t0), min(c0 + n, t0 + 128)
                if lo < hi:
                    self.dma(sap[lo - t0:hi - t0, :], ap2d[lo - c0:hi - c0, :], [], [sb])
            for (c0, bq16) in self.bq_special:
                if t0 <= c0 < t0 + 128:
                    assert c0 + 8 <= t0 + 128
                    r = c0 - t0
                    self.dma(sap[r:r + 4, 0:64], bq16[0:4, :], [], [sb])
                    self.dma(sap[r:r + 4, 64:128], bq16[4:8, :], [], [sb])
                    self.dma(sap[r + 4:r + 8, 0:64], bq16[8:12, :], [], [sb])
                    self.dma(sap[r + 4:r + 8, 64:128], bq16[12:16, :], [], [sb])
            ps, pb = self.PS.next()
            self.tr(ps[:, 0:128], sap, [sb, self.BC], [pb])
            self.act(self.PARAMS[:, t0:t0 + 128], ps[:, 0:128], AF.Identity, [pb], [self.BP])

        finals = []
        self.finals = finals
        for sg in range(2):
            self.load_x(sg)
            for l in range(4):
                kind, j = l % 3, l // 3
                if kind == 0:
                    self.attn(l, j, sg)
                elif kind == 1:
                    self.convmod(l, sg)
                else:
                    self.cmlp(l, sg)
                self.ffn(l, sg)
            self.store_y(sg)
        S.emit(final_wait_ops=finals)
        return nc

    def load_x(self, sg):
        self.scratch_reset()
        xin = self.d["xin"]
        stg = self.ring("xstg", 3, 1024)
        n = self.nsg(sg)
        for t in range(n // 128):
            r0 = sg * NPR + t * 128 if t < 9 else TP
            w = t // 3
            sap, sb = stg.next()
            self.dma(sap, xin[r0:r0 + 128, :], [], [sb])
            for h in range(2):
                ps, pb = self.PS.next()
                for kk in range(4):
                    k = h * 4 + kk
                    self.tr(ps[:, kk * 128:(kk + 1) * 128], sap[:, k * 128:(k + 1) * 128], [sb, self.BC], [pb])
                self.act(self.X[:, h * 4:(h + 1) * 4, t * 128:(t + 1) * 128],
                         ps.rearrange("p (a b) -> p a b", a=4), AF.Identity, [pb], [self.BX[w]])

    def store_y(self, sg):
        self.scratch_reset()
        yout = self.d["yout"]
        stg = self.ring("ystg", 3, 1024)
        n = self.nsg(sg)
        for t in range(n // 128):
            r0 = sg * NPR + t * 128 if t < 9 else TP
            w = t // 3
            sap, sb = stg.next()
            for h in range(2):
                ps, pb = self.PS.next()
                for kk in range(4):
                    k = h * 4 + kk
                    self.tr(ps[:, kk * 128:(kk + 1) * 128], self.X[:, k, t * 128:(t + 1) * 128], [self.BX[w], self.BC], [pb])
                self.act(sap[:, h * 512:(h + 1) * 512], ps, AF.Identity, [pb], [sb])
            self.finals.append(self.dma(yout[r0:r0 + 128, :], sap, [sb], []))

    def rstd_of(self, src3, n, R):
        sq, sqb = self.SQ.next()
        self.act(sq[:, :, 0:n], src3, AF.Square, R, [sqb])
        ps, pb = self.PS.next()
        for k in range(8):
            self.mm(ps[:, 0:n], self.ONESM, sq[:, k, 0:n], k == 0, k == 7, [sqb, self.BC], [pb])
        rs, rb = self.RS.next()
        self.act(rs[:, 0:n], ps[:, 0:n], AF.Sqrt, [pb, self.BC], [rb], bias=self.EPS[:, 0:1])
        self.recip(rs[:, 0:n], rs[:, 0:n], [rb], [rb])
        return rs, rb

    def norm_h(self, sg, gcol, out_fn, BH):
        for (w, t0, n, samp) in self.windows(sg):
            rs, rb = self.rstd_of(self.X[:, :, t0:t0 + n], n, [self.BX[w]])
            for k in range(8):
                dst = out_fn(k, t0, n, samp)
                xin_, rin = self.X[:, k, t0:t0 + n], rs[:, 0:n]
                if len(dst.shape) == 3:
                    xin_ = xin_.rearrange("p (s q) -> p s q", s=2)
                    rin = rin.rearrange("p (s q) -> p s q", s=2)
                self.stt(dst, xin_, self.PARAMS[:, gcol + k:gcol + k + 1], rin, ALU.mult, ALU.mult,
                         [self.BX[w], rb, self.BP], [BH[w]])

    def resid(self, w, t0, n, Y3, BY, gcol):
        rs, rb = self.rstd_of(Y3, n, [BY])
        for k in range(8):
            self.stt(Y3[:, k, :], Y3[:, k, :], self.PARAMS[:, gcol + k:gcol + k + 1], rs[:, 0:n], ALU.mult, ALU.mult,
                     [BY, rb, self.BP], [BY])
        self.tt(self.X[:, :, t0:t0 + n], self.X[:, :, t0:t0 + n], Y3, ALU.add, [BY, self.BX[w]], [self.BX[w]])

    def resid_tail(self, w, t0, n, Y3, BY, gcol, psn, pbn):
        rs, rb = self.RS.next()
        self.act(rs[:, 0:n], psn[:, 0:n], AF.Sqrt, [pbn, self.BC], [rb], bias=self.EPS[:, 0:1])
        self.recip(rs[:, 0:n], rs[:, 0:n], [rb], [rb])
        for k in range(8):
            self.stt(Y3[:, k, :], Y3[:, k, :], self.PARAMS[:, gcol + k:gcol + k + 1], rs[:, 0:n], ALU.mult, ALU.mult,
                     [BY, rb, self.BP], [BY])
        self.tt(self.X[:, :, t0:t0 + n], self.X[:, :, t0:t0 + n], Y3, ALU.add, [BY, self.BX[w]], [self.BX[w]])

    def resid_tail_g(self, w, t0, n, Yg3, BY, psn, pbn):
        rs, rb = self.RS.next()
        self.act(rs[:, 0:n], psn[:, 0:n], AF.Sqrt, [pbn, self.BC], [rb], bias=self.EPS[:, 0:1])
        self.recip(rs[:, 0:n], rs[:, 0:n], [rb], [rb])
        self.tt(Yg3, Yg3, rs[:, 0:n].unsqueeze(1).to_broadcast([128, 8, n]), ALU.mult, [BY, rb], [BY])
        self.tt(self.X[:, :, t0:t0 + n], self.X[:, :, t0:t0 + n], Yg3, ALU.add, [BY, self.BX[w]], [self.BX[w]])

    def gain_bias(self, gcol, bcol):
        GB = self.carve(8)
        bgb = self.mkbuf("gb")
        self.tt(GB, self.PARAMS[:, gcol:gcol + 8], self.PARAMS[:, bcol:bcol + 8], ALU.mult, [self.BP], [bgb])
        return GB, bgb

    def wslot(self, kc):
        sl, sb = self.RW.next()
        v = sl.bitcast(BF16)[:, 0:kc * 128].rearrange("p (k c) -> p k c", k=kc)
        return v, sb

    def load_w(self, dst, src, sb):
        self.dma(dst, src, [], [sb], q="pool")

    def out_linear(self, sg, KC, rhs_fn, wload, bcol, gcol, RB, Y, BY):
        Ys = Y if isinstance(Y, list) else [(Y, BY)]
        GB, bgb = self.gain_bias(gcol, bcol)
        for wi_, (w, t0, n, samp) in enumerate(self.windows(sg)):
            Yw, BYw = Ys[wi_ % len(Ys)]
            sq, sqb = self.SQ.next()
            psn = pbn = None
            pend = None
            for oc in range(8):
                wv, wb = self.wslot(KC)
                wload(wv, oc, wb)
                ps, pb = self.PS.next()
                for k in range(KC):
                    self.mm(ps[:, 0:n], wv[:, k, :], rhs_fn(k, t0, n), k == 0, k == KC - 1, [wb, RB[w]], [pb])
                if pend is not None:
                    if psn is None:
                        psn, pbn = self.PS.next()
                    self.mm(psn[:, 0:n], self.ONESM, sq[:, pend, 0:n], pend == 0, False, [sqb, self.BC], [pbn])
                b_ap = self.PARAMS[:, bcol + oc:bcol + oc + 1]
                self.act(Yw[:, oc, 0:n], ps[:, 0:n], AF.Identity, [pb, self.BP, bgb], [BYw], bias=GB[:, oc:oc + 1],
                         scale=self.PARAMS[:, gcol + oc:gcol + oc + 1])
                self.act(sq[:, oc, 0:n], ps[:, 0:n], AF.Square, [pb, self.BP], [sqb], bias=b_ap)
                pend = oc
            self.mm(psn[:, 0:n], self.ONESM, sq[:, 7, 0:n], False, True, [sqb, self.BC], [pbn])
            self.resid_tail_g(w, t0, n, Yw[:, :, 0:n], BYw, psn, pbn)

    def ffn(self, l, sg):
        self.scratch_reset()
        d = self.d
        HW = 2 + NPR + 132
        HT = self.carve_bf(8 * HW).rearrange("p (k t) -> p k t", k=8)
        BH = self.mkbufs(["fh0", "fh1", "fh2", "fh3", "fhc"])
        BHC = BH[4]
        ACTH = self.carve_bf(11 * 1280).rearrange("p (k t) -> p k t", k=11)
        BA = self.mkbufs(["fa0", "fa1", "fa2", "fa3"])
        Y = self.carve(8 * 1280).rearrange("p (k t) -> p k t", k=8)
        BY = self.mkbufs(["fy0", "fy1", "fy2", "fy3"])
        TG = self.ring("tg", 3, WIN)
        TU = self.ring("tu", 3, WIN)
        UPT = self.carve(NFC * 6).rearrange("p (c t) -> p c t", c=NFC)
        BUP = self.mkbuf("upt")
        OST = self.ring("ost", 2, 512)
        gc2, gc3 = self.pcol["ng"] + (l * 4 + 2) * 8, self.pcol["ng"] + (l * 4 + 3) * 8
        wcol, bcol, scol = self.pcol[("fwd", l)], self.pcol[("fbd", l)], self.pcol[("stf", l)]
        P = self.PARAMS
        if sg == 0:
            self.memset(HT[:, :, 0:2], 0.0, [BHC])
        else:
            self.cp(HT[:, :, 0:2], self.FCARRY[l], [self.BCARRY], [BHC])
            self.memset(HT[:, :, 2 + NPR:HW].rearrange("p k (s q) -> p k s q", s=2)[:, :, :, 0:2], 0.0, [BH[3]])

        def out_fn(k, t0, nn, samp):
            if samp:
                return HT[:, k, 2 + NPR:HW].rearrange("p (s q) -> p s q", s=2)[:, :, 2:66]
            return HT[:, k, 2 + t0:2 + t0 + nn]

        self.norm_h(sg, gc2, out_fn, BH)
        if sg == 0:
            self.cp(self.FCARRY[l], HT[:, :, NPR:NPR + 2], [BH[2]], [self.BCARRY])
        wup = d["ffn_w_up"][l].rearrange("(k p) c -> p k c", p=128)
        wdn = d["ffn_w_down"][l].rearrange("(k p) c -> p k c", p=128)
        def load_pair(i):
            wv, wb = self.wslot(16)
            self.load_w(wv[:, 0:8, :], wup[:, :, i * 128:(i + 1) * 128], wb)
            self.load_w(wv[:, 8:16, :], wup[:, :, DFF + i * 128:DFF + (i + 1) * 128], wb)
            return wv, wb

        def load_dn(half, oc):
            wv, wb = self.wslot(11)
            self.load_w(wv, wdn[:, half * 11:(half + 1) * 11, oc * 128:(oc + 1) * 128], wb)
            return wv, wb
        stream = []
        for half in range(2):
            stream += [("up", half * 11 + ii) for ii in range(11)] + [("dn", half, oc) for oc in range(8)]
        AHEAD = 2
        loaded = []

        def issue(n):
            while len(loaded) < min(n, len(stream)):
                it = stream[len(loaded)]
                loaded.append(load_pair(it[1]) if it[0] == "up" else load_dn(it[1], it[2]))
        pos = [0]

        def take():
            issue(pos[0] + 1 + AHEAD)
            r = loaded[pos[0]]
            pos[0] += 1
            return r
        for half in range(2):
            for ii in range(11):
                i = half * 11 + ii
                wv, wb = take()
                for (w, t0, nn, samp) in self.windows(sg):
                    res = []
                    hr = [BH[w]] if samp else [BH[w], (BHC if w == 0 else BH[w - 1])]
                    for gu in range(2):
                        c = i + 22 * gu
                        ps, pb = self.PS.next()
                        if samp:
                            c0, N = 2 + NPR, 132
                        else:
                            c0, N = t0, nn + 2
                        for k in range(8):
                            self.mm(ps[:, 0:N], wv[:, gu * 8 + k, :], HT[:, k, c0:c0 + N], k == 0, k == 7, [wb] + hr, [pb])
                        if samp:
                            pv3 = ps[:, 0:132].rearrange("p (s q) -> p s q", s=2)
                            stv = P[:, scol + c:scol + c + 4 * NFC].rearrange("p (s c) -> p s c", s=4)[:, :, 0]
                            self.cp(pv3[:, :, 0:2], stv.rearrange("p (s r) -> p s r", s=2), [self.BP, pb], [pb])
                            a2, a1, a0 = pv3[:, :, 2:66], pv3[:, :, 1:65], pv3[:, :, 0:64]
                        else:
                            a2, a1, a0 = ps[:, 2:nn + 2], ps[:, 1:nn + 1], ps[:, 0:nn]
                        tring = TG if gu == 0 else TU
                        tb_, tbb = tring.next()
                        tv = tb_[:, 0:nn]
                        if samp:
                            tv = tv.rearrange("p (s q) -> p s q", s=2)
                        w0 = P[:, wcol + c:wcol + c + 1]
                        w1 = P[:, wcol + NFC + c:wcol + NFC + c + 1]
                        w2 = P[:, wcol + 2 * NFC + c:wcol + 2 * NFC + c + 1]
                        self.act(tv, a2, AF.Identity, [pb, self.BP], [tbb], bias=P[:, bcol + c:bcol + c + 1], scale=w2)
                        self.stt(tv, a1, w1, tv, ALU.mult, ALU.add, [pb, tbb, self.BP], [tbb])
                        self.stt(tv, a0, w0, tv, ALU.mult, ALU.add, [pb, tbb, self.BP], [tbb])
                        if sg == 1 and samp:
                            self.act(UPT[:, c, 2:6].rearrange("p (s r) -> p s r", s=2), pv3[:, :, 64:66], AF.Identity, [pb], [BUP])
                        elif sg == 1 and w == 2:
                            self.act(UPT[:, c, 0:2], ps[:, nn:nn + 2], AF.Identity, [pb], [BUP])
                        res.append((tv, tbb))
                    (tg, tgb), (tu, tub) = res
                    self.act(tg, tg, AF.Gelu_apprx_tanh, [tgb], [tgb])
                    dst = ACTH[:, ii, t0:t0 + nn]
                    if samp:
                        dst = dst.rearrange("p (s q) -> p s q", s=2)
                    self.tt(dst, tg, tu, ALU.mult, [tgb, tub], [BA[w]], eng="pool")
            for oc in range(8):
                wv, wb = take()
                for (w, t0, nn, samp) in self.windows(sg):
                    ps, pb = self.PS.next()
                    for k in range(11):
                        self.mm(ps[:, 0:nn], wv[:, k, :], ACTH[:, k, t0:t0 + nn], k == 0, k == 10, [wb, BA[w]], [pb])
                    if half == 0:
                        self.act(Y[:, oc, t0:t0 + nn], ps[:, 0:nn], AF.Identity, [pb], [BY[w]])
                    else:
                        self.tt(Y[:, oc, t0:t0 + nn], Y[:, oc, t0:t0 + nn], ps[:, 0:nn], ALU.add, [pb, BY[w]], [BY[w]])
        for (w, t0, nn, samp) in self.windows(sg):
            self.resid(w, t0, nn, Y[:, :, t0:t0 + nn], BY[w], gc3)
        if sg == 1:
            for cb in range(11):
                ps, pb = self.PS.next()
                for cc in range(4):
                    c = cb * 4 + cc
                    self.tr(ps[0:6, cc * 128:(cc + 1) * 128], UPT[:, c, :], [BUP, self.BC], [pb])
                oa, ob = OST.next()
                self.act(oa[0:6, :], ps[0:6, :], AF.Identity, [pb], [ob])
                self.finals.append(self.dma(d["pffn"][l][:, cb * 512:(cb + 1) * 512], oa[0:2, :], [ob], []))
                self.finals.append(self.dma(d["sffn"][l].rearrange("s r c -> (s r) c")[:, cb * 512:(cb + 1) * 512], oa[2:6, :], [ob], []))

    def attn(self, l, j, sg):
        self.scratch_reset()
        d = self.d
        P = self.PARAMS
        HW = 128 + 1280
        HT = self.carve_bf(8 * HW).rearrange("p (k t) -> p k t", k=8)
        ht_rng = self.last_rng
        BH = self.mkbufs(["ah0", "ah1", "ah2", "ah3", "ahc"])
        BHC = BH[4]

        def hb(c0, c1):
            out = []
            if c0 < 128:
                out.append(BHC)
            for w in range(4):
                a, b = 128 + w * WIN, 128 + min((w + 1) * WIN, 1280)
                if c0 < b and a < c1 and not (w == 3 and sg == 0):
                    out.append(BH[w])
            return out

        QT = self.carve_bf(8 * 1280).rearrange("p (k t) -> p k t", k=8)
        qt_rng = self.last_rng
        BQ = self.mkbufs(["aq0", "aq1", "aq2", "aq3"])
        kv0 = self.aoff
        KT = self.carve_bf(2 * HW).rearrange("p (k t) -> p k t", k=2)
        BK = self.mkbuf("a_kt")
        a0 = self.aoff
        VA = self.carve_bf(11 * 256).rearrange("p (a c) -> p a c", a=11)
        VB = self.carve_bf(11 * 256).rearrange("p (a c) -> p a c", a=11)
        BV = self.mkbuf("a_v", (a0, self.aoff))
        WKV = self.carve_bf(8 * 512).rearrange("p (k c) -> p k c", k=8)
        wkv_rng = self.last_rng
        BWKV = self.mkbuf("a_wkv")
        BKVB = self.carve(512)
        BBKV = self.mkbuf("a_bkv")
        KVO = self.ring("kvo", 1, 512)
        CK = self.ring("ckr", 2, 256)
        a0 = self.aoff
        KTC = self.carve_bf(2 * 2 * 128).rearrange("p (s k t) -> p s k t", s=2, k=2)
        VC = self.carve_bf(2 * 256).rearrange("p (s c) -> p s c", s=2)
        BCACHE = self.mkbuf("a_cache", (a0, self.aoff))
        a0 = self.aoff
        EBT = [self.carve(1024).rearrange("p (h j q) -> p h j q", h=2, j=8) for _ in range(2)]
        ESKR = self.carve_bf(1024).rearrange("p (h j q) -> p h j q", h=2, j=8)
        BEB = self.mkbuf("a_eb", (a0, self.aoff))
        OHT = self.ring("oht", 2, 4 * 96)
        a0 = self.aoff
        TAB = self.carve(16)
        SNK = self.carve(16)
        TABH = self.carve_bf(16)
        TABT = self.carve(16)
        BTAB = self.mkbuf("a_tab", (a0, self.aoff))
        e0 = self.aoff
        ET = self.ring("et", 3, 512)
        PT = self.ring("pt", 4, 512, bf16=True)
        DEN = self.ring("den", 2, 256)
        e1 = self.aoff
        assert e1 - e0 == 8 * WIN
        Y = self.ARENA[:, e0:e1].rearrange("p (k t) -> p k t", k=8)
        WOA = self.carve_bf(6 * 8 * 128).rearrange("p (o k c) -> p o k c", o=6, k=8)
        BWOA = self.mkbuf("a_woa")
        gc0, gc1 = self.pcol["ng"] + (l * 4 + 0) * 8, self.pcol["ng"] + (l * 4 + 1) * 8

        wo_e = d["attn_w_o"][j]
        for oc in range(6):
            for hf in range(2):
                for grp in range(2):
                    h0 = (0, 8)[grp] if hf == 0 else (4, 12)[grp]
                    src = wo_e[h0 * 64:(h0 + 4) * 64, oc * 128:(oc + 1) * 128].rearrange("(j p) c -> p j c", p=64)
                    self.dma(WOA[hf * 64:(hf + 1) * 64, oc, grp * 4:(grp + 1) * 4, :], src, [], [BWOA], q="pool")
        if sg == 0:
            self.memset(HT[:, :, 0:128], 0.0, [BHC])
        else:
            self.cp(HT[:, :, 0:128], self.ACARRY[j], [self.BCARRY], [BHC])
        self.norm_h(sg, gc0, lambda k, t0, nn, samp: HT[:, k, 128 + t0:128 + t0 + nn], BH)
        if sg == 0:
            self.cp(self.ACARRY[j], HT[:, :, NPR:NPR + 128], hb(NPR, NPR + 128), [self.BCARRY])

        self.dma(TAB[0:32, :], d["rel_bias_table"][:, :], [], [BTAB])
        self.dma(TAB[32:64, :], d["rel_bias_table"][:, :], [], [BTAB])
        self.dma(SNK, d["attn_sinks"][j].partition_broadcast(128), [], [BTAB])
        self.cp(TABH[0:64, :], TAB[0:64, :], [BTAB], [BTAB])
        self.tt(TABT[32:64, :], TAB[32:64, :], TABH[32:64, :], ALU.subtract, [BTAB], [BTAB])
        self.cp(TABH[32:64, :], TABT[32:64, :], [BTAB], [BTAB])
        for q4 in range(16):
            oa_, ob = OHT.next()
            oa = oa_.bitcast(BF16).rearrange("p (q j) -> p q j", q=4)
            self.dma(oa[0:64, :, :], d["oh"][:, q4 * 4:(q4 + 1) * 4, :], [], [ob])
            if q4 % 4 == 0:
                psf, pbf = self.PS.next()
                pso, pbo = self.PS.next()
            for qi in range(4):
                qq = (q4 % 4) * 4 + qi
                self.mm(psf[:, qq * 16:(qq + 1) * 16], oa[0:64, qi, 0:128], TABH[0:64, :], True, True, [ob, BTAB], [pbf])
                self.mm(pso[0:64, qq * 16:(qq + 1) * 16], oa[0:64, qi, 128:192], TABH[0:64, :], True, True, [ob, BTAB], [pbo])
            if q4 % 4 == 3:
                q0 = (q4 // 4) * 16
                for (src, pbx, dst, npart) in ((psf, pbf, EBT[0], 128), (pso, pbo, EBT[1], 64)):
                    sv_ = src[0:npart, 0:256].rearrange("p (q h) -> p h q", h=16)
                    for hf in range(2):
                        for grp in range(2):
                            h0 = (0, 8)[grp] if hf == 0 else (4, 12)[grp]
                            self.act(dst[0:npart, hf, grp * 4:(grp + 1) * 4, q0:q0 + 16], sv_[:, h0:h0 + 4, :], AF.Exp, [pbx], [BEB])
        for hf in range(2):
            for grp in range(2):
                h0 = (0, 8)[grp] if hf == 0 else (4, 12)[grp]
                self.act(ESKR[64:65, hf, grp * 4:(grp + 1) * 4, :], SNK[64:65, h0:h0 + 4].unsqueeze(2).to_broadcast([1, 4, 64]), AF.Exp, [BTAB], [BEB])

        wq = d["attn_w_qkv"][j].rearrange("(k p) c -> p k c", p=128)
        bqc, bkc, boc = self.pcol[("bq", j)], self.pcol[("bk", j)], self.pcol[("bo", j)]
        for jq in range(8):
            wv, wb = self.wslot(8)
            lo, up = q_lo(jq), q_up(jq)
            self.load_w(wv[:, :, 0:64], wq[:, :, lo * 64:(lo + 1) * 64], wb)
            self.load_w(wv[:, :, 64:128], wq[:, :, up * 64:(up + 1) * 64], wb)
            for (w, t0, nn, samp) in self.windows(sg):
                ps, pb = self.PS.next()
                for k in range(8):
                    self.mm(ps[:, 0:nn], wv[:, k, :], HT[:, k, 128 + t0:128 + t0 + nn], k == 0, k == 7, [wb, BH[w]], [pb])
                self.act(QT[:, jq, t0:t0 + nn], ps[:, 0:nn], AF.Identity, [pb, self.BP], [BQ[w]], bias=P[:, bqc + jq:bqc + jq + 1])
        for jk in range(2):
            wv, wb = self.wslot(8)
            self.load_w(wv, wq[:, :, 1024 + jk * 128:1024 + (jk + 1) * 128], wb)
            for (c0, nn) in [(0, 128)] + [(128 + t0, nn) for (w, t0, nn, samp) in self.windows(sg)]:
                ps, pb = self.PS.next()
                for k in range(8):
                    self.mm(ps[:, 0:nn], wv[:, k, :], HT[:, k, c0:c0 + nn], k == 0, k == 7, [wb] + hb(c0, c0 + nn), [pb])
                self.act(KT[:, jk, c0:c0 + nn], ps[:, 0:nn], AF.Identity, [pb, self.BP], [BK], bias=P[:, bkc + jk:bkc + jk + 1])
        self.dma(WKV, wq[:, :, 1024:1536], [], [BWKV], q="pool")
        self.dma(BKVB, d["attn_b_qkv"][j, 1024:1536].partition_broadcast(128), [], [BBKV])
        na = 10 + (1 if sg == 1 else 0)
        tiles = [("A", a, a * 128, 128) for a in range(na)] + [("B", b, 64 + b * 128, 128) for b in range(10)]
        if sg == 1:
            tiles.append(("B", 10, 64 + 10 * 128, 64))
        for (kind, idx, c0, m) in tiles:
            ps, pb = self.PS.next()
            hr = hb(c0, min(c0 + m, 128 + self.nsg(sg)))
            for k in range(8):
                self.mm(ps[0:m, :], HT[:, k, c0:c0 + m], WKV[:, k, :], k == 0, k == 7, hr + [BWKV], [pb])
            dst = (VA if kind == "A" else VB)[0:m, idx, :]
            self.tt(dst, ps[0:m, 256:512], BKVB[0:m, 256:512], ALU.add, [pb, BBKV], [BV])
            if sg == 1 and kind == "A" and idx in (9, 10):
                oa, ob = KVO.next()
                self.tt(oa, ps, BKVB, ALU.add, [pb, BBKV], [ob])
                dk, dv = (d["pk"], d["pv"]) if idx == 9 else (d["sk"], d["sv"])
                self.finals.append(self.dma(dk[j], oa[:, 0:256], [ob], []))
                self.finals.append(self.dma(dv[j], oa[:, 256:512], [ob], []))
        if sg == 1:
            for s in range(2):
                ca, cb_ = CK.next()
                self.dma(ca, d["ck"][j, s], [], [cb_])
                ps, pb = self.PS.next()
                for jk in range(2):
                    self.tr(ps[:, jk * 128:(jk + 1) * 128], ca[:, jk * 128:(jk + 1) * 128], [cb_, self.BC], [pb])
                self.act(KTC[:, s, :, :], ps[:, 0:256].rearrange("p (k t) -> p k t", k=2), AF.Identity, [pb], [BCACHE])
                self.dma(VC[:, s, :], d["cv"][j, s], [], [BCACHE], q="pool")

        WOB = WKV.rearrange("p k c -> p (k c)")[:, 0:2 * 8 * 128].rearrange("p (o k c) -> p o k c", o=2, k=8)
        BWOB = self.mkbuf("a_wob", wkv_rng)
        wo_ = d["attn_w_o"][j]

        def wo_ap(oc):
            return (WOA[:, oc], BWOA) if oc < 6 else (WOB[:, oc - 6], BWOB)

        OT = HT
        BO = self.mkbufs(["ao0", "ao1", "ao2", "ao3"], ht_rng)
        items = [("p", c) for c in range(18)] + ([("s", 0), ("s", 1)] if sg == 1 else [])
        work = []
        for (typ, c) in items:
            if typ == "p":
                qc0 = c * 64
                w = c // 6
                gc = c + 18 * sg
                pieces = []
                if gc >= 2:
                    vfull = VA[:, c // 2, :] if c % 2 == 0 else VB[:, (c - 1) // 2, :]
                    pieces.append((c * 64, 128, vfull, 0, BV))
                    vown = VA[0:64, 1 + c // 2, :] if c % 2 == 0 else VB[0:64, (c + 1) // 2, :]
                    pieces.append((128 + c * 64, 64, vown, 1, BV))
                elif gc == 1:
                    pieces.append((64, 128, VB[:, 0, :], 0, BV, True))
                    pieces.append((128 + c * 64, 64, VB[0:64, 1, :], 1, BV))
                else:
                    pieces.append((128, 64, VA[0:64, 1, :], 1, BV))
            else:
                qc0 = NPR + c * 64
                w = 3
                vown = VA[0:64, 10, :] if c == 0 else VB[0:64, 10, :]
                pieces = [(None, 128, VC[:, c, :], 0, BCACHE), (128 + NPR + c * 64, 64, vown, 1, BV)]
            for kv in range(4):
                work.append((c, qc0, w, pieces, kv))

        def stage_a(it):
            (c, qc0, w, pieces, kv) = it
            hf, jk, j0 = kv % 2, kv // 2, (kv // 2) * 4
            rows = slice(hf * 64, hf * 64 + 64)
            rhs_q = QT[rows, j0:j0 + 4, qc0:qc0 + 64]
            pss, pbs = self.PS.next()
            et, eb = ET.next()
            pt, ptb = PT.next()
            offs = []
            off = 0
            for pc in pieces:
                (kc0, nk, vap, bt, vbuf) = pc[0:5]
                if kc0 is None:
                    lk, rk = KTC[rows, c, jk, :], [BCACHE]
                else:
                    lk, rk = KT[rows, jk, kc0:kc0 + nk], [BK]
                self.mm(pss[0:nk, off:off + 256], lk, rhs_q, True, True, rk + [BQ[w]], [pbs])
                offs.append(off)
                off += 256
            for pi, pc in enumerate(pieces):
                (kc0, nk, vap, bt, vbuf) = pc[0:5]
                o_ = offs[pi]
                self.act(et[0:nk, o_:o_ + 256], pss[0:nk, o_:o_ + 256], AF.Exp, [pbs], [eb], scale=0.125)
                self.tt(pt[0:nk, o_:o_ + 256].rearrange("p (j q) -> p j q", j=4),
                        et[0:nk, o_:o_ + 256].rearrange("p (j q) -> p j q", j=4),
                        EBT[bt][0:nk, hf, j0:j0 + 4, :], ALU.mult, [eb, BEB], [ptb], eng="pool")
                if len(pc) > 5:
                    self.memset(pt[0:64, o_:o_ + 256], 0.0, [ptb], eng="pool")
            return (pt, ptb, offs)

        def stage_b(it, st):
            (c, qc0, w, pieces, kv) = it
            (pt, ptb, offs) = st
            hf, jk, j0 = kv % 2, kv // 2, (kv // 2) * 4
            rows = slice(hf * 64, hf * 64 + 64)
            pso, pbo = self.PS.next()
            np_ = len(pieces)
            for pi, pc in enumerate(pieces):
                (kc0, nk, vap, bt, vbuf) = pc[0:5]
                o_ = offs[pi]
                self.mm(pso[:, 0:256], vap[:, jk * 128:(jk + 1) * 128], pt[0:nk, o_:o_ + 256], pi == 0, pi == np_ - 1, [vbuf, ptb], [pbo])
            for pi, pc in enumerate(pieces):
                (kc0, nk, vap, bt, vbuf) = pc[0:5]
                o_ = offs[pi]
                nks = nk + 1 if nk == 64 else nk
                self.mm(pso[:, 256:512], self.ONES1[0:nks, :], pt[0:nks, o_:o_ + 256], pi == 0, pi == np_ - 1, [self.BC, ptb], [pbo])
            dn, dnb = DEN.next()
            self.recip(dn[rows, :], pso[rows, 256:512], [pbo], [dnb])
            self.tt(OT[rows, j0:j0 + 4, qc0:qc0 + 64], pso[rows, 0:256].rearrange("p (j q) -> p j q", j=4),
                    dn[rows, :].rearrange("p (j q) -> p j q", j=4), ALU.mult, [pbo, dnb], [BO[w]])

        assert len(PT.aps) == 4 and PT.i == 0
        for s4 in range(4):
            hf4, j04 = s4 % 2, (s4 // 2) * 4
            for o4 in (0, 256):
                self.cp(PT.aps[s4][64:65, o4:o4 + 256].rearrange("p (j q) -> p j q", j=4), ESKR[64:65, hf4, j04:j04 + 4, :],
                        [BEB], [PT.bufs[s4]], eng="pool")
        DEPTH = 3
        sts = {}
        for i in range(min(DEPTH, len(work))):
            sts[i] = stage_a(work[i])
        for i in range(len(work)):
            stage_b(work[i], sts.pop(i))
            if i + DEPTH < len(work):
                sts[i + DEPTH] = stage_a(work[i + DEPTH])

        for oc in range(6, 8):
            dst, dbuf = wo_ap(oc)
            for hf in range(2):
                for grp in range(2):
                    h0 = (0, 8)[grp] if hf == 0 else (4, 12)[grp]
                    src = wo_[h0 * 64:(h0 + 4) * 64, oc * 128:(oc + 1) * 128].rearrange("(j p) c -> p j c", p=64)
                    self.dma(dst[hf * 64:(hf + 1) * 64, grp * 4:(grp + 1) * 4, :], src, [], [dbuf], q="pool")
        BY = self.mkbuf("a_y", (e0, e1))
        Y2 = self.ARENA[:, kv0:kv0 + 8 * WIN].rearrange("p (k t) -> p k t", k=8)
        BY2 = self.mkbuf("a_y2", (kv0, kv0 + 8 * WIN))
        Ys = [(Y, BY), (Y2, BY2)]
        GB, bgb = self.gain_bias(gc1, boc)
        for wi_, (w, t0, n, samp) in enumerate(self.windows(sg)):
            Yw, BYw = Ys[wi_ % 2]
            sq, sqb = self.SQ.next()
            psn = pbn = None
            pend = None
            for oc in range(8):
                wo_t, wo_b = wo_ap(oc)
                ps, pb = self.PS.next()
                for k in range(8):
                    self.mm(ps[:, 0:n], wo_t[:, k, :], OT[:, k, t0:t0 + n], k == 0, k == 7, [wo_b, BO[w]], [pb])
                if pend is not None:
                    if psn is None:
                        psn, pbn = self.PS.next()
                    self.mm(psn[:, 0:n], self.ONESM, sq[:, pend, 0:n], pend == 0, False, [sqb, self.BC], [pbn])
                bo_ap = P[:, boc + oc:boc + oc + 1]
                self.act(Yw[:, oc, 0:n], ps[:, 0:n], AF.Identity, [pb, self.BP, bgb], [BYw], bias=GB[:, oc:oc + 1],
                         scale=P[:, gc1 + oc:gc1 + oc + 1])
                self.act(sq[:, oc, 0:n], ps[:, 0:n], AF.Square, [pb, self.BP], [sqb], bias=bo_ap)
                pend = oc
            self.mm(psn[:, 0:n], self.ONESM, sq[:, 7, 0:n], False, True, [sqb, self.BC], [pbn])
            self.resid_tail_g(w, t0, n, Yw[:, :, 0:n], BYw, psn, pbn)

    def convmod(self, l, sg):
        self.scratch_reset()
        d = self.d
        P = self.PARAMS
        HT = self.carve_bf(8 * 1280).rearrange("p (k t) -> p k t", k=8)
        ht_rng = self.last_rng
        BH = self.mkbufs(["ch0", "ch1", "ch2", "ch3"])
        GW = 30 + NPR + 2 * 94
        GLU = self.carve_bf(8 * GW).rearrange("p (k t) -> p k t", k=8)
        glu_rng = self.last_rng
        BG = self.mkbufs(["cg0", "cg1", "cg2", "cg3", "cgc"])
        BGC = BG[4]
        ZB = self.carve_bf(8 * 1280).rearrange("p (k t) -> p k t", k=8)
        BZ = self.mkbufs(["cz0", "cz1", "cz2", "cz3"])
        a0 = self.aoff
        S1 = self.carve(1280)
        S2 = self.carve(1280)
        BS = self.mkbufs(["cs0", "cs1", "cs2", "cs3"], (a0, self.aoff))
        DWR = self.ring("dw", 2, 31 * 128, bf16=True)
        ZSQ = self.ring("zsq", 3, WIN, bf16=True)
        SIG = self.ring("sig", 2, WIN)
        TMP = self.ring("ctmp", 2, WIN)
        GT = self.carve(8 * 96).rearrange("p (k t) -> p k t", k=8)
        BGT = self.mkbuf("c_gt")
        GOST = self.ring("gost", 1, 1024)
        Y = self.carve(8 * WIN).rearrange("p (k t) -> p k t", k=8)
        BY = self.mkbuf("c_y")
        gc0, gc1 = self.pcol["ng"] + (l * 4 + 0) * 8, self.pcol["ng"] + (l * 4 + 1) * 8
        cb1, cwd, cbd, clg, clb, cb2, stc = (self.pcol[k] for k in ("cb1", "cwd", "cbd", "clg", "clb", "cb2", "stc"))
        self.norm_h(sg, gc0, lambda k, t0, nn, samp: HT[:, k, t0:t0 + nn], BH)
        if sg == 0:
            self.memset(GLU[:, :, 0:30], 0.0, [BGC])
        else:
            self.cp(GLU[:, :, 0:30], self.CCARRY, [self.BCARRY], [BGC])
            for s in range(2):
                src = P[:, stc + s * 240:stc + (s + 1) * 240].rearrange("p (r k) -> p k r", k=8)
                self.cp(GLU[:, :, 30 + NPR + s * 94:30 + NPR + s * 94 + 30], src, [self.BP], [BG[3]])
        w1 = d["conv_w_pw1"][0].rearrange("(k p) c -> p k c", p=128)
        for jc in range(8):
            wv, wb = self.wslot(16)
            self.load_w(wv[:, 0:8, :], w1[:, :, jc * 128:(jc + 1) * 128], wb)
            self.load_w(wv[:, 8:16, :], w1[:, :, 1024 + jc * 128:1024 + (jc + 1) * 128], wb)
            for (w, t0, nn, samp) in self.windows(sg):
                psa, pba = self.PS.next()
                psg, pbg = self.PS.next()
                for k in range(8):
                    self.mm(psa[:, 0:nn], wv[:, k, :], HT[:, k, t0:t0 + nn], k == 0, k == 7, [wb, BH[w]], [pba])
                for k in range(8):
                    self.mm(psg[:, 0:nn], wv[:, 8 + k, :], HT[:, k, t0:t0 + nn], k == 0, k == 7, [wb, BH[w]], [pbg])
                sg_, sgb = SIG.next()
                self.act(sg_[:, 0:nn], psg[:, 0:nn], AF.Sigmoid, [pbg, self.BP], [sgb], bias=P[:, cb1 + 8 + jc:cb1 + 9 + jc])
                ba = P[:, cb1 + jc:cb1 + jc + 1]
                if samp:
                    dst = GLU[:, jc, 30 + NPR:GW].rearrange("p (s q) -> p s q", s=2)[:, :, 30:94]
                    a3 = psa[:, 0:128].rearrange("p (s q) -> p s q", s=2)
                    s3 = sg_[:, 0:128].rearrange("p (s q) -> p s q", s=2)
                    self.stt(dst, a3, ba, s3, ALU.add, ALU.mult, [pba, sgb, self.BP], [BG[3]])
                    self.stt(GT[:, jc, 32:96].rearrange("p (s q) -> p s q", s=2)[:, :, 0:30], a3[:, :, 34:64], ba, s3[:, :, 34:64],
                             ALU.add, ALU.mult, [pba, sgb, self.BP], [BGT])
                else:
                    self.stt(GLU[:, jc, 30 + t0:30 + t0 + nn], psa[:, 0:nn], ba, sg_[:, 0:nn], ALU.add, ALU.mult, [pba, sgb, self.BP], [BG[w]])
                    if sg == 1 and w == 2:
                        self.stt(GT[:, jc, 0:30], psa[:, nn - 30:nn], ba, sg_[:, nn - 30:nn], ALU.add, ALU.mult, [pba, sgb, self.BP], [BGT])
        if sg == 0:
            self.cp(self.CCARRY, GLU[:, :, NPR:NPR + 30], [BG[2]], [self.BCARRY])
        else:
            for seg in range(3):
                c0 = 0 if seg == 0 else 32 * seg
                oa, ob = GOST.next()
                for h in range(2):
                    ps, pb = self.PS.next()
                    for kk in range(4):
                        self.tr(ps[0:30, kk * 128:(kk + 1) * 128], GT[:, h * 4 + kk, c0:c0 + 30], [BGT, self.BC], [pb])
                    self.act(oa[0:30, h * 512:(h + 1) * 512], ps[0:30, :], AF.Identity, [pb], [ob])
                dst = d["pconv"] if seg == 0 else d["sconv"][seg - 1]
                self.finals.append(self.dma(dst[:, :], oa[0:30, :], [ob], []))
        C = HT
        BCt = self.mkbufs(["cc0", "cc1", "cc2", "cc3"], ht_rng)
        pend_stats = None

        def emit_stats(jc, w, t0, nn, zq, zqb):
            ps1, pb1 = self.PS.next()
            self.mm(ps1[:, 0:nn], self.ONESM, ZB[:, jc, t0:t0 + nn], True, True, [self.BC, BZ[w]], [pb1])
            ps2, pb2 = self.PS.next()
            self.mm(ps2[:, 0:nn], self.ONESM, zq[:, 0:nn], True, True, [self.BC, zqb], [pb2])
            if jc == 0:
                self.cp(S1[:, t0:t0 + nn], ps1[:, 0:nn], [pb1], [BS[w]])
                self.cp(S2[:, t0:t0 + nn], ps2[:, 0:nn], [pb2], [BS[w]])
            else:
                self.tt(S1[:, t0:t0 + nn], S1[:, t0:t0 + nn], ps1[:, 0:nn], ALU.add, [pb1, BS[w]], [BS[w]])
                self.tt(S2[:, t0:t0 + nn], S2[:, t0:t0 + nn], ps2[:, 0:nn], ALU.add, [pb2, BS[w]], [BS[w]])
        for jc in range(8):
            dw_, dwb = DWR.next()
            dw = dw_.rearrange("p (k c) -> p k c", k=31)
            wk = P[:, cwd + jc:cwd + jc + 31 * 8].rearrange("p (k j) -> p k j", j=8)[:, :, 0:1]
            self.tt(dw, self.IDENT.unsqueeze(1).to_broadcast([128, 31, 128]), wk.to_broadcast([128, 31, 128]), ALU.mult,
                    [self.BC, self.BP], [dwb])
            for (w, t0, nn, samp) in self.windows(sg):
                ps, pb = self.PS.next()
                gr = [BG[3]] if samp else [BG[w], (BGC if w == 0 else BG[w - 1])]
                for k in range(31):
                    if samp:
                        rhs = GLU[:, jc, 30 + NPR:GW].rearrange("p (s q) -> p s q", s=2)[:, :, k:k + 64]
                    else:
                        rhs = GLU[:, jc, t0 + k:t0 + k + nn]
                    self.mm(ps[:, 0:nn], dw[:, k, :], rhs, k == 0, k == 30, [dwb] + gr, [pb])
                bd = P[:, cbd + jc:cbd + jc + 1]
                self.act(ZB[:, jc, t0:t0 + nn], ps[:, 0:nn], AF.Identity, [pb, self.BP], [BZ[w]], bias=bd)
                zq, zqb = ZSQ.next()
                self.act(zq[:, 0:nn], ps[:, 0:nn], AF.Square, [pb, self.BP], [zqb], bias=bd)
                if pend_stats is not None:
                    emit_stats(*pend_stats)
                pend_stats = (jc, w, t0, nn, zq, zqb)
        emit_stats(*pend_stats)
        for (w, t0, nn, samp) in self.windows(sg):
            tm, tmb = TMP.next()
            self.tt(tm[:, 0:nn], S1[:, t0:t0 + nn], S1[:, t0:t0 + nn], ALU.mult, [BS[w]], [tmb])
            self.tt(S2[:, t0:t0 + nn], S2[:, t0:t0 + nn], tm[:, 0:nn], ALU.subtract, [BS[w], tmb], [BS[w]])
            self.act(S2[:, t0:t0 + nn], S2[:, t0:t0 + nn], AF.Sqrt, [BS[w], self.BC], [BS[w]], bias=self.EPS[:, 1:2])
            self.recip(S2[:, t0:t0 + nn], S2[:, t0:t0 + nn], [BS[w]], [BS[w]])
            for jc in range(8):
                tm, tmb = TMP.next()
                self.tt(tm[:, 0:nn], ZB[:, jc, t0:t0 + nn], S1[:, t0:t0 + nn], ALU.subtract, [BZ[w], BS[w]], [tmb])
                self.tt(tm[:, 0:nn], tm[:, 0:nn], S2[:, t0:t0 + nn], ALU.mult, [tmb, BS[w]], [tmb])
                self.act(C[:, jc, t0:t0 + nn], tm[:, 0:nn], AF.Silu, [tmb, self.BP], [BCt[w]],
                         bias=P[:, clb + jc:clb + jc + 1], scale=P[:, clg + jc:clg + jc + 1])
        w2 = d["conv_w_pw2"][0].rearrange("(k p) c -> p k c", p=128)
        assert glu_rng[1] - glu_rng[0] >= 8 * WIN
        Y2 = self.ARENA[:, glu_rng[0]:glu_rng[0] + 8 * WIN].rearrange("p (k t) -> p k t", k=8)
        BY2 = self.mkbuf("c_y2", (glu_rng[0], glu_rng[0] + 8 * WIN))
        self.out_linear(sg, 8, lambda k, t0, nn: C[:, k, t0:t0 + nn],
                        lambda dst, oc, wb: self.load_w(dst, w2[:, :, oc * 128:(oc + 1) * 128], wb),
                        cb2, gc1, BCt, [(Y, BY), (Y2, BY2)], None)

    def cmlp(self, l, sg):
        self.scratch_reset()
        d = self.d
        P = self.PARAMS
        WV = self.carve_bf(8 * 2048).rearrange("p (k c) -> p k c", k=8)
        BWV = self.mkbuf("m_wv")
        a0 = self.aoff
        LNG = self.carve(2048)
        LNB = self.carve(2048)
        BVB = self.carve(2048)
        BLN = self.mkbuf("m_ln", (a0, self.aoff))
        HT = self.carve_bf(8 * WIN).rearrange("p (k t) -> p k t", k=8)
        BH = self.mkbuf("m_ht")
        U = self.carve_bf(16 * WIN).rearrange("p (k t) -> p k t", k=16)
        BU = self.mkbuf("m_u")
        VT = self.carve_bf(3 * 2048).rearrange("p (a c) -> p a c", a=3)
        BVT = self.mkbufs(["m_vt0", "m_vt1", "m_vt2"])
        VR = self.carve(2048)
        BVR = self.mkbuf("m_vr")
        a0 = self.aoff
        STAT = self.carve(4 * 6).rearrange("p (a b) -> p a b", a=4)
        MV = self.carve(4)
        BST = self.mkbuf("m_st", (a0, self.aoff))
        WSIN = self.carve(4 * 128).rearrange("p (g j) -> p g j", g=4)
        wsin_rng = self.last_rng
        BWSI = self.mkbuf("m_wsi")
        WST = self.carve_bf(2 * 4 * 128).rearrange("p (v g i) -> p v g i", v=2, g=4)
        BWS = self.mkbuf("m_ws")
        a0 = self.aoff
        BSR = self.carve(4 * 128).rearrange("p (g i) -> p g i", g=4)
        BSH = self.carve_bf(2 * 4 * 128).rearrange("p (v g i) -> p v g i", v=2, g=4)
        BBS = self.mkbuf("m_bs", (a0, self.aoff))
        Y = self.carve(8 * WIN).rearrange("p (k t) -> p k t", k=8)
        BY = self.mkbuf("m_y")
        gc0, gc1 = self.pcol["ng"] + (l * 4 + 0) * 8, self.pcol["ng"] + (l * 4 + 1) * 8
        mbu, mbo = self.pcol["mbu"], self.pcol["mbo"]
        wi = d["cmlp_w_in"][0].rearrange("(k p) c -> p k c", p=128)
        wo_ = d["cmlp_w_out"][0].rearrange("(k p) c -> p k c", p=128)
        for cb in range(4):
            self.dma(WV[:, :, cb * 512:(cb + 1) * 512], wi[:, :, 2048 + cb * 512:2048 + (cb + 1) * 512], [], [BWV], q="pool")
        self.dma(LNG, d["cmlp_ln_g"][0].partition_broadcast(128), [], [BLN])
        self.dma(LNB, d["cmlp_ln_b"][0].partition_broadcast(128), [], [BLN])
        self.dma(BVB, d["cmlp_b_in"][0, 2048:4096].partition_broadcast(128), [], [BLN])
        ws = d["cmlp_w_s"][0]
        bs = d["cmlp_b_s"][0]
        for v in range(2 if sg == 1 else 1):
            if v == 0:
                self.dma(WSIN, ws.rearrange("g i j -> i g j"), [BWSI], [BWSI])
            else:
                self.memset(WSIN, 0.0, [BWSI])
                self.dma(WSIN[0:64, :, 0:64], ws[:, 0:64, 0:64].rearrange("g i j -> i g j"), [BWSI], [BWSI])
                self.dma(WSIN[64:128, :, 64:128], ws[:, 0:64, 0:64].rearrange("g i j -> i g j"), [BWSI], [BWSI])
            ps, pb = self.PS.next()
            for g in range(4):
                self.tr(ps[:, g * 128:(g + 1) * 128], WSIN[:, g, :], [BWSI, self.BC], [pb])
            self.act(WST[:, v, :, :], ps.rearrange("p (g i) -> p g i", g=4), AF.Identity, [pb], [BWS])
            if v == 0:
                self.memset(WST[64:128, 0, :, 0:64], 0.0, [BWS])

        BSL = self.ARENA[:, wsin_rng[0]:wsin_rng[1]].bitcast(BF16).rearrange("p (v g i) -> p v g i", v=2, g=4)
        BBL = self.mkbuf("m_bsl", wsin_rng)
        for v in range(2 if sg == 1 else 1):
            if v == 0:
                self.dma(BSR[0:1, :, :], bs.rearrange("g i -> (g i)").rearrange("(o g i) -> o g i", o=1, g=4), [BBS], [BBS])
            else:
                self.dma(BSR[0:1, :, 0:64], bs[:, 0:64].unsqueeze(0), [BBS], [BBS])
                self.dma(BSR[0:1, :, 64:128], bs[:, 0:64].unsqueeze(0), [BBS], [BBS])
            self.cp(BSH[0:1, v], BSR[0:1], [BBS], [BBS])
            self.tt(BSR[0:1], BSR[0:1], BSH[0:1, v], ALU.subtract, [BBS], [BBS])
            self.cp(BSL[0:1, v], BSR[0:1], [BBS], [BBL])
        def make_ht(wn):
            (w_, t0_, nn_, samp_) = wn
            rs, rb = self.rstd_of(self.X[:, :, t0_:t0_ + nn_], nn_, [self.BX[w_]])
            for k in range(8):
                self.stt(HT[:, k, 0:nn_], self.X[:, k, t0_:t0_ + nn_], P[:, gc0 + k:gc0 + k + 1], rs[:, 0:nn_], ALU.mult, ALU.mult,
                         [self.BX[w_], rb, self.BP], [BH])
        wlist = self.windows(sg)
        GB, bgb = self.gain_bias(gc1, mbo)
        make_ht(wlist[0])
        for widx, (w, t0, nn, samp) in enumerate(wlist):
            var = 1 if samp else 0
            def u_chunk(jc):
                wv, wb = self.wslot(8)
                self.load_w(wv, wi[:, :, jc * 128:(jc + 1) * 128], wb)
                ps, pb = self.PS.next()
                for k in range(8):
                    self.mm(ps[:, 0:nn], wv[:, k, :], HT[:, k, 0:nn], k == 0, k == 7, [wb, BH], [pb])
                self.act(U[:, jc, 0:nn], ps[:, 0:nn], AF.Gelu_apprx_tanh, [pb, self.BP], [BU], bias=P[:, mbu + jc:mbu + jc + 1])

            def v_tile(t):
                for cb in range(4):
                    ps, pb = self.PS.next()
                    for k in range(8):
                        self.mm(ps, HT[:, k, t * 128:(t + 1) * 128], WV[:, k, cb * 512:(cb + 1) * 512], k == 0, k == 7, [BH, BWV], [pb])
                    vs = VR[:, cb * 512:(cb + 1) * 512]
                    self.tt(vs, ps, BVB[:, cb * 512:(cb + 1) * 512], ALU.add, [pb, BLN], [BVR])
                    self.act(vs, vs, AF.Gelu_apprx_tanh, [BVR], [BVR])
                    self.S.op("dve", (lambda o, i: lambda e: e.bn_stats(out=o, in_=i))(STAT[:, cb, :], vs), [BVR], [BST])
                self.S.op("dve", lambda e: e.bn_aggr(out=MV[:, 0:2], in_=STAT), [BST], [BST])
                self.act(MV[:, 2:3], MV[:, 1:2], AF.Sqrt, [BST, self.BC], [BST], bias=self.EPS[:, 1:2])
                self.recip(MV[:, 2:3], MV[:, 2:3], [BST], [BST])
                self.stt(MV[:, 3:4], MV[:, 0:1], -1.0, MV[:, 2:3], ALU.mult, ALU.mult, [BST], [BST])
                self.act(VR, VR, AF.Identity, [BVR, BST], [BVR], bias=MV[:, 3:4], scale=MV[:, 2:3])
                self.tt(VR, VR, LNG, ALU.mult, [BVR, BLN], [BVR])
                if samp:
                    self.tt(VR, VR, LNB, ALU.add, [BVR, BLN], [BVR])
                    self.cp(VT[:, t, :], VR, [BVR], [BVT[t]])
                    self.finals.append(self.dma(d["scv"][:, :], VR, [BVR], []))
                else:
                    self.tt(VT[:, t, :], VR, LNB, ALU.add, [BVR, BLN], [BVT[t]])

            def spatial(t):
                for g in range(4):
                    ps, pb = self.PS.next()
                    for cc in range(4):
                        ch = g * 4 + cc
                        self.mm(ps[:, cc * 128:(cc + 1) * 128], VT[:, t, ch * 128:(ch + 1) * 128], WST[:, var, g, :], True, False, [BVT[t], BWS], [pb])
                        self.mm(ps[:, cc * 128:(cc + 1) * 128], self.ONES1[0:1, :], BSH[0:1, var, g, :], False, False, [self.BC, BBS], [pb])
                        self.mm(ps[:, cc * 128:(cc + 1) * 128], self.ONES1[0:1, :], BSL[0:1, var, g, :], False, True, [self.BC, BBL], [pb])
                    uu = U[:, g * 4:(g + 1) * 4, t * 128:(t + 1) * 128]
                    self.tt(uu, ps.rearrange("p (c i) -> p c i", c=4), uu, ALU.mult, [pb, BU], [BU])

            nt = nn // 128
            sched_u = {0: range(0, 6), 1: range(6, 11), 2: range(11, 16)} if nt == 3 else {0: range(0, 16)}
            for t in range(nt):
                v_tile(t)
                for jc in sched_u[t]:
                    u_chunk(jc)
            for t in range(nt):
                spatial(t)
            if widx + 1 < len(wlist):
                make_ht(wlist[widx + 1])
            sq, sqb = self.SQ.next()
            psn = pbn = None
            pend = None
            for oc in range(8):
                wv, wb = self.wslot(16)
                self.load_w(wv, wo_[:, :, oc * 128:(oc + 1) * 128], wb)
                ps, pb = self.PS.next()
                for k in range(16):
                    self.mm(ps[:, 0:nn], wv[:, k, :], U[:, k, 0:nn], k == 0, k == 15, [wb, BU], [pb])
                if pend is not None:
                    if psn is None:
                        psn, pbn = self.PS.next()
                    self.mm(psn[:, 0:nn], self.ONESM, sq[:, pend, 0:nn], pend == 0, False, [sqb, self.BC], [pbn])
                b_ap = P[:, mbo + oc:mbo + oc + 1]
                self.act(Y[:, oc, 0:nn], ps[:, 0:nn], AF.Identity, [pb, self.BP, bgb], [BY], bias=GB[:, oc:oc + 1],
                         scale=P[:, gc1 + oc:gc1 + oc + 1])
                self.act(sq[:, oc, 0:nn], ps[:, 0:nn], AF.Square, [pb, self.BP], [sqb], bias=b_ap)
                pend = oc
            self.mm(psn[:, 0:nn], self.ONESM, sq[:, 7, 0:nn], False, True, [sqb, self.BC], [pbn])
            self.resid_tail_g(w, t0, nn, Y[:, :, 0:nn], BY, psn, pbn)


_PROG = None
_OH = None


def _get_prog():
    global _PROG, _OH
    if _PROG is None:
        p = Prog()
        p.build()
        _PROG = p
        _OH = onehot2_const()
    return _PROG


WEIGHT_KEYS = ["rel_bias_table", "norm_gain", "attn_w_qkv", "attn_b_qkv", "attn_w_o", "attn_b_o", "attn_sinks",
               "conv_w_pw1", "conv_b_pw1", "conv_w_dw", "conv_b_dw", "conv_ln_g", "conv_ln_b", "conv_w_pw2",
               "conv_b_pw2", "cmlp_w_in", "cmlp_b_in", "cmlp_ln_g", "cmlp_ln_b", "cmlp_w_s", "cmlp_b_s",
               "cmlp_w_out", "cmlp_b_out", "ffn_w_up", "ffn_w_dw", "ffn_b_dw", "ffn_w_down"]


def kernel(**inputs):
    p = _get_prog()
    f = lambda a: np.ascontiguousarray(np.asarray(a, dtype=np.float32))
    xp, xs = f(inputs["x_prompt"]), f(inputs["x_sample"])
    cak, cav = f(inputs["cache_attn_k"]), f(inputs["cache_attn_v"])
    stc, stf = f(inputs["state_conv"]), f(inputs["state_ffn_conv"])
    wts = {k: f(inputs[k]) for k in WEIGHT_KEYS}
    in_maps = []
    for c in range(8):
        b, hf = c // 2, c % 2
        s0 = 0 if hf == 0 else 4096 - TP
        m = dict(wts)
        m["xin"] = np.ascontiguousarray(np.concatenate([xp[b, s0:s0 + TP], xs[2 * c], xs[2 * c + 1]], axis=0))
        m["ck"] = np.ascontiguousarray(cak[:, 2 * c:2 * c + 2].reshape(2, 2, 128, 256))
        m["cv"] = np.ascontiguousarray(cav[:, 2 * c:2 * c + 2].reshape(2, 2, 128, 256))
        m["stc"] = np.ascontiguousarray(stc[0, 2 * c:2 * c + 2])
        m["stf"] = np.ascontiguousarray(stf[:, 2 * c:2 * c + 2])
        m["oh"] = _OH
        in_maps.append(m)
    res = run_bass_kernel_spmd(p.nc, in_maps, core_ids=list(range(8)))
    R = res.results
    y_prompt = np.zeros((4, 4096, 1024), np.float32)
    y_sample = np.zeros((16, 64, 1024), np.float32)
    p_k = np.zeros((2, 4, 128, 4, 64), np.float32)
    p_v = np.zeros((2, 4, 128, 4, 64), np.float32)
    p_conv = np.zeros((1, 4, 30, 1024), np.float32)
    p_ffn = np.zeros((4, 4, 2, 5632), np.float32)
    s_k = np.zeros((2, 16, 64, 4, 64), np.float32)
    s_v = np.zeros((2, 16, 64, 4, 64), np.float32)
    s_conv = np.zeros((1, 16, 30, 1024), np.float32)
    s_cv = np.zeros((1, 16, 64, 2048), np.float32)
    s_ffn = np.zeros((4, 16, 2, 5632), np.float32)
    for c in range(8):
        b, hf = c // 2, c % 2
        r = R[c]
        yo = np.asarray(r["yout"])
        if hf == 0:
            y_prompt[b, 0:TP] = yo[0:TP]
        else:
            y_prompt[b, TP:4096] = yo[2 * TP - 4096:TP]
            p_k[:, b] = np.asarray(r["pk"]).reshape(2, 128, 4, 64)
            p_v[:, b] = np.asarray(r["pv"]).reshape(2, 128, 4, 64)
            p_conv[0, b] = np.asarray(r["pconv"])
            p_ffn[:, b] = np.asarray(r["pffn"])
        y_sample[2 * c] = yo[TP:TP + 64]
        y_sample[2 * c + 1] = yo[TP + 64:TP + 128]
        s_k[:, 2 * c:2 * c + 2] = np.asarray(r["sk"]).reshape(2, 2, 64, 4, 64)
        s_v[:, 2 * c:2 * c + 2] = np.asarray(r["sv"]).reshape(2, 2, 64, 4, 64)
        s_conv[0, 2 * c:2 * c + 2] = np.asarray(r["sconv"])
        s_cv[0, 2 * c:2 * c + 2] = np.asarray(r["scv"]).reshape(2, 64, 2048)
        s_ffn[:, 2 * c:2 * c + 2] = np.asarray(r["sffn"])
    return (y_prompt, y_sample, p_k, p_v, p_conv, p_ffn, s_k, s_v, s_conv, s_cv, s_ffn)
```
